# Optimizing a Trainium2 kernel written in Bass

```python
import math
import jax, jax.numpy as jnp
from jax import lax
import numpy as np

D_MODEL = 1024
BATCH = 1
SEQ = 16384
DEPTH = 2

N_MIXERS = 2
RET_HEADS = 4
RET_QK_DIM = D_MODEL // RET_HEADS
RET_V_DIM = 2 * D_MODEL // RET_HEADS
RET_CHUNK = 128
ROPE_BASE = 10000.0
MLSTM_HEADS = 4
MLSTM_INNER = 2 * D_MODEL
MLSTM_QK_DIM = D_MODEL // MLSTM_HEADS
MLSTM_V_DIM = MLSTM_INNER // MLSTM_HEADS
MLSTM_CONV = 4
MLSTM_CHUNK = 128
D_FF = 2816
FFN_CONV = 3
EPS = 1e-6

N_RET = (DEPTH + 1) // 2
N_MLSTM = DEPTH // 2
RET_IN = 2 * RET_HEADS * RET_QK_DIM + 2 * RET_HEADS * RET_V_DIM
MLSTM_IN = 2 * MLSTM_HEADS * MLSTM_QK_DIM + 2 * MLSTM_INNER + 2 * MLSTM_HEADS

kernel_name = "hybrid_retention_mlstm_convffn"


def rmsnorm(x, g):
    xf = x.astype(jnp.float32)
    xn = xf * lax.rsqrt(jnp.mean(xf * xf, axis=-1, keepdims=True) + EPS)
    return xn.astype(x.dtype) * g


def head_groupnorm(y, g):
    yf = y.astype(jnp.float32)
    mu = jnp.mean(yf, axis=-1, keepdims=True)
    var = jnp.mean((yf - mu) ** 2, axis=-1, keepdims=True)
    yn = (yf - mu) * lax.rsqrt(var + EPS)
    Bt, S = y.shape[0], y.shape[1]
    return yn.reshape(Bt, S, -1) * g


def causal_dwconv(x, w, b):
    K = w.shape[0]
    S = x.shape[1]
    xp = jnp.pad(x, ((0, 0), (K - 1, 0), (0, 0)))
    out = b + xp[:, 0:S] * w[0]
    for j in range(1, K):
        out = out + xp[:, j:j + S] * w[j]
    return out


def rope(t, positions):
    d = t.shape[-1]
    inv_freq = ROPE_BASE ** (-jnp.arange(0, d, 2, dtype=jnp.float32) / d)
    ang = positions.astype(jnp.float32)[..., None] * inv_freq
    cos = jnp.cos(ang)[:, :, None, :].astype(t.dtype)
    sin = jnp.sin(ang)[:, :, None, :].astype(t.dtype)
    t1, t2 = t[..., : d // 2], t[..., d // 2:]
    return jnp.concatenate([t1 * cos - t2 * sin, t1 * sin + t2 * cos], axis=-1)


def to_chunks(t, C):
    Bt, S, H, d = t.shape
    return t.reshape(Bt, S // C, C, H, d).transpose(1, 0, 3, 2, 4)


def gates_to_chunks(t, C):
    Bt, S, H = t.shape
    return t.reshape(Bt, S // C, C, H).transpose(1, 0, 3, 2)


def from_chunks(t):
    nC, Bt, H, C, d = t.shape
    return t.transpose(1, 0, 3, 2, 4).reshape(Bt, nC * C, H, d)


def retention_mixer(h, positions, w_in, gn_g, w_out):
    Bt, S, _ = h.shape
    H, dk, dv, C = RET_HEADS, RET_QK_DIM, RET_V_DIM, RET_CHUNK
    proj = h @ w_in
    q, k, v, g = jnp.split(proj, [H * dk, 2 * H * dk, 2 * H * dk + H * dv], axis=-1)
    q = rope(q.reshape(Bt, S, H, dk), positions)
    k = rope(k.reshape(Bt, S, H, dk), positions) * (dk ** -0.5)
    v = v.reshape(Bt, S, H, dv)
    log_gamma = jnp.log(1.0 - 2.0 ** (-5.0 - jnp.arange(H, dtype=jnp.float32)))
    idx = jnp.arange(C, dtype=jnp.float32)
    rel = idx[:, None] - idx[None, :]
    decay = jnp.where(rel >= 0, jnp.exp(rel[None] * log_gamma[:, None, None]), 0.0)
    xi = jnp.exp((idx + 1.0)[None] * log_gamma[:, None])
    zeta = jnp.exp((C - 1.0 - idx)[None] * log_gamma[:, None])
    gamma_C = jnp.exp(C * log_gamma)

    def step(R, inp):
        qc, kc, vc = inp
        s = jnp.einsum('bhid,bhjd->bhij', qc, kc) * decay
        intra = jnp.einsum('bhij,bhjv->bhiv', s, vc)
        cross = jnp.einsum('bhid,bhdv->bhiv', qc, R) * xi[:, :, None]
        R_new = R * gamma_C[:, None, None] + jnp.einsum('bhjd,bhjv->bhdv', kc * zeta[:, :, None], vc)
        return R_new, intra + cross

    R0 = jnp.zeros((Bt, H, dk, dv), jnp.float32)
    _, ys = lax.scan(step, R0, (to_chunks(q, C), to_chunks(k, C), to_chunks(v, C)))
    y = head_groupnorm(from_chunks(ys), gn_g).astype(h.dtype)
    return (jax.nn.silu(g) * y) @ w_out


def mlstm_mixer(h, w_in, b_gate, conv_w, conv_b, gn_g, w_out):
    Bt, S, _ = h.shape
    H, dk, dv, C = MLSTM_HEADS, MLSTM_QK_DIM, MLSTM_V_DIM, MLSTM_CHUNK
    proj = h @ w_in
    qk, v, o_pre, gate_pre = jnp.split(
        proj, [2 * H * dk, 2 * H * dk + MLSTM_INNER, 2 * H * dk + 2 * MLSTM_INNER], axis=-1)
    qk = jax.nn.silu(causal_dwconv(qk, conv_w, conv_b))
    q, k = jnp.split(qk, 2, axis=-1)
    q = q.reshape(Bt, S, H, dk)
    k = k.reshape(Bt, S, H, dk) * (dk ** -0.5)
    v = v.reshape(Bt, S, H, dv)
    gate_pre = gate_pre.astype(jnp.float32) + b_gate.astype(jnp.float32)
    log_i = gate_pre[..., :H]
    log_f = jax.nn.log_sigmoid(gate_pre[..., H:])
    g_cum = lax.cumsum(gates_to_chunks(log_f, C), axis=3)
    ic_all = gates_to_chunks(log_i, C)
    causal = jnp.tril(jnp.ones((C, C), dtype=bool))

    def step(carry, inp):
        Cm, n, m = carry
        qc, kc, vc, gc, ic = inp
        a = gc + m[..., None]
        dlog = gc[..., :, None] - gc[..., None, :] + ic[..., None, :]
        dlog = jnp.where(causal, dlog, -jnp.inf)
        m_row = jnp.maximum(a, jnp.max(dlog, axis=-1))
        w_intra = jnp.exp(dlog - m_row[..., None])
        w_inter = jnp.exp(a - m_row)
        s = jnp.einsum('bhid,bhjd->bhij', qc, kc) * w_intra
        num = jnp.einsum('bhij,bhjv->bhiv', s, vc) + w_inter[..., None] * jnp.einsum('bhid,bhdv->bhiv', qc, Cm)
        den = jnp.sum(s, axis=-1) + w_inter * jnp.einsum('bhid,bhd->bhi', qc, n)
        h_til = num / jnp.maximum(jnp.abs(den), jnp.exp(-m_row))[..., None]
        gL = gc[..., -1]
        dec_end = gL[..., None] - gc + ic
        m_new = jnp.maximum(gL + m, jnp.max(dec_end, axis=-1))
        w_state = jnp.exp(dec_end - m_new[..., None])
        scale_prev = jnp.exp(gL + m - m_new)
        Cm_new = scale_prev[..., None, None] * Cm + jnp.einsum('bhj,bhjd,bhjv->bhdv', w_state, kc, vc)
        n_new = scale_prev[..., None] * n + jnp.einsum('bhj,bhjd->bhd', w_state, kc)
        return (Cm_new, n_new, m_new), h_til

    carry0 = (jnp.zeros((Bt, H, dk, dv), jnp.float32),
              jnp.zeros((Bt, H, dk), jnp.float32),
              jnp.zeros((Bt, H), jnp.float32))
    _, hs = lax.scan(step, carry0, (to_chunks(q, C), to_chunks(k, C), to_chunks(v, C), g_cum, ic_all))
    h_til = from_chunks(hs)
    o = jax.nn.sigmoid(o_pre.astype(jnp.float32)).reshape(Bt, S, H, dv)
    y = head_groupnorm(o * h_til, gn_g).astype(h.dtype)
    return y @ w_out


def conv_ffn(h, w_up, conv_w, conv_b, w_down):
    u = causal_dwconv(h @ w_up, conv_w, conv_b)
    a, b = jnp.split(u, 2, axis=-1)
    return (jax.nn.silu(a) * b) @ w_down


def setup_inputs(seed: int = 0) -> dict:
    key = jax.random.key(seed)
    ks = jax.random.split(key, 24)
    f32 = jnp.float32
    nrm = lambda k, shape, s: jax.random.normal(k, shape, f32) * s
    D = D_MODEL
    x = nrm(ks[0], (BATCH, SEQ, D), 1.0)
    c = nrm(ks[1], (BATCH, D), 1.0)
    positions = jnp.broadcast_to(jnp.arange(SEQ, dtype=jnp.int32), (BATCH, SEQ))
    ada_w = nrm(ks[2], (DEPTH, D, 6 * D), 0.5 * D ** -0.5)
    ada_b = nrm(ks[3], (DEPTH, 6 * D), 0.02)
    norm_tok_g = 1.0 + nrm(ks[4], (DEPTH, D), 0.02)
    norm_ffn_g = 1.0 + nrm(ks[5], (DEPTH, D), 0.02)
    ret_w_in = nrm(ks[6], (N_RET, D, RET_IN), D ** -0.5)
    ret_gn_g = 1.0 + nrm(ks[7], (N_RET, RET_HEADS * RET_V_DIM), 0.02)
    ret_w_out = nrm(ks[8], (N_RET, RET_HEADS * RET_V_DIM, D), (RET_HEADS * RET_V_DIM) ** -0.5)
    ml_w_in = nrm(ks[9], (N_MLSTM, D, MLSTM_IN), D ** -0.5)
    i_bias = nrm(ks[10], (N_MLSTM, MLSTM_HEADS), 0.1)
    f_bias = jnp.linspace(3.0, 6.0, MLSTM_HEADS, dtype=f32)[None] + nrm(ks[11], (N_MLSTM, MLSTM_HEADS), 0.1)
    ml_b_gate = jnp.concatenate([i_bias, f_bias], axis=-1)
    qk_width = 2 * MLSTM_HEADS * MLSTM_QK_DIM
    ml_conv_w = nrm(ks[12], (N_MLSTM, MLSTM_CONV, qk_width), MLSTM_CONV ** -0.5)
    ml_conv_b = nrm(ks[13], (N_MLSTM, qk_width), 0.02)
    ml_gn_g = 1.0 + nrm(ks[14], (N_MLSTM, MLSTM_INNER), 0.02)
    ml_w_out = nrm(ks[15], (N_MLSTM, MLSTM_INNER, D), MLSTM_INNER ** -0.5)
    ffn_w_up = nrm(ks[16], (DEPTH, D, 2 * D_FF), D ** -0.5)
    ffn_conv_w = nrm(ks[17], (DEPTH, FFN_CONV, 2 * D_FF), FFN_CONV ** -0.5)
    ffn_conv_b = nrm(ks[18], (DEPTH, 2 * D_FF), 0.02)
    ffn_w_down = nrm(ks[19], (DEPTH, D_FF, D), D_FF ** -0.5)
    final_g = 1.0 + nrm(ks[20], (D,), 0.02)
    return {"x": x, "c": c, "positions": positions,
            "ada_w": ada_w, "ada_b": ada_b, "norm_tok_g": norm_tok_g, "norm_ffn_g": norm_ffn_g,
            "ret_w_in": ret_w_in, "ret_gn_g": ret_gn_g, "ret_w_out": ret_w_out,
            "ml_w_in": ml_w_in, "ml_b_gate": ml_b_gate, "ml_conv_w": ml_conv_w, "ml_conv_b": ml_conv_b,
            "ml_gn_g": ml_gn_g, "ml_w_out": ml_w_out,
            "ffn_w_up": ffn_w_up, "ffn_conv_w": ffn_conv_w, "ffn_conv_b": ffn_conv_b, "ffn_w_down": ffn_w_down,
            "final_g": final_g}


def reference(x, c, positions, ada_w, ada_b, norm_tok_g, norm_ffn_g,
              ret_w_in, ret_gn_g, ret_w_out,
              ml_w_in, ml_b_gate, ml_conv_w, ml_conv_b, ml_gn_g, ml_w_out,
              ffn_w_up, ffn_conv_w, ffn_conv_b, ffn_w_down, final_g):
    c_act = jax.nn.silu(c)
    for i in range(DEPTH):
        mod = c_act @ ada_w[i] + ada_b[i]
        sh_t, sc_t, gt_t, sh_f, sc_f, gt_f = [m[:, None, :] for m in jnp.split(mod, 6, axis=-1)]
        h = rmsnorm(x, norm_tok_g[i]) * (1.0 + sc_t) + sh_t
        j = i // N_MIXERS
        if i % N_MIXERS == 0:
            y = retention_mixer(h, positions, ret_w_in[j], ret_gn_g[j], ret_w_out[j])
        else:
            y = mlstm_mixer(h, ml_w_in[j], ml_b_gate[j], ml_conv_w[j], ml_conv_b[j], ml_gn_g[j], ml_w_out[j])
        x = x + gt_t * y
        h = rmsnorm(x, norm_ffn_g[i]) * (1.0 + sc_f) + sh_f
        x = x + gt_f * conv_ffn(h, ffn_w_up[i], ffn_conv_w[i], ffn_conv_b[i], ffn_w_down[i])
    return rmsnorm(x, final_g)
```

```python
import contextlib
import math
import numpy as np
import concourse.bass as bass
import concourse.mybir as mybir
from concourse.bass_utils import run_bass_kernel_spmd

F32 = mybir.dt.float32
BF16 = mybir.dt.bfloat16
I32 = mybir.dt.int32
AF = mybir.ActivationFunctionType
ALU = mybir.AluOpType

NCORES = 8
SEQ = 16384
D = 1024
T = SEQ // NCORES
CH = 128
G = 4
GT = G * CH
NG = T // GT
KC = D // 128
H = 4
DK = 256
DV = 512
DFF = 2816
EPS = 1e-6
HW = 3
TWO_PI = 2.0 * math.pi
C1 = 6.28125
C2 = TWO_PI - C1
PI_SAFE = 3.1415925


class Buf:
    __slots__ = ("name", "w", "r")

    def __init__(self, name):
        self.name = name
        self.w = None
        self.r = {}


class Sched:
    ENGS = ("pe", "dve", "act", "pool", "sp")

    def __init__(self, nc, stack):
        self.nc = nc
        self.stack = stack
        self.eng = {"pe": nc.tensor, "dve": nc.vector, "act": nc.scalar, "pool": nc.gpsimd, "sp": nc.sync}
        self.sems = {}
        self.cnt = {}
        self.seen = {e: {} for e in self.ENGS}
        for e in self.ENGS:
            self.sems[e] = stack.enter_context(nc.semaphore("s_" + e))
            self.cnt[e] = 0
        self.ninst = 0

    def buf(self, name):
        return Buf(name)

    def _dma_sem(self, key):
        k = "dma_" + key
        if k not in self.sems:
            self.sems[k] = self.stack.enter_context(self.nc.semaphore("s_" + k))
            self.cnt[k] = 0
        return k

    def _wait(self, e, deps):
        need = {}
        for d in deps:
            if d is None:
                continue
            k, v = d
            if k == e and e == "pe":
                continue
            if need.get(k, 0) < v:
                need[k] = v
        for k, v in need.items():
            if self.seen[e].get(k, 0) < v:
                self.eng[e].wait_ge(self.sems[k], v)
                self.seen[e][k] = v

    @staticmethod
    def _deps(reads, writes):
        deps = []
        for b in reads:
            deps.append(b.w)
        for b in writes:
            deps.append(b.w)
            deps.extend(b.r.items())
        return deps

    @staticmethod
    def _record(ev, reads, writes):
        k, v = ev
        for b in reads:
            if b.r.get(k, 0) < v:
                b.r[k] = v
        for b in writes:
            b.w = ev
            b.r = {}

    def op(self, e, fn, reads=(), writes=()):
        self._wait(e, self._deps(reads, writes))
        ins = fn(self.eng[e])
        self.cnt[e] += 1
        ins.then_inc(self.sems[e], 1)
        self.ninst += 1
        self._record((e, self.cnt[e]), reads, writes)
        return ins

    def dma(self, q, out, in_, reads=(), writes=(), key=None, **kw):
        k = self._dma_sem(key)
        self._wait(q, self._deps(reads, writes))
        ins = self.eng[q].dma_start(out=out, in_=in_, **kw)
        self.cnt[k] += 16
        ins.then_inc(self.sems[k], 16)
        self.ninst += 1
        self._record((k, self.cnt[k]), reads, writes)
        return ins

    def wait_bufs(self, e, bufs):
        deps = []
        for b in bufs:
            deps.append(b.w)
            deps.extend(b.r.items())
        self._wait(e, deps)

    def barrier(self):
        for e in self.ENGS:
            deps = [(k, v) for k, v in self.cnt.items() if v > 0]
            self._wait(e, deps)


def _const_tables():
    t = {}
    n = np.arange(128, dtype=np.float32)
    inv_freq = (10000.0 ** (-(np.arange(0, DK, 2, dtype=np.float32)) / DK)).astype(np.float32)
    t["inv_freq"] = inv_freq.reshape(128, 1).astype(np.float32)
    lg = np.log(1.0 - 2.0 ** (-5.0 - np.arange(H, dtype=np.float64)))
    i = np.arange(128)[None, :]
    j = np.arange(128)[:, None]
    dt = np.zeros((128, H, 128), np.float64)
    for h in range(H):
        dt[:, h, :] = np.where(i >= j, np.exp((i - j) * lg[h]), 0.0) * (DK ** -0.5)
    t["ret_dt"] = dt.reshape(128, H * 128).astype(np.float32)
    t["ret_xi"] = np.exp((np.arange(128)[:, None] + 1.0) * lg[None, :]).astype(np.float32)
    t["ret_zs"] = (np.exp((127.0 - np.arange(128)[:, None]) * lg[None, :]) * (DK ** -0.5)).astype(np.float32)
    t["neg"] = np.where(j <= i, 0.0, -30000.0).astype(np.float32)
    t["ut"] = np.where(j <= i, 1.0, 0.0).astype(np.float32)
    t["ident"] = np.eye(128, dtype=np.float32)
    return t, lg


def _core_tables(c, lg):
    sel = np.zeros((128, NCORES), np.float32)
    if c > 0:
        sel[:, c - 1] = 1.0
    nf = np.full((128, 1), 0.0 if c == 0 else 1.0, np.float32)
    rc = np.zeros((128, NCORES * H), np.float32)
    for cp in range(c):
        for h in range(H):
            rc[:, cp * H + h] = np.exp(T * (c - 1 - cp) * lg[h])
    valid = np.zeros((128, NCORES * H), np.float32)
    for cp in range(c):
        valid[:, cp * H:(cp + 1) * H] = 1.0
    msel = np.zeros((NCORES, NCORES, H), np.float32)
    for cpp in range(NCORES):
        for cp in range(NCORES):
            if cp < cpp < c:
                msel[cpp, cp, :] = 1.0
    selmat = np.zeros((NCORES * HW, HW), np.float32)
    if c > 0:
        for r in range(HW):
            selmat[(c - 1) * HW + r, r] = 1.0
    return {"selmat": selmat, "sel": sel, "nf": nf, "ret_coef": rc, "valid": valid, "msel": msel.reshape(NCORES, NCORES * H)}


EX_SIZES = {1: (8 * 128, 512), 2: (HW, D), 3: (HW, D), 4: (9 * 128, 512), 5: (HW, D)}


class Prog:
    def __init__(self, mode="host", stop_after=None, phase=None):
        self.mode = mode
        self.stop_after = stop_after
        self.phase = phase
        self.in_names = []
        self.layers = [0, 1] if phase in (None, 1) else ([0] if phase <= 3 else [1])
        self.mod_layers = [0, 1] if phase in (None, 1) else []
        self.nc = bass.Bass("TRN2", target_bir_lowering=False)
        self.st = contextlib.ExitStack()
        self.debug = stop_after is not None and stop_after.startswith("dbg")
        self.dbg_bufs = []

    def din(self, name, shape, dt=F32):
        self.in_names.append(name)
        return self.nc.dram_tensor(name, list(shape), dt, kind="ExternalInput").ap()

    def dout(self, name, shape, dt=F32):
        return self.nc.dram_tensor(name, list(shape), dt, kind="ExternalOutput").ap()

    def dint(self, name, shape, dt=F32):
        return self.nc.dram_tensor(name, list(shape), dt, kind="Internal").ap()

    def sb(self, name, shape, dt=F32):
        t = self.st.enter_context(self.nc.sbuf_tensor("sb_" + name, list(shape), dt))
        b = Buf(name)
        return t, b

    def ps(self, name, shape, dt=F32):
        t = self.st.enter_context(self.nc.psum_tensor("ps_" + name, list(shape), dt))
        return t

    def build(self):
        with self.st:
            self.S = Sched(self.nc, self.st)
            self._declare()
            self._setup()
            self._body()
            self._finish()
        return self.nc

    def _declare(self):
        p = self
        p.x = p.din("x", [T, D])
        p.pos = p.din("pos", [1, T], I32)
        p.c_in = p.din("cT", [128, KC])
        p.ada_w = p.din("ada_w", [2, D, 6 * D])
        p.ada_b = p.din("ada_bT", [128, 2, 48])
        p.ntg = p.din("ntgT", [128, 2, KC])
        p.nfg = p.din("nfgT", [128, 2, KC])
        p.ret_w_in = p.din("ret_w_in", [D, 6144])
        p.ret_gn = p.din("ret_gnT", [128, 16])
        p.ret_w_out = p.din("ret_w_out", [2048, D])
        p.ml_w_in = p.din("ml_w_in", [D, 6152])
        p.ml_bg = p.din("ml_b_gate", [1, 8])
        p.ml_cw = p.din("ml_cwT", [128, 64])
        p.ml_cb = p.din("ml_cbT", [128, 16])
        p.ml_gn = p.din("ml_gnT", [128, 16])
        p.ml_w_out = p.din("ml_w_out", [2048, D])
        p.ffn_w_up = p.din("ffn_w_up", [2, D, 2 * DFF])
        p.ffn_cw = p.din("ffn_cwT", [128, 2, 132])
        p.ffn_cb = p.din("ffn_cbT", [128, 2, 44])
        p.ffn_w_down = p.din("ffn_w_down", [2, DFF, D])
        p.final_g = p.din("final_g", [1, D])
        p.t_inv_freq = p.din("inv_freq", [128, 1])
        p.t_ret_dt = p.din("ret_dt", [128, 512])
        p.t_ret_xi = p.din("ret_xi", [128, 4])
        p.t_ret_zs = p.din("ret_zs", [128, 4])
        p.t_neg = p.din("neg", [128, 128])
        p.t_ut = p.din("ut", [128, 128])
        p.t_ident = p.din("ident", [128, 128])
        p.t_sel = p.din("sel", [128, NCORES])
        p.t_selmat = p.din("selmat", [NCORES * HW, HW])
        p.t_nf = p.din("nf", [128, 1])
        p.t_ret_coef = p.din("ret_coef", [128, NCORES * H])
        p.t_valid = p.din("valid", [128, NCORES * H])
        p.t_msel = p.din("msel", [NCORES, NCORES * H])
        ph = p.phase
        p.out = p.dout("out", [T, D]) if ph in (None, 6) or p.stop_after else None
        if ph is None:
            p.modscr = p.dint("modscr", [2, 48, 128])
        elif ph == 1:
            p.modscr = p.dout("modscr_o", [2, 48, 128])
            p.modT_o = p.dout("modT_o", [128, 96])
        else:
            p.modscr = p.din("modscr_i", [2, 48, 128])
            p.modT_i = p.din("modT_i", [128, 96])
        p.b_modscr = Buf("modscr")
        p.b_modT_o = Buf("modT_o")
        kinds = {None: ("int", "int"), 1: (None, None), 2: ("out", None), 3: ("in", "out"), 4: (None, "in"), 5: ("out", "in"), 6: ("in", None)}[ph]
        mk = {"int": p.dint, "in": p.din, "out": p.dout, None: (lambda *a: None)}
        p.xa = mk[kinds[0]]("xa", [T, D])
        p.xb = mk[kinds[1]]("xb", [T, D])
        p.relay = ph in (1, 2, 4, 5)
        if ph in (1, 4):
            p.kst = p.dout("kst_o", [NG, 128, 8 * GT], BF16)
            p.vst = p.dout("vst_o", [NG, 128, G * 2048], BF16)
        elif ph in (2, 5):
            p.kst = p.din("kst_i", [NG, 128, 8 * GT], BF16)
            p.vst = p.din("vst_i", [NG, 128, G * 2048], BF16)
        p.b_kvst = Buf("kvst")
        sizes = dict(EX_SIZES)
        loc_ph = {1: 1, 2: 2, 3: 3, 4: 4, 5: 5}
        gath_ph = {1: (2,), 2: (3,), 3: (4, 5), 4: (5,), 5: (6,)}
        p.loc = {}
        p.gath = {}
        for k, (r, cdim) in sizes.items():
            if p.mode == "host":
                if ph is None or loc_ph[k] == ph:
                    p.loc[k] = p.dout(f"loc{k}", [r, cdim])
                if ph is None or ph in gath_ph[k]:
                    p.gath[k] = p.din(f"gath{k}", [NCORES * r, cdim])
            else:
                p.loc[k] = p.dint(f"loc{k}", [r, cdim])
                p.gath[k] = p.dint(f"gath{k}", [NCORES * r, cdim])
        p.b_loc = {k: Buf(f"loc{k}") for k in sizes}
        p.b_gath = {k: Buf(f"gath{k}") for k in sizes}
        p.b_xa = [Buf(f"xa{g}") for g in range(NG)]
        p.b_xb = [Buf(f"xb{g}") for g in range(NG)]
        p.b_out = [Buf(f"out{g}") for g in range(NG)]

        p.ident_f, p.b_ident_f = p.sb("ident_f", [128, 128])
        p.ident_b, p.b_ident_b = p.sb("ident_b", [128, 128], BF16)
        p.ones_f, p.b_ones_f = p.sb("ones_f", [128, 128])
        p.ones_b, p.b_ones_b = p.sb("ones_b", [128, 8], BF16)
        p.ret_dt, p.b_ret_dt = p.sb("ret_dt", [128, 512])
        p.ret_xi, p.b_ret_xi = p.sb("ret_xi", [128, 4])
        p.ret_zs, p.b_ret_zs = p.sb("ret_zs", [128, 4])
        p.neg, p.b_neg = p.sb("negm", [128, 128])
        p.ut, p.b_ut = p.sb("utm", [128, 128])
        p.inv_freq, p.b_inv_freq = p.sb("inv_freq_s", [128, 1])
        p.sel, p.b_sel = p.sb("sel_s", [128, NCORES])
        p.selmat, p.b_selmat = p.sb("selmat_s", [NCORES * HW, HW])
        p.adabT, p.b_adabT = p.sb("adabT", [128, 2, 48])
        p.cT_f, p.b_cT_f = p.sb("cT_f", [128, KC])
        p.nf, p.b_nf = p.sb("nf_s", [128, 1])
        p.ret_coef, p.b_ret_coef = p.sb("ret_coef_s", [128, NCORES * H])
        p.valid, p.b_valid = p.sb("valid_s", [128, NCORES * H])
        p.msel, p.b_msel = p.sb("msel_s", [NCORES, NCORES * H])
        p.consts = [p.b_ident_f, p.b_ident_b, p.b_ones_f, p.b_ones_b]
        p.modT, p.b_modT = p.sb("modT", [128, 2, 48])
        p.ntgT, p.b_ntgT = p.sb("ntgT", [128, 2, KC])
        p.nfgT, p.b_nfgT = p.sb("nfgT", [128, 2, KC])
        p.gsc, p.b_gsc = p.sb("gsc", [128, 4, KC])
        p.ret_gnT, p.b_ret_gnT = p.sb("ret_gnT", [128, 16])
        p.ml_gnT, p.b_ml_gnT = p.sb("ml_gnT", [128, 16])
        p.ml_cwT, p.b_ml_cwT = p.sb("ml_cwT", [128, 64])
        p.ml_cbT, p.b_ml_cbT = p.sb("ml_cbT", [128, 16])
        p.ffn_cwT, p.b_ffn_cwT = p.sb("ffn_cwT", [128, 2, 132])
        p.ffn_cbT, p.b_ffn_cbT = p.sb("ffn_cbT", [128, 2, 44])
        p.bg_bc, p.b_bg_bc = p.sb("bg_bc", [128, 8])
        p.gt_bc, p.b_gt_bc = p.sb("gt_bc", [128, D])
        p.cT_b, p.b_cT_b = p.sb("cT_b", [128, KC], BF16)
        p.stage, p.b_stage = p.sb("stage", [128, 128])
        p.NR = 3
        p.wt = []
        p.b_wt = []
        for i in range(p.NR):
            t, b = p.sb(f"wt{i}", [128, 4096], BF16)
            p.wt.append(t)
            p.b_wt.append(b)
        p.ring_pos = 0
        p.rot_bank = 0
        p.rot_acc = 0
        p.rot_ub = 0
        p.b_actT = [Buf(f"actT{i}") for i in range(22)]
        p.x_g, _ = p.sb("x_g", [128, G, D])
        p.b_xg = [Buf(f"xg{c}") for c in range(G)]
        p.sg_all = p.x_g[:].bitcast(BF16)
        p.xn, p.b_xn = p.sb("xn", [128, D])
        p.xn2, p.b_xn2 = p.sb("xn2", [128, D])
        p.junk, p.b_junk = p.sb("junk", [128, D], BF16)
        p.uhalo, p.b_uhalo = p.sb("uhalo", [128, 44, HW])
        p.ss, p.b_ss = p.sb("ss", [128, 8])
        p.rstd, p.b_rstd = p.sb("rstd", [128, 8])
        p.hT, p.b_hT = p.sb("hT", [128, KC, GT], BF16)
        p.xh, p.b_xh = p.sb("xh", [32, D])
        p.hTh, p.b_hTh = p.sb("hTh", [128, KC, 32], BF16)
        p.big_a, p.b_big_a = p.sb("big_a", [128, 22 * GT], BF16)
        p.qT, p.b_qT = p.sb("qT", [128, 8, GT], BF16)
        p.kT, p.b_kT = p.sb("kT", [128, 8, GT], BF16)
        p.v_all, _ = p.sb("v_all", [128, G, 2048], BF16)
        p.b_v = [Buf(f"v{c}") for c in range(G)]
        p.R, _ = p.sb("R", [128, 8, 512])
        p.b_R = [Buf(f"R{i}") for i in range(8)]
        p.Rb, _ = p.sb("Rb", [128, 8, 512], BF16)
        p.b_Rb = [Buf(f"Rb{i}") for i in range(8)]
        p.nst, p.b_nst = p.sb("nst", [128, 8])
        p.nstb, p.b_nstb = p.sb("nstb", [128, 8], BF16)
        p.fsum, p.b_fsum = p.sb("fsum", [128, 4])
        p.wk = []
        p.b_wk = []
        for i in range(5):
            t, b = p.sb(f"wk{i}", [128, 512])
            p.wk.append(t)
            p.b_wk.append(b)
        p.cos, p.b_cos = p.sb("cos", [128, GT])
        p.sin, p.b_sin = p.sb("sin", [128, GT])
        p.yg, p.b_yg = p.sb("yg", [128, 2048], BF16)
        p.sTm, p.b_sTm = p.sb("sTm", [128, 512], BF16)
        p.kz, p.b_kz = p.sb("kz", [128, 1024], BF16)
        p.sm, p.b_sm = p.sb("sm", [128, 96])
        p.mcoef, p.b_mcoef = p.sb("mcoef", [128, NCORES * H])
        p.gat, p.b_gat = p.sb("gat", [128, G, 8])
        p.gm, p.b_gm = p.sb("gm", [128, 7, 4 * G])
        p.ubuf = []
        p.b_ubuf = []
        for i in range(2):
            t, b = p.sb(f"ubuf{i}", [128, HW + GT])
            p.ubuf.append(t)
            p.b_ubuf.append(b)
        p.P = [p.ps(f"P{i}", [128, 512]) for i in range(6)]
        p.b_P = [Buf(f"P{i}") for i in range(6)]
        p.Pb = [p.ps(f"Pb{i}", [128, 1024], BF16) for i in range(2)]
        p.b_Pb = [Buf(f"Pb{i}") for i in range(2)]

    def dma_group(self, q, key, pairs, reads=(), writes=()):
        S = self.S
        k = S._dma_sem(key)
        S._wait(q, S._deps(reads, writes))
        for out, in_ in pairs:
            ins = S.eng[q].dma_start(out=out, in_=in_)
            S.cnt[k] += 16
            ins.then_inc(S.sems[k], 16)
            S.ninst += 1
        S._record((k, S.cnt[k]), reads, writes)

    def dump(self, name, ap, bufs):
        if not self.debug:
            return
        shape = list(ap.shape)
        d = self.nc.dram_tensor("dbg_" + name, shape, ap.dtype, kind="ExternalOutput").ap()
        b = Buf("dbg_" + name)
        self.dbg_bufs.append(b)
        self.S.dma("sp", d, ap, reads=bufs, writes=[b], key="dbg_" + name)

    def load_T(self, src_rows, n, dst_ap, dst_buf):
        p, S = self, self.S
        S.dma("sp", p.stage[0:n, :], src_rows, writes=[p.b_stage], key="stage")
        S.op("pe", lambda e: e.transpose(p.P[5][:, 0:n], p.stage[0:n, :], p.ident_f[0:n, 0:n]),
             reads=[p.b_stage, p.b_ident_f], writes=[p.b_P[5]])
        S.op("dve", lambda e: e.tensor_copy(out=dst_ap, in_=p.P[5][:, 0:n]), reads=[p.b_P[5]], writes=[dst_buf])

    def ring(self):
        i = self.ring_pos % self.NR
        self.ring_pos += 1
        return self.wt[i], self.b_wt[i], f"w{i}"

    def load_std_piece(self, W2d, c0, w=512):
        t, b, key = self.ring()
        view = t[:, 0:8 * w].rearrange("p (k n) -> p k n", k=8)
        src = W2d.rearrange("(k p) n -> p k n", p=128)
        pairs = [(view[:, 0:4, :], src[:, 0:4, c0:c0 + w]), (view[:, 4:8, :], src[:, 4:8, c0:c0 + w])]
        self.dma_group("pool", key, pairs, writes=[b])
        return view, b

    def load_rows_piece(self, W2d, k0, nk, c0, w):
        t, b, key = self.ring()
        view = t[:, 0:nk * w].rearrange("p (k n) -> p k n", k=nk)
        src = W2d.rearrange("(k p) n -> p k n", p=128)
        pairs = []
        step = 4
        for a in range(0, nk, step):
            e = min(nk, a + step)
            pairs.append((view[:, a:e, :], src[:, k0 + a:k0 + e, c0:c0 + w]))
        self.dma_group("pool", key, pairs, writes=[b])
        return view, b

    def _setup(self):
        p, S = self, self.S
        loads = [(p.ident_f, p.b_ident_f, p.t_ident), (p.ret_dt, p.b_ret_dt, p.t_ret_dt),
                 (p.ret_xi, p.b_ret_xi, p.t_ret_xi), (p.ret_zs, p.b_ret_zs, p.t_ret_zs),
                 (p.neg, p.b_neg, p.t_neg), (p.ut, p.b_ut, p.t_ut), (p.inv_freq, p.b_inv_freq, p.t_inv_freq),
                 (p.sel, p.b_sel, p.t_sel), (p.nf, p.b_nf, p.t_nf), (p.ret_coef, p.b_ret_coef, p.t_ret_coef),
                 (p.valid, p.b_valid, p.t_valid), (p.msel, p.b_msel, p.t_msel)]
        self.dma_group("sp", "setup", [(t[:], src) for t, b, src in loads], writes=[b for t, b, s in loads])
        S.dma("sp", p.bg_bc[:], p.ml_bg[0:1, :].partition_broadcast(128).rearrange("p o n -> p (o n)"),
              writes=[p.b_bg_bc], key="setup2")
        S.op("dve", lambda e: e.memset(p.ones_f[:], 1.0), writes=[p.b_ones_f])
        S.op("dve", lambda e: e.memset(p.ones_b[:], 1.0), writes=[p.b_ones_b])
        S.op("dve", lambda e: e.memset(p.xh[:], 0.0), writes=[p.b_xh])
        S.op("dve", lambda e: e.tensor_copy(out=p.ident_b[:], in_=p.ident_f[:]), reads=[p.b_ident_f], writes=[p.b_ident_b])
        vec = [(p.ntgT, p.b_ntgT, p.ntg), (p.nfgT, p.b_nfgT, p.nfg), (p.ffn_cwT, p.b_ffn_cwT, p.ffn_cw), (p.ffn_cbT, p.b_ffn_cbT, p.ffn_cb),
               (p.ret_gnT, p.b_ret_gnT, p.ret_gn), (p.ml_gnT, p.b_ml_gnT, p.ml_gn), (p.ml_cwT, p.b_ml_cwT, p.ml_cw), (p.ml_cbT, p.b_ml_cbT, p.ml_cb),
               (p.adabT, p.b_adabT, p.ada_b), (p.cT_f, p.b_cT_f, p.c_in), (p.selmat, p.b_selmat, p.t_selmat)]
        self.dma_group("sp", "setup3", [(t[:], s) for t, b, s in vec], writes=[b for t, b, s in vec])
        if p.mod_layers:
            S.op("act", lambda e: e.activation(out=p.cT_b[:], in_=p.cT_f[:], func=AF.Silu), reads=[p.b_cT_f], writes=[p.b_cT_b])
            for l in p.mod_layers:
                for pc in range(12):
                    wv, wb = self.load_std_piece(p.ada_w[l], pc * 512)
                    for ct in range(4):
                        col = l * 48 + pc * 4 + ct
                        for kc in range(KC):
                            S.op("pe", lambda e: e.matmul(p.P[4][:, col:col + 1], lhsT=wv[:, kc, ct * 128:(ct + 1) * 128],
                                                          rhs=p.cT_b[:, kc:kc + 1], start=(kc == 0), stop=(kc == KC - 1)),
                                 reads=[wb, p.b_cT_b], writes=[p.b_P[4]])
            for l in p.mod_layers:
                S.op("dve", lambda e: e.tensor_tensor(out=p.modT[:, l, :], in0=p.adabT[:, l, :], in1=p.P[4][:, l * 48:(l + 1) * 48], op=ALU.add),
                     reads=[p.b_P[4], p.b_adabT], writes=[p.b_modT])
                S.op("pe", lambda e: e.transpose(p.P[5][0:48, 0:128], p.modT[:, l, :], p.ident_f[:, :]),
                     reads=[p.b_modT, p.b_ident_f], writes=[p.b_P[5]])
                S.op("dve", lambda e: e.tensor_copy(out=p.stage[0:48, :], in_=p.P[5][0:48, 0:128]), reads=[p.b_P[5]], writes=[p.b_stage])
                S.dma("sp", p.modscr[l], p.stage[0:48, :], reads=[p.b_stage], writes=[p.b_modscr], key="modscr")
            if p.phase == 1:
                S.dma("sp", p.modT_o[:, :], p.modT[:].rearrange("p l n -> p (l n)"), reads=[p.b_modT], writes=[p.b_modT_o], key="modT_o")
        else:
            S.dma("sp", p.modT[:].rearrange("p l n -> p (l n)"), p.modT_i[:, :], writes=[p.b_modT], key="modT_i")
        for l in p.layers:
            S.op("dve", lambda e: e.scalar_tensor_tensor(out=p.gsc[:, 2 * l, :], in0=p.modT[:, l, 8:16], scalar=1.0,
                                                         in1=p.ntgT[:, l, :], op0=ALU.add, op1=ALU.mult),
                 reads=[p.b_modT, p.b_ntgT], writes=[p.b_gsc])
            S.op("dve", lambda e: e.scalar_tensor_tensor(out=p.gsc[:, 2 * l + 1, :], in0=p.modT[:, l, 32:40], scalar=1.0,
                                                         in1=p.nfgT[:, l, :], op0=ALU.add, op1=ALU.mult),
                 reads=[p.b_modT, p.b_nfgT], writes=[p.b_gsc])

    def load_gate(self, l, which):
        p, S = self, self.S
        r0 = 16 if which == 0 else 40
        src = p.modscr[l, r0:r0 + 8, :].rearrange("(o a) b -> o (a b)", o=1).partition_broadcast(128).rearrange("p o n -> p (o n)")
        S.dma("sp", p.gt_bc[:], src, reads=[p.b_modscr], writes=[p.b_gt_bc], key="gt")

    def norm_rows(self, xt, bx, npart, gidx, sh, dst_fn, dst_buf, col):
        p, S = self, self.S
        S.op("act", lambda e: e.activation(out=p.xn[0:npart, :], in_=xt, func=AF.Square, accum_out=p.ss[0:npart, col:col + 1]),
             reads=[bx], writes=[p.b_xn, p.b_ss])
        S.op("dve", lambda e: e.tensor_scalar(out=p.ss[0:npart, col:col + 1], in0=p.ss[0:npart, col:col + 1], scalar1=1.0 / D, scalar2=EPS,
                                              op0=ALU.mult, op1=ALU.add), reads=[p.b_ss], writes=[p.b_ss])
        S.op("act", lambda e: e.activation(out=p.ss[0:npart, col:col + 1], in_=p.ss[0:npart, col:col + 1], func=AF.Sqrt),
             reads=[p.b_ss], writes=[p.b_ss])
        S.op("dve", lambda e: e.reciprocal(out=p.rstd[0:npart, col:col + 1], in_=p.ss[0:npart, col:col + 1]),
             reads=[p.b_ss], writes=[p.b_rstd])
        S.op("act", lambda e: e.activation(out=p.xn[0:npart, :], in_=xt, func=AF.Copy, scale=p.rstd[0:npart, col:col + 1]),
             reads=[bx, p.b_rstd], writes=[p.b_xn])
        for half in range(2):
            bank = p.P[half]
            for k4 in range(4):
                kc = half * 4 + k4
                S.op("pe", lambda e: e.transpose(bank[:, k4 * 128:k4 * 128 + npart], p.xn[0:npart, kc * 128:(kc + 1) * 128],
                                                 p.ident_f[0:npart, 0:npart]),
                     reads=[p.b_xn, p.b_ident_f], writes=[p.b_P[half]])
            for k4 in range(4):
                kc = half * 4 + k4
                src = bank[:, k4 * 128:k4 * 128 + npart]
                if kc % 2 == 0:
                    S.op("act", lambda e: e.activation(out=dst_fn(kc), in_=src, func=AF.Identity,
                                                       scale=p.gsc[:, gidx, kc:kc + 1], bias=sh[:, kc:kc + 1]),
                         reads=[p.b_P[half], p.b_gsc, p.b_modT], writes=[dst_buf])
                else:
                    S.op("dve", lambda e: e.tensor_scalar(out=dst_fn(kc), in0=src, scalar1=p.gsc[:, gidx, kc:kc + 1],
                                                          scalar2=sh[:, kc:kc + 1], op0=ALU.mult, op1=ALU.add),
                         reads=[p.b_P[half], p.b_gsc, p.b_modT], writes=[dst_buf])

    def norm_group(self, src, src_bufs, g, gidx, sh):
        p, S = self, self.S
        for c in range(G):
            r0 = g * GT + c * CH
            S.dma("sp", p.x_g[:, c, :], src[r0:r0 + CH, :], reads=src_bufs, writes=[p.b_xg[c]], key=f"xg{c}")
        for c in range(G):
            S.op("act", lambda e: e.activation(out=p.junk[:], in_=p.x_g[:, c, :], func=AF.Square, accum_out=p.ss[:, c:c + 1]),
                 reads=[p.b_xg[c]], writes=[p.b_junk, p.b_ss])
        S.op("dve", lambda e: e.tensor_scalar(out=p.ss[:, 0:G], in0=p.ss[:, 0:G], scalar1=1.0 / D, scalar2=EPS, op0=ALU.mult, op1=ALU.add),
             reads=[p.b_ss], writes=[p.b_ss])
        S.op("act", lambda e: e.activation(out=p.ss[:, 0:G], in_=p.ss[:, 0:G], func=AF.Sqrt), reads=[p.b_ss], writes=[p.b_ss])
        S.op("dve", lambda e: e.reciprocal(out=p.rstd[:, 0:G], in_=p.ss[:, 0:G]), reads=[p.b_ss], writes=[p.b_rstd])
        for c in range(G):
            xn, b_xn = (p.xn, p.b_xn) if c % 2 == 0 else (p.xn2, p.b_xn2)
            S.op("act", lambda e: e.activation(out=xn[:], in_=p.x_g[:, c, :], func=AF.Copy, scale=p.rstd[:, c:c + 1]),
                 reads=[p.b_xg[c], p.b_rstd], writes=[b_xn])
            for half in range(2):
                bi = 2 * (c % 2) + half
                bank = p.P[bi]
                for k4 in range(4):
                    kc = half * 4 + k4
                    S.op("pe", lambda e: e.transpose(bank[:, k4 * 128:(k4 + 1) * 128], xn[:, kc * 128:(kc + 1) * 128], p.ident_f[:, :]),
                         reads=[b_xn, p.b_ident_f], writes=[p.b_P[bi]])
                for k4 in range(4):
                    kc = half * 4 + k4
                    srcp = bank[:, k4 * 128:(k4 + 1) * 128]
                    dst = p.hT[:, kc, c * CH:(c + 1) * CH]
                    if kc % 2 == 0:
                        S.op("act", lambda e: e.activation(out=dst, in_=srcp, func=AF.Identity,
                                                           scale=p.gsc[:, gidx, kc:kc + 1], bias=sh[:, kc:kc + 1]),
                             reads=[p.b_P[bi], p.b_gsc, p.b_modT], writes=[p.b_hT])
                    else:
                        S.op("dve", lambda e: e.tensor_scalar(out=dst, in0=srcp, scalar1=p.gsc[:, gidx, kc:kc + 1],
                                                              scalar2=sh[:, kc:kc + 1], op0=ALU.mult, op1=ALU.add),
                             reads=[p.b_P[bi], p.b_gsc, p.b_modT], writes=[p.b_hT])

    def load_halo(self, src, src_bufs, g, ex):
        p, S = self, self.S
        if g > 0:
            r0 = g * GT - HW
            S.dma("sp", p.xh[0:HW, :], src[r0:r0 + HW, :], reads=src_bufs, writes=[p.b_xh], key="xh")
        else:
            gt = p.gath[ex]
            nr = NCORES * HW
            S.dma("sp", p.xn[0:nr, :], gt[:, :], reads=[p.b_gath[ex]], writes=[p.b_xn], key="xnh")
            for half in range(2):
                S.op("pe", lambda e: e.matmul(p.P[half][0:HW, :], lhsT=p.selmat[0:nr, 0:HW], rhs=p.xn[0:nr, half * 512:(half + 1) * 512],
                                              start=True, stop=True), reads=[p.b_selmat, p.b_xn], writes=[p.b_P[half]])
                S.op("dve", lambda e: e.tensor_copy(out=p.xh[0:HW, half * 512:(half + 1) * 512], in_=p.P[half][0:HW, :]),
                     reads=[p.b_P[half]], writes=[p.b_xh])

    def norm_halo(self, gidx, sh):
        p = self
        self.norm_rows(p.xh[0:32, :], p.b_xh, 32, gidx, sh, lambda kc: p.hTh[:, kc, :], p.b_hTh, 4)

    def rope_tables(self, g):
        p, S = self, self.S
        posi = p.wk[4][:].bitcast(I32)
        src = p.pos[0:1, g * GT:(g + 1) * GT].partition_broadcast(128).rearrange("p o n -> p (o n)")
        S.dma("sp", posi, src, writes=[p.b_wk[4]], key="posi")
        ang, b_ang = p.wk[0], p.b_wk[0]
        S.op("dve", lambda e: e.tensor_copy(out=p.wk[1][:], in_=posi), reads=[p.b_wk[4]], writes=[p.b_wk[1]])
        S.op("dve", lambda e: e.tensor_scalar(out=ang[:], in0=p.wk[1][:], scalar1=p.inv_freq[:, 0:1], scalar2=None, op0=ALU.mult),
             reads=[p.b_wk[1], p.b_inv_freq], writes=[b_ang])
        for dst, b_dst, shift in ((p.sin, p.b_sin, 0.0), (p.cos, p.b_cos, 0.5 * math.pi)):
            xs, b_xs = p.wk[1], p.b_wk[1]
            kf, b_kf = p.wk[2], p.b_wk[2]
            ki = p.wk[3][:].bitcast(I32)
            b_ki = p.b_wk[3]
            S.op("dve", lambda e: e.tensor_scalar(out=xs[:], in0=ang[:], scalar1=shift, scalar2=None, op0=ALU.add),
                 reads=[b_ang], writes=[b_xs])
            S.op("dve", lambda e: e.tensor_scalar(out=kf[:], in0=xs[:], scalar1=1.0 / TWO_PI, scalar2=None, op0=ALU.mult),
                 reads=[b_xs], writes=[b_kf])
            S.op("dve", lambda e: e.tensor_copy(out=ki, in_=kf[:]), reads=[b_kf], writes=[b_ki])
            S.op("dve", lambda e: e.tensor_copy(out=kf[:], in_=ki), reads=[b_ki], writes=[b_kf])
            S.op("dve", lambda e: e.scalar_tensor_tensor(out=xs[:], in0=kf[:], scalar=-C1, in1=xs[:], op0=ALU.mult, op1=ALU.add),
                 reads=[b_kf, b_xs], writes=[b_xs])
            S.op("dve", lambda e: e.scalar_tensor_tensor(out=xs[:], in0=kf[:], scalar=-C2, in1=xs[:], op0=ALU.mult, op1=ALU.add),
                 reads=[b_kf, b_xs], writes=[b_xs])
            S.op("dve", lambda e: e.tensor_scalar(out=kf[:], in0=xs[:], scalar1=-math.pi, scalar2=TWO_PI, op0=ALU.is_lt, op1=ALU.mult),
                 reads=[b_xs], writes=[b_kf])
            S.op("dve", lambda e: e.tensor_tensor(out=xs[:], in0=xs[:], in1=kf[:], op=ALU.add), reads=[b_xs, b_kf], writes=[b_xs])
            S.op("dve", lambda e: e.tensor_scalar(out=kf[:], in0=xs[:], scalar1=math.pi, scalar2=-TWO_PI, op0=ALU.is_gt, op1=ALU.mult),
                 reads=[b_xs], writes=[b_kf])
            S.op("dve", lambda e: e.tensor_tensor(out=xs[:], in0=xs[:], in1=kf[:], op=ALU.add), reads=[b_xs, b_kf], writes=[b_xs])
            S.op("dve", lambda e: e.tensor_scalar(out=xs[:], in0=xs[:], scalar1=-PI_SAFE, scalar2=PI_SAFE, op0=ALU.max, op1=ALU.min),
                 reads=[b_xs], writes=[b_xs])
            S.op("act", lambda e: e.activation(out=dst[:], in_=xs[:], func=AF.Sin), reads=[b_xs], writes=[b_dst])

    def rope_pair(self, hh, dst, b_dst):
        p, S = self, self.S
        i1, i2 = 2 * (hh % 2), 2 * (hh % 2) + 1
        b1, b2 = p.P[i1], p.P[i2]
        A, Bm, C_, Dm = p.wk[0], p.wk[1], p.wk[2], p.wk[3]
        S.op("dve", lambda e: e.tensor_tensor(out=A[:], in0=b1[:], in1=p.cos[:], op=ALU.mult), reads=[p.b_P[i1], p.b_cos], writes=[p.b_wk[0]])
        S.op("dve", lambda e: e.tensor_tensor(out=Bm[:], in0=b2[:], in1=p.sin[:], op=ALU.mult), reads=[p.b_P[i2], p.b_sin], writes=[p.b_wk[1]])
        S.op("dve", lambda e: e.tensor_tensor(out=C_[:], in0=b1[:], in1=p.sin[:], op=ALU.mult), reads=[p.b_P[i1], p.b_sin], writes=[p.b_wk[2]])
        S.op("dve", lambda e: e.tensor_tensor(out=Dm[:], in0=b2[:], in1=p.cos[:], op=ALU.mult), reads=[p.b_P[i2], p.b_cos], writes=[p.b_wk[3]])
        S.op("dve", lambda e: e.tensor_tensor(out=dst[:, 2 * hh, :], in0=A[:], in1=Bm[:], op=ALU.subtract),
             reads=[p.b_wk[0], p.b_wk[1]], writes=[b_dst])
        S.op("dve", lambda e: e.tensor_tensor(out=dst[:, 2 * hh + 1, :], in0=C_[:], in1=Dm[:], op=ALU.add),
             reads=[p.b_wk[2], p.b_wk[3]], writes=[b_dst])

    def proj_A(self, wv, wb, ct, bank_i, hT=None, b_hT=None, n=GT):
        p, S = self, self.S
        hT = p.hT if hT is None else hT
        b_hT = p.b_hT if b_hT is None else b_hT
        for kc in range(KC):
            S.op("pe", lambda e: e.matmul(p.P[bank_i][:, 0:n], lhsT=wv[:, kc, ct * 128:(ct + 1) * 128], rhs=hT[:, kc, 0:n],
                                          start=(kc == 0), stop=(kc == KC - 1)),
                 reads=[wb, b_hT], writes=[p.b_P[bank_i]])

    def proj_B(self, wv, wb, c, bank_i, w=512):
        p, S = self, self.S
        for kc in range(KC):
            S.op("pe", lambda e: e.matmul(p.P[bank_i][:, 0:w], lhsT=p.hT[:, kc, c * CH:(c + 1) * CH], rhs=wv[:, kc, 0:w],
                                          start=(kc == 0), stop=(kc == KC - 1)),
                 reads=[wb, p.b_hT], writes=[p.b_P[bank_i]])

    def exchange(self, k):
        p, S = self, self.S
        if p.mode == "host":
            return
        S.wait_bufs("pool", [p.b_loc[k], p.b_gath[k]])
        ins = p.nc.gpsimd.collective_compute("AllGather", ALU.bypass, replica_groups=[list(range(NCORES))],
                                             ins=[p.loc[k][:, :]], outs=[p.gath[k][:, :]])
        key = S._dma_sem(f"cc{k}")
        S.cnt[key] += 16
        ins.then_inc(S.sems[key], 16)
        S._record((key, S.cnt[key]), [p.b_loc[k]], [p.b_gath[k]])

    def make_kz(self, c, scale_ap_fn):
        p, S = self, self.S
        cs = slice(c * CH, (c + 1) * CH)
        for i in range(8):
            S.op("pe", lambda e: e.transpose(p.Pb[0][:, i * 128:(i + 1) * 128], p.kT[:, i, cs], p.ident_b[:, :]),
                 reads=[p.b_kT, p.b_ident_b], writes=[p.b_Pb[0]])
        for hh in range(H):
            sc, sbufs = scale_ap_fn(hh)
            S.op("act", lambda e: e.activation(out=p.kz[:, hh * 256:(hh + 1) * 256], in_=p.Pb[0][:, hh * 256:(hh + 1) * 256],
                                               func=AF.Copy, scale=sc),
                 reads=[p.b_Pb[0]] + sbufs, writes=[p.b_kz])

    def state_update(self, c, hh, decay, dbufs=()):
        p, S = self, self.S
        dbufs = list(dbufs)
        for half in range(2):
            i = 2 * hh + half
            S.op("pe", lambda e: e.matmul(p.P[2 + half][:, :], lhsT=p.kz[:, hh * 256 + half * 128: hh * 256 + (half + 1) * 128],
                                          rhs=p.v_all[:, c, hh * 512:(hh + 1) * 512], start=True, stop=True),
                 reads=[p.b_kz, p.b_v[c]], writes=[p.b_P[2 + half]])
            S.op("dve", lambda e: e.scalar_tensor_tensor(out=p.R[:, i, :], in0=p.R[:, i, :], scalar=decay, in1=p.P[2 + half][:, :],
                                                         op0=ALU.mult, op1=ALU.add),
                 reads=[p.b_R[i], p.b_P[2 + half]] + dbufs, writes=[p.b_R[i]])

    def refresh_Rb(self, hh):
        p, S = self, self.S
        for half in range(2):
            i = 2 * hh + half
            S.op("act", lambda e: e.activation(out=p.Rb[:, i, :], in_=p.R[:, i, :], func=AF.Copy),
                 reads=[p.b_R[i]], writes=[p.b_Rb[i]])

    def groupnorm_heads(self, ywk, c, gate):
        p, S = self, self.S
        for hh in range(H):
            S.op("dve", lambda e: e.bn_stats(out=p.sm[:, 6 * hh:6 * hh + 6], in_=p.wk[ywk[hh]][:]), reads=[p.b_wk[ywk[hh]]], writes=[p.b_sm])
            S.op("dve", lambda e: e.bn_aggr(out=p.sm[:, 24 + 2 * hh:26 + 2 * hh], in_=p.sm[:, 6 * hh:6 * hh + 6]), reads=[p.b_sm], writes=[p.b_sm])
        var = p.sm[:, 24:32].rearrange("p (h t) -> p h t", t=2)[:, :, 1]
        S.op("dve", lambda e: e.tensor_scalar(out=p.sm[:, 32:36], in0=var, scalar1=EPS, scalar2=None, op0=ALU.add), reads=[p.b_sm], writes=[p.b_sm])
        S.op("act", lambda e: e.activation(out=p.sm[:, 32:36], in_=p.sm[:, 32:36], func=AF.Ln), reads=[p.b_sm], writes=[p.b_sm])
        S.op("act", lambda e: e.activation(out=p.sm[:, 32:36], in_=p.sm[:, 32:36], func=AF.Exp, scale=-0.5), reads=[p.b_sm], writes=[p.b_sm])
        for hh in range(H):
            w = p.wk[ywk[hh]]
            if gate:
                S.op("dve", lambda e: e.tensor_scalar(out=w[:], in0=w[:], scalar1=p.sm[:, 24 + 2 * hh:25 + 2 * hh], scalar2=p.sm[:, 32 + hh:33 + hh],
                                                      op0=ALU.subtract, op1=ALU.mult), reads=[p.b_wk[ywk[hh]], p.b_sm], writes=[p.b_wk[ywk[hh]]])
                sgv = p.sg_all[:, c, hh * 512:(hh + 1) * 512]
                S.op("dve", lambda e: e.tensor_tensor(out=p.yg[:, hh * 512:(hh + 1) * 512], in0=w[:], in1=sgv, op=ALU.mult),
                     reads=[p.b_wk[ywk[hh]], p.b_xg[c]], writes=[p.b_yg])
            else:
                S.op("dve", lambda e: e.tensor_scalar(out=p.yg[:, hh * 512:(hh + 1) * 512], in0=w[:], scalar1=p.sm[:, 24 + 2 * hh:25 + 2 * hh],
                                                      scalar2=p.sm[:, 32 + hh:33 + hh], op0=ALU.subtract, op1=ALU.mult),
                     reads=[p.b_wk[ywk[hh]], p.b_sm], writes=[p.b_yg])

    def make_ynT(self, c, gnT, b_gnT):
        p, S = self, self.S
        ynT = p.big_a[:, 0:16 * GT].rearrange("p (k n) -> p k n", k=16)
        for kc in range(16):
            bi = kc // 8
            S.op("pe", lambda e: e.transpose(p.Pb[bi][:, (kc % 8) * 128:(kc % 8 + 1) * 128], p.yg[:, kc * 128:(kc + 1) * 128], p.ident_b[:, :]),
                 reads=[p.b_yg, p.b_ident_b], writes=[p.b_Pb[bi]])
        for kc in range(16):
            bi = kc // 8
            src = p.Pb[bi][:, (kc % 8) * 128:(kc % 8 + 1) * 128]
            dst = ynT[:, kc, c * CH:(c + 1) * CH]
            if kc % 2 == 0:
                S.op("act", lambda e: e.activation(out=dst, in_=src, func=AF.Copy, scale=gnT[:, kc:kc + 1]),
                     reads=[p.b_Pb[bi], b_gnT], writes=[p.b_big_a])
            else:
                S.op("dve", lambda e: e.tensor_scalar(out=dst, in0=src, scalar1=gnT[:, kc:kc + 1], scalar2=None, op0=ALU.mult),
                     reads=[p.b_Pb[bi], b_gnT], writes=[p.b_big_a])

    def out_proj(self, g, W_out, src, src_bufs, dst, dst_bufs, halo_ex):
        p, S = self, self.S
        ynT = p.big_a[:, 0:16 * GT].rearrange("p (k n) -> p k n", k=16)
        for c in range(G):
            r0 = g * GT + c * CH
            S.dma("sp", p.x_g[:, c, :], src[r0:r0 + CH, :], reads=src_bufs, writes=[p.b_xg[c]], key=f"xg{c}")
        for cp in range(4):
            wv, wb = self.load_rows_piece(W_out, 0, 16, cp * 256, 256)
            for c in range(G):
                bi = (cp * G + c) % 6
                for kc in range(16):
                    S.op("pe", lambda e: e.matmul(p.P[bi][:, 0:256], lhsT=ynT[:, kc, c * CH:(c + 1) * CH], rhs=wv[:, kc, :],
                                                  start=(kc == 0), stop=(kc == 15)),
                         reads=[wb, p.b_big_a], writes=[p.b_P[bi]])
                ti = (cp * G + c) % 5
                tmp = p.wk[ti]
                S.op("dve", lambda e: e.tensor_tensor(out=tmp[:, 0:256], in0=p.P[bi][:, 0:256], in1=p.gt_bc[:, cp * 256:(cp + 1) * 256], op=ALU.mult),
                     reads=[p.b_P[bi], p.b_gt_bc], writes=[p.b_wk[ti]])
                S.op("dve", lambda e: e.tensor_tensor(out=p.x_g[:, c, cp * 256:(cp + 1) * 256], in0=p.x_g[:, c, cp * 256:(cp + 1) * 256],
                                                      in1=tmp[:, 0:256], op=ALU.add),
                     reads=[p.b_wk[ti], p.b_xg[c]], writes=[p.b_xg[c]])
        self.store_group(g, dst, dst_bufs, halo_ex)

    def store_group(self, g, dst, dst_bufs, halo_ex):
        p, S = self, self.S
        pairs = [(dst[g * GT + c * CH: g * GT + (c + 1) * CH, :], p.x_g[:, c, :]) for c in range(G)]
        self.dma_group("sp", f"st_{dst_bufs[0].name[:2]}", pairs, reads=p.b_xg, writes=[dst_bufs[g]])
        if g == NG - 1 and halo_ex is not None:
            S.dma("sp", p.loc[halo_ex][:, :], p.x_g[128 - HW:128, G - 1, :], reads=[p.b_xg[G - 1]], writes=[p.b_loc[halo_ex]], key=f"loc{halo_ex}")

    def kv_store(self, g):
        p = self
        self.dma_group("sp", "kvst", [(p.kst[g], p.kT[:].rearrange("p a n -> p (a n)")), (p.vst[g], p.v_all[:].rearrange("p a n -> p (a n)"))],
                       reads=[p.b_kT] + p.b_v, writes=[p.b_kvst])

    def kv_load(self, g):
        p = self
        self.dma_group("sp", "kvld", [(p.kT[:].rearrange("p a n -> p (a n)"), p.kst[g]), (p.v_all[:].rearrange("p a n -> p (a n)"), p.vst[g])],
                       writes=[p.b_kT] + p.b_v)

    def ret_group(self, g, full):
        p, S = self, self.S
        sh = p.modT[:, 0, 0:8]
        self.norm_group(p.x, [], g, 0, sh)
        self.rope_tables(g)
        W = p.ret_w_in
        reuse = full and p.relay
        plist = ([0, 1] if full else []) + ([] if reuse else [2, 3])
        if reuse:
            self.kv_load(g)
        for pc in plist:
            wv, wb = self.load_std_piece(W, pc * 512)
            dst, b_dst = (p.qT, p.b_qT) if pc < 2 else (p.kT, p.b_kT)
            for ct in range(4):
                self.proj_A(wv, wb, ct, ct)
            for j in range(2):
                self.rope_pair(2 * (pc % 2) + j, dst, b_dst)
        for hh in ([] if reuse else range(H)):
            wv, wb = self.load_std_piece(W, 2048 + hh * 512)
            for c in range(G):
                self.proj_B(wv, wb, c, c)
                dstv = p.v_all[:, c, hh * 512:(hh + 1) * 512]
                if c % 2 == 0:
                    S.op("act", lambda e: e.activation(out=dstv, in_=p.P[c][:, :], func=AF.Copy), reads=[p.b_P[c]], writes=[p.b_v[c]])
                else:
                    S.op("dve", lambda e: e.tensor_copy(out=dstv, in_=p.P[c][:, :]), reads=[p.b_P[c]], writes=[p.b_v[c]])
        if (not full) and p.relay:
            self.kv_store(g)
        if full:
            for hh in range(H):
                wv, wb = self.load_std_piece(W, 4096 + hh * 512)
                for c in range(G):
                    self.proj_B(wv, wb, c, c)
                    dsts = p.sg_all[:, c, hh * 512:(hh + 1) * 512]
                    S.op("act", lambda e: e.activation(out=dsts, in_=p.P[c][:, :], func=AF.Silu), reads=[p.b_P[c]], writes=[p.b_xg[c]])
        if full and g == 0:
            self.dump("hT", p.hT[:], [p.b_hT])
            self.dump("cos", p.cos[:], [p.b_cos])
            self.dump("sin", p.sin[:], [p.b_sin])
            self.dump("qT", p.qT[:], [p.b_qT])
            self.dump("kT", p.kT[:], [p.b_kT])
            self.dump("v_all", p.v_all[:], p.b_v)
            self.dump("sg_all", p.sg_all, p.b_xg)
        for c in range(G):
            self.ret_chunk(c, full)
            if full and g == 0 and c == 1:
                self.dump("yg", p.yg[:], [p.b_yg])
                self.dump("sm", p.sm[:], [p.b_sm])
                self.dump("wk0", p.wk[0][:], [p.b_wk[0]])
                self.dump("wk3", p.wk[3][:], [p.b_wk[3]])
                self.dump("sTm", p.sTm[:], [p.b_sTm])
                self.dump("kz", p.kz[:], [p.b_kz])
                self.dump("R", p.R[:], p.b_R)
        if full and g == 0:
            self.dump("ynT", p.big_a[:, 0:16 * GT], [p.b_big_a])
        if full:
            self.out_proj(g, p.ret_w_out, p.x, [], p.xa, p.b_xa, 2)
        if full and g == 0:
            self.dump("xg", p.x_g[:], p.b_xg)

    def ret_chunk(self, c, full):
        p, S = self, self.S
        cs = slice(c * CH, (c + 1) * CH)
        self.make_kz(c, lambda hh: (p.ret_zs[:, hh:hh + 1], [p.b_ret_zs]))
        gam = [float(np.exp(128.0 * np.log(1.0 - 2.0 ** (-5.0 - h)))) for h in range(H)]
        if not full:
            for hh in range(H):
                self.state_update(c, hh, gam[hh])
            return
        for hh in range(H):
            for half in range(2):
                S.op("pe", lambda e: e.matmul(p.P[4][:, hh * 128:(hh + 1) * 128], lhsT=p.kT[:, 2 * hh + half, cs], rhs=p.qT[:, 2 * hh + half, cs],
                                              start=(half == 0), stop=(half == 1)),
                     reads=[p.b_kT, p.b_qT], writes=[p.b_P[4]])
        S.op("dve", lambda e: e.tensor_tensor(out=p.sTm[:], in0=p.P[4][:, :], in1=p.ret_dt[:], op=ALU.mult),
             reads=[p.b_P[4], p.b_ret_dt], writes=[p.b_sTm])
        for hh in range(H):
            S.op("pe", lambda e: e.matmul(p.P[0][:, :], lhsT=p.sTm[:, hh * 128:(hh + 1) * 128], rhs=p.v_all[:, c, hh * 512:(hh + 1) * 512],
                                          start=True, stop=True), reads=[p.b_sTm, p.b_v[c]], writes=[p.b_P[0]])
            for half in range(2):
                i = 2 * hh + half
                S.op("pe", lambda e: e.matmul(p.P[1][:, :], lhsT=p.qT[:, i, cs], rhs=p.Rb[:, i, :], start=(half == 0), stop=(half == 1)),
                     reads=[p.b_qT, p.b_Rb[i]], writes=[p.b_P[1]])
            S.op("act", lambda e: e.activation(out=p.wk[4][:], in_=p.P[1][:, :], func=AF.Copy, scale=p.ret_xi[:, hh:hh + 1]),
                 reads=[p.b_P[1], p.b_ret_xi], writes=[p.b_wk[4]])
            S.op("dve", lambda e: e.tensor_tensor(out=p.wk[hh][:], in0=p.wk[4][:], in1=p.P[0][:, :], op=ALU.add),
                 reads=[p.b_wk[4], p.b_P[0]], writes=[p.b_wk[hh]])
            self.state_update(c, hh, gam[hh])
            self.refresh_Rb(hh)
        self.groupnorm_heads([0, 1, 2, 3], c, gate=True)
        self.make_ynT(c, p.ret_gnT, p.b_ret_gnT)

    def ret_layer(self):
        p, S = self, self.S
        for i in range(8):
            S.op("dve", lambda e: e.memset(p.R[:, i, :], 0.0), writes=[p.b_R[i]])
        if p.stop_after == "dbg_ret":
            for hh in range(H):
                self.refresh_Rb(hh)
            self.load_gate(0, 0)
            self.dump("modT", p.modT[:], [p.b_modT])
            self.dump("gsc", p.gsc[:], [p.b_gsc])
            self.dump("gt_bc", p.gt_bc[:], [p.b_gt_bc])
            self.ret_group(0, full=True)
            return
        if p.phase in (None, 1):
            self.ret_part_a()
        if p.phase is None:
            self.exchange(1)
        if p.phase in (None, 2):
            self.ret_part_b()

    def ret_part_a(self):
        p, S = self, self.S
        for g in range(NG):
            self.ret_group(g, full=False)
        self.dma_group("sp", "loc1", [(p.loc[1][i * 128:(i + 1) * 128, :], p.R[:, i, :]) for i in range(8)],
                       reads=p.b_R, writes=[p.b_loc[1]])

    def ret_part_b(self):
        p, S = self, self.S
        self.combine_state(1, p.ret_coef, p.b_ret_coef, nrows=8)
        for hh in range(H):
            self.refresh_Rb(hh)
        self.load_gate(0, 0)
        for g in range(NG):
            self.ret_group(g, full=True)

    def combine_state(self, ex, coef, b_coef, nrows):
        p, S = self, self.S
        gt = p.gath[ex]
        per = nrows * 128 if ex == 1 else 9 * 128
        for i in range(8):
            hh = i // 2
            for cp in range(NCORES):
                tmp, bt = p.wk[cp % 2], p.b_wk[cp % 2]
                r0 = cp * per + i * 128
                S.dma("sp", tmp[:], gt[r0:r0 + 128, :], reads=[p.b_gath[ex]], writes=[bt], key=f"cmb{cp % 2}")
                if cp == 0:
                    S.op("dve", lambda e: e.tensor_scalar(out=p.R[:, i, :], in0=tmp[:], scalar1=coef[:, cp * H + hh: cp * H + hh + 1], scalar2=None,
                                                          op0=ALU.mult), reads=[bt, b_coef], writes=[p.b_R[i]])
                else:
                    S.op("dve", lambda e: e.scalar_tensor_tensor(out=p.R[:, i, :], in0=tmp[:], scalar=coef[:, cp * H + hh: cp * H + hh + 1],
                                                                 in1=p.R[:, i, :], op0=ALU.mult, op1=ALU.add),
                         reads=[bt, b_coef, p.b_R[i]], writes=[p.b_R[i]])

    def ffn_layer(self, l, src, src_bufs, dst, dst_bufs, ex_in, ex_out, final):
        p, S = self, self.S
        S.barrier()
        self.load_gate(l, 1)
        sh = p.modT[:, l, 24:32]
        gidx = 2 * l + 1
        actT = p.big_a[:, :].rearrange("p (k n) -> p k n", k=22)
        cw = p.ffn_cwT
        cb = p.ffn_cbT
        Wup = p.ffn_w_up[l]
        Wdn = p.ffn_w_down[l]
        for g in range(NG):
            if g == 0:
                self.load_halo(src, src_bufs, g, ex_in)
                self.norm_halo(gidx, sh)
            self.norm_group(src, src_bufs, g, gidx, sh)
            srcw = Wup.rearrange("(k p) n -> p k n", p=128)

            def load_up(pc):
                t, b, key = self.ring()
                wv = t[:, 0:4096].rearrange("p (k n) -> p k n", k=8)
                pairs = []
                for kh in range(2):
                    pairs.append((wv[:, 4 * kh:4 * kh + 4, 0:256], srcw[:, 4 * kh:4 * kh + 4, pc * 256:(pc + 1) * 256]))
                    pairs.append((wv[:, 4 * kh:4 * kh + 4, 256:512], srcw[:, 4 * kh:4 * kh + 4, DFF + pc * 256: DFF + (pc + 1) * 256]))
                self.dma_group("pool", key, pairs, writes=[b])
                return wv, b

            loaders = [(lambda pc=pc: load_up(pc)) for pc in range(11)]
            loaders += [(lambda cp=cp, kh=kh: self.load_rows_piece(Wdn, 11 * kh, 11, cp * 256, 256)) for cp in range(4) for kh in range(2)]
            loaded = {}

            def get_piece(i, ahead=2):
                for j in range(i, min(i + ahead + 1, len(loaders))):
                    if j not in loaded:
                        loaded[j] = loaders[j]()
                return loaded.pop(i)

            banks = [0, 1, 2, 3, 5]
            for pc in range(11):
                wv, b = get_piece(pc)
                accs = {}
                for ct in (0, 2, 1, 3):
                    tile_i = 2 * pc + (ct % 2)
                    chan = tile_i if ct < 2 else 22 + tile_i
                    bi = banks[self.rot_bank % len(banks)]
                    self.rot_bank += 1
                    ai = self.rot_acc % 5
                    self.rot_acc += 1
                    ui = self.rot_ub % 2
                    self.rot_ub += 1
                    self.proj_A(wv, b, ct, bi)
                    ub, b_ub = p.ubuf[ui], p.b_ubuf[ui]
                    if g == 0:
                        for kc in range(KC):
                            S.op("pe", lambda e: e.matmul(p.P[4][:, 0:HW], lhsT=wv[:, kc, ct * 128:(ct + 1) * 128], rhs=p.hTh[:, kc, 0:HW],
                                                          start=(kc == 0), stop=(kc == KC - 1)), reads=[b, p.b_hTh], writes=[p.b_P[4]])
                        S.op("dve", lambda e: e.tensor_scalar(out=ub[:, 0:HW], in0=p.P[4][:, 0:HW], scalar1=p.nf[:, 0:1], scalar2=None, op0=ALU.mult),
                             reads=[p.b_P[4], p.b_nf], writes=[b_ub])
                    else:
                        S.op("dve", lambda e: e.tensor_copy(out=ub[:, 0:HW], in_=p.uhalo[:, chan, :]), reads=[p.b_uhalo], writes=[b_ub])
                    S.op("act", lambda e: e.activation(out=ub[:, HW:HW + GT], in_=p.P[bi][:, :], func=AF.Copy), reads=[p.b_P[bi]], writes=[b_ub])
                    if g < NG - 1:
                        S.op("dve", lambda e: e.tensor_copy(out=p.uhalo[:, chan, :], in_=ub[:, GT:GT + HW]), reads=[b_ub], writes=[p.b_uhalo])
                    acc, b_acc = p.wk[ai], p.b_wk[ai]
                    accs[ct] = (acc, b_acc)
                    S.op("act", lambda e: e.activation(out=acc[:], in_=p.P[bi][:, :], func=AF.Identity, scale=cw[:, l, 88 + chan:89 + chan],
                                                       bias=cb[:, l, chan:chan + 1]), reads=[p.b_P[bi], p.b_ffn_cwT, p.b_ffn_cbT], writes=[b_acc])
                    S.op("dve", lambda e: e.scalar_tensor_tensor(out=acc[:], in0=ub[:, HW - 1:HW - 1 + GT], scalar=cw[:, l, 44 + chan:45 + chan], in1=acc[:],
                                                                 op0=ALU.mult, op1=ALU.add), reads=[b_ub, b_acc, p.b_ffn_cwT], writes=[b_acc])
                    S.op("dve", lambda e: e.scalar_tensor_tensor(out=acc[:], in0=ub[:, HW - 2:HW - 2 + GT], scalar=cw[:, l, chan:chan + 1], in1=acc[:],
                                                                 op0=ALU.mult, op1=ALU.add), reads=[b_ub, b_acc, p.b_ffn_cwT], writes=[b_acc])
                    if ct < 2:
                        S.op("act", lambda e: e.activation(out=acc[:], in_=acc[:], func=AF.Silu), reads=[b_acc], writes=[b_acc])
                    else:
                        j = ct - 2
                        (aa, b_aa), (ab, b_ab) = accs[j], accs[ct]
                        S.op("dve", lambda e: e.tensor_tensor(out=actT[:, 2 * pc + j, :], in0=aa[:], in1=ab[:], op=ALU.mult),
                             reads=[b_aa, b_ab], writes=[p.b_actT[2 * pc + j]])
            for cp in range(4):
                halves = [get_piece(11 + cp * 2, ahead=2), get_piece(12 + cp * 2, ahead=1)]
                for c in range(G):
                    bi = (cp * G + c) % 6
                    for kh in range(2):
                        wv, wb = halves[kh]
                        for k in range(11):
                            S.op("pe", lambda e: e.matmul(p.P[bi][:, 0:256], lhsT=actT[:, 11 * kh + k, c * CH:(c + 1) * CH], rhs=wv[:, k, :],
                                                          start=(kh == 0 and k == 0), stop=(kh == 1 and k == 10)),
                                 reads=[wb, p.b_actT[11 * kh + k]], writes=[p.b_P[bi]])
                    ti = self.rot_acc % 5
                    self.rot_acc += 1
                    tmp = p.wk[ti]
                    S.op("dve", lambda e: e.tensor_tensor(out=tmp[:, 0:256], in0=p.P[bi][:, 0:256], in1=p.gt_bc[:, cp * 256:(cp + 1) * 256], op=ALU.mult),
                         reads=[p.b_P[bi], p.b_gt_bc], writes=[p.b_wk[ti]])
                    S.op("dve", lambda e: e.tensor_tensor(out=p.x_g[:, c, cp * 256:(cp + 1) * 256], in0=p.x_g[:, c, cp * 256:(cp + 1) * 256],
                                                          in1=tmp[:, 0:256], op=ALU.add),
                         reads=[p.b_wk[ti], p.b_xg[c]], writes=[p.b_xg[c]])
            if final:
                self.final_norm_group()
            self.store_group(g, dst, dst_bufs, ex_out)
        S.barrier()

    def final_norm_group(self):
        p, S = self, self.S
        for c in range(G):
            S.op("act", lambda e: e.activation(out=p.xn[:], in_=p.x_g[:, c, :], func=AF.Square, accum_out=p.ss[:, c:c + 1]),
                 reads=[p.b_xg[c]], writes=[p.b_xn, p.b_ss])
        S.op("dve", lambda e: e.tensor_scalar(out=p.ss[:, 0:G], in0=p.ss[:, 0:G], scalar1=1.0 / D, scalar2=EPS, op0=ALU.mult, op1=ALU.add),
             reads=[p.b_ss], writes=[p.b_ss])
        S.op("act", lambda e: e.activation(out=p.ss[:, 0:G], in_=p.ss[:, 0:G], func=AF.Sqrt), reads=[p.b_ss], writes=[p.b_ss])
        S.op("dve", lambda e: e.reciprocal(out=p.rstd[:, 0:G], in_=p.ss[:, 0:G]), reads=[p.b_ss], writes=[p.b_rstd])
        for c in range(G):
            S.op("act", lambda e: e.activation(out=p.x_g[:, c, :], in_=p.x_g[:, c, :], func=AF.Copy, scale=p.rstd[:, c:c + 1]),
                 reads=[p.b_xg[c], p.b_rstd], writes=[p.b_xg[c]])
            for half, (t, b) in enumerate(((p.cos, p.b_cos), (p.sin, p.b_sin))):
                S.op("dve", lambda e: e.tensor_tensor(out=p.x_g[:, c, half * 512:(half + 1) * 512], in0=p.x_g[:, c, half * 512:(half + 1) * 512],
                                                      in1=t[:], op=ALU.mult), reads=[p.b_xg[c], b], writes=[p.b_xg[c]])

    LN16 = math.log(16.0)

    def ml_group(self, g, full):
        p, S = self, self.S
        sh = p.modT[:, 1, 0:8]
        W = p.ml_w_in
        if g == 0:
            self.load_halo(p.xb, p.b_xb, g, 3)
            self.norm_halo(2, sh)
        self.norm_group(p.xb, p.b_xb, g, 2, sh)
        reuse = full and p.relay
        plist = ([0, 1] if full else []) + ([] if reuse else [2, 3])
        if reuse:
            self.kv_load(g)
        for pc in plist:
            wv, wb = self.load_std_piece(W, pc * 512)
            dst, b_dst = (p.qT, p.b_qT) if pc < 2 else (p.kT, p.b_kT)
            for ct in range(4):
                cti = pc * 4 + ct
                self.proj_A(wv, wb, ct, ct)
                ub, b_ub = p.ubuf[ct % 2], p.b_ubuf[ct % 2]
                if g == 0:
                    for kc in range(KC):
                        S.op("pe", lambda e: e.matmul(p.P[4][:, 0:HW], lhsT=wv[:, kc, ct * 128:(ct + 1) * 128], rhs=p.hTh[:, kc, 0:HW],
                                                      start=(kc == 0), stop=(kc == KC - 1)), reads=[wb, p.b_hTh], writes=[p.b_P[4]])
                    S.op("dve", lambda e: e.tensor_scalar(out=ub[:, 0:HW], in0=p.P[4][:, 0:HW], scalar1=p.nf[:, 0:1], scalar2=None, op0=ALU.mult),
                         reads=[p.b_P[4], p.b_nf], writes=[b_ub])
                else:
                    S.op("dve", lambda e: e.tensor_copy(out=ub[:, 0:HW], in_=p.uhalo[:, cti, :]), reads=[p.b_uhalo], writes=[b_ub])
                S.op("act", lambda e: e.activation(out=ub[:, HW:HW + GT], in_=p.P[ct][:, :], func=AF.Copy), reads=[p.b_P[ct]], writes=[b_ub])
                if g < NG - 1:
                    S.op("dve", lambda e: e.tensor_copy(out=p.uhalo[:, cti, :], in_=ub[:, GT:GT + HW]), reads=[b_ub], writes=[p.b_uhalo])
                acc, b_acc = p.wk[ct], p.b_wk[ct]
                S.op("act", lambda e: e.activation(out=acc[:], in_=p.P[ct][:, :], func=AF.Identity, scale=p.ml_cwT[:, 48 + cti:49 + cti],
                                                   bias=p.ml_cbT[:, cti:cti + 1]), reads=[p.b_P[ct], p.b_ml_cwT, p.b_ml_cbT], writes=[b_acc])
                for j in (2, 1, 0):
                    off = HW - (3 - j)
                    S.op("dve", lambda e: e.scalar_tensor_tensor(out=acc[:], in0=ub[:, off:off + GT], scalar=p.ml_cwT[:, j * 16 + cti: j * 16 + cti + 1],
                                                                 in1=acc[:], op0=ALU.mult, op1=ALU.add), reads=[b_ub, b_acc, p.b_ml_cwT], writes=[b_acc])
                S.op("act", lambda e: e.activation(out=dst[:, cti % 8, :], in_=acc[:], func=AF.Silu), reads=[b_acc], writes=[b_dst])
        for hh in ([] if reuse else range(H)):
            wv, wb = self.load_std_piece(W, 2048 + hh * 512)
            for c in range(G):
                self.proj_B(wv, wb, c, c)
                dstv = p.v_all[:, c, hh * 512:(hh + 1) * 512]
                if c % 2 == 0:
                    S.op("act", lambda e: e.activation(out=dstv, in_=p.P[c][:, :], func=AF.Copy), reads=[p.b_P[c]], writes=[p.b_v[c]])
                else:
                    S.op("dve", lambda e: e.tensor_copy(out=dstv, in_=p.P[c][:, :]), reads=[p.b_P[c]], writes=[p.b_v[c]])
        if (not full) and p.relay:
            self.kv_store(g)
        wv, wb = self.load_std_piece(W, 6144, w=8)
        for c in range(G):
            self.proj_B(wv, wb, c, c, w=8)
            S.op("dve", lambda e: e.tensor_tensor(out=p.gat[:, c, :], in0=p.P[c][:, 0:8], in1=p.bg_bc[:], op=ALU.add),
                 reads=[p.b_P[c], p.b_bg_bc], writes=[p.b_gat])
        if full:
            for hh in range(H):
                wv, wb = self.load_std_piece(W, 4096 + hh * 512)
                for c in range(G):
                    self.proj_B(wv, wb, c, c)
                    dsts = p.sg_all[:, c, hh * 512:(hh + 1) * 512]
                    S.op("act", lambda e: e.activation(out=dsts, in_=p.P[c][:, :], func=AF.Sigmoid), reads=[p.b_P[c]], writes=[p.b_xg[c]])
        self.ml_gates_group(full)
        for c in range(G):
            self.ml_chunk(c, full)
        if full:
            self.out_proj(g, p.ml_w_out, p.xb, p.b_xb, p.xa, p.b_xa, 5)

    def ml_gates_group(self, full):
        p, S = self, self.S
        gm, bg = p.gm, p.b_gm
        v3 = lambda k: gm[:, k, :].rearrange("p (c h) -> p c h", h=4)
        z = p.gat[:, :, 4:8]
        li = p.gat[:, :, 0:4]
        S.op("act", lambda e: e.activation(out=v3(0), in_=z, func=AF.Exp, scale=-1.0), reads=[p.b_gat], writes=[bg])
        S.op("dve", lambda e: e.tensor_scalar(out=gm[:, 0, :], in0=gm[:, 0, :], scalar1=1.0, scalar2=None, op0=ALU.add), reads=[bg], writes=[bg])
        S.op("act", lambda e: e.activation(out=gm[:, 0, :], in_=gm[:, 0, :], func=AF.Ln), reads=[bg], writes=[bg])
        S.op("dve", lambda e: e.tensor_scalar(out=gm[:, 0, :], in0=gm[:, 0, :], scalar1=-1.0, scalar2=None, op0=ALU.mult), reads=[bg], writes=[bg])
        n = 4 * G
        S.op("pe", lambda e: e.matmul(p.P[5][:, 0:n], lhsT=p.ut[:, :], rhs=gm[:, 0, :], start=True, stop=True), reads=[p.b_ut, bg], writes=[p.b_P[5]])
        S.op("pe", lambda e: e.matmul(p.P[5][:, n:2 * n], lhsT=p.ones_f[:, :], rhs=gm[:, 0, :], start=True, stop=True), reads=[p.b_ones_f, bg], writes=[p.b_P[5]])
        S.op("dve", lambda e: e.tensor_copy(out=gm[:, 1, :], in_=p.P[5][:, 0:n]), reads=[p.b_P[5]], writes=[bg])
        S.op("dve", lambda e: e.tensor_copy(out=gm[:, 2, :], in_=p.P[5][:, n:2 * n]), reads=[p.b_P[5]], writes=[bg])
        S.op("dve", lambda e: e.tensor_tensor(out=gm[:, 3, :], in0=gm[:, 2, :], in1=gm[:, 1, :], op=ALU.subtract), reads=[bg], writes=[bg])
        S.op("dve", lambda e: e.scalar_tensor_tensor(out=v3(3), in0=v3(3), scalar=-self.LN16, in1=li, op0=ALU.add, op1=ALU.add),
             reads=[bg, p.b_gat], writes=[bg])
        S.op("act", lambda e: e.activation(out=gm[:, 3, :], in_=gm[:, 3, :], func=AF.Exp), reads=[bg], writes=[bg])
        S.op("act", lambda e: e.activation(out=gm[:, 4, :], in_=gm[:, 2, :], func=AF.Exp), reads=[bg], writes=[bg])
        if not full:
            for c in range(G):
                S.op("dve", lambda e: e.tensor_tensor(out=p.fsum[:], in0=p.fsum[:], in1=gm[:, 2, 4 * c:4 * c + 4], op=ALU.add),
                     reads=[bg, p.b_fsum], writes=[p.b_fsum])
        else:
            S.op("dve", lambda e: e.tensor_tensor(out=v3(5), in0=li, in1=v3(1), op=ALU.subtract), reads=[bg, p.b_gat], writes=[bg])
            S.op("dve", lambda e: e.tensor_scalar(out=gm[:, 5, :], in0=gm[:, 5, :], scalar1=-self.LN16, scalar2=None, op0=ALU.add), reads=[bg], writes=[bg])
            S.op("act", lambda e: e.activation(out=gm[:, 6, :], in_=gm[:, 1, :], func=AF.Exp), reads=[bg], writes=[bg])

    def ml_chunk(self, c, full):
        p, S = self, self.S
        cs = slice(c * CH, (c + 1) * CH)
        sm, bs = p.sm, p.b_sm
        gm, bg = p.gm, p.b_gm
        col = 4 * c
        g_b = lambda hh: gm[:, 1, col + hh:col + hh + 1]
        g_ws = lambda hh: gm[:, 3, col + hh:col + hh + 1]
        g_sp = lambda hh: gm[:, 4, col + hh:col + hh + 1]
        g_bj = lambda hh: gm[:, 5, col + hh:col + hh + 1]
        g_wi = lambda hh: gm[:, 6, col + hh:col + hh + 1]
        self.make_kz(c, lambda hh: (g_ws(hh), [bg]))
        if full:
            for hh in range(H):
                S.op("dve", lambda e: e.tensor_scalar(out=p.xn[:, hh * 128:(hh + 1) * 128], in0=p.ident_f[:, :], scalar1=g_b(hh), scalar2=None, op0=ALU.mult),
                     reads=[bg, p.b_ident_f], writes=[p.b_xn])
                S.op("pe", lambda e: e.matmul(p.P[5][:, hh * 128:(hh + 1) * 128], lhsT=p.ones_f[:, :], rhs=p.xn[:, hh * 128:(hh + 1) * 128], start=True, stop=False),
                     reads=[p.b_ones_f, p.b_xn], writes=[p.b_P[5]])
                S.op("pe", lambda e: e.matmul(p.P[5][:, hh * 128:(hh + 1) * 128], lhsT=p.ident_f[:, :], rhs=p.neg[:, :], start=False, stop=True),
                     reads=[p.b_ident_f, p.b_neg], writes=[p.b_P[5]])
            for hh in range(H):
                S.op("act", lambda e: e.activation(out=p.xn2[:, hh * 128:(hh + 1) * 128], in_=p.P[5][:, hh * 128:(hh + 1) * 128], func=AF.Exp,
                                                   bias=g_bj(hh)), reads=[p.b_P[5], bg], writes=[p.b_xn2])
            for hh in range(H):
                for half in range(2):
                    S.op("pe", lambda e: e.matmul(p.P[4][:, hh * 128:(hh + 1) * 128], lhsT=p.kT[:, 2 * hh + half, cs], rhs=p.qT[:, 2 * hh + half, cs],
                                                  start=(half == 0), stop=(half == 1)), reads=[p.b_kT, p.b_qT], writes=[p.b_P[4]])
            S.op("dve", lambda e: e.tensor_tensor(out=p.sTm[:], in0=p.P[4][:, :], in1=p.xn2[:, 0:512], op=ALU.mult),
                 reads=[p.b_P[4], p.b_xn2], writes=[p.b_sTm])
        for hh in range(H):
            if full:
                S.op("pe", lambda e: e.matmul(p.P[0][:, :], lhsT=p.sTm[:, hh * 128:(hh + 1) * 128], rhs=p.v_all[:, c, hh * 512:(hh + 1) * 512],
                                              start=True, stop=True), reads=[p.b_sTm, p.b_v[c]], writes=[p.b_P[0]])
                for half in range(2):
                    i = 2 * hh + half
                    S.op("pe", lambda e: e.matmul(p.P[1][:, :], lhsT=p.qT[:, i, cs], rhs=p.Rb[:, i, :], start=(half == 0), stop=(half == 1)),
                         reads=[p.b_qT, p.b_Rb[i]], writes=[p.b_P[1]])
                S.op("pe", lambda e: e.matmul(p.P[5][:, 0:1], lhsT=p.sTm[:, hh * 128:(hh + 1) * 128], rhs=p.ones_b[:, 0:1], start=True, stop=True),
                     reads=[p.b_sTm, p.b_ones_b], writes=[p.b_P[5]])
                for half in range(2):
                    i = 2 * hh + half
                    S.op("pe", lambda e: e.matmul(p.P[5][:, 1:2], lhsT=p.qT[:, i, cs], rhs=p.nstb[:, i:i + 1], start=(half == 0), stop=(half == 1)),
                         reads=[p.b_qT, p.b_nstb], writes=[p.b_P[5]])
                S.op("act", lambda e: e.activation(out=p.wk[4][:], in_=p.P[1][:, :], func=AF.Copy, scale=g_wi(hh)),
                     reads=[p.b_P[1], bg], writes=[p.b_wk[4]])
                S.op("dve", lambda e: e.tensor_tensor(out=p.wk[hh][:], in0=p.wk[4][:], in1=p.P[0][:, :], op=ALU.add),
                     reads=[p.b_wk[4], p.b_P[0]], writes=[p.b_wk[hh]])
                S.op("dve", lambda e: e.tensor_copy(out=sm[:, 72:74], in_=p.P[5][:, 0:2]), reads=[p.b_P[5]], writes=[bs])
                S.op("dve", lambda e: e.scalar_tensor_tensor(out=sm[:, 74:75], in0=sm[:, 73:74], scalar=g_wi(hh), in1=sm[:, 72:73],
                                                             op0=ALU.mult, op1=ALU.add), reads=[bs, bg], writes=[bs])
                S.op("dve", lambda e: e.scalar_tensor_tensor(out=sm[:, 76:77], in0=sm[:, 74:75], scalar=-1.0, in1=sm[:, 74:75], op0=ALU.mult, op1=ALU.max),
                     reads=[bs], writes=[bs])
                S.op("dve", lambda e: e.tensor_scalar(out=sm[:, 74:75], in0=sm[:, 76:77], scalar1=1.0, scalar2=None, op0=ALU.max),
                     reads=[bs], writes=[bs])
                S.op("dve", lambda e: e.reciprocal(out=sm[:, 75:76], in_=sm[:, 74:75]), reads=[bs], writes=[bs])
                so = p.sg_all[:, c, hh * 512:(hh + 1) * 512]
                S.op("dve", lambda e: e.scalar_tensor_tensor(out=p.wk[hh][:], in0=p.wk[hh][:], scalar=sm[:, 75:76], in1=so, op0=ALU.mult, op1=ALU.mult),
                     reads=[p.b_wk[hh], bs, p.b_xg[c]], writes=[p.b_wk[hh]])
            self.state_update(c, hh, g_sp(hh), [bg])
            for half in range(2):
                i = 2 * hh + half
                S.op("pe", lambda e: e.matmul(p.P[5][:, 16 + i:17 + i], lhsT=p.kz[:, hh * 256 + half * 128: hh * 256 + (half + 1) * 128],
                                              rhs=p.ones_b[:, 0:1], start=True, stop=True), reads=[p.b_kz, p.b_ones_b], writes=[p.b_P[5]])
                S.op("dve", lambda e: e.scalar_tensor_tensor(out=p.nst[:, i:i + 1], in0=p.nst[:, i:i + 1], scalar=g_sp(hh),
                                                             in1=p.P[5][:, 16 + i:17 + i], op0=ALU.mult, op1=ALU.add),
                     reads=[p.b_nst, bg, p.b_P[5]], writes=[p.b_nst])
            if full:
                self.refresh_Rb(hh)
                S.op("act", lambda e: e.activation(out=p.nstb[:, 2 * hh:2 * hh + 2], in_=p.nst[:, 2 * hh:2 * hh + 2], func=AF.Copy),
                     reads=[p.b_nst], writes=[p.b_nstb])
        if full:
            self.groupnorm_heads([0, 1, 2, 3], c, gate=False)
            self.make_ynT(c, p.ml_gnT, p.b_ml_gnT)

    def ml_layer(self):
        p, S = self, self.S
        if p.phase in (None, 4):
            self.ml_part_a()
        if p.phase is None:
            self.exchange(4)
        if p.phase in (None, 5):
            self.ml_part_b()

    def ml_part_a(self):
        p, S = self, self.S
        for i in range(8):
            S.op("dve", lambda e: e.memset(p.R[:, i, :], 0.0), writes=[p.b_R[i]])
        S.op("dve", lambda e: e.memset(p.nst[:], 0.0), writes=[p.b_nst])
        S.op("dve", lambda e: e.memset(p.fsum[:], 0.0), writes=[p.b_fsum])
        for g in range(NG):
            self.ml_group(g, full=False)
        S.op("dve", lambda e: e.memset(p.wk[4][:], 0.0), writes=[p.b_wk[4]])
        S.op("dve", lambda e: e.tensor_copy(out=p.wk[4][:, 0:8], in_=p.nst[:]), reads=[p.b_nst], writes=[p.b_wk[4]])
        S.op("dve", lambda e: e.tensor_copy(out=p.wk[4][:, 8:12], in_=p.fsum[:]), reads=[p.b_fsum], writes=[p.b_wk[4]])
        pairs = [(p.loc[4][i * 128:(i + 1) * 128, :], p.R[:, i, :]) for i in range(8)] + [(p.loc[4][1024:1152, :], p.wk[4][:])]
        self.dma_group("sp", "loc4", pairs, reads=p.b_R + [p.b_wk[4]], writes=[p.b_loc[4]])

    def ml_part_b(self):
        p, S = self, self.S
        gt = p.gath[4]
        pairs = [(p.stage[cp:cp + 1, 0:4], gt[cp * 1152 + 1024: cp * 1152 + 1025, 8:12]) for cp in range(NCORES)]
        self.dma_group("sp", "stage", pairs, reads=[p.b_gath[4]], writes=[p.b_stage])
        for cp in range(NCORES):
            S.op("dve", lambda e: e.tensor_tensor(out=p.stage[0:8, 32 + cp * 4:36 + cp * 4], in0=p.msel[0:8, cp * 4:cp * 4 + 4], in1=p.stage[0:8, 0:4], op=ALU.mult),
                 reads=[p.b_stage, p.b_msel], writes=[p.b_stage])
        S.op("pe", lambda e: e.matmul(p.P[5][:, 0:32], lhsT=p.ones_f[0:8, :], rhs=p.stage[0:8, 32:64], start=True, stop=True),
             reads=[p.b_ones_f, p.b_stage], writes=[p.b_P[5]])
        S.op("act", lambda e: e.activation(out=p.mcoef[:], in_=p.P[5][:, 0:32], func=AF.Exp), reads=[p.b_P[5]], writes=[p.b_mcoef])
        S.op("dve", lambda e: e.tensor_tensor(out=p.mcoef[:], in0=p.mcoef[:], in1=p.valid[:], op=ALU.mult), reads=[p.b_mcoef, p.b_valid], writes=[p.b_mcoef])
        self.combine_state(4, p.mcoef, p.b_mcoef, nrows=9)
        S.op("dve", lambda e: e.memset(p.nst[:], 0.0), writes=[p.b_nst])
        for cp in range(NCORES):
            tmp, bt = p.wk[cp % 2], p.b_wk[cp % 2]
            r0 = cp * 1152 + 1024
            S.dma("sp", tmp[:, 0:8], gt[r0:r0 + 128, 0:8], reads=[p.b_gath[4]], writes=[bt], key=f"cmb{cp % 2}")
            for hh in range(H):
                S.op("dve", lambda e: e.scalar_tensor_tensor(out=p.nst[:, 2 * hh:2 * hh + 2], in0=tmp[:, 2 * hh:2 * hh + 2],
                                                             scalar=p.mcoef[:, cp * H + hh:cp * H + hh + 1], in1=p.nst[:, 2 * hh:2 * hh + 2],
                                                             op0=ALU.mult, op1=ALU.add), reads=[bt, p.b_mcoef, p.b_nst], writes=[p.b_nst])
        for hh in range(H):
            self.refresh_Rb(hh)
        S.op("act", lambda e: e.activation(out=p.nstb[:], in_=p.nst[:], func=AF.Copy), reads=[p.b_nst], writes=[p.b_nstb])
        self.load_gate(1, 0)
        for g in range(NG):
            self.ml_group(g, full=True)

    def _body(self):
        p, S = self, self.S
        ph = p.phase
        if ph in (None, 1, 2):
            self.ret_layer()
        if p.stop_after == "dbg_ret":
            return
        if p.stop_after == "ret":
            return self.copy_out(p.xa, p.b_xa)
        if ph is None:
            self.exchange(2)
        if ph in (None, 3):
            self.ffn_layer(0, p.xa, p.b_xa, p.xb, p.b_xb, 2, 3, final=False)
        if p.stop_after == "ffn0":
            return self.copy_out(p.xb, p.b_xb)
        if ph is None:
            self.exchange(3)
        if ph in (None, 4, 5):
            self.ml_layer()
        if p.stop_after == "ml":
            return self.copy_out(p.xa, p.b_xa)
        if ph is None:
            self.exchange(5)
        if ph in (None, 6):
            S.dma("sp", p.cos[:], p.final_g[0:1, 0:512].partition_broadcast(128).rearrange("p o n -> p (o n)"), writes=[p.b_cos], key="fg0")
            S.dma("sp", p.sin[:], p.final_g[0:1, 512:1024].partition_broadcast(128).rearrange("p o n -> p (o n)"), writes=[p.b_sin], key="fg1")
            self.ffn_layer(1, p.xa, p.b_xa, p.out, p.b_out, 5, None, final=True)

    def copy_out(self, src, src_bufs):
        p, S = self, self.S
        for g in range(NG):
            for c in range(G):
                r0 = g * GT + c * CH
                S.dma("sp", p.x_g[:, c, :], src[r0:r0 + CH, :], reads=src_bufs, writes=[p.b_xg[c]], key=f"xg{c}")
            pairs = [(p.out[g * GT + c * CH: g * GT + (c + 1) * CH, :], p.x_g[:, c, :]) for c in range(G)]
            self.dma_group("sp", "st_ou", pairs, reads=p.b_xg, writes=[p.b_out[g]])

    def _finish(self):
        p, S = self, self.S
        bufs = list(p.b_out) + [p.b_loc[k] for k in p.b_loc] + p.dbg_bufs + list(p.b_xa) + list(p.b_xb) + [p.b_modscr, p.b_modT_o, p.b_kvst]
        S.wait_bufs("sp", bufs)
        S.barrier()


_PROG_CACHE = {}
MODE = "host6"
STOP_AFTER = None


def _get_prog(mode, stop_after, phase=None):
    key = (mode, stop_after, phase)
    if key not in _PROG_CACHE:
        pr = Prog("host" if mode.startswith("host") else mode, stop_after, phase)
        pr.build()
        _PROG_CACHE[key] = pr
    return _PROG_CACHE[key]


def _in_maps(inputs):
    f = lambda a: np.ascontiguousarray(np.asarray(a), dtype=np.float32)
    tabs, lg = _const_tables()
    x = f(inputs["x"]).reshape(SEQ, D)
    pos = np.ascontiguousarray(np.asarray(inputs["positions"]).astype(np.int32)).reshape(SEQ)
    shared = {
        "cT": np.ascontiguousarray(f(inputs["c"]).reshape(KC, 128).T),
        "ada_w": f(inputs["ada_w"]),
        "ada_bT": np.ascontiguousarray(f(inputs["ada_b"]).reshape(2, 48, 128).transpose(2, 0, 1)),
        "ntgT": np.ascontiguousarray(f(inputs["norm_tok_g"]).reshape(2, KC, 128).transpose(2, 0, 1)),
        "nfgT": np.ascontiguousarray(f(inputs["norm_ffn_g"]).reshape(2, KC, 128).transpose(2, 0, 1)),
        "ret_w_in": f(inputs["ret_w_in"]).reshape(D, 6144),
        "ret_gnT": np.ascontiguousarray(f(inputs["ret_gn_g"]).reshape(16, 128).T),
        "ret_w_out": f(inputs["ret_w_out"]).reshape(2048, D),
        "ml_w_in": f(inputs["ml_w_in"]).reshape(D, 6152),
        "ml_b_gate": f(inputs["ml_b_gate"]).reshape(1, 8),
        "ml_cwT": np.ascontiguousarray(f(inputs["ml_conv_w"]).reshape(64, 128).T),
        "ml_cbT": np.ascontiguousarray(f(inputs["ml_conv_b"]).reshape(16, 128).T),
        "ml_gnT": np.ascontiguousarray(f(inputs["ml_gn_g"]).reshape(16, 128).T),
        "ml_w_out": f(inputs["ml_w_out"]).reshape(2048, D),
        "ffn_w_up": f(inputs["ffn_w_up"]),
        "ffn_cwT": np.ascontiguousarray(f(inputs["ffn_conv_w"]).reshape(2, 132, 128).transpose(2, 0, 1)),
        "ffn_cbT": np.ascontiguousarray(f(inputs["ffn_conv_b"]).reshape(2, 44, 128).transpose(2, 0, 1)),
        "ffn_w_down": f(inputs["ffn_w_down"]),
        "final_g": f(inputs["final_g"]).reshape(1, D),
    }
    shared.update(tabs)
    maps = []
    for c in range(NCORES):
        m = dict(shared)
        m["x"] = x[c * T:(c + 1) * T]
        m["pos"] = pos[c * T:(c + 1) * T].reshape(1, T)
        m.update(_core_tables(c, lg))
        maps.append(m)
    return maps


def _launch(pr, maps, extra):
    ms = []
    for c, m in enumerate(maps):
        mm = {k: v for k, v in m.items() if k in pr.in_names}
        for k, v in extra.items():
            if k in pr.in_names:
                mm[k] = v[c] if isinstance(v, list) else v
        ms.append(mm)
    return run_bass_kernel_spmd(pr.nc, ms, core_ids=list(range(NCORES))).results


def kernel(**inputs):
    maps = _in_maps(inputs)
    if MODE == "host6":
        extra = {}
        res = None
        for ph in range(1, 7):
            pr = _get_prog("host", None, ph)
            res = _launch(pr, maps, extra)
            for k in (1, 2, 3, 4, 5):
                if f"loc{k}" in res[0]:
                    extra[f"gath{k}"] = np.concatenate([res[c][f"loc{k}"] for c in range(NCORES)], axis=0)
            for nm in ("xa", "xb"):
                if nm in res[0]:
                    extra[nm] = [res[c][nm] for c in range(NCORES)]
            if "kst_o" in res[0]:
                extra["kst_i"] = [res[c]["kst_o"] for c in range(NCORES)]
                extra["vst_i"] = [res[c]["vst_o"] for c in range(NCORES)]
            if "modT_o" in res[0]:
                extra["modT_i"] = [res[c]["modT_o"] for c in range(NCORES)]
                extra["modscr_i"] = [res[c]["modscr_o"] for c in range(NCORES)]
    elif MODE == "host":
        pr = _get_prog("host", STOP_AFTER)
        extra = {f"gath{k}": np.zeros((NCORES * r, cdim), np.float32) for k, (r, cdim) in EX_SIZES.items()}
        order = {None: [1, 2, 3, 4, 5], "ret": [1], "ffn0": [1, 2], "ml": [1, 2, 3, 4]}[STOP_AFTER]
        res = None
        for step in range(len(order) + 1):
            res = _launch(pr, maps, extra)
            if step < len(order):
                k = order[step]
                extra[f"gath{k}"] = np.concatenate([res[c][f"loc{k}"] for c in range(NCORES)], axis=0)
    else:
        pr = _get_prog("cc", None)
        res = _launch(pr, maps, {})
    out = np.concatenate([res[c]["out"] for c in range(NCORES)], axis=0)
    return out.reshape(1, SEQ, D).astype(np.float32)
```

```python
import contextlib
import math
import numpy as np
import concourse.bass as bass
import concourse.mybir as mybir
from concourse.bass_utils import run_bass_kernel_spmd

F32 = mybir.dt.float32
BF16 = mybir.dt.bfloat16
I32 = mybir.dt.int32
AF = mybir.ActivationFunctionType
ALU = mybir.AluOpType

NCORES = 8
SEQ = 16384
D = 1024
T = SEQ // NCORES
CH = 128
G = 4
GT = G * CH
NG = T // GT
KC = D // 128
H = 4
DK = 256
DV = 512
DFF = 2816
EPS = 1e-6
HW = 3
TWO_PI = 2.0 * math.pi
C1 = 6.28125
C2 = TWO_PI - C1
PI_SAFE = 3.1415925


class Buf:
    __slots__ = ("name", "w", "r")

    def __init__(self, name):
        self.name = name
        self.w = None
        self.r = {}


class Sched:
    ENGS = ("pe", "dve", "act", "pool", "sp")

    def __init__(self, nc, stack):
        self.nc = nc
        self.stack = stack
        self.eng = {"pe": nc.tensor, "dve": nc.vector, "act": nc.scalar, "pool": nc.gpsimd, "sp": nc.sync}
        self.sems = {}
        self.cnt = {}
        self.seen = {e: {} for e in self.ENGS}
        for e in self.ENGS:
            self.sems[e] = stack.enter_context(nc.semaphore("s_" + e))
            self.cnt[e] = 0
        self.ninst = 0

    def buf(self, name):
        return Buf(name)

    def _dma_sem(self, key):
        k = "dma_" + key
        if k not in self.sems:
            self.sems[k] = self.stack.enter_context(self.nc.semaphore("s_" + k))
            self.cnt[k] = 0
        return k

    def _wait(self, e, deps):
        need = {}
        for d in deps:
            if d is None:
                continue
            k, v = d
            if k == e and e == "pe":
                continue
            if need.get(k, 0) < v:
                need[k] = v
        for k, v in need.items():
            if self.seen[e].get(k, 0) < v:
                self.eng[e].wait_ge(self.sems[k], v)
                self.seen[e][k] = v

    @staticmethod
    def _deps(reads, writes):
        deps = []
        for b in reads:
            deps.append(b.w)
        for b in writes:
            deps.append(b.w)
            deps.extend(b.r.items())
        return deps

    @staticmethod
    def _record(ev, reads, writes):
        k, v = ev
        for b in reads:
            if b.r.get(k, 0) < v:
                b.r[k] = v
        for b in writes:
            b.w = ev
            b.r = {}

    def op(self, e, fn, reads=(), writes=()):
        self._wait(e, self._deps(reads, writes))
        ins = fn(self.eng[e])
        self.cnt[e] += 1
        ins.then_inc(self.sems[e], 1)
        self.ninst += 1
        self._record((e, self.cnt[e]), reads, writes)
        return ins

    def dma(self, q, out, in_, reads=(), writes=(), key=None, **kw):
        k = self._dma_sem(key)
        self._wait(q, self._deps(reads, writes))
        ins = self.eng[q].dma_start(out=out, in_=in_, **kw)
        self.cnt[k] += 16
        ins.then_inc(self.sems[k], 16)
        self.ninst += 1
        self._record((k, self.cnt[k]), reads, writes)
        return ins

    def wait_bufs(self, e, bufs):
        deps = []
        for b in bufs:
            deps.append(b.w)
            deps.extend(b.r.items())
        self._wait(e, deps)

    def barrier(self):
        for e in self.ENGS:
            deps = [(k, v) for k, v in self.cnt.items() if v > 0]
            self._wait(e, deps)


def _const_tables():
    t = {}
    n = np.arange(128, dtype=np.float32)
    inv_freq = (10000.0 ** (-(np.arange(0, DK, 2, dtype=np.float32)) / DK)).astype(np.float32)
    t["inv_freq"] = inv_freq.reshape(128, 1).astype(np.float32)
    lg = np.log(1.0 - 2.0 ** (-5.0 - np.arange(H, dtype=np.float64)))
    i = np.arange(128)[None, :]
    j = np.arange(128)[:, None]
    dt = np.zeros((128, H, 128), np.float64)
    for h in range(H):
        dt[:, h, :] = np.where(i >= j, np.exp((i - j) * lg[h]), 0.0) * (DK ** -0.5)
    t["ret_dt"] = dt.reshape(128, H * 128).astype(np.float32)
    t["ret_xi"] = np.exp((np.arange(128)[:, None] + 1.0) * lg[None, :]).astype(np.float32)
    t["ret_zs"] = (np.exp((127.0 - np.arange(128)[:, None]) * lg[None, :]) * (DK ** -0.5)).astype(np.float32)
    t["neg"] = np.where(j <= i, 0.0, -30000.0).astype(np.float32)
    t["ut"] = np.where(j <= i, 1.0, 0.0).astype(np.float32)
    t["ident"] = np.eye(128, dtype=np.float32)
    return t, lg


def _core_tables(c, lg):
    sel = np.zeros((128, NCORES), np.float32)
    if c > 0:
        sel[:, c - 1] = 1.0
    nf = np.full((128, 1), 0.0 if c == 0 else 1.0, np.float32)
    rc = np.zeros((128, NCORES * H), np.float32)
    for cp in range(c):
        for h in range(H):
            rc[:, cp * H + h] = np.exp(T * (c - 1 - cp) * lg[h])
    valid = np.zeros((128, NCORES * H), np.float32)
    for cp in range(c):
        valid[:, cp * H:(cp + 1) * H] = 1.0
    msel = np.zeros((NCORES, NCORES, H), np.float32)
    for cpp in range(NCORES):
        for cp in range(NCORES):
            if cp < cpp < c:
                msel[cpp, cp, :] = 1.0
    selmat = np.zeros((NCORES * HW, HW), np.float32)
    if c > 0:
        for r in range(HW):
            selmat[(c - 1) * HW + r, r] = 1.0
    return {"selmat": selmat, "sel": sel, "nf": nf, "ret_coef": rc, "valid": valid, "msel": msel.reshape(NCORES, NCORES * H)}


EX_SIZES = {1: (8 * 128, 512), 2: (HW, D), 3: (HW, D), 4: (9 * 128, 512), 5: (HW, D)}


class Prog:
    def __init__(self, mode="host", stop_after=None, phase=None):
        self.mode = mode
        self.stop_after = stop_after
        self.phase = phase
        self.in_names = []
        self.layers = [0, 1] if phase in (None, 1) else ([0] if phase <= 3 else [1])
        self.mod_layers = [0, 1] if phase in (None, 1) else []
        self.nc = bass.Bass("TRN2", target_bir_lowering=False)
        self.st = contextlib.ExitStack()
        self.debug = stop_after is not None and stop_after.startswith("dbg")
        self.dbg_bufs = []

    def din(self, name, shape, dt=F32):
        self.in_names.append(name)
        return self.nc.dram_tensor(name, list(shape), dt, kind="ExternalInput").ap()

    def dout(self, name, shape, dt=F32):
        return self.nc.dram_tensor(name, list(shape), dt, kind="ExternalOutput").ap()

    def dint(self, name, shape, dt=F32):
        return self.nc.dram_tensor(name, list(shape), dt, kind="Internal").ap()

    def sb(self, name, shape, dt=F32):
        t = self.st.enter_context(self.nc.sbuf_tensor("sb_" + name, list(shape), dt))
        b = Buf(name)
        return t, b

    def ps(self, name, shape, dt=F32):
        t = self.st.enter_context(self.nc.psum_tensor("ps_" + name, list(shape), dt))
        return t

    def build(self):
        with self.st:
            self.S = Sched(self.nc, self.st)
            self._declare()
            self._setup()
            self._body()
            self._finish()
        return self.nc

    def _declare(self):
        p = self
        p.x = p.din("x", [T, D])
        p.pos = p.din("pos", [1, T], I32)
        p.c_in = p.din("cT", [128, KC])
        p.ada_w = p.din("ada_w", [2, D, 6 * D])
        p.ada_b = p.din("ada_bT", [128, 2, 48])
        p.ntg = p.din("ntgT", [128, 2, KC])
        p.nfg = p.din("nfgT", [128, 2, KC])
        p.ret_w_in = p.din("ret_w_in", [D, 6144])
        p.ret_gn = p.din("ret_gnT", [128, 16])
        p.ret_w_out = p.din("ret_w_out", [2048, D])
        p.ml_w_in = p.din("ml_w_in", [D, 6152])
        p.ml_bg = p.din("ml_b_gate", [1, 8])
        p.ml_cw = p.din("ml_cwT", [128, 64])
        p.ml_cb = p.din("ml_cbT", [128, 16])
        p.ml_gn = p.din("ml_gnT", [128, 16])
        p.ml_w_out = p.din("ml_w_out", [2048, D])
        p.ffn_w_up = p.din("ffn_w_up", [2, D, 2 * DFF])
        p.ffn_cw = p.din("ffn_cwT", [128, 2, 132])
        p.ffn_cb = p.din("ffn_cbT", [128, 2, 44])
        p.ffn_w_down = p.din("ffn_w_down", [2, DFF, D])
        p.final_g = p.din("final_g", [1, D])
        p.t_inv_freq = p.din("inv_freq", [128, 1])
        p.t_ret_dt = p.din("ret_dt", [128, 512])
        p.t_ret_xi = p.din("ret_xi", [128, 4])
        p.t_ret_zs = p.din("ret_zs", [128, 4])
        p.t_neg = p.din("neg", [128, 128])
        p.t_ut = p.din("ut", [128, 128])
        p.t_ident = p.din("ident", [128, 128])
        p.t_sel = p.din("sel", [128, NCORES])
        p.t_selmat = p.din("selmat", [NCORES * HW, HW])
        p.t_nf = p.din("nf", [128, 1])
        p.t_ret_coef = p.din("ret_coef", [128, NCORES * H])
        p.t_valid = p.din("valid", [128, NCORES * H])
        p.t_msel = p.din("msel", [NCORES, NCORES * H])
        ph = p.phase
        p.out = p.dout("out", [T, D]) if ph in (None, 6) or p.stop_after else None
        if ph is None:
            p.modscr = p.dint("modscr", [2, 48, 128])
        elif ph == 1:
            p.modscr = p.dout("modscr_o", [2, 48, 128])
            p.modT_o = p.dout("modT_o", [128, 96])
        else:
            p.modscr = p.din("modscr_i", [2, 48, 128])
            p.modT_i = p.din("modT_i", [128, 96])
        p.b_modscr = Buf("modscr")
        p.b_modT_o = Buf("modT_o")
        kinds = {None: ("int", "int"), 1: (None, None), 2: ("out", None), 3: ("in", "out"), 4: (None, "in"), 5: ("out", "in"), 6: ("in", None)}[ph]
        mk = {"int": p.dint, "in": p.din, "out": p.dout, None: (lambda *a: None)}
        p.xa = mk[kinds[0]]("xa", [T, D])
        p.xb = mk[kinds[1]]("xb", [T, D])
        p.relay = ph in (1, 2, 4, 5)
        if ph in (1, 4):
            p.kst = p.dout("kst_o", [NG, 128, 8 * GT], BF16)
            p.vst = p.dout("vst_o", [NG, 128, G * 2048], BF16)
        elif ph in (2, 5):
            p.kst = p.din("kst_i", [NG, 128, 8 * GT], BF16)
            p.vst = p.din("vst_i", [NG, 128, G * 2048], BF16)
        p.b_kvst = Buf("kvst")
        sizes = dict(EX_SIZES)
        loc_ph = {1: 1, 2: 2, 3: 3, 4: 4, 5: 5}
        gath_ph = {1: (2,), 2: (3,), 3: (4, 5), 4: (5,), 5: (6,)}
        p.loc = {}
        p.gath = {}
        for k, (r, cdim) in sizes.items():
            if p.mode == "host":
                if ph is None or loc_ph[k] == ph:
                    p.loc[k] = p.dout(f"loc{k}", [r, cdim])
                if ph is None or ph in gath_ph[k]:
                    p.gath[k] = p.din(f"gath{k}", [NCORES * r, cdim])
            else:
                p.loc[k] = p.dint(f"loc{k}", [r, cdim])
                p.gath[k] = p.dint(f"gath{k}", [NCORES * r, cdim])
        p.b_loc = {k: Buf(f"loc{k}") for k in sizes}
        p.b_gath = {k: Buf(f"gath{k}") for k in sizes}
        p.b_xa = [Buf(f"xa{g}") for g in range(NG)]
        p.b_xb = [Buf(f"xb{g}") for g in range(NG)]
        p.b_out = [Buf(f"out{g}") for g in range(NG)]

        p.ident_f, p.b_ident_f = p.sb("ident_f", [128, 128])
        p.ident_b, p.b_ident_b = p.sb("ident_b", [128, 128], BF16)
        p.ones_f, p.b_ones_f = p.sb("ones_f", [128, 128])
        p.ones_b, p.b_ones_b = p.sb("ones_b", [128, 8], BF16)
        p.ret_dt, p.b_ret_dt = p.sb("ret_dt", [128, 512])
        p.ret_xi, p.b_ret_xi = p.sb("ret_xi", [128, 4])
        p.ret_zs, p.b_ret_zs = p.sb("ret_zs", [128, 4])
        p.neg, p.b_neg = p.sb("negm", [128, 128])
        p.ut, p.b_ut = p.sb("utm", [128, 128])
        p.inv_freq, p.b_inv_freq = p.sb("inv_freq_s", [128, 1])
        p.sel, p.b_sel = p.sb("sel_s", [128, NCORES])
        p.selmat, p.b_selmat = p.sb("selmat_s", [NCORES * HW, HW])
        p.adabT, p.b_adabT = p.sb("adabT", [128, 2, 48])
        p.cT_f, p.b_cT_f = p.sb("cT_f", [128, KC])
        p.nf, p.b_nf = p.sb("nf_s", [128, 1])
        p.ret_coef, p.b_ret_coef = p.sb("ret_coef_s", [128, NCORES * H])
        p.valid, p.b_valid = p.sb("valid_s", [128, NCORES * H])
        p.msel, p.b_msel = p.sb("msel_s", [NCORES, NCORES * H])
        p.consts = [p.b_ident_f, p.b_ident_b, p.b_ones_f, p.b_ones_b]
        p.modT, p.b_modT = p.sb("modT", [128, 2, 48])
        p.ntgT, p.b_ntgT = p.sb("ntgT", [128, 2, KC])
        p.nfgT, p.b_nfgT = p.sb("nfgT", [128, 2, KC])
        p.gsc, p.b_gsc = p.sb("gsc", [128, 4, KC])
        p.ret_gnT, p.b_ret_gnT = p.sb("ret_gnT", [128, 16])
        p.ml_gnT, p.b_ml_gnT = p.sb("ml_gnT", [128, 16])
        p.ml_cwT, p.b_ml_cwT = p.sb("ml_cwT", [128, 64])
        p.ml_cbT, p.b_ml_cbT = p.sb("ml_cbT", [128, 16])
        p.ffn_cwT, p.b_ffn_cwT = p.sb("ffn_cwT", [128, 2, 132])
        p.ffn_cbT, p.b_ffn_cbT = p.sb("ffn_cbT", [128, 2, 44])
        p.bg_bc, p.b_bg_bc = p.sb("bg_bc", [128, 8])
        p.gt_bc, p.b_gt_bc = p.sb("gt_bc", [128, D])
        p.cT_b, p.b_cT_b = p.sb("cT_b", [128, KC], BF16)
        p.stage, p.b_stage = p.sb("stage", [128, 128])
        p.NR = 3
        p.wt = []
        p.b_wt = []
        for i in range(p.NR):
            t, b = p.sb(f"wt{i}", [128, 4096], BF16)
            p.wt.append(t)
            p.b_wt.append(b)
        p.ring_pos = 0
        p.rot_bank = 0
        p.rot_acc = 0
        p.rot_ub = 0
        p.b_actT = [Buf(f"actT{i}") for i in range(22)]
        p.x_g, _ = p.sb("x_g", [128, G, D])
        p.b_xg = [Buf(f"xg{c}") for c in range(G)]
        p.sg_all = p.x_g[:].bitcast(BF16)
        p.xn, p.b_xn = p.sb("xn", [128, D])
        p.xn2, p.b_xn2 = p.sb("xn2", [128, D])
        p.junk, p.b_junk = p.sb("junk", [128, D], BF16)
        p.uhalo, p.b_uhalo = p.sb("uhalo", [128, 44, HW])
        p.ss, p.b_ss = p.sb("ss", [128, 8])
        p.rstd, p.b_rstd = p.sb("rstd", [128, 8])
        p.hT, p.b_hT = p.sb("hT", [128, KC, GT], BF16)
        p.xh, p.b_xh = p.sb("xh", [32, D])
        p.hTh, p.b_hTh = p.sb("hTh", [128, KC, 32], BF16)
        p.big_a, p.b_big_a = p.sb("big_a", [128, 22 * GT], BF16)
        p.qT, p.b_qT = p.sb("qT", [128, 8, GT], BF16)
        p.kT, p.b_kT = p.sb("kT", [128, 8, GT], BF16)
        p.v_all, _ = p.sb("v_all", [128, G, 2048], BF16)
        p.b_v = [Buf(f"v{c}") for c in range(G)]
        p.R, _ = p.sb("R", [128, 8, 512])
        p.b_R = [Buf(f"R{i}") for i in range(8)]
        p.Rb, _ = p.sb("Rb", [128, 8, 512], BF16)
        p.b_Rb = [Buf(f"Rb{i}") for i in range(8)]
        p.nst, p.b_nst = p.sb("nst", [128, 8])
        p.nstb, p.b_nstb = p.sb("nstb", [128, 8], BF16)
        p.fsum, p.b_fsum = p.sb("fsum", [128, 4])
        p.wk = []
        p.b_wk = []
        for i in range(5):
            t, b = p.sb(f"wk{i}", [128, 512])
            p.wk.append(t)
            p.b_wk.append(b)
        p.cos, p.b_cos = p.sb("cos", [128, GT])
        p.sin, p.b_sin = p.sb("sin", [128, GT])
        p.yg, p.b_yg = p.sb("yg", [128, 2048], BF16)
        p.sTm, p.b_sTm = p.sb("sTm", [128, 512], BF16)
        p.kz, p.b_kz = p.sb("kz", [128, 1024], BF16)
        p.sm, p.b_sm = p.sb("sm", [128, 96])
        p.mcoef, p.b_mcoef = p.sb("mcoef", [128, NCORES * H])
        p.gat, p.b_gat = p.sb("gat", [128, G, 8])
        p.gmx, p.b_gmx = p.sb("gmx", [128, G, 8])
        p.gm, p.b_gm = p.sb("gm", [128, 7, 4 * G])
        p.ubuf = []
        p.b_ubuf = []
        for i in range(2):
            t, b = p.sb(f"ubuf{i}", [128, HW + GT])
            p.ubuf.append(t)
            p.b_ubuf.append(b)
        p.P = [p.ps(f"P{i}", [128, 512]) for i in range(6)]
        p.b_P = [Buf(f"P{i}") for i in range(6)]
        p.Pb = [p.ps(f"Pb{i}", [128, 1024], BF16) for i in range(2)]
        p.b_Pb = [Buf(f"Pb{i}") for i in range(2)]

    def dma_group(self, q, key, pairs, reads=(), writes=()):
        S = self.S
        k = S._dma_sem(key)
        S._wait(q, S._deps(reads, writes))
        for out, in_ in pairs:
            ins = S.eng[q].dma_start(out=out, in_=in_)
            S.cnt[k] += 16
            ins.then_inc(S.sems[k], 16)
            S.ninst += 1
        S._record((k, S.cnt[k]), reads, writes)

    def dump(self, name, ap, bufs):
        if not self.debug:
            return
        shape = list(ap.shape)
        d = self.nc.dram_tensor("dbg_" + name, shape, ap.dtype, kind="ExternalOutput").ap()
        b = Buf("dbg_" + name)
        self.dbg_bufs.append(b)
        self.S.dma("sp", d, ap, reads=bufs, writes=[b], key="dbg_" + name)

    def load_T(self, src_rows, n, dst_ap, dst_buf):
        p, S = self, self.S
        S.dma("sp", p.stage[0:n, :], src_rows, writes=[p.b_stage], key="stage")
        S.op("pe", lambda e: e.transpose(p.P[5][:, 0:n], p.stage[0:n, :], p.ident_f[0:n, 0:n]),
             reads=[p.b_stage, p.b_ident_f], writes=[p.b_P[5]])
        S.op("dve", lambda e: e.tensor_copy(out=dst_ap, in_=p.P[5][:, 0:n]), reads=[p.b_P[5]], writes=[dst_buf])

    def ring(self):
        i = self.ring_pos % self.NR
        self.ring_pos += 1
        return self.wt[i], self.b_wt[i], f"w{i}"

    def load_std_piece(self, W2d, c0, w=512):
        t, b, key = self.ring()
        view = t[:, 0:8 * w].rearrange("p (k n) -> p k n", k=8)
        src = W2d.rearrange("(k p) n -> p k n", p=128)
        pairs = [(view[:, 0:4, :], src[:, 0:4, c0:c0 + w]), (view[:, 4:8, :], src[:, 4:8, c0:c0 + w])]
        self.dma_group("pool", key, pairs, writes=[b])
        return view, b

    def load_rows_piece(self, W2d, k0, nk, c0, w):
        t, b, key = self.ring()
        view = t[:, 0:nk * w].rearrange("p (k n) -> p k n", k=nk)
        src = W2d.rearrange("(k p) n -> p k n", p=128)
        pairs = []
        step = 4
        for a in range(0, nk, step):
            e = min(nk, a + step)
            pairs.append((view[:, a:e, :], src[:, k0 + a:k0 + e, c0:c0 + w]))
        self.dma_group("pool", key, pairs, writes=[b])
        return view, b

    def _setup(self):
        p, S = self, self.S
        loads = [(p.ident_f, p.b_ident_f, p.t_ident), (p.ret_dt, p.b_ret_dt, p.t_ret_dt),
                 (p.ret_xi, p.b_ret_xi, p.t_ret_xi), (p.ret_zs, p.b_ret_zs, p.t_ret_zs),
                 (p.neg, p.b_neg, p.t_neg), (p.ut, p.b_ut, p.t_ut), (p.inv_freq, p.b_inv_freq, p.t_inv_freq),
                 (p.sel, p.b_sel, p.t_sel), (p.nf, p.b_nf, p.t_nf), (p.ret_coef, p.b_ret_coef, p.t_ret_coef),
                 (p.valid, p.b_valid, p.t_valid), (p.msel, p.b_msel, p.t_msel)]
        self.dma_group("sp", "setup", [(t[:], src) for t, b, src in loads], writes=[b for t, b, s in loads])
        S.dma("sp", p.bg_bc[:], p.ml_bg[0:1, :].partition_broadcast(128).rearrange("p o n -> p (o n)"),
              writes=[p.b_bg_bc], key="setup2")
        S.op("dve", lambda e: e.memset(p.ones_f[:], 1.0), writes=[p.b_ones_f])
        S.op("dve", lambda e: e.memset(p.ones_b[:], 1.0), writes=[p.b_ones_b])
        S.op("dve", lambda e: e.memset(p.xh[:], 0.0), writes=[p.b_xh])
        S.op("dve", lambda e: e.tensor_copy(out=p.ident_b[:], in_=p.ident_f[:]), reads=[p.b_ident_f], writes=[p.b_ident_b])
        vec = [(p.ntgT, p.b_ntgT, p.ntg), (p.nfgT, p.b_nfgT, p.nfg), (p.ffn_cwT, p.b_ffn_cwT, p.ffn_cw), (p.ffn_cbT, p.b_ffn_cbT, p.ffn_cb),
               (p.ret_gnT, p.b_ret_gnT, p.ret_gn), (p.ml_gnT, p.b_ml_gnT, p.ml_gn), (p.ml_cwT, p.b_ml_cwT, p.ml_cw), (p.ml_cbT, p.b_ml_cbT, p.ml_cb),
               (p.adabT, p.b_adabT, p.ada_b), (p.cT_f, p.b_cT_f, p.c_in), (p.selmat, p.b_selmat, p.t_selmat)]
        self.dma_group("sp", "setup3", [(t[:], s) for t, b, s in vec], writes=[b for t, b, s in vec])
        if p.mod_layers:
            S.op("act", lambda e: e.activation(out=p.cT_b[:], in_=p.cT_f[:], func=AF.Silu), reads=[p.b_cT_f], writes=[p.b_cT_b])
            for l in p.mod_layers:
                for pc in range(12):
                    wv, wb = self.load_std_piece(p.ada_w[l], pc * 512)
                    for ct in range(4):
                        col = l * 48 + pc * 4 + ct
                        for kc in range(KC):
                            S.op("pe", lambda e: e.matmul(p.P[4][:, col:col + 1], lhsT=wv[:, kc, ct * 128:(ct + 1) * 128],
                                                          rhs=p.cT_b[:, kc:kc + 1], start=(kc == 0), stop=(kc == KC - 1)),
                                 reads=[wb, p.b_cT_b], writes=[p.b_P[4]])
            for l in p.mod_layers:
                S.op("dve", lambda e: e.tensor_tensor(out=p.modT[:, l, :], in0=p.adabT[:, l, :], in1=p.P[4][:, l * 48:(l + 1) * 48], op=ALU.add),
                     reads=[p.b_P[4], p.b_adabT], writes=[p.b_modT])
                S.op("pe", lambda e: e.transpose(p.P[5][0:48, 0:128], p.modT[:, l, :], p.ident_f[:, :]),
                     reads=[p.b_modT, p.b_ident_f], writes=[p.b_P[5]])
                S.op("dve", lambda e: e.tensor_copy(out=p.stage[0:48, :], in_=p.P[5][0:48, 0:128]), reads=[p.b_P[5]], writes=[p.b_stage])
                S.dma("sp", p.modscr[l], p.stage[0:48, :], reads=[p.b_stage], writes=[p.b_modscr], key="modscr")
            if p.phase == 1:
                S.dma("sp", p.modT_o[:, :], p.modT[:].rearrange("p l n -> p (l n)"), reads=[p.b_modT], writes=[p.b_modT_o], key="modT_o")
        else:
            S.dma("sp", p.modT[:].rearrange("p l n -> p (l n)"), p.modT_i[:, :], writes=[p.b_modT], key="modT_i")
        for l in p.layers:
            S.op("dve", lambda e: e.scalar_tensor_tensor(out=p.gsc[:, 2 * l, :], in0=p.modT[:, l, 8:16], scalar=1.0,
                                                         in1=p.ntgT[:, l, :], op0=ALU.add, op1=ALU.mult),
                 reads=[p.b_modT, p.b_ntgT], writes=[p.b_gsc])
            S.op("dve", lambda e: e.scalar_tensor_tensor(out=p.gsc[:, 2 * l + 1, :], in0=p.modT[:, l, 32:40], scalar=1.0,
                                                         in1=p.nfgT[:, l, :], op0=ALU.add, op1=ALU.mult),
                 reads=[p.b_modT, p.b_nfgT], writes=[p.b_gsc])

    def load_gate(self, l, which):
        p, S = self, self.S
        r0 = 16 if which == 0 else 40
        src = p.modscr[l, r0:r0 + 8, :].rearrange("(o a) b -> o (a b)", o=1).partition_broadcast(128).rearrange("p o n -> p (o n)")
        S.dma("sp", p.gt_bc[:], src, reads=[p.b_modscr], writes=[p.b_gt_bc], key="gt")

    def norm_rows(self, xt, bx, npart, gidx, sh, dst_fn, dst_buf, col):
        p, S = self, self.S
        S.op("act", lambda e: e.activation(out=p.xn[0:npart, :], in_=xt, func=AF.Square, accum_out=p.ss[0:npart, col:col + 1]),
             reads=[bx], writes=[p.b_xn, p.b_ss])
        S.op("dve", lambda e: e.tensor_scalar(out=p.ss[0:npart, col:col + 1], in0=p.ss[0:npart, col:col + 1], scalar1=1.0 / D, scalar2=EPS,
                                              op0=ALU.mult, op1=ALU.add), reads=[p.b_ss], writes=[p.b_ss])
        S.op("act", lambda e: e.activation(out=p.ss[0:npart, col:col + 1], in_=p.ss[0:npart, col:col + 1], func=AF.Sqrt),
             reads=[p.b_ss], writes=[p.b_ss])
        S.op("dve", lambda e: e.reciprocal(out=p.rstd[0:npart, col:col + 1], in_=p.ss[0:npart, col:col + 1]),
             reads=[p.b_ss], writes=[p.b_rstd])
        S.op("act", lambda e: e.activation(out=p.xn[0:npart, :], in_=xt, func=AF.Copy, scale=p.rstd[0:npart, col:col + 1]),
             reads=[bx, p.b_rstd], writes=[p.b_xn])
        for half in range(2):
            bank = p.P[half]
            for k4 in range(4):
                kc = half * 4 + k4
                S.op("pe", lambda e: e.transpose(bank[:, k4 * 128:k4 * 128 + npart], p.xn[0:npart, kc * 128:(kc + 1) * 128],
                                                 p.ident_f[0:npart, 0:npart]),
                     reads=[p.b_xn, p.b_ident_f], writes=[p.b_P[half]])
            for k4 in range(4):
                kc = half * 4 + k4
                src = bank[:, k4 * 128:k4 * 128 + npart]
                if kc % 2 == 0:
                    S.op("act", lambda e: e.activation(out=dst_fn(kc), in_=src, func=AF.Identity,
                                                       scale=p.gsc[:, gidx, kc:kc + 1], bias=sh[:, kc:kc + 1]),
                         reads=[p.b_P[half], p.b_gsc, p.b_modT], writes=[dst_buf])
                else:
                    S.op("dve", lambda e: e.tensor_scalar(out=dst_fn(kc), in0=src, scalar1=p.gsc[:, gidx, kc:kc + 1],
                                                          scalar2=sh[:, kc:kc + 1], op0=ALU.mult, op1=ALU.add),
                         reads=[p.b_P[half], p.b_gsc, p.b_modT], writes=[dst_buf])

    def norm_group(self, src, src_bufs, g, gidx, sh):
        p, S = self, self.S
        for c in range(G):
            r0 = g * GT + c * CH
            S.dma("sp", p.x_g[:, c, :], src[r0:r0 + CH, :], reads=src_bufs, writes=[p.b_xg[c]], key=f"xg{c}")
        for c in range(G):
            S.op("act", lambda e: e.activation(out=p.junk[:], in_=p.x_g[:, c, :], func=AF.Square, accum_out=p.ss[:, c:c + 1]),
                 reads=[p.b_xg[c]], writes=[p.b_junk, p.b_ss])
        S.op("dve", lambda e: e.tensor_scalar(out=p.ss[:, 0:G], in0=p.ss[:, 0:G], scalar1=1.0 / D, scalar2=EPS, op0=ALU.mult, op1=ALU.add),
             reads=[p.b_ss], writes=[p.b_ss])
        S.op("act", lambda e: e.activation(out=p.ss[:, 0:G], in_=p.ss[:, 0:G], func=AF.Sqrt), reads=[p.b_ss], writes=[p.b_ss])
        S.op("dve", lambda e: e.reciprocal(out=p.rstd[:, 0:G], in_=p.ss[:, 0:G]), reads=[p.b_ss], writes=[p.b_rstd])
        for c in range(G):
            xn, b_xn = (p.xn, p.b_xn) if c % 2 == 0 else (p.xn2, p.b_xn2)
            S.op("act", lambda e: e.activation(out=xn[:], in_=p.x_g[:, c, :], func=AF.Copy, scale=p.rstd[:, c:c + 1]),
                 reads=[p.b_xg[c], p.b_rstd], writes=[b_xn])
            for half in range(2):
                bi = 2 * (c % 2) + half
                bank = p.P[bi]
                for k4 in range(4):
                    kc = half * 4 + k4
                    S.op("pe", lambda e: e.transpose(bank[:, k4 * 128:(k4 + 1) * 128], xn[:, kc * 128:(kc + 1) * 128], p.ident_f[:, :]),
                         reads=[b_xn, p.b_ident_f], writes=[p.b_P[bi]])
                for k4 in range(4):
                    kc = half * 4 + k4
                    srcp = bank[:, k4 * 128:(k4 + 1) * 128]
                    dst = p.hT[:, kc, c * CH:(c + 1) * CH]
                    if kc % 2 == 0:
                        S.op("act", lambda e: e.activation(out=dst, in_=srcp, func=AF.Identity,
                                                           scale=p.gsc[:, gidx, kc:kc + 1], bias=sh[:, kc:kc + 1]),
                             reads=[p.b_P[bi], p.b_gsc, p.b_modT], writes=[p.b_hT])
                    else:
                        S.op("dve", lambda e: e.tensor_scalar(out=dst, in0=srcp, scalar1=p.gsc[:, gidx, kc:kc + 1],
                                                              scalar2=sh[:, kc:kc + 1], op0=ALU.mult, op1=ALU.add),
                             reads=[p.b_P[bi], p.b_gsc, p.b_modT], writes=[p.b_hT])

    def load_halo(self, src, src_bufs, g, ex):
        p, S = self, self.S
        if g > 0:
            r0 = g * GT - HW
            S.dma("sp", p.xh[0:HW, :], src[r0:r0 + HW, :], reads=src_bufs, writes=[p.b_xh], key="xh")
        else:
            gt = p.gath[ex]
            nr = NCORES * HW
            S.dma("sp", p.xn[0:nr, :], gt[:, :], reads=[p.b_gath[ex]], writes=[p.b_xn], key="xnh")
            for half in range(2):
                S.op("pe", lambda e: e.matmul(p.P[half][0:HW, :], lhsT=p.selmat[0:nr, 0:HW], rhs=p.xn[0:nr, half * 512:(half + 1) * 512],
                                              start=True, stop=True), reads=[p.b_selmat, p.b_xn], writes=[p.b_P[half]])
                S.op("dve", lambda e: e.tensor_copy(out=p.xh[0:HW, half * 512:(half + 1) * 512], in_=p.P[half][0:HW, :]),
                     reads=[p.b_P[half]], writes=[p.b_xh])

    def norm_halo(self, gidx, sh):
        p = self
        self.norm_rows(p.xh[0:32, :], p.b_xh, 32, gidx, sh, lambda kc: p.hTh[:, kc, :], p.b_hTh, 4)

    def rope_tables(self, g):
        p, S = self, self.S
        posi = p.wk[4][:].bitcast(I32)
        src = p.pos[0:1, g * GT:(g + 1) * GT].partition_broadcast(128).rearrange("p o n -> p (o n)")
        S.dma("sp", posi, src, writes=[p.b_wk[4]], key="posi")
        ang, b_ang = p.wk[0], p.b_wk[0]
        S.op("dve", lambda e: e.tensor_copy(out=p.wk[1][:], in_=posi), reads=[p.b_wk[4]], writes=[p.b_wk[1]])
        S.op("dve", lambda e: e.tensor_scalar(out=ang[:], in0=p.wk[1][:], scalar1=p.inv_freq[:, 0:1], scalar2=None, op0=ALU.mult),
             reads=[p.b_wk[1], p.b_inv_freq], writes=[b_ang])
        for dst, b_dst, shift in ((p.sin, p.b_sin, 0.0), (p.cos, p.b_cos, 0.5 * math.pi)):
            xs, b_xs = p.wk[1], p.b_wk[1]
            kf, b_kf = p.wk[2], p.b_wk[2]
            ki = p.wk[3][:].bitcast(I32)
            b_ki = p.b_wk[3]
            S.op("dve", lambda e: e.tensor_scalar(out=xs[:], in0=ang[:], scalar1=shift, scalar2=None, op0=ALU.add),
                 reads=[b_ang], writes=[b_xs])
            S.op("dve", lambda e: e.tensor_scalar(out=kf[:], in0=xs[:], scalar1=1.0 / TWO_PI, scalar2=None, op0=ALU.mult),
                 reads=[b_xs], writes=[b_kf])
            S.op("dve", lambda e: e.tensor_copy(out=ki, in_=kf[:]), reads=[b_kf], writes=[b_ki])
            S.op("dve", lambda e: e.tensor_copy(out=kf[:], in_=ki), reads=[b_ki], writes=[b_kf])
            S.op("dve", lambda e: e.scalar_tensor_tensor(out=xs[:], in0=kf[:], scalar=-C1, in1=xs[:], op0=ALU.mult, op1=ALU.add),
                 reads=[b_kf, b_xs], writes=[b_xs])
            S.op("dve", lambda e: e.scalar_tensor_tensor(out=xs[:], in0=kf[:], scalar=-C2, in1=xs[:], op0=ALU.mult, op1=ALU.add),
                 reads=[b_kf, b_xs], writes=[b_xs])
            S.op("dve", lambda e: e.tensor_scalar(out=kf[:], in0=xs[:], scalar1=-math.pi, scalar2=TWO_PI, op0=ALU.is_lt, op1=ALU.mult),
                 reads=[b_xs], writes=[b_kf])
            S.op("dve", lambda e: e.tensor_tensor(out=xs[:], in0=xs[:], in1=kf[:], op=ALU.add), reads=[b_xs, b_kf], writes=[b_xs])
            S.op("dve", lambda e: e.tensor_scalar(out=kf[:], in0=xs[:], scalar1=math.pi, scalar2=-TWO_PI, op0=ALU.is_gt, op1=ALU.mult),
                 reads=[b_xs], writes=[b_kf])
            S.op("dve", lambda e: e.tensor_tensor(out=xs[:], in0=xs[:], in1=kf[:], op=ALU.add), reads=[b_xs, b_kf], writes=[b_xs])
            S.op("dve", lambda e: e.tensor_scalar(out=xs[:], in0=xs[:], scalar1=-PI_SAFE, scalar2=PI_SAFE, op0=ALU.max, op1=ALU.min),
                 reads=[b_xs], writes=[b_xs])
            S.op("act", lambda e: e.activation(out=dst[:], in_=xs[:], func=AF.Sin), reads=[b_xs], writes=[b_dst])

    def rope_pair(self, hh, dst, b_dst):
        p, S = self, self.S
        i1, i2 = 2 * (hh % 2), 2 * (hh % 2) + 1
        b1, b2 = p.P[i1], p.P[i2]
        A, Bm, C_, Dm = p.wk[0], p.wk[1], p.wk[2], p.wk[3]
        S.op("dve", lambda e: e.tensor_tensor(out=A[:], in0=b1[:], in1=p.cos[:], op=ALU.mult), reads=[p.b_P[i1], p.b_cos], writes=[p.b_wk[0]])
        S.op("dve", lambda e: e.tensor_tensor(out=Bm[:], in0=b2[:], in1=p.sin[:], op=ALU.mult), reads=[p.b_P[i2], p.b_sin], writes=[p.b_wk[1]])
        S.op("dve", lambda e: e.tensor_tensor(out=C_[:], in0=b1[:], in1=p.sin[:], op=ALU.mult), reads=[p.b_P[i1], p.b_sin], writes=[p.b_wk[2]])
        S.op("dve", lambda e: e.tensor_tensor(out=Dm[:], in0=b2[:], in1=p.cos[:], op=ALU.mult), reads=[p.b_P[i2], p.b_cos], writes=[p.b_wk[3]])
        S.op("dve", lambda e: e.tensor_tensor(out=dst[:, 2 * hh, :], in0=A[:], in1=Bm[:], op=ALU.subtract),
             reads=[p.b_wk[0], p.b_wk[1]], writes=[b_dst])
        S.op("dve", lambda e: e.tensor_tensor(out=dst[:, 2 * hh + 1, :], in0=C_[:], in1=Dm[:], op=ALU.add),
             reads=[p.b_wk[2], p.b_wk[3]], writes=[b_dst])

    def proj_A(self, wv, wb, ct, bank_i, hT=None, b_hT=None, n=GT):
        p, S = self, self.S
        hT = p.hT if hT is None else hT
        b_hT = p.b_hT if b_hT is None else b_hT
        for kc in range(KC):
            S.op("pe", lambda e: e.matmul(p.P[bank_i][:, 0:n], lhsT=wv[:, kc, ct * 128:(ct + 1) * 128], rhs=hT[:, kc, 0:n],
                                          start=(kc == 0), stop=(kc == KC - 1)),
                 reads=[wb, b_hT], writes=[p.b_P[bank_i]])

    def proj_B(self, wv, wb, c, bank_i, w=512):
        p, S = self, self.S
        for kc in range(KC):
            S.op("pe", lambda e: e.matmul(p.P[bank_i][:, 0:w], lhsT=p.hT[:, kc, c * CH:(c + 1) * CH], rhs=wv[:, kc, 0:w],
                                          start=(kc == 0), stop=(kc == KC - 1)),
                 reads=[wb, p.b_hT], writes=[p.b_P[bank_i]])

    def exchange(self, k):
        p, S = self, self.S
        if p.mode == "host":
            return
        S.wait_bufs("pool", [p.b_loc[k], p.b_gath[k]])
        ins = p.nc.gpsimd.collective_compute("AllGather", ALU.bypass, replica_groups=[list(range(NCORES))],
                                             ins=[p.loc[k][:, :]], outs=[p.gath[k][:, :]])
        key = S._dma_sem(f"cc{k}")
        S.cnt[key] += 16
        ins.then_inc(S.sems[key], 16)
        S._record((key, S.cnt[key]), [p.b_loc[k]], [p.b_gath[k]])

    def make_kz(self, c, scale_ap_fn):
        p, S = self, self.S
        cs = slice(c * CH, (c + 1) * CH)
        for i in range(8):
            S.op("pe", lambda e: e.transpose(p.Pb[0][:, i * 128:(i + 1) * 128], p.kT[:, i, cs], p.ident_b[:, :]),
                 reads=[p.b_kT, p.b_ident_b], writes=[p.b_Pb[0]])
        for hh in range(H):
            sc, sbufs = scale_ap_fn(hh)
            S.op("act", lambda e: e.activation(out=p.kz[:, hh * 256:(hh + 1) * 256], in_=p.Pb[0][:, hh * 256:(hh + 1) * 256],
                                               func=AF.Copy, scale=sc),
                 reads=[p.b_Pb[0]] + sbufs, writes=[p.b_kz])

    def state_update(self, c, hh, decay, dbufs=()):
        p, S = self, self.S
        dbufs = list(dbufs)
        for half in range(2):
            i = 2 * hh + half
            S.op("pe", lambda e: e.matmul(p.P[2 + half][:, :], lhsT=p.kz[:, hh * 256 + half * 128: hh * 256 + (half + 1) * 128],
                                          rhs=p.v_all[:, c, hh * 512:(hh + 1) * 512], start=True, stop=True),
                 reads=[p.b_kz, p.b_v[c]], writes=[p.b_P[2 + half]])
            S.op("dve", lambda e: e.scalar_tensor_tensor(out=p.R[:, i, :], in0=p.R[:, i, :], scalar=decay, in1=p.P[2 + half][:, :],
                                                         op0=ALU.mult, op1=ALU.add),
                 reads=[p.b_R[i], p.b_P[2 + half]] + dbufs, writes=[p.b_R[i]])

    def refresh_Rb(self, hh):
        p, S = self, self.S
        for half in range(2):
            i = 2 * hh + half
            S.op("act", lambda e: e.activation(out=p.Rb[:, i, :], in_=p.R[:, i, :], func=AF.Copy),
                 reads=[p.b_R[i]], writes=[p.b_Rb[i]])

    def groupnorm_heads(self, ywk, c, gate):
        p, S = self, self.S
        for hh in range(H):
            S.op("dve", lambda e: e.bn_stats(out=p.sm[:, 6 * hh:6 * hh + 6], in_=p.wk[ywk[hh]][:]), reads=[p.b_wk[ywk[hh]]], writes=[p.b_sm])
            S.op("dve", lambda e: e.bn_aggr(out=p.sm[:, 24 + 2 * hh:26 + 2 * hh], in_=p.sm[:, 6 * hh:6 * hh + 6]), reads=[p.b_sm], writes=[p.b_sm])
        var = p.sm[:, 24:32].rearrange("p (h t) -> p h t", t=2)[:, :, 1]
        S.op("dve", lambda e: e.tensor_scalar(out=p.sm[:, 32:36], in0=var, scalar1=EPS, scalar2=None, op0=ALU.add), reads=[p.b_sm], writes=[p.b_sm])
        S.op("act", lambda e: e.activation(out=p.sm[:, 32:36], in_=p.sm[:, 32:36], func=AF.Ln), reads=[p.b_sm], writes=[p.b_sm])
        S.op("act", lambda e: e.activation(out=p.sm[:, 32:36], in_=p.sm[:, 32:36], func=AF.Exp, scale=-0.5), reads=[p.b_sm], writes=[p.b_sm])
        for hh in range(H):
            w = p.wk[ywk[hh]]
            if gate:
                S.op("dve", lambda e: e.tensor_scalar(out=w[:], in0=w[:], scalar1=p.sm[:, 24 + 2 * hh:25 + 2 * hh], scalar2=p.sm[:, 32 + hh:33 + hh],
                                                      op0=ALU.subtract, op1=ALU.mult), reads=[p.b_wk[ywk[hh]], p.b_sm], writes=[p.b_wk[ywk[hh]]])
                sgv = p.sg_all[:, c, hh * 512:(hh + 1) * 512]
                S.op("dve", lambda e: e.tensor_tensor(out=p.yg[:, hh * 512:(hh + 1) * 512], in0=w[:], in1=sgv, op=ALU.mult),
                     reads=[p.b_wk[ywk[hh]], p.b_xg[c]], writes=[p.b_yg])
            else:
                S.op("dve", lambda e: e.tensor_scalar(out=p.yg[:, hh * 512:(hh + 1) * 512], in0=w[:], scalar1=p.sm[:, 24 + 2 * hh:25 + 2 * hh],
                                                      scalar2=p.sm[:, 32 + hh:33 + hh], op0=ALU.subtract, op1=ALU.mult),
                     reads=[p.b_wk[ywk[hh]], p.b_sm], writes=[p.b_yg])

    def make_ynT(self, c, gnT, b_gnT):
        p, S = self, self.S
        ynT = p.big_a[:, 0:16 * GT].rearrange("p (k n) -> p k n", k=16)
        for kc in range(16):
            bi = kc // 8
            S.op("pe", lambda e: e.transpose(p.Pb[bi][:, (kc % 8) * 128:(kc % 8 + 1) * 128], p.yg[:, kc * 128:(kc + 1) * 128], p.ident_b[:, :]),
                 reads=[p.b_yg, p.b_ident_b], writes=[p.b_Pb[bi]])
        for kc in range(16):
            bi = kc // 8
            src = p.Pb[bi][:, (kc % 8) * 128:(kc % 8 + 1) * 128]
            dst = ynT[:, kc, c * CH:(c + 1) * CH]
            if kc % 2 == 0:
                S.op("act", lambda e: e.activation(out=dst, in_=src, func=AF.Copy, scale=gnT[:, kc:kc + 1]),
                     reads=[p.b_Pb[bi], b_gnT], writes=[p.b_big_a])
            else:
                S.op("dve", lambda e: e.tensor_scalar(out=dst, in0=src, scalar1=gnT[:, kc:kc + 1], scalar2=None, op0=ALU.mult),
                     reads=[p.b_Pb[bi], b_gnT], writes=[p.b_big_a])

    def out_proj(self, g, W_out, src, src_bufs, dst, dst_bufs, halo_ex):
        p, S = self, self.S
        ynT = p.big_a[:, 0:16 * GT].rearrange("p (k n) -> p k n", k=16)
        for c in range(G):
            r0 = g * GT + c * CH
            S.dma("sp", p.x_g[:, c, :], src[r0:r0 + CH, :], reads=src_bufs, writes=[p.b_xg[c]], key=f"xg{c}")
        for cp in range(4):
            wv, wb = self.load_rows_piece(W_out, 0, 16, cp * 256, 256)
            for c in range(G):
                bi = (cp * G + c) % 6
                for kc in range(16):
                    S.op("pe", lambda e: e.matmul(p.P[bi][:, 0:256], lhsT=ynT[:, kc, c * CH:(c + 1) * CH], rhs=wv[:, kc, :],
                                                  start=(kc == 0), stop=(kc == 15)),
                         reads=[wb, p.b_big_a], writes=[p.b_P[bi]])
                ti = (cp * G + c) % 5
                tmp = p.wk[ti]
                S.op("dve", lambda e: e.tensor_tensor(out=tmp[:, 0:256], in0=p.P[bi][:, 0:256], in1=p.gt_bc[:, cp * 256:(cp + 1) * 256], op=ALU.mult),
                     reads=[p.b_P[bi], p.b_gt_bc], writes=[p.b_wk[ti]])
                S.op("dve", lambda e: e.tensor_tensor(out=p.x_g[:, c, cp * 256:(cp + 1) * 256], in0=p.x_g[:, c, cp * 256:(cp + 1) * 256],
                                                      in1=tmp[:, 0:256], op=ALU.add),
                     reads=[p.b_wk[ti], p.b_xg[c]], writes=[p.b_xg[c]])
        self.store_group(g, dst, dst_bufs, halo_ex)

    def store_group(self, g, dst, dst_bufs, halo_ex):
        p, S = self, self.S
        pairs = [(dst[g * GT + c * CH: g * GT + (c + 1) * CH, :], p.x_g[:, c, :]) for c in range(G)]
        self.dma_group("sp", f"st_{dst_bufs[0].name[:2]}", pairs, reads=p.b_xg, writes=[dst_bufs[g]])
        if g == NG - 1 and halo_ex is not None:
            S.dma("sp", p.loc[halo_ex][:, :], p.x_g[128 - HW:128, G - 1, :], reads=[p.b_xg[G - 1]], writes=[p.b_loc[halo_ex]], key=f"loc{halo_ex}")

    def kv_store(self, g):
        p = self
        self.dma_group("sp", "kvst", [(p.kst[g], p.kT[:].rearrange("p a n -> p (a n)")), (p.vst[g], p.v_all[:].rearrange("p a n -> p (a n)"))],
                       reads=[p.b_kT] + p.b_v, writes=[p.b_kvst])

    def kv_load(self, g):
        p = self
        self.dma_group("sp", "kvld", [(p.kT[:].rearrange("p a n -> p (a n)"), p.kst[g]), (p.v_all[:].rearrange("p a n -> p (a n)"), p.vst[g])],
                       writes=[p.b_kT] + p.b_v)

    def ret_group(self, g, full):
        p, S = self, self.S
        sh = p.modT[:, 0, 0:8]
        self.norm_group(p.x, [], g, 0, sh)
        self.rope_tables(g)
        W = p.ret_w_in
        reuse = full and p.relay
        plist = ([0, 1] if full else []) + ([] if reuse else [2, 3])
        if reuse:
            self.kv_load(g)
        for pc in plist:
            wv, wb = self.load_std_piece(W, pc * 512)
            dst, b_dst = (p.qT, p.b_qT) if pc < 2 else (p.kT, p.b_kT)
            for ct in range(4):
                self.proj_A(wv, wb, ct, ct)
            for j in range(2):
                self.rope_pair(2 * (pc % 2) + j, dst, b_dst)
        for hh in ([] if reuse else range(H)):
            wv, wb = self.load_std_piece(W, 2048 + hh * 512)
            for c in range(G):
                self.proj_B(wv, wb, c, c)
                dstv = p.v_all[:, c, hh * 512:(hh + 1) * 512]
                if c % 2 == 0:
                    S.op("act", lambda e: e.activation(out=dstv, in_=p.P[c][:, :], func=AF.Copy), reads=[p.b_P[c]], writes=[p.b_v[c]])
                else:
                    S.op("dve", lambda e: e.tensor_copy(out=dstv, in_=p.P[c][:, :]), reads=[p.b_P[c]], writes=[p.b_v[c]])
        if (not full) and p.relay:
            self.kv_store(g)
        if full:
            for hh in range(H):
                wv, wb = self.load_std_piece(W, 4096 + hh * 512)
                for c in range(G):
                    self.proj_B(wv, wb, c, c)
                    dsts = p.sg_all[:, c, hh * 512:(hh + 1) * 512]
                    S.op("act", lambda e: e.activation(out=dsts, in_=p.P[c][:, :], func=AF.Silu), reads=[p.b_P[c]], writes=[p.b_xg[c]])
        if full and g == 0:
            self.dump("hT", p.hT[:], [p.b_hT])
            self.dump("cos", p.cos[:], [p.b_cos])
            self.dump("sin", p.sin[:], [p.b_sin])
            self.dump("qT", p.qT[:], [p.b_qT])
            self.dump("kT", p.kT[:], [p.b_kT])
            self.dump("v_all", p.v_all[:], p.b_v)
            self.dump("sg_all", p.sg_all, p.b_xg)
        for c in range(G):
            self.ret_chunk(c, full)
            if full and g == 0 and c == 1:
                self.dump("yg", p.yg[:], [p.b_yg])
                self.dump("sm", p.sm[:], [p.b_sm])
                self.dump("wk0", p.wk[0][:], [p.b_wk[0]])
                self.dump("wk3", p.wk[3][:], [p.b_wk[3]])
                self.dump("sTm", p.sTm[:], [p.b_sTm])
                self.dump("kz", p.kz[:], [p.b_kz])
                self.dump("R", p.R[:], p.b_R)
        if full and g == 0:
            self.dump("ynT", p.big_a[:, 0:16 * GT], [p.b_big_a])
        if full:
            self.out_proj(g, p.ret_w_out, p.x, [], p.xa, p.b_xa, 2)
        if full and g == 0:
            self.dump("xg", p.x_g[:], p.b_xg)

    def ret_chunk(self, c, full):
        p, S = self, self.S
        cs = slice(c * CH, (c + 1) * CH)
        self.make_kz(c, lambda hh: (p.ret_zs[:, hh:hh + 1], [p.b_ret_zs]))
        gam = [float(np.exp(128.0 * np.log(1.0 - 2.0 ** (-5.0 - h)))) for h in range(H)]
        if not full:
            for hh in range(H):
                self.state_update(c, hh, gam[hh])
            return
        for hh in range(H):
            for half in range(2):
                S.op("pe", lambda e: e.matmul(p.P[4][:, hh * 128:(hh + 1) * 128], lhsT=p.kT[:, 2 * hh + half, cs], rhs=p.qT[:, 2 * hh + half, cs],
                                              start=(half == 0), stop=(half == 1)),
                     reads=[p.b_kT, p.b_qT], writes=[p.b_P[4]])
        S.op("dve", lambda e: e.tensor_tensor(out=p.sTm[:], in0=p.P[4][:, :], in1=p.ret_dt[:], op=ALU.mult),
             reads=[p.b_P[4], p.b_ret_dt], writes=[p.b_sTm])
        for hh in range(H):
            S.op("pe", lambda e: e.matmul(p.P[0][:, :], lhsT=p.sTm[:, hh * 128:(hh + 1) * 128], rhs=p.v_all[:, c, hh * 512:(hh + 1) * 512],
                                          start=True, stop=True), reads=[p.b_sTm, p.b_v[c]], writes=[p.b_P[0]])
            for half in range(2):
                i = 2 * hh + half
                S.op("pe", lambda e: e.matmul(p.P[1][:, :], lhsT=p.qT[:, i, cs], rhs=p.Rb[:, i, :], start=(half == 0), stop=(half == 1)),
                     reads=[p.b_qT, p.b_Rb[i]], writes=[p.b_P[1]])
            S.op("act", lambda e: e.activation(out=p.wk[4][:], in_=p.P[1][:, :], func=AF.Copy, scale=p.ret_xi[:, hh:hh + 1]),
                 reads=[p.b_P[1], p.b_ret_xi], writes=[p.b_wk[4]])
            S.op("dve", lambda e: e.tensor_tensor(out=p.wk[hh][:], in0=p.wk[4][:], in1=p.P[0][:, :], op=ALU.add),
                 reads=[p.b_wk[4], p.b_P[0]], writes=[p.b_wk[hh]])
            self.state_update(c, hh, gam[hh])
            self.refresh_Rb(hh)
        self.groupnorm_heads([0, 1, 2, 3], c, gate=True)
        self.make_ynT(c, p.ret_gnT, p.b_ret_gnT)

    def ret_layer(self):
        p, S = self, self.S
        for i in range(8):
            S.op("dve", lambda e: e.memset(p.R[:, i, :], 0.0), writes=[p.b_R[i]])
        if p.stop_after == "dbg_ret":
            for hh in range(H):
                self.refresh_Rb(hh)
            self.load_gate(0, 0)
            self.dump("modT", p.modT[:], [p.b_modT])
            self.dump("gsc", p.gsc[:], [p.b_gsc])
            self.dump("gt_bc", p.gt_bc[:], [p.b_gt_bc])
            self.ret_group(0, full=True)
            return
        if p.phase in (None, 1):
            self.ret_part_a()
        if p.phase is None:
            self.exchange(1)
        if p.phase in (None, 2):
            self.ret_part_b()

    def ret_part_a(self):
        p, S = self, self.S
        for g in range(NG):
            self.ret_group(g, full=False)
        self.dma_group("sp", "loc1", [(p.loc[1][i * 128:(i + 1) * 128, :], p.R[:, i, :]) for i in range(8)],
                       reads=p.b_R, writes=[p.b_loc[1]])

    def ret_part_b(self):
        p, S = self, self.S
        self.combine_state(1, p.ret_coef, p.b_ret_coef, nrows=8)
        for hh in range(H):
            self.refresh_Rb(hh)
        self.load_gate(0, 0)
        for g in range(NG):
            self.ret_group(g, full=True)

    def combine_state(self, ex, coef, b_coef, nrows):
        p, S = self, self.S
        gt = p.gath[ex]
        per = nrows * 128 if ex == 1 else 9 * 128
        for i in range(8):
            hh = i // 2
            for cp in range(NCORES):
                tmp, bt = p.wk[cp % 2], p.b_wk[cp % 2]
                r0 = cp * per + i * 128
                S.dma("sp", tmp[:], gt[r0:r0 + 128, :], reads=[p.b_gath[ex]], writes=[bt], key=f"cmb{cp % 2}")
                if cp == 0:
                    S.op("dve", lambda e: e.tensor_scalar(out=p.R[:, i, :], in0=tmp[:], scalar1=coef[:, cp * H + hh: cp * H + hh + 1], scalar2=None,
                                                          op0=ALU.mult), reads=[bt, b_coef], writes=[p.b_R[i]])
                else:
                    S.op("dve", lambda e: e.scalar_tensor_tensor(out=p.R[:, i, :], in0=tmp[:], scalar=coef[:, cp * H + hh: cp * H + hh + 1],
                                                                 in1=p.R[:, i, :], op0=ALU.mult, op1=ALU.add),
                         reads=[bt, b_coef, p.b_R[i]], writes=[p.b_R[i]])

    def ffn_layer(self, l, src, src_bufs, dst, dst_bufs, ex_in, ex_out, final):
        p, S = self, self.S
        S.barrier()
        self.load_gate(l, 1)
        sh = p.modT[:, l, 24:32]
        gidx = 2 * l + 1
        actT = p.big_a[:, :].rearrange("p (k n) -> p k n", k=22)
        cw = p.ffn_cwT
        cb = p.ffn_cbT
        Wup = p.ffn_w_up[l]
        Wdn = p.ffn_w_down[l]
        for g in range(NG):
            if g == 0:
                self.load_halo(src, src_bufs, g, ex_in)
                self.norm_halo(gidx, sh)
            self.norm_group(src, src_bufs, g, gidx, sh)
            srcw = Wup.rearrange("(k p) n -> p k n", p=128)

            def load_up(pc):
                t, b, key = self.ring()
                wv = t[:, 0:4096].rearrange("p (k n) -> p k n", k=8)
                pairs = []
                for kh in range(2):
                    pairs.append((wv[:, 4 * kh:4 * kh + 4, 0:256], srcw[:, 4 * kh:4 * kh + 4, pc * 256:(pc + 1) * 256]))
                    pairs.append((wv[:, 4 * kh:4 * kh + 4, 256:512], srcw[:, 4 * kh:4 * kh + 4, DFF + pc * 256: DFF + (pc + 1) * 256]))
                self.dma_group("pool", key, pairs, writes=[b])
                return wv, b

            loaders = [(lambda pc=pc: load_up(pc)) for pc in range(11)]
            loaders += [(lambda cp=cp, kh=kh: self.load_rows_piece(Wdn, 11 * kh, 11, cp * 256, 256)) for cp in range(4) for kh in range(2)]
            loaded = {}

            def get_piece(i, ahead=2):
                for j in range(i, min(i + ahead + 1, len(loaders))):
                    if j not in loaded:
                        loaded[j] = loaders[j]()
                return loaded.pop(i)

            banks = [0, 1, 2, 3, 5]
            for pc in range(11):
                wv, b = get_piece(pc)
                accs = {}
                for ct in (0, 2, 1, 3):
                    tile_i = 2 * pc + (ct % 2)
                    chan = tile_i if ct < 2 else 22 + tile_i
                    bi = banks[self.rot_bank % len(banks)]
                    self.rot_bank += 1
                    ai = self.rot_acc % 5
                    self.rot_acc += 1
                    ui = self.rot_ub % 2
                    self.rot_ub += 1
                    self.proj_A(wv, b, ct, bi)
                    ub, b_ub = p.ubuf[ui], p.b_ubuf[ui]
                    if g == 0:
                        for kc in range(KC):
                            S.op("pe", lambda e: e.matmul(p.P[4][:, 0:HW], lhsT=wv[:, kc, ct * 128:(ct + 1) * 128], rhs=p.hTh[:, kc, 0:HW],
                                                          start=(kc == 0), stop=(kc == KC - 1)), reads=[b, p.b_hTh], writes=[p.b_P[4]])
                        S.op("dve", lambda e: e.tensor_scalar(out=ub[:, 0:HW], in0=p.P[4][:, 0:HW], scalar1=p.nf[:, 0:1], scalar2=None, op0=ALU.mult),
                             reads=[p.b_P[4], p.b_nf], writes=[b_ub])
                    else:
                        S.op("dve", lambda e: e.tensor_copy(out=ub[:, 0:HW], in_=p.uhalo[:, chan, :]), reads=[p.b_uhalo], writes=[b_ub])
                    S.op("act", lambda e: e.activation(out=ub[:, HW:HW + GT], in_=p.P[bi][:, :], func=AF.Copy), reads=[p.b_P[bi]], writes=[b_ub])
                    if g < NG - 1:
                        S.op("dve", lambda e: e.tensor_copy(out=p.uhalo[:, chan, :], in_=ub[:, GT:GT + HW]), reads=[b_ub], writes=[p.b_uhalo])
                    acc, b_acc = p.wk[ai], p.b_wk[ai]
                    accs[ct] = (acc, b_acc)
                    S.op("act", lambda e: e.activation(out=acc[:], in_=p.P[bi][:, :], func=AF.Identity, scale=cw[:, l, 88 + chan:89 + chan],
                                                       bias=cb[:, l, chan:chan + 1]), reads=[p.b_P[bi], p.b_ffn_cwT, p.b_ffn_cbT], writes=[b_acc])
                    S.op("dve", lambda e: e.scalar_tensor_tensor(out=acc[:], in0=ub[:, HW - 1:HW - 1 + GT], scalar=cw[:, l, 44 + chan:45 + chan], in1=acc[:],
                                                                 op0=ALU.mult, op1=ALU.add), reads=[b_ub, b_acc, p.b_ffn_cwT], writes=[b_acc])
                    S.op("dve", lambda e: e.scalar_tensor_tensor(out=acc[:], in0=ub[:, HW - 2:HW - 2 + GT], scalar=cw[:, l, chan:chan + 1], in1=acc[:],
                                                                 op0=ALU.mult, op1=ALU.add), reads=[b_ub, b_acc, p.b_ffn_cwT], writes=[b_acc])
                    if ct < 2:
                        S.op("act", lambda e: e.activation(out=acc[:], in_=acc[:], func=AF.Silu), reads=[b_acc], writes=[b_acc])
                    else:
                        j = ct - 2
                        (aa, b_aa), (ab, b_ab) = accs[j], accs[ct]
                        S.op("dve", lambda e: e.tensor_tensor(out=actT[:, 2 * pc + j, :], in0=aa[:], in1=ab[:], op=ALU.mult),
                             reads=[b_aa, b_ab], writes=[p.b_actT[2 * pc + j]])
            for cp in range(4):
                for kh in range(2):
                    wv, wb = get_piece(11 + cp * 2 + kh)
                    for c in range(G):
                        for k in range(11):
                            S.op("pe", lambda e: e.matmul(p.P[c][:, 0:256], lhsT=actT[:, 11 * kh + k, c * CH:(c + 1) * CH], rhs=wv[:, k, :],
                                                          start=(kh == 0 and k == 0), stop=(kh == 1 and k == 10)),
                                 reads=[wb, p.b_actT[11 * kh + k]], writes=[p.b_P[c]])
                for c in range(G):
                    ti = self.rot_acc % 5
                    self.rot_acc += 1
                    tmp = p.wk[ti]
                    S.op("dve", lambda e: e.tensor_tensor(out=tmp[:, 0:256], in0=p.P[c][:, 0:256], in1=p.gt_bc[:, cp * 256:(cp + 1) * 256], op=ALU.mult),
                         reads=[p.b_P[c], p.b_gt_bc], writes=[p.b_wk[ti]])
                    S.op("dve", lambda e: e.tensor_tensor(out=p.x_g[:, c, cp * 256:(cp + 1) * 256], in0=p.x_g[:, c, cp * 256:(cp + 1) * 256],
                                                           in1=tmp[:, 0:256], op=ALU.add),
                         reads=[p.b_wk[ti], p.b_xg[c]], writes=[p.b_xg[c]])
            if final:
                self.final_norm_group()
            self.store_group(g, dst, dst_bufs, ex_out)
        S.barrier()

    def final_norm_group(self):
        p, S = self, self.S
        for c in range(G):
            S.op("act", lambda e: e.activation(out=p.xn[:], in_=p.x_g[:, c, :], func=AF.Square, accum_out=p.ss[:, c:c + 1]),
                 reads=[p.b_xg[c]], writes=[p.b_xn, p.b_ss])
        S.op("dve", lambda e: e.tensor_scalar(out=p.ss[:, 0:G], in0=p.ss[:, 0:G], scalar1=1.0 / D, scalar2=EPS, op0=ALU.mult, op1=ALU.add),
             reads=[p.b_ss], writes=[p.b_ss])
        S.op("act", lambda e: e.activation(out=p.ss[:, 0:G], in_=p.ss[:, 0:G], func=AF.Sqrt), reads=[p.b_ss], writes=[p.b_ss])
        S.op("dve", lambda e: e.reciprocal(out=p.rstd[:, 0:G], in_=p.ss[:, 0:G]), reads=[p.b_ss], writes=[p.b_rstd])
        for c in range(G):
            S.op("act", lambda e: e.activation(out=p.x_g[:, c, :], in_=p.x_g[:, c, :], func=AF.Copy, scale=p.rstd[:, c:c + 1]),
                 reads=[p.b_xg[c], p.b_rstd], writes=[p.b_xg[c]])
            for half, (t, b) in enumerate(((p.cos, p.b_cos), (p.sin, p.b_sin))):
                S.op("dve", lambda e: e.tensor_tensor(out=p.x_g[:, c, half * 512:(half + 1) * 512], in0=p.x_g[:, c, half * 512:(half + 1) * 512],
                                                      in1=t[:], op=ALU.mult), reads=[p.b_xg[c], b], writes=[p.b_xg[c]])

    LN16 = math.log(16.0)

    def ml_group(self, g, full):
        p, S = self, self.S
        sh = p.modT[:, 1, 0:8]
        W = p.ml_w_in
        if g == 0:
            self.load_halo(p.xb, p.b_xb, g, 3)
            self.norm_halo(2, sh)
        self.norm_group(p.xb, p.b_xb, g, 2, sh)
        reuse = full and p.relay
        plist = ([0, 1] if full else []) + ([] if reuse else [2, 3])
        if reuse:
            self.kv_load(g)
        for pc in plist:
            wv, wb = self.load_std_piece(W, pc * 512)
            dst, b_dst = (p.qT, p.b_qT) if pc < 2 else (p.kT, p.b_kT)
            for ct in range(4):
                cti = pc * 4 + ct
                self.proj_A(wv, wb, ct, ct)
                ub, b_ub = p.ubuf[ct % 2], p.b_ubuf[ct % 2]
                if g == 0:
                    for kc in range(KC):
                        S.op("pe", lambda e: e.matmul(p.P[4][:, 0:HW], lhsT=wv[:, kc, ct * 128:(ct + 1) * 128], rhs=p.hTh[:, kc, 0:HW],
                                                      start=(kc == 0), stop=(kc == KC - 1)), reads=[wb, p.b_hTh], writes=[p.b_P[4]])
                    S.op("dve", lambda e: e.tensor_scalar(out=ub[:, 0:HW], in0=p.P[4][:, 0:HW], scalar1=p.nf[:, 0:1], scalar2=None, op0=ALU.mult),
                         reads=[p.b_P[4], p.b_nf], writes=[b_ub])
                else:
                    S.op("dve", lambda e: e.tensor_copy(out=ub[:, 0:HW], in_=p.uhalo[:, cti, :]), reads=[p.b_uhalo], writes=[b_ub])
                S.op("act", lambda e: e.activation(out=ub[:, HW:HW + GT], in_=p.P[ct][:, :], func=AF.Copy), reads=[p.b_P[ct]], writes=[b_ub])
                if g < NG - 1:
                    S.op("dve", lambda e: e.tensor_copy(out=p.uhalo[:, cti, :], in_=ub[:, GT:GT + HW]), reads=[b_ub], writes=[p.b_uhalo])
                acc, b_acc = p.wk[ct], p.b_wk[ct]
                S.op("act", lambda e: e.activation(out=acc[:], in_=p.P[ct][:, :], func=AF.Identity, scale=p.ml_cwT[:, 48 + cti:49 + cti],
                                                   bias=p.ml_cbT[:, cti:cti + 1]), reads=[p.b_P[ct], p.b_ml_cwT, p.b_ml_cbT], writes=[b_acc])
                for j in (2, 1, 0):
                    off = HW - (3 - j)
                    S.op("dve", lambda e: e.scalar_tensor_tensor(out=acc[:], in0=ub[:, off:off + GT], scalar=p.ml_cwT[:, j * 16 + cti: j * 16 + cti + 1],
                                                                 in1=acc[:], op0=ALU.mult, op1=ALU.add), reads=[b_ub, b_acc, p.b_ml_cwT], writes=[b_acc])
                S.op("act", lambda e: e.activation(out=dst[:, cti % 8, :], in_=acc[:], func=AF.Silu), reads=[b_acc], writes=[b_dst])
        for hh in ([] if reuse else range(H)):
            wv, wb = self.load_std_piece(W, 2048 + hh * 512)
            for c in range(G):
                self.proj_B(wv, wb, c, c)
                dstv = p.v_all[:, c, hh * 512:(hh + 1) * 512]
                if c % 2 == 0:
                    S.op("act", lambda e: e.activation(out=dstv, in_=p.P[c][:, :], func=AF.Copy), reads=[p.b_P[c]], writes=[p.b_v[c]])
                else:
                    S.op("dve", lambda e: e.tensor_copy(out=dstv, in_=p.P[c][:, :]), reads=[p.b_P[c]], writes=[p.b_v[c]])
        if (not full) and p.relay:
            self.kv_store(g)
        wv, wb = self.load_std_piece(W, 6144, w=8)
        for c in range(G):
            self.proj_B(wv, wb, c, c, w=8)
            S.op("dve", lambda e: e.tensor_tensor(out=p.gat[:, c, :], in0=p.P[c][:, 0:8], in1=p.bg_bc[:], op=ALU.add),
                 reads=[p.b_P[c], p.b_bg_bc], writes=[p.b_gat])
        if full:
            for hh in range(H):
                wv, wb = self.load_std_piece(W, 4096 + hh * 512)
                for c in range(G):
                    self.proj_B(wv, wb, c, c)
                    dsts = p.sg_all[:, c, hh * 512:(hh + 1) * 512]
                    S.op("act", lambda e: e.activation(out=dsts, in_=p.P[c][:, :], func=AF.Sigmoid), reads=[p.b_P[c]], writes=[p.b_xg[c]])
        self.ml_gates_group(full)
        for c in range(G):
            self.ml_chunk(c, full)
        if full:
            self.out_proj(g, p.ml_w_out, p.xb, p.b_xb, p.xa, p.b_xa, 5)

    def ml_gates_group(self, full):
        p, S = self, self.S
        gm, bg = p.gm, p.b_gm
        v3 = lambda k: gm[:, k, :].rearrange("p (c h) -> p c h", h=4)
        z = p.gat[:, :, 4:8]
        li = p.gat[:, :, 0:4]
        S.op("act", lambda e: e.activation(out=v3(0), in_=z, func=AF.Exp, scale=-1.0), reads=[p.b_gat], writes=[bg])
        S.op("dve", lambda e: e.tensor_scalar(out=gm[:, 0, :], in0=gm[:, 0, :], scalar1=1.0, scalar2=None, op0=ALU.add), reads=[bg], writes=[bg])
        S.op("act", lambda e: e.activation(out=gm[:, 0, :], in_=gm[:, 0, :], func=AF.Ln), reads=[bg], writes=[bg])
        S.op("dve", lambda e: e.tensor_scalar(out=gm[:, 0, :], in0=gm[:, 0, :], scalar1=-1.0, scalar2=None, op0=ALU.mult), reads=[bg], writes=[bg])
        n = 4 * G
        S.op("pe", lambda e: e.matmul(p.P[5][:, 0:n], lhsT=p.ut[:, :], rhs=gm[:, 0, :], start=True, stop=True), reads=[p.b_ut, bg], writes=[p.b_P[5]])
        S.op("pe", lambda e: e.matmul(p.P[5][:, n:2 * n], lhsT=p.ones_f[:, :], rhs=gm[:, 0, :], start=True, stop=True), reads=[p.b_ones_f, bg], writes=[p.b_P[5]])
        S.op("dve", lambda e: e.tensor_copy(out=gm[:, 1, :], in_=p.P[5][:, 0:n]), reads=[p.b_P[5]], writes=[bg])
        S.op("dve", lambda e: e.tensor_copy(out=gm[:, 2, :], in_=p.P[5][:, n:2 * n]), reads=[p.b_P[5]], writes=[bg])
        S.op("dve", lambda e: e.tensor_tensor(out=gm[:, 3, :], in0=gm[:, 2, :], in1=gm[:, 1, :], op=ALU.subtract), reads=[bg], writes=[bg])
        S.op("dve", lambda e: e.scalar_tensor_tensor(out=v3(3), in0=v3(3), scalar=-self.LN16, in1=li, op0=ALU.add, op1=ALU.add),
             reads=[bg, p.b_gat], writes=[bg])
        S.op("act", lambda e: e.activation(out=gm[:, 3, :], in_=gm[:, 3, :], func=AF.Exp), reads=[bg], writes=[bg])
        S.op("act", lambda e: e.activation(out=gm[:, 4, :], in_=gm[:, 2, :], func=AF.Exp), reads=[bg], writes=[bg])
        gx = p.gmx[:].rearrange("p c (h t) -> p c h t", t=2)
        for t in range(2):
            S.op("dve", lambda e: e.tensor_copy(out=gx[:, :, :, t], in_=v3(4)), reads=[bg], writes=[p.b_gmx])
        if not full:
            for c in range(G):
                S.op("dve", lambda e: e.tensor_tensor(out=p.fsum[:], in0=p.fsum[:], in1=gm[:, 2, 4 * c:4 * c + 4], op=ALU.add),
                     reads=[bg, p.b_fsum], writes=[p.b_fsum])
        else:
            S.op("dve", lambda e: e.tensor_tensor(out=v3(5), in0=li, in1=v3(1), op=ALU.subtract), reads=[bg, p.b_gat], writes=[bg])
            S.op("dve", lambda e: e.tensor_scalar(out=gm[:, 5, :], in0=gm[:, 5, :], scalar1=-self.LN16, scalar2=None, op0=ALU.add), reads=[bg], writes=[bg])
            S.op("act", lambda e: e.activation(out=gm[:, 6, :], in_=gm[:, 1, :], func=AF.Exp), reads=[bg], writes=[bg])

    def ml_chunk(self, c, full):
        p, S = self, self.S
        cs = slice(c * CH, (c + 1) * CH)
        sm, bs = p.sm, p.b_sm
        gm, bg = p.gm, p.b_gm
        col = 4 * c
        g_b = lambda hh: gm[:, 1, col + hh:col + hh + 1]
        g_ws = lambda hh: gm[:, 3, col + hh:col + hh + 1]
        g_sp = lambda hh: gm[:, 4, col + hh:col + hh + 1]
        g_bj = lambda hh: gm[:, 5, col + hh:col + hh + 1]
        g_wi = lambda hh: gm[:, 6, col + hh:col + hh + 1]
        self.make_kz(c, lambda hh: (g_ws(hh), [bg]))
        if full:
            for hh in range(H):
                S.op("dve", lambda e: e.tensor_scalar(out=p.xn[:, hh * 128:(hh + 1) * 128], in0=p.ident_f[:, :], scalar1=g_b(hh), scalar2=None, op0=ALU.mult),
                     reads=[bg, p.b_ident_f], writes=[p.b_xn])
                S.op("pe", lambda e: e.matmul(p.P[5][:, hh * 128:(hh + 1) * 128], lhsT=p.ones_f[:, :], rhs=p.xn[:, hh * 128:(hh + 1) * 128], start=True, stop=False),
                     reads=[p.b_ones_f, p.b_xn], writes=[p.b_P[5]])
                S.op("pe", lambda e: e.matmul(p.P[5][:, hh * 128:(hh + 1) * 128], lhsT=p.ident_f[:, :], rhs=p.neg[:, :], start=False, stop=True),
                     reads=[p.b_ident_f, p.b_neg], writes=[p.b_P[5]])
            for hh in range(H):
                S.op("act", lambda e: e.activation(out=p.xn2[:, hh * 128:(hh + 1) * 128], in_=p.P[5][:, hh * 128:(hh + 1) * 128], func=AF.Exp,
                                                   bias=g_bj(hh)), reads=[p.b_P[5], bg], writes=[p.b_xn2])
            for hh in range(H):
                for half in range(2):
                    S.op("pe", lambda e: e.matmul(p.P[4][:, hh * 128:(hh + 1) * 128], lhsT=p.kT[:, 2 * hh + half, cs], rhs=p.qT[:, 2 * hh + half, cs],
                                                  start=(half == 0), stop=(half == 1)), reads=[p.b_kT, p.b_qT], writes=[p.b_P[4]])
            S.op("dve", lambda e: e.tensor_tensor(out=p.sTm[:], in0=p.P[4][:, :], in1=p.xn2[:, 0:512], op=ALU.mult),
                 reads=[p.b_P[4], p.b_xn2], writes=[p.b_sTm])
        if full:
            for hh in range(H):
                S.op("pe", lambda e: e.matmul(p.P[5][:, 2 * hh:2 * hh + 1], lhsT=p.sTm[:, hh * 128:(hh + 1) * 128], rhs=p.ones_b[:, 0:1], start=True, stop=True),
                     reads=[p.b_sTm, p.b_ones_b], writes=[p.b_P[5]])
                for half in range(2):
                    i = 2 * hh + half
                    S.op("pe", lambda e: e.matmul(p.P[5][:, 2 * hh + 1:2 * hh + 2], lhsT=p.qT[:, i, cs], rhs=p.nstb[:, i:i + 1], start=(half == 0), stop=(half == 1)),
                         reads=[p.b_qT, p.b_nstb], writes=[p.b_P[5]])
            S.op("dve", lambda e: e.tensor_copy(out=sm[:, 64:72], in_=p.P[5][:, 0:8]), reads=[p.b_P[5]], writes=[bs])
            dv = sm[:, 64:72].rearrange("p (h t) -> p h t", t=2)
            S.op("dve", lambda e: e.tensor_tensor(out=sm[:, 72:76], in0=dv[:, :, 1], in1=gm[:, 6, col:col + 4], op=ALU.mult), reads=[bs, bg], writes=[bs])
            S.op("dve", lambda e: e.tensor_tensor(out=sm[:, 72:76], in0=sm[:, 72:76], in1=dv[:, :, 0], op=ALU.add), reads=[bs], writes=[bs])
            S.op("dve", lambda e: e.scalar_tensor_tensor(out=sm[:, 76:80], in0=sm[:, 72:76], scalar=-1.0, in1=sm[:, 72:76], op0=ALU.mult, op1=ALU.max),
                 reads=[bs], writes=[bs])
            S.op("dve", lambda e: e.tensor_scalar(out=sm[:, 76:80], in0=sm[:, 76:80], scalar1=1.0, scalar2=None, op0=ALU.max), reads=[bs], writes=[bs])
            S.op("dve", lambda e: e.reciprocal(out=sm[:, 80:84], in_=sm[:, 76:80]), reads=[bs], writes=[bs])
        for hh in range(H):
            if full:
                S.op("pe", lambda e: e.matmul(p.P[0][:, :], lhsT=p.sTm[:, hh * 128:(hh + 1) * 128], rhs=p.v_all[:, c, hh * 512:(hh + 1) * 512],
                                              start=True, stop=True), reads=[p.b_sTm, p.b_v[c]], writes=[p.b_P[0]])
                for half in range(2):
                    i = 2 * hh + half
                    S.op("pe", lambda e: e.matmul(p.P[1][:, :], lhsT=p.qT[:, i, cs], rhs=p.Rb[:, i, :], start=(half == 0), stop=(half == 1)),
                         reads=[p.b_qT, p.b_Rb[i]], writes=[p.b_P[1]])
                S.op("act", lambda e: e.activation(out=p.wk[4][:], in_=p.P[1][:, :], func=AF.Copy, scale=g_wi(hh)),
                     reads=[p.b_P[1], bg], writes=[p.b_wk[4]])
                S.op("dve", lambda e: e.tensor_tensor(out=p.wk[hh][:], in0=p.wk[4][:], in1=p.P[0][:, :], op=ALU.add),
                     reads=[p.b_wk[4], p.b_P[0]], writes=[p.b_wk[hh]])
                so = p.sg_all[:, c, hh * 512:(hh + 1) * 512]
                S.op("dve", lambda e: e.scalar_tensor_tensor(out=p.wk[hh][:], in0=p.wk[hh][:], scalar=sm[:, 80 + hh:81 + hh], in1=so, op0=ALU.mult, op1=ALU.mult),
                     reads=[p.b_wk[hh], bs, p.b_xg[c]], writes=[p.b_wk[hh]])
            self.state_update(c, hh, g_sp(hh), [bg])
            if full:
                self.refresh_Rb(hh)
        for i in range(8):
            hh, half = i // 2, i % 2
            S.op("pe", lambda e: e.matmul(p.P[5][:, 16 + i:17 + i], lhsT=p.kz[:, hh * 256 + half * 128: hh * 256 + (half + 1) * 128],
                                          rhs=p.ones_b[:, 0:1], start=True, stop=True), reads=[p.b_kz, p.b_ones_b], writes=[p.b_P[5]])
        S.op("dve", lambda e: e.tensor_tensor(out=p.nst[:], in0=p.nst[:], in1=p.gmx[:, c, :], op=ALU.mult), reads=[p.b_nst, p.b_gmx], writes=[p.b_nst])
        S.op("dve", lambda e: e.tensor_tensor(out=p.nst[:], in0=p.nst[:], in1=p.P[5][:, 16:24], op=ALU.add), reads=[p.b_nst, p.b_P[5]], writes=[p.b_nst])
        if full:
            S.op("act", lambda e: e.activation(out=p.nstb[:], in_=p.nst[:], func=AF.Copy), reads=[p.b_nst], writes=[p.b_nstb])
        if full:
            self.groupnorm_heads([0, 1, 2, 3], c, gate=False)
            self.make_ynT(c, p.ml_gnT, p.b_ml_gnT)

    def ml_layer(self):
        p, S = self, self.S
        if p.phase in (None, 4):
            self.ml_part_a()
        if p.phase is None:
            self.exchange(4)
        if p.phase in (None, 5):
            self.ml_part_b()

    def ml_part_a(self):
        p, S = self, self.S
        for i in range(8):
            S.op("dve", lambda e: e.memset(p.R[:, i, :], 0.0), writes=[p.b_R[i]])
        S.op("dve", lambda e: e.memset(p.nst[:], 0.0), writes=[p.b_nst])
        S.op("dve", lambda e: e.memset(p.fsum[:], 0.0), writes=[p.b_fsum])
        for g in range(NG):
            self.ml_group(g, full=False)
        S.op("dve", lambda e: e.memset(p.wk[4][:], 0.0), writes=[p.b_wk[4]])
        S.op("dve", lambda e: e.tensor_copy(out=p.wk[4][:, 0:8], in_=p.nst[:]), reads=[p.b_nst], writes=[p.b_wk[4]])
        S.op("dve", lambda e: e.tensor_copy(out=p.wk[4][:, 8:12], in_=p.fsum[:]), reads=[p.b_fsum], writes=[p.b_wk[4]])
        pairs = [(p.loc[4][i * 128:(i + 1) * 128, :], p.R[:, i, :]) for i in range(8)] + [(p.loc[4][1024:1152, :], p.wk[4][:])]
        self.dma_group("sp", "loc4", pairs, reads=p.b_R + [p.b_wk[4]], writes=[p.b_loc[4]])

    def ml_part_b(self):
        p, S = self, self.S
        gt = p.gath[4]
        pairs = [(p.stage[cp:cp + 1, 0:4], gt[cp * 1152 + 1024: cp * 1152 + 1025, 8:12]) for cp in range(NCORES)]
        self.dma_group("sp", "stage", pairs, reads=[p.b_gath[4]], writes=[p.b_stage])
        for cp in range(NCORES):
            S.op("dve", lambda e: e.tensor_tensor(out=p.stage[0:8, 32 + cp * 4:36 + cp * 4], in0=p.msel[0:8, cp * 4:cp * 4 + 4], in1=p.stage[0:8, 0:4], op=ALU.mult),
                 reads=[p.b_stage, p.b_msel], writes=[p.b_stage])
        S.op("pe", lambda e: e.matmul(p.P[5][:, 0:32], lhsT=p.ones_f[0:8, :], rhs=p.stage[0:8, 32:64], start=True, stop=True),
             reads=[p.b_ones_f, p.b_stage], writes=[p.b_P[5]])
        S.op("act", lambda e: e.activation(out=p.mcoef[:], in_=p.P[5][:, 0:32], func=AF.Exp), reads=[p.b_P[5]], writes=[p.b_mcoef])
        S.op("dve", lambda e: e.tensor_tensor(out=p.mcoef[:], in0=p.mcoef[:], in1=p.valid[:], op=ALU.mult), reads=[p.b_mcoef, p.b_valid], writes=[p.b_mcoef])
        self.combine_state(4, p.mcoef, p.b_mcoef, nrows=9)
        S.op("dve", lambda e: e.memset(p.nst[:], 0.0), writes=[p.b_nst])
        for cp in range(NCORES):
            tmp, bt = p.wk[cp % 2], p.b_wk[cp % 2]
            r0 = cp * 1152 + 1024
            S.dma("sp", tmp[:, 0:8], gt[r0:r0 + 128, 0:8], reads=[p.b_gath[4]], writes=[bt], key=f"cmb{cp % 2}")
            for hh in range(H):
                S.op("dve", lambda e: e.scalar_tensor_tensor(out=p.nst[:, 2 * hh:2 * hh + 2], in0=tmp[:, 2 * hh:2 * hh + 2],
                                                             scalar=p.mcoef[:, cp * H + hh:cp * H + hh + 1], in1=p.nst[:, 2 * hh:2 * hh + 2],
                                                             op0=ALU.mult, op1=ALU.add), reads=[bt, p.b_mcoef, p.b_nst], writes=[p.b_nst])
        for hh in range(H):
            self.refresh_Rb(hh)
        S.op("act", lambda e: e.activation(out=p.nstb[:], in_=p.nst[:], func=AF.Copy), reads=[p.b_nst], writes=[p.b_nstb])
        self.load_gate(1, 0)
        for g in range(NG):
            self.ml_group(g, full=True)

    def _body(self):
        p, S = self, self.S
        ph = p.phase
        if ph in (None, 1, 2):
            self.ret_layer()
        if p.stop_after == "dbg_ret":
            return
        if p.stop_after == "ret":
            return self.copy_out(p.xa, p.b_xa)
        if ph is None:
            self.exchange(2)
        if ph in (None, 3):
            self.ffn_layer(0, p.xa, p.b_xa, p.xb, p.b_xb, 2, 3, final=False)
        if p.stop_after == "ffn0":
            return self.copy_out(p.xb, p.b_xb)
        if ph is None:
            self.exchange(3)
        if ph in (None, 4, 5):
            self.ml_layer()
        if p.stop_after == "ml":
            return self.copy_out(p.xa, p.b_xa)
        if ph is None:
            self.exchange(5)
        if ph in (None, 6):
            S.dma("sp", p.cos[:], p.final_g[0:1, 0:512].partition_broadcast(128).rearrange("p o n -> p (o n)"), writes=[p.b_cos], key="fg0")
            S.dma("sp", p.sin[:], p.final_g[0:1, 512:1024].partition_broadcast(128).rearrange("p o n -> p (o n)"), writes=[p.b_sin], key="fg1")
            self.ffn_layer(1, p.xa, p.b_xa, p.out, p.b_out, 5, None, final=True)

    def copy_out(self, src, src_bufs):
        p, S = self, self.S
        for g in range(NG):
            for c in range(G):
                r0 = g * GT + c * CH
                S.dma("sp", p.x_g[:, c, :], src[r0:r0 + CH, :], reads=src_bufs, writes=[p.b_xg[c]], key=f"xg{c}")
            pairs = [(p.out[g * GT + c * CH: g * GT + (c + 1) * CH, :], p.x_g[:, c, :]) for c in range(G)]
            self.dma_group("sp", "st_ou", pairs, reads=p.b_xg, writes=[p.b_out[g]])

    def _finish(self):
        p, S = self, self.S
        bufs = list(p.b_out) + [p.b_loc[k] for k in p.b_loc] + p.dbg_bufs + list(p.b_xa) + list(p.b_xb) + [p.b_modscr, p.b_modT_o, p.b_kvst]
        S.wait_bufs("sp", bufs)
        S.barrier()


_PROG_CACHE = {}
MODE = "host6"
STOP_AFTER = None


def _get_prog(mode, stop_after, phase=None):
    key = (mode, stop_after, phase)
    if key not in _PROG_CACHE:
        pr = Prog("host" if mode.startswith("host") else mode, stop_after, phase)
        pr.build()
        _PROG_CACHE[key] = pr
    return _PROG_CACHE[key]


def _in_maps(inputs):
    f = lambda a: np.ascontiguousarray(np.asarray(a), dtype=np.float32)
    tabs, lg = _const_tables()
    x = f(inputs["x"]).reshape(SEQ, D)
    pos = np.ascontiguousarray(np.asarray(inputs["positions"]).astype(np.int32)).reshape(SEQ)
    shared = {
        "cT": np.ascontiguousarray(f(inputs["c"]).reshape(KC, 128).T),
        "ada_w": f(inputs["ada_w"]),
        "ada_bT": np.ascontiguousarray(f(inputs["ada_b"]).reshape(2, 48, 128).transpose(2, 0, 1)),
        "ntgT": np.ascontiguousarray(f(inputs["norm_tok_g"]).reshape(2, KC, 128).transpose(2, 0, 1)),
        "nfgT": np.ascontiguousarray(f(inputs["norm_ffn_g"]).reshape(2, KC, 128).transpose(2, 0, 1)),
        "ret_w_in": f(inputs["ret_w_in"]).reshape(D, 6144),
        "ret_gnT": np.ascontiguousarray(f(inputs["ret_gn_g"]).reshape(16, 128).T),
        "ret_w_out": f(inputs["ret_w_out"]).reshape(2048, D),
        "ml_w_in": f(inputs["ml_w_in"]).reshape(D, 6152),
        "ml_b_gate": f(inputs["ml_b_gate"]).reshape(1, 8),
        "ml_cwT": np.ascontiguousarray(f(inputs["ml_conv_w"]).reshape(64, 128).T),
        "ml_cbT": np.ascontiguousarray(f(inputs["ml_conv_b"]).reshape(16, 128).T),
        "ml_gnT": np.ascontiguousarray(f(inputs["ml_gn_g"]).reshape(16, 128).T),
        "ml_w_out": f(inputs["ml_w_out"]).reshape(2048, D),
        "ffn_w_up": f(inputs["ffn_w_up"]),
        "ffn_cwT": np.ascontiguousarray(f(inputs["ffn_conv_w"]).reshape(2, 132, 128).transpose(2, 0, 1)),
        "ffn_cbT": np.ascontiguousarray(f(inputs["ffn_conv_b"]).reshape(2, 44, 128).transpose(2, 0, 1)),
        "ffn_w_down": f(inputs["ffn_w_down"]),
        "final_g": f(inputs["final_g"]).reshape(1, D),
    }
    shared.update(tabs)
    maps = []
    for c in range(NCORES):
        m = dict(shared)
        m["x"] = x[c * T:(c + 1) * T]
        m["pos"] = pos[c * T:(c + 1) * T].reshape(1, T)
        m.update(_core_tables(c, lg))
        maps.append(m)
    return maps


def _launch(pr, maps, extra):
    ms = []
    for c, m in enumerate(maps):
        mm = {k: v for k, v in m.items() if k in pr.in_names}
        for k, v in extra.items():
            if k in pr.in_names:
                mm[k] = v[c] if isinstance(v, list) else v
        ms.append(mm)
    return run_bass_kernel_spmd(pr.nc, ms, core_ids=list(range(NCORES))).results


def kernel(**inputs):
    maps = _in_maps(inputs)
    if MODE == "host6":
        extra = {}
        res = None
        for ph in range(1, 7):
            pr = _get_prog("host", None, ph)
            res = _launch(pr, maps, extra)
            for k in (1, 2, 3, 4, 5):
                if f"loc{k}" in res[0]:
                    extra[f"gath{k}"] = np.concatenate([res[c][f"loc{k}"] for c in range(NCORES)], axis=0)
            for nm in ("xa", "xb"):
                if nm in res[0]:
                    extra[nm] = [res[c][nm] for c in range(NCORES)]
            if "kst_o" in res[0]:
                extra["kst_i"] = [res[c]["kst_o"] for c in range(NCORES)]
                extra["vst_i"] = [res[c]["vst_o"] for c in range(NCORES)]
            if "modT_o" in res[0]:
                extra["modT_i"] = [res[c]["modT_o"] for c in range(NCORES)]
                extra["modscr_i"] = [res[c]["modscr_o"] for c in range(NCORES)]
    elif MODE == "host":
        pr = _get_prog("host", STOP_AFTER)
        extra = {f"gath{k}": np.zeros((NCORES * r, cdim), np.float32) for k, (r, cdim) in EX_SIZES.items()}
        order = {None: [1, 2, 3, 4, 5], "ret": [1], "ffn0": [1, 2], "ml": [1, 2, 3, 4]}[STOP_AFTER]
        res = None
        for step in range(len(order) + 1):
            res = _launch(pr, maps, extra)
            if step < len(order):
                k = order[step]
                extra[f"gath{k}"] = np.concatenate([res[c][f"loc{k}"] for c in range(NCORES)], axis=0)
    else:
        pr = _get_prog("cc", None)
        res = _launch(pr, maps, {})
    out = np.concatenate([res[c]["out"] for c in range(NCORES)], axis=0)
    return out.reshape(1, SEQ, D).astype(np.float32)
```

```python
import contextlib
import math
import numpy as np
import concourse.bass as bass
import concourse.mybir as mybir
from concourse.bass_utils import run_bass_kernel_spmd

F32 = mybir.dt.float32
BF16 = mybir.dt.bfloat16
I32 = mybir.dt.int32
AF = mybir.ActivationFunctionType
ALU = mybir.AluOpType

NCORES = 8
SEQ = 16384
D = 1024
T = SEQ // NCORES
CH = 128
G = 4
GT = G * CH
NG = T // GT
KC = D // 128
H = 4
DK = 256
DV = 512
DFF = 2816
EPS = 1e-6
HW = 3
TWO_PI = 2.0 * math.pi
C1 = 6.28125
C2 = TWO_PI - C1
PI_SAFE = 3.1415925


class Buf:
    __slots__ = ("name", "w", "r")

    def __init__(self, name):
        self.name = name
        self.w = None
        self.r = {}


class Sched:
    ENGS = ("pe", "dve", "act", "pool", "sp")

    def __init__(self, nc, stack):
        self.nc = nc
        self.stack = stack
        self.eng = {"pe": nc.tensor, "dve": nc.vector, "act": nc.scalar, "pool": nc.gpsimd, "sp": nc.sync}
        self.sems = {}
        self.cnt = {}
        self.seen = {e: {} for e in self.ENGS}
        for e in self.ENGS:
            self.sems[e] = stack.enter_context(nc.semaphore("s_" + e))
            self.cnt[e] = 0
        self.ninst = 0

    def buf(self, name):
        return Buf(name)

    def _dma_sem(self, key):
        k = "dma_" + key
        if k not in self.sems:
            self.sems[k] = self.stack.enter_context(self.nc.semaphore("s_" + k))
            self.cnt[k] = 0
        return k

    def _wait(self, e, deps):
        need = {}
        for d in deps:
            if d is None:
                continue
            k, v = d
            if k == e and e == "pe":
                continue
            if need.get(k, 0) < v:
                need[k] = v
        for k, v in need.items():
            if self.seen[e].get(k, 0) < v:
                self.eng[e].wait_ge(self.sems[k], v)
                self.seen[e][k] = v

    @staticmethod
    def _deps(reads, writes):
        deps = []
        for b in reads:
            deps.append(b.w)
        for b in writes:
            deps.append(b.w)
            deps.extend(b.r.items())
        return deps

    @staticmethod
    def _record(ev, reads, writes):
        k, v = ev
        for b in reads:
            if b.r.get(k, 0) < v:
                b.r[k] = v
        for b in writes:
            b.w = ev
            b.r = {}

    def op(self, e, fn, reads=(), writes=()):
        self._wait(e, self._deps(reads, writes))
        ins = fn(self.eng[e])
        self.cnt[e] += 1
        ins.then_inc(self.sems[e], 1)
        self.ninst += 1
        self._record((e, self.cnt[e]), reads, writes)
        return ins

    def dma(self, q, out, in_, reads=(), writes=(), key=None, **kw):
        k = self._dma_sem(key)
        self._wait(q, self._deps(reads, writes))
        ins = self.eng[q].dma_start(out=out, in_=in_, **kw)
        self.cnt[k] += 16
        ins.then_inc(self.sems[k], 16)
        self.ninst += 1
        self._record((k, self.cnt[k]), reads, writes)
        return ins

    def wait_bufs(self, e, bufs):
        deps = []
        for b in bufs:
            deps.append(b.w)
            deps.extend(b.r.items())
        self._wait(e, deps)

    def barrier(self):
        for e in self.ENGS:
            deps = [(k, v) for k, v in self.cnt.items() if v > 0]
            self._wait(e, deps)


def _const_tables():
    t = {}
    n = np.arange(128, dtype=np.float32)
    inv_freq = (10000.0 ** (-(np.arange(0, DK, 2, dtype=np.float32)) / DK)).astype(np.float32)
    t["inv_freq"] = inv_freq.reshape(128, 1).astype(np.float32)
    lg = np.log(1.0 - 2.0 ** (-5.0 - np.arange(H, dtype=np.float64)))
    i = np.arange(128)[None, :]
    j = np.arange(128)[:, None]
    dt = np.zeros((128, H, 128), np.float64)
    for h in range(H):
        dt[:, h, :] = np.where(i >= j, np.exp((i - j) * lg[h]), 0.0) * (DK ** -0.5)
    t["ret_dt"] = dt.reshape(128, H * 128).astype(np.float32)
    t["ret_xi"] = np.exp((np.arange(128)[:, None] + 1.0) * lg[None, :]).astype(np.float32)
    t["ret_zs"] = (np.exp((127.0 - np.arange(128)[:, None]) * lg[None, :]) * (DK ** -0.5)).astype(np.float32)
    t["neg"] = np.where(j <= i, 0.0, -30000.0).astype(np.float32)
    t["ut"] = np.where(j <= i, 1.0, 0.0).astype(np.float32)
    t["ident"] = np.eye(128, dtype=np.float32)
    return t, lg


def _core_tables(c, lg):
    sel = np.zeros((128, NCORES), np.float32)
    if c > 0:
        sel[:, c - 1] = 1.0
    nf = np.full((128, 1), 0.0 if c == 0 else 1.0, np.float32)
    rc = np.zeros((128, NCORES * H), np.float32)
    for cp in range(c):
        for h in range(H):
            rc[:, cp * H + h] = np.exp(T * (c - 1 - cp) * lg[h])
    valid = np.zeros((128, NCORES * H), np.float32)
    for cp in range(c):
        valid[:, cp * H:(cp + 1) * H] = 1.0
    msel = np.zeros((NCORES, NCORES, H), np.float32)
    for cpp in range(NCORES):
        for cp in range(NCORES):
            if cp < cpp < c:
                msel[cpp, cp, :] = 1.0
    selmat = np.zeros((NCORES * HW, HW), np.float32)
    if c > 0:
        for r in range(HW):
            selmat[(c - 1) * HW + r, r] = 1.0
    return {"selmat": selmat, "sel": sel, "nf": nf, "ret_coef": rc, "valid": valid, "msel": msel.reshape(NCORES, NCORES * H)}


EX_SIZES = {1: (8 * 128, 512), 2: (HW, D), 3: (HW, D), 4: (9 * 128, 512), 5: (HW, D)}


class Prog:
    def __init__(self, mode="host", stop_after=None, phase=None):
        self.mode = mode
        self.stop_after = stop_after
        self.phase = phase
        self.in_names = []
        self.layers = [0, 1] if phase in (None, 1) else ([0] if phase <= 3 else [1])
        self.mod_layers = [0, 1] if phase in (None, 1) else []
        self.nc = bass.Bass("TRN2", target_bir_lowering=False)
        self.st = contextlib.ExitStack()
        self.debug = stop_after is not None and stop_after.startswith("dbg")
        self.dbg_bufs = []

    def din(self, name, shape, dt=F32):
        self.in_names.append(name)
        return self.nc.dram_tensor(name, list(shape), dt, kind="ExternalInput").ap()

    def dout(self, name, shape, dt=F32):
        return self.nc.dram_tensor(name, list(shape), dt, kind="ExternalOutput").ap()

    def dint(self, name, shape, dt=F32):
        return self.nc.dram_tensor(name, list(shape), dt, kind="Internal").ap()

    def sb(self, name, shape, dt=F32):
        t = self.st.enter_context(self.nc.sbuf_tensor("sb_" + name, list(shape), dt))
        b = Buf(name)
        return t, b

    def ps(self, name, shape, dt=F32):
        t = self.st.enter_context(self.nc.psum_tensor("ps_" + name, list(shape), dt))
        return t

    def build(self):
        with self.st:
            self.S = Sched(self.nc, self.st)
            self._declare()
            self._setup()
            self._body()
            self._finish()
        return self.nc

    def _declare(self):
        p = self
        p.x = p.din("x", [T, D])
        p.pos = p.din("pos", [1, T], I32)
        p.c_in = p.din("cT", [128, KC])
        p.ada_w = p.din("ada_w", [2, D, 6 * D])
        p.ada_b = p.din("ada_bT", [128, 2, 48])
        p.ntg = p.din("ntgT", [128, 2, KC])
        p.nfg = p.din("nfgT", [128, 2, KC])
        p.ret_w_in = p.din("ret_w_in", [D, 6144])
        p.ret_gn = p.din("ret_gnT", [128, 16])
        p.ret_w_out = p.din("ret_w_out", [2048, D])
        p.ml_w_in = p.din("ml_w_in", [D, 6152])
        p.ml_bg = p.din("ml_b_gate", [1, 8])
        p.ml_cw = p.din("ml_cwT", [128, 64])
        p.ml_cb = p.din("ml_cbT", [128, 16])
        p.ml_gn = p.din("ml_gnT", [128, 16])
        p.ml_w_out = p.din("ml_w_out", [2048, D])
        p.ffn_w_up = p.din("ffn_w_up", [2, D, 2 * DFF])
        p.ffn_cw = p.din("ffn_cwT", [128, 2, 132])
        p.ffn_cb = p.din("ffn_cbT", [128, 2, 44])
        p.ffn_w_down = p.din("ffn_w_down", [2, DFF, D])
        p.final_g = p.din("final_g", [1, D])
        p.t_inv_freq = p.din("inv_freq", [128, 1])
        p.t_ret_dt = p.din("ret_dt", [128, 512])
        p.t_ret_xi = p.din("ret_xi", [128, 4])
        p.t_ret_zs = p.din("ret_zs", [128, 4])
        p.t_neg = p.din("neg", [128, 128])
        p.t_ut = p.din("ut", [128, 128])
        p.t_ident = p.din("ident", [128, 128])
        p.t_sel = p.din("sel", [128, NCORES])
        p.t_selmat = p.din("selmat", [NCORES * HW, HW])
        p.t_nf = p.din("nf", [128, 1])
        p.t_ret_coef = p.din("ret_coef", [128, NCORES * H])
        p.t_valid = p.din("valid", [128, NCORES * H])
        p.t_msel = p.din("msel", [NCORES, NCORES * H])
        ph = p.phase
        p.out = p.dout("out", [T, D]) if ph in (None, 6) or p.stop_after else None
        if ph is None:
            p.modscr = p.dint("modscr", [2, 48, 128])
        elif ph == 1:
            p.modscr = p.dout("modscr_o", [2, 48, 128])
            p.modT_o = p.dout("modT_o", [128, 96])
        else:
            p.modscr = p.din("modscr_i", [2, 48, 128])
            p.modT_i = p.din("modT_i", [128, 96])
        p.b_modscr = Buf("modscr")
        p.b_modT_o = Buf("modT_o")
        kinds = {None: ("int", "int"), 1: (None, None), 2: ("out", None), 3: ("in", "out"), 4: (None, "in"), 5: ("out", "in"), 6: ("in", None)}[ph]
        mk = {"int": p.dint, "in": p.din, "out": p.dout, None: (lambda *a: None)}
        p.xa = mk[kinds[0]]("xa", [T, D])
        p.xb = mk[kinds[1]]("xb", [T, D])
        p.relay = ph in (1, 2, 4, 5)
        if ph in (1, 4):
            p.kst = p.dout("kst_o", [NG, 128, 8 * GT], BF16)
            p.vst = p.dout("vst_o", [NG, 128, G * 2048], BF16)
        elif ph in (2, 5):
            p.kst = p.din("kst_i", [NG, 128, 8 * GT], BF16)
            p.vst = p.din("vst_i", [NG, 128, G * 2048], BF16)
        p.b_kvst = Buf("kvst")
        sizes = dict(EX_SIZES)
        loc_ph = {1: 1, 2: 2, 3: 3, 4: 4, 5: 5}
        gath_ph = {1: (2,), 2: (3,), 3: (4, 5), 4: (5,), 5: (6,)}
        p.loc = {}
        p.gath = {}
        for k, (r, cdim) in sizes.items():
            if p.mode == "host":
                if ph is None or loc_ph[k] == ph:
                    p.loc[k] = p.dout(f"loc{k}", [r, cdim])
                if ph is None or ph in gath_ph[k]:
                    p.gath[k] = p.din(f"gath{k}", [NCORES * r, cdim])
            else:
                p.loc[k] = p.dint(f"loc{k}", [r, cdim])
                p.gath[k] = p.dint(f"gath{k}", [NCORES * r, cdim])
        p.b_loc = {k: Buf(f"loc{k}") for k in sizes}
        p.b_gath = {k: Buf(f"gath{k}") for k in sizes}
        p.b_xa = [Buf(f"xa{g}") for g in range(NG)]
        p.b_xb = [Buf(f"xb{g}") for g in range(NG)]
        p.b_out = [Buf(f"out{g}") for g in range(NG)]

        p.ident_f, p.b_ident_f = p.sb("ident_f", [128, 128])
        p.ident_b, p.b_ident_b = p.sb("ident_b", [128, 128], BF16)
        p.ones_f, p.b_ones_f = p.sb("ones_f", [128, 128])
        p.ones_b, p.b_ones_b = p.sb("ones_b", [128, 8], BF16)
        p.ret_dt, p.b_ret_dt = p.sb("ret_dt", [128, 512])
        p.ret_xi, p.b_ret_xi = p.sb("ret_xi", [128, 4])
        p.ret_zs, p.b_ret_zs = p.sb("ret_zs", [128, 4])
        p.neg, p.b_neg = p.sb("negm", [128, 128])
        p.ut, p.b_ut = p.sb("utm", [128, 128])
        p.inv_freq, p.b_inv_freq = p.sb("inv_freq_s", [128, 1])
        p.sel, p.b_sel = p.sb("sel_s", [128, NCORES])
        p.selmat, p.b_selmat = p.sb("selmat_s", [NCORES * HW, HW])
        p.adabT, p.b_adabT = p.sb("adabT", [128, 2, 48])
        p.cT_f, p.b_cT_f = p.sb("cT_f", [128, KC])
        p.nf, p.b_nf = p.sb("nf_s", [128, 1])
        p.ret_coef, p.b_ret_coef = p.sb("ret_coef_s", [128, NCORES * H])
        p.valid, p.b_valid = p.sb("valid_s", [128, NCORES * H])
        p.msel, p.b_msel = p.sb("msel_s", [NCORES, NCORES * H])
        p.consts = [p.b_ident_f, p.b_ident_b, p.b_ones_f, p.b_ones_b]
        p.modT, p.b_modT = p.sb("modT", [128, 2, 48])
        p.ntgT, p.b_ntgT = p.sb("ntgT", [128, 2, KC])
        p.nfgT, p.b_nfgT = p.sb("nfgT", [128, 2, KC])
        p.gsc, p.b_gsc = p.sb("gsc", [128, 4, KC])
        p.ret_gnT, p.b_ret_gnT = p.sb("ret_gnT", [128, 16])
        p.ml_gnT, p.b_ml_gnT = p.sb("ml_gnT", [128, 16])
        p.ml_cwT, p.b_ml_cwT = p.sb("ml_cwT", [128, 64])
        p.ml_cbT, p.b_ml_cbT = p.sb("ml_cbT", [128, 16])
        p.ffn_cwT, p.b_ffn_cwT = p.sb("ffn_cwT", [128, 2, 132])
        p.ffn_cbT, p.b_ffn_cbT = p.sb("ffn_cbT", [128, 2, 44])
        p.bg_bc, p.b_bg_bc = p.sb("bg_bc", [128, 8])
        p.gt_bc, p.b_gt_bc = p.sb("gt_bc", [128, D])
        p.cT_b, p.b_cT_b = p.sb("cT_b", [128, KC], BF16)
        p.stage, p.b_stage = p.sb("stage", [128, 128])
        p.NR = 3
        p.wt = []
        p.b_wt = []
        for i in range(p.NR):
            t, b = p.sb(f"wt{i}", [128, 4096], BF16)
            p.wt.append(t)
            p.b_wt.append(b)
        p.ring_pos = 0
        p.rot_bank = 0
        p.rot_acc = 0
        p.rot_ub = 0
        p.b_actT = [Buf(f"actT{i}") for i in range(22)]
        p.x_g, _ = p.sb("x_g", [128, G, D])
        p.b_xg = [Buf(f"xg{c}") for c in range(G)]
        p.sg_all = p.x_g[:].bitcast(BF16)
        p.xn, p.b_xn = p.sb("xn", [128, D])
        p.xn2, p.b_xn2 = p.sb("xn2", [128, D])
        p.junk, p.b_junk = p.sb("junk", [128, D], BF16)
        p.uhalo, p.b_uhalo = p.sb("uhalo", [128, 44, HW])
        p.ss, p.b_ss = p.sb("ss", [128, 8])
        p.rstd, p.b_rstd = p.sb("rstd", [128, 8])
        p.hT, p.b_hT = p.sb("hT", [128, KC, GT], BF16)
        p.xh, p.b_xh = p.sb("xh", [32, D])
        p.hTh, p.b_hTh = p.sb("hTh", [128, KC, 32], BF16)
        p.big_a, p.b_big_a = p.sb("big_a", [128, 22 * GT], BF16)
        p.qT, p.b_qT = p.sb("qT", [128, 8, GT], BF16)
        p.kT, p.b_kT = p.sb("kT", [128, 8, GT], BF16)
        p.v_all, _ = p.sb("v_all", [128, G, 2048], BF16)
        p.b_v = [Buf(f"v{c}") for c in range(G)]
        p.R, _ = p.sb("R", [128, 8, 512])
        p.b_R = [Buf(f"R{i}") for i in range(8)]
        p.Rb, _ = p.sb("Rb", [128, 8, 512], BF16)
        p.b_Rb = [Buf(f"Rb{i}") for i in range(8)]
        p.nst, p.b_nst = p.sb("nst", [128, 8])
        p.nstb, p.b_nstb = p.sb("nstb", [128, 8], BF16)
        p.fsum, p.b_fsum = p.sb("fsum", [128, 4])
        p.wk = []
        p.b_wk = []
        for i in range(5):
            t, b = p.sb(f"wk{i}", [128, 512])
            p.wk.append(t)
            p.b_wk.append(b)
        p.cos, p.b_cos = p.sb("cos", [128, GT])
        p.sin, p.b_sin = p.sb("sin", [128, GT])
        p.yg, p.b_yg = p.sb("yg", [128, 2048], BF16)
        p.sTm, p.b_sTm = p.sb("sTm", [128, 512], BF16)
        p.kz, p.b_kz = p.sb("kz", [128, 1024], BF16)
        p.sm, p.b_sm = p.sb("sm", [128, 96])
        p.mcoef, p.b_mcoef = p.sb("mcoef", [128, NCORES * H])
        p.gat, p.b_gat = p.sb("gat", [128, G, 8])
        p.gmx, p.b_gmx = p.sb("gmx", [128, G, 8])
        p.gm, p.b_gm = p.sb("gm", [128, 7, 4 * G])
        p.ubuf = []
        p.b_ubuf = []
        for i in range(2):
            t, b = p.sb(f"ubuf{i}", [128, HW + GT])
            p.ubuf.append(t)
            p.b_ubuf.append(b)
        p.P = [p.ps(f"P{i}", [128, 512]) for i in range(6)]
        p.b_P = [Buf(f"P{i}") for i in range(6)]
        p.Pb = [p.ps(f"Pb{i}", [128, 1024], BF16) for i in range(2)]
        p.b_Pb = [Buf(f"Pb{i}") for i in range(2)]

    def dma_group(self, q, key, pairs, reads=(), writes=()):
        S = self.S
        k = S._dma_sem(key)
        S._wait(q, S._deps(reads, writes))
        for out, in_ in pairs:
            ins = S.eng[q].dma_start(out=out, in_=in_)
            S.cnt[k] += 16
            ins.then_inc(S.sems[k], 16)
            S.ninst += 1
        S._record((k, S.cnt[k]), reads, writes)

    def dump(self, name, ap, bufs):
        if not self.debug:
            return
        shape = list(ap.shape)
        d = self.nc.dram_tensor("dbg_" + name, shape, ap.dtype, kind="ExternalOutput").ap()
        b = Buf("dbg_" + name)
        self.dbg_bufs.append(b)
        self.S.dma("sp", d, ap, reads=bufs, writes=[b], key="dbg_" + name)

    def load_T(self, src_rows, n, dst_ap, dst_buf):
        p, S = self, self.S
        S.dma("sp", p.stage[0:n, :], src_rows, writes=[p.b_stage], key="stage")
        S.op("pe", lambda e: e.transpose(p.P[5][:, 0:n], p.stage[0:n, :], p.ident_f[0:n, 0:n]),
             reads=[p.b_stage, p.b_ident_f], writes=[p.b_P[5]])
        S.op("dve", lambda e: e.tensor_copy(out=dst_ap, in_=p.P[5][:, 0:n]), reads=[p.b_P[5]], writes=[dst_buf])

    def ring(self):
        i = self.ring_pos % self.NR
        self.ring_pos += 1
        return self.wt[i], self.b_wt[i], f"w{i}"

    def load_std_piece(self, W2d, c0, w=512):
        t, b, key = self.ring()
        view = t[:, 0:8 * w].rearrange("p (k n) -> p k n", k=8)
        src = W2d.rearrange("(k p) n -> p k n", p=128)
        pairs = [(view[:, 0:4, :], src[:, 0:4, c0:c0 + w]), (view[:, 4:8, :], src[:, 4:8, c0:c0 + w])]
        self.dma_group("pool", key, pairs, writes=[b])
        return view, b

    def load_rows_piece(self, W2d, k0, nk, c0, w):
        t, b, key = self.ring()
        view = t[:, 0:nk * w].rearrange("p (k n) -> p k n", k=nk)
        src = W2d.rearrange("(k p) n -> p k n", p=128)
        pairs = []
        step = 4
        for a in range(0, nk, step):
            e = min(nk, a + step)
            pairs.append((view[:, a:e, :], src[:, k0 + a:k0 + e, c0:c0 + w]))
        self.dma_group("pool", key, pairs, writes=[b])
        return view, b

    def _setup(self):
        p, S = self, self.S
        loads = [(p.ident_f, p.b_ident_f, p.t_ident), (p.ret_dt, p.b_ret_dt, p.t_ret_dt),
                 (p.ret_xi, p.b_ret_xi, p.t_ret_xi), (p.ret_zs, p.b_ret_zs, p.t_ret_zs),
                 (p.neg, p.b_neg, p.t_neg), (p.ut, p.b_ut, p.t_ut), (p.inv_freq, p.b_inv_freq, p.t_inv_freq),
                 (p.sel, p.b_sel, p.t_sel), (p.nf, p.b_nf, p.t_nf), (p.ret_coef, p.b_ret_coef, p.t_ret_coef),
                 (p.valid, p.b_valid, p.t_valid), (p.msel, p.b_msel, p.t_msel)]
        self.dma_group("sp", "setup", [(t[:], src) for t, b, src in loads], writes=[b for t, b, s in loads])
        S.dma("sp", p.bg_bc[:], p.ml_bg[0:1, :].partition_broadcast(128).rearrange("p o n -> p (o n)"),
              writes=[p.b_bg_bc], key="setup2")
        S.op("dve", lambda e: e.memset(p.ones_f[:], 1.0), writes=[p.b_ones_f])
        S.op("dve", lambda e: e.memset(p.ones_b[:], 1.0), writes=[p.b_ones_b])
        S.op("dve", lambda e: e.memset(p.xh[:], 0.0), writes=[p.b_xh])
        S.op("dve", lambda e: e.tensor_copy(out=p.ident_b[:], in_=p.ident_f[:]), reads=[p.b_ident_f], writes=[p.b_ident_b])
        vec = [(p.ntgT, p.b_ntgT, p.ntg), (p.nfgT, p.b_nfgT, p.nfg), (p.ffn_cwT, p.b_ffn_cwT, p.ffn_cw), (p.ffn_cbT, p.b_ffn_cbT, p.ffn_cb),
               (p.ret_gnT, p.b_ret_gnT, p.ret_gn), (p.ml_gnT, p.b_ml_gnT, p.ml_gn), (p.ml_cwT, p.b_ml_cwT, p.ml_cw), (p.ml_cbT, p.b_ml_cbT, p.ml_cb),
               (p.adabT, p.b_adabT, p.ada_b), (p.cT_f, p.b_cT_f, p.c_in), (p.selmat, p.b_selmat, p.t_selmat)]
        self.dma_group("sp", "setup3", [(t[:], s) for t, b, s in vec], writes=[b for t, b, s in vec])
        if p.mod_layers:
            S.op("act", lambda e: e.activation(out=p.cT_b[:], in_=p.cT_f[:], func=AF.Silu), reads=[p.b_cT_f], writes=[p.b_cT_b])
            for l in p.mod_layers:
                for pc in range(12):
                    wv, wb = self.load_std_piece(p.ada_w[l], pc * 512)
                    for ct in range(4):
                        col = l * 48 + pc * 4 + ct
                        for kc in range(KC):
                            S.op("pe", lambda e: e.matmul(p.P[4][:, col:col + 1], lhsT=wv[:, kc, ct * 128:(ct + 1) * 128],
                                                          rhs=p.cT_b[:, kc:kc + 1], start=(kc == 0), stop=(kc == KC - 1)),
                                 reads=[wb, p.b_cT_b], writes=[p.b_P[4]])
            for l in p.mod_layers:
                S.op("dve", lambda e: e.tensor_tensor(out=p.modT[:, l, :], in0=p.adabT[:, l, :], in1=p.P[4][:, l * 48:(l + 1) * 48], op=ALU.add),
                     reads=[p.b_P[4], p.b_adabT], writes=[p.b_modT])
                S.op("pe", lambda e: e.transpose(p.P[5][0:48, 0:128], p.modT[:, l, :], p.ident_f[:, :]),
                     reads=[p.b_modT, p.b_ident_f], writes=[p.b_P[5]])
                S.op("dve", lambda e: e.tensor_copy(out=p.stage[0:48, :], in_=p.P[5][0:48, 0:128]), reads=[p.b_P[5]], writes=[p.b_stage])
                S.dma("sp", p.modscr[l], p.stage[0:48, :], reads=[p.b_stage], writes=[p.b_modscr], key="modscr")
            if p.phase == 1:
                S.dma("sp", p.modT_o[:, :], p.modT[:].rearrange("p l n -> p (l n)"), reads=[p.b_modT], writes=[p.b_modT_o], key="modT_o")
        else:
            S.dma("sp", p.modT[:].rearrange("p l n -> p (l n)"), p.modT_i[:, :], writes=[p.b_modT], key="modT_i")
        for l in p.layers:
            S.op("dve", lambda e: e.scalar_tensor_tensor(out=p.gsc[:, 2 * l, :], in0=p.modT[:, l, 8:16], scalar=1.0,
                                                         in1=p.ntgT[:, l, :], op0=ALU.add, op1=ALU.mult),
                 reads=[p.b_modT, p.b_ntgT], writes=[p.b_gsc])
            S.op("dve", lambda e: e.scalar_tensor_tensor(out=p.gsc[:, 2 * l + 1, :], in0=p.modT[:, l, 32:40], scalar=1.0,
                                                         in1=p.nfgT[:, l, :], op0=ALU.add, op1=ALU.mult),
                 reads=[p.b_modT, p.b_nfgT], writes=[p.b_gsc])

    def load_gate(self, l, which):
        p, S = self, self.S
        r0 = 16 if which == 0 else 40
        src = p.modscr[l, r0:r0 + 8, :].rearrange("(o a) b -> o (a b)", o=1).partition_broadcast(128).rearrange("p o n -> p (o n)")
        S.dma("sp", p.gt_bc[:], src, reads=[p.b_modscr], writes=[p.b_gt_bc], key="gt")

    def norm_rows(self, xt, bx, npart, gidx, sh, dst_fn, dst_buf, col):
        p, S = self, self.S
        S.op("act", lambda e: e.activation(out=p.xn[0:npart, :], in_=xt, func=AF.Square, accum_out=p.ss[0:npart, col:col + 1]),
             reads=[bx], writes=[p.b_xn, p.b_ss])
        S.op("dve", lambda e: e.tensor_scalar(out=p.ss[0:npart, col:col + 1], in0=p.ss[0:npart, col:col + 1], scalar1=1.0 / D, scalar2=EPS,
                                              op0=ALU.mult, op1=ALU.add), reads=[p.b_ss], writes=[p.b_ss])
        S.op("act", lambda e: e.activation(out=p.ss[0:npart, col:col + 1], in_=p.ss[0:npart, col:col + 1], func=AF.Sqrt),
             reads=[p.b_ss], writes=[p.b_ss])
        S.op("dve", lambda e: e.reciprocal(out=p.rstd[0:npart, col:col + 1], in_=p.ss[0:npart, col:col + 1]),
             reads=[p.b_ss], writes=[p.b_rstd])
        S.op("act", lambda e: e.activation(out=p.xn[0:npart, :], in_=xt, func=AF.Copy, scale=p.rstd[0:npart, col:col + 1]),
             reads=[bx, p.b_rstd], writes=[p.b_xn])
        for half in range(2):
            bank = p.P[half]
            for k4 in range(4):
                kc = half * 4 + k4
                S.op("pe", lambda e: e.transpose(bank[:, k4 * 128:k4 * 128 + npart], p.xn[0:npart, kc * 128:(kc + 1) * 128],
                                                 p.ident_f[0:npart, 0:npart]),
                     reads=[p.b_xn, p.b_ident_f], writes=[p.b_P[half]])
            for k4 in range(4):
                kc = half * 4 + k4
                src = bank[:, k4 * 128:k4 * 128 + npart]
                if kc % 2 == 0:
                    S.op("act", lambda e: e.activation(out=dst_fn(kc), in_=src, func=AF.Identity,
                                                       scale=p.gsc[:, gidx, kc:kc + 1], bias=sh[:, kc:kc + 1]),
                         reads=[p.b_P[half], p.b_gsc, p.b_modT], writes=[dst_buf])
                else:
                    S.op("dve", lambda e: e.tensor_scalar(out=dst_fn(kc), in0=src, scalar1=p.gsc[:, gidx, kc:kc + 1],
                                                          scalar2=sh[:, kc:kc + 1], op0=ALU.mult, op1=ALU.add),
                         reads=[p.b_P[half], p.b_gsc, p.b_modT], writes=[dst_buf])

    def norm_group(self, src, src_bufs, g, gidx, sh):
        p, S = self, self.S
        for c in range(G):
            r0 = g * GT + c * CH
            S.dma("sp", p.x_g[:, c, :], src[r0:r0 + CH, :], reads=src_bufs, writes=[p.b_xg[c]], key=f"xg{c}")
        for c in range(G):
            S.op("act", lambda e: e.activation(out=p.junk[:], in_=p.x_g[:, c, :], func=AF.Square, accum_out=p.ss[:, c:c + 1]),
                 reads=[p.b_xg[c]], writes=[p.b_junk, p.b_ss])
        S.op("dve", lambda e: e.tensor_scalar(out=p.ss[:, 0:G], in0=p.ss[:, 0:G], scalar1=1.0 / D, scalar2=EPS, op0=ALU.mult, op1=ALU.add),
             reads=[p.b_ss], writes=[p.b_ss])
        S.op("act", lambda e: e.activation(out=p.ss[:, 0:G], in_=p.ss[:, 0:G], func=AF.Sqrt), reads=[p.b_ss], writes=[p.b_ss])
        S.op("dve", lambda e: e.reciprocal(out=p.rstd[:, 0:G], in_=p.ss[:, 0:G]), reads=[p.b_ss], writes=[p.b_rstd])
        for c in range(G):
            xn, b_xn = (p.xn, p.b_xn) if c % 2 == 0 else (p.xn2, p.b_xn2)
            S.op("act", lambda e: e.activation(out=xn[:], in_=p.x_g[:, c, :], func=AF.Copy, scale=p.rstd[:, c:c + 1]),
                 reads=[p.b_xg[c], p.b_rstd], writes=[b_xn])
            for half in range(2):
                bi = 2 * (c % 2) + half
                bank = p.P[bi]
                for k4 in range(4):
                    kc = half * 4 + k4
                    S.op("pe", lambda e: e.transpose(bank[:, k4 * 128:(k4 + 1) * 128], xn[:, kc * 128:(kc + 1) * 128], p.ident_f[:, :]),
                         reads=[b_xn, p.b_ident_f], writes=[p.b_P[bi]])
                for k4 in range(4):
                    kc = half * 4 + k4
                    srcp = bank[:, k4 * 128:(k4 + 1) * 128]
                    dst = p.hT[:, kc, c * CH:(c + 1) * CH]
                    if kc % 2 == 0:
                        S.op("act", lambda e: e.activation(out=dst, in_=srcp, func=AF.Identity,
                                                           scale=p.gsc[:, gidx, kc:kc + 1], bias=sh[:, kc:kc + 1]),
                             reads=[p.b_P[bi], p.b_gsc, p.b_modT], writes=[p.b_hT])
                    else:
                        S.op("dve", lambda e: e.tensor_scalar(out=dst, in0=srcp, scalar1=p.gsc[:, gidx, kc:kc + 1],
                                                              scalar2=sh[:, kc:kc + 1], op0=ALU.mult, op1=ALU.add),
                             reads=[p.b_P[bi], p.b_gsc, p.b_modT], writes=[p.b_hT])

    def load_halo(self, src, src_bufs, g, ex):
        p, S = self, self.S
        if g > 0:
            r0 = g * GT - HW
            S.dma("sp", p.xh[0:HW, :], src[r0:r0 + HW, :], reads=src_bufs, writes=[p.b_xh], key="xh")
        else:
            gt = p.gath[ex]
            nr = NCORES * HW
            S.dma("sp", p.xn[0:nr, :], gt[:, :], reads=[p.b_gath[ex]], writes=[p.b_xn], key="xnh")
            for half in range(2):
                S.op("pe", lambda e: e.matmul(p.P[half][0:HW, :], lhsT=p.selmat[0:nr, 0:HW], rhs=p.xn[0:nr, half * 512:(half + 1) * 512],
                                              start=True, stop=True), reads=[p.b_selmat, p.b_xn], writes=[p.b_P[half]])
                S.op("dve", lambda e: e.tensor_copy(out=p.xh[0:HW, half * 512:(half + 1) * 512], in_=p.P[half][0:HW, :]),
                     reads=[p.b_P[half]], writes=[p.b_xh])

    def norm_halo(self, gidx, sh):
        p = self
        self.norm_rows(p.xh[0:32, :], p.b_xh, 32, gidx, sh, lambda kc: p.hTh[:, kc, :], p.b_hTh, 4)

    def rope_tables(self, g):
        p, S = self, self.S
        posi = p.wk[4][:].bitcast(I32)
        src = p.pos[0:1, g * GT:(g + 1) * GT].partition_broadcast(128).rearrange("p o n -> p (o n)")
        S.dma("sp", posi, src, writes=[p.b_wk[4]], key="posi")
        ang, b_ang = p.wk[0], p.b_wk[0]
        S.op("dve", lambda e: e.tensor_copy(out=p.wk[1][:], in_=posi), reads=[p.b_wk[4]], writes=[p.b_wk[1]])
        S.op("dve", lambda e: e.tensor_scalar(out=ang[:], in0=p.wk[1][:], scalar1=p.inv_freq[:, 0:1], scalar2=None, op0=ALU.mult),
             reads=[p.b_wk[1], p.b_inv_freq], writes=[b_ang])
        for dst, b_dst, shift in ((p.sin, p.b_sin, 0.0), (p.cos, p.b_cos, 0.5 * math.pi)):
            xs, b_xs = p.wk[1], p.b_wk[1]
            kf, b_kf = p.wk[2], p.b_wk[2]
            ki = p.wk[3][:].bitcast(I32)
            b_ki = p.b_wk[3]
            S.op("dve", lambda e: e.tensor_scalar(out=xs[:], in0=ang[:], scalar1=shift, scalar2=None, op0=ALU.add),
                 reads=[b_ang], writes=[b_xs])
            S.op("dve", lambda e: e.tensor_scalar(out=kf[:], in0=xs[:], scalar1=1.0 / TWO_PI, scalar2=None, op0=ALU.mult),
                 reads=[b_xs], writes=[b_kf])
            S.op("dve", lambda e: e.tensor_copy(out=ki, in_=kf[:]), reads=[b_kf], writes=[b_ki])
            S.op("dve", lambda e: e.tensor_copy(out=kf[:], in_=ki), reads=[b_ki], writes=[b_kf])
            S.op("dve", lambda e: e.scalar_tensor_tensor(out=xs[:], in0=kf[:], scalar=-C1, in1=xs[:], op0=ALU.mult, op1=ALU.add),
                 reads=[b_kf, b_xs], writes=[b_xs])
            S.op("dve", lambda e: e.scalar_tensor_tensor(out=xs[:], in0=kf[:], scalar=-C2, in1=xs[:], op0=ALU.mult, op1=ALU.add),
                 reads=[b_kf, b_xs], writes=[b_xs])
            S.op("dve", lambda e: e.tensor_scalar(out=kf[:], in0=xs[:], scalar1=-math.pi, scalar2=TWO_PI, op0=ALU.is_lt, op1=ALU.mult),
                 reads=[b_xs], writes=[b_kf])
            S.op("dve", lambda e: e.tensor_tensor(out=xs[:], in0=xs[:], in1=kf[:], op=ALU.add), reads=[b_xs, b_kf], writes=[b_xs])
            S.op("dve", lambda e: e.tensor_scalar(out=kf[:], in0=xs[:], scalar1=math.pi, scalar2=-TWO_PI, op0=ALU.is_gt, op1=ALU.mult),
                 reads=[b_xs], writes=[b_kf])
            S.op("dve", lambda e: e.tensor_tensor(out=xs[:], in0=xs[:], in1=kf[:], op=ALU.add), reads=[b_xs, b_kf], writes=[b_xs])
            S.op("dve", lambda e: e.tensor_scalar(out=xs[:], in0=xs[:], scalar1=-PI_SAFE, scalar2=PI_SAFE, op0=ALU.max, op1=ALU.min),
                 reads=[b_xs], writes=[b_xs])
            S.op("act", lambda e: e.activation(out=dst[:], in_=xs[:], func=AF.Sin), reads=[b_xs], writes=[b_dst])

    def rope_pair(self, hh, dst, b_dst):
        p, S = self, self.S
        i1, i2 = 2 * (hh % 2), 2 * (hh % 2) + 1
        b1, b2 = p.P[i1], p.P[i2]
        A, Bm, C_, Dm = p.wk[0], p.wk[1], p.wk[2], p.wk[3]
        S.op("dve", lambda e: e.tensor_tensor(out=A[:], in0=b1[:], in1=p.cos[:], op=ALU.mult), reads=[p.b_P[i1], p.b_cos], writes=[p.b_wk[0]])
        S.op("dve", lambda e: e.tensor_tensor(out=Bm[:], in0=b2[:], in1=p.sin[:], op=ALU.mult), reads=[p.b_P[i2], p.b_sin], writes=[p.b_wk[1]])
        S.op("dve", lambda e: e.tensor_tensor(out=C_[:], in0=b1[:], in1=p.sin[:], op=ALU.mult), reads=[p.b_P[i1], p.b_sin], writes=[p.b_wk[2]])
        S.op("dve", lambda e: e.tensor_tensor(out=Dm[:], in0=b2[:], in1=p.cos[:], op=ALU.mult), reads=[p.b_P[i2], p.b_cos], writes=[p.b_wk[3]])
        S.op("dve", lambda e: e.tensor_tensor(out=dst[:, 2 * hh, :], in0=A[:], in1=Bm[:], op=ALU.subtract),
             reads=[p.b_wk[0], p.b_wk[1]], writes=[b_dst])
        S.op("dve", lambda e: e.tensor_tensor(out=dst[:, 2 * hh + 1, :], in0=C_[:], in1=Dm[:], op=ALU.add),
             reads=[p.b_wk[2], p.b_wk[3]], writes=[b_dst])

    def proj_A(self, wv, wb, ct, bank_i, hT=None, b_hT=None, n=GT):
        p, S = self, self.S
        hT = p.hT if hT is None else hT
        b_hT = p.b_hT if b_hT is None else b_hT
        for kc in range(KC):
            S.op("pe", lambda e: e.matmul(p.P[bank_i][:, 0:n], lhsT=wv[:, kc, ct * 128:(ct + 1) * 128], rhs=hT[:, kc, 0:n],
                                          start=(kc == 0), stop=(kc == KC - 1)),
                 reads=[wb, b_hT], writes=[p.b_P[bank_i]])

    def proj_B(self, wv, wb, c, bank_i, w=512):
        p, S = self, self.S
        for kc in range(KC):
            S.op("pe", lambda e: e.matmul(p.P[bank_i][:, 0:w], lhsT=p.hT[:, kc, c * CH:(c + 1) * CH], rhs=wv[:, kc, 0:w],
                                          start=(kc == 0), stop=(kc == KC - 1)),
                 reads=[wb, p.b_hT], writes=[p.b_P[bank_i]])

    def exchange(self, k):
        p, S = self, self.S
        if p.mode == "host":
            return
        S.wait_bufs("pool", [p.b_loc[k], p.b_gath[k]])
        ins = p.nc.gpsimd.collective_compute("AllGather", ALU.bypass, replica_groups=[list(range(NCORES))],
                                             ins=[p.loc[k][:, :]], outs=[p.gath[k][:, :]])
        key = S._dma_sem(f"cc{k}")
        S.cnt[key] += 16
        ins.then_inc(S.sems[key], 16)
        S._record((key, S.cnt[key]), [p.b_loc[k]], [p.b_gath[k]])

    def make_kz(self, c, scale_ap_fn):
        p, S = self, self.S
        cs = slice(c * CH, (c + 1) * CH)
        for i in range(8):
            S.op("pe", lambda e: e.transpose(p.Pb[0][:, i * 128:(i + 1) * 128], p.kT[:, i, cs], p.ident_b[:, :]),
                 reads=[p.b_kT, p.b_ident_b], writes=[p.b_Pb[0]])
        for hh in range(H):
            sc, sbufs = scale_ap_fn(hh)
            S.op("act", lambda e: e.activation(out=p.kz[:, hh * 256:(hh + 1) * 256], in_=p.Pb[0][:, hh * 256:(hh + 1) * 256],
                                               func=AF.Copy, scale=sc),
                 reads=[p.b_Pb[0]] + sbufs, writes=[p.b_kz])

    def state_update(self, c, hh, decay, dbufs=()):
        p, S = self, self.S
        dbufs = list(dbufs)
        for half in range(2):
            i = 2 * hh + half
            S.op("pe", lambda e: e.matmul(p.P[2 + half][:, :], lhsT=p.kz[:, hh * 256 + half * 128: hh * 256 + (half + 1) * 128],
                                          rhs=p.v_all[:, c, hh * 512:(hh + 1) * 512], start=True, stop=True),
                 reads=[p.b_kz, p.b_v[c]], writes=[p.b_P[2 + half]])
            S.op("dve", lambda e: e.scalar_tensor_tensor(out=p.R[:, i, :], in0=p.R[:, i, :], scalar=decay, in1=p.P[2 + half][:, :],
                                                         op0=ALU.mult, op1=ALU.add),
                 reads=[p.b_R[i], p.b_P[2 + half]] + dbufs, writes=[p.b_R[i]])

    def refresh_Rb(self, hh):
        p, S = self, self.S
        for half in range(2):
            i = 2 * hh + half
            S.op("act", lambda e: e.activation(out=p.Rb[:, i, :], in_=p.R[:, i, :], func=AF.Copy),
                 reads=[p.b_R[i]], writes=[p.b_Rb[i]])

    def groupnorm_heads(self, ywk, c, gate):
        p, S = self, self.S
        for hh in range(H):
            S.op("dve", lambda e: e.bn_stats(out=p.sm[:, 6 * hh:6 * hh + 6], in_=p.wk[ywk[hh]][:]), reads=[p.b_wk[ywk[hh]]], writes=[p.b_sm])
            S.op("dve", lambda e: e.bn_aggr(out=p.sm[:, 24 + 2 * hh:26 + 2 * hh], in_=p.sm[:, 6 * hh:6 * hh + 6]), reads=[p.b_sm], writes=[p.b_sm])
        var = p.sm[:, 24:32].rearrange("p (h t) -> p h t", t=2)[:, :, 1]
        S.op("dve", lambda e: e.tensor_scalar(out=p.sm[:, 32:36], in0=var, scalar1=EPS, scalar2=None, op0=ALU.add), reads=[p.b_sm], writes=[p.b_sm])
        S.op("act", lambda e: e.activation(out=p.sm[:, 32:36], in_=p.sm[:, 32:36], func=AF.Ln), reads=[p.b_sm], writes=[p.b_sm])
        S.op("act", lambda e: e.activation(out=p.sm[:, 32:36], in_=p.sm[:, 32:36], func=AF.Exp, scale=-0.5), reads=[p.b_sm], writes=[p.b_sm])
        for hh in range(H):
            w = p.wk[ywk[hh]]
            if gate:
                S.op("dve", lambda e: e.tensor_scalar(out=w[:], in0=w[:], scalar1=p.sm[:, 24 + 2 * hh:25 + 2 * hh], scalar2=p.sm[:, 32 + hh:33 + hh],
                                                      op0=ALU.subtract, op1=ALU.mult), reads=[p.b_wk[ywk[hh]], p.b_sm], writes=[p.b_wk[ywk[hh]]])
                sgv = p.sg_all[:, c, hh * 512:(hh + 1) * 512]
                S.op("dve", lambda e: e.tensor_tensor(out=p.yg[:, hh * 512:(hh + 1) * 512], in0=w[:], in1=sgv, op=ALU.mult),
                     reads=[p.b_wk[ywk[hh]], p.b_xg[c]], writes=[p.b_yg])
            else:
                S.op("dve", lambda e: e.tensor_scalar(out=p.yg[:, hh * 512:(hh + 1) * 512], in0=w[:], scalar1=p.sm[:, 24 + 2 * hh:25 + 2 * hh],
                                                      scalar2=p.sm[:, 32 + hh:33 + hh], op0=ALU.subtract, op1=ALU.mult),
                     reads=[p.b_wk[ywk[hh]], p.b_sm], writes=[p.b_yg])

    def make_ynT(self, c, gnT, b_gnT):
        p, S = self, self.S
        ynT = p.big_a[:, 0:16 * GT].rearrange("p (k n) -> p k n", k=16)
        for kc in range(16):
            bi = kc // 8
            S.op("pe", lambda e: e.transpose(p.Pb[bi][:, (kc % 8) * 128:(kc % 8 + 1) * 128], p.yg[:, kc * 128:(kc + 1) * 128], p.ident_b[:, :]),
                 reads=[p.b_yg, p.b_ident_b], writes=[p.b_Pb[bi]])
        for kc in range(16):
            bi = kc // 8
            src = p.Pb[bi][:, (kc % 8) * 128:(kc % 8 + 1) * 128]
            dst = ynT[:, kc, c * CH:(c + 1) * CH]
            if kc % 2 == 0:
                S.op("act", lambda e: e.activation(out=dst, in_=src, func=AF.Copy, scale=gnT[:, kc:kc + 1]),
                     reads=[p.b_Pb[bi], b_gnT], writes=[p.b_big_a])
            else:
                S.op("dve", lambda e: e.tensor_scalar(out=dst, in0=src, scalar1=gnT[:, kc:kc + 1], scalar2=None, op0=ALU.mult),
                     reads=[p.b_Pb[bi], b_gnT], writes=[p.b_big_a])

    def out_proj(self, g, W_out, src, src_bufs, dst, dst_bufs, halo_ex):
        p, S = self, self.S
        ynT = p.big_a[:, 0:16 * GT].rearrange("p (k n) -> p k n", k=16)
        for c in range(G):
            r0 = g * GT + c * CH
            S.dma("sp", p.x_g[:, c, :], src[r0:r0 + CH, :], reads=src_bufs, writes=[p.b_xg[c]], key=f"xg{c}")
        for cp in range(4):
            wv, wb = self.load_rows_piece(W_out, 0, 16, cp * 256, 256)
            for c in range(G):
                bi = (cp * G + c) % 6
                for kc in range(16):
                    S.op("pe", lambda e: e.matmul(p.P[bi][:, 0:256], lhsT=ynT[:, kc, c * CH:(c + 1) * CH], rhs=wv[:, kc, :],
                                                  start=(kc == 0), stop=(kc == 15)),
                         reads=[wb, p.b_big_a], writes=[p.b_P[bi]])
                ti = (cp * G + c) % 5
                tmp = p.wk[ti]
                S.op("dve", lambda e: e.tensor_tensor(out=tmp[:, 0:256], in0=p.P[bi][:, 0:256], in1=p.gt_bc[:, cp * 256:(cp + 1) * 256], op=ALU.mult),
                     reads=[p.b_P[bi], p.b_gt_bc], writes=[p.b_wk[ti]])
                S.op("dve", lambda e: e.tensor_tensor(out=p.x_g[:, c, cp * 256:(cp + 1) * 256], in0=p.x_g[:, c, cp * 256:(cp + 1) * 256],
                                                      in1=tmp[:, 0:256], op=ALU.add),
                     reads=[p.b_wk[ti], p.b_xg[c]], writes=[p.b_xg[c]])
        self.store_group(g, dst, dst_bufs, halo_ex)

    def store_group(self, g, dst, dst_bufs, halo_ex):
        p, S = self, self.S
        pairs = [(dst[g * GT + c * CH: g * GT + (c + 1) * CH, :], p.x_g[:, c, :]) for c in range(G)]
        self.dma_group("sp", f"st_{dst_bufs[0].name[:2]}", pairs, reads=p.b_xg, writes=[dst_bufs[g]])
        if g == NG - 1 and halo_ex is not None:
            S.dma("sp", p.loc[halo_ex][:, :], p.x_g[128 - HW:128, G - 1, :], reads=[p.b_xg[G - 1]], writes=[p.b_loc[halo_ex]], key=f"loc{halo_ex}")

    def kv_store(self, g):
        p = self
        self.dma_group("sp", "kvst", [(p.kst[g], p.kT[:].rearrange("p a n -> p (a n)")), (p.vst[g], p.v_all[:].rearrange("p a n -> p (a n)"))],
                       reads=[p.b_kT] + p.b_v, writes=[p.b_kvst])

    def kv_load(self, g):
        p = self
        self.dma_group("sp", "kvld", [(p.kT[:].rearrange("p a n -> p (a n)"), p.kst[g]), (p.v_all[:].rearrange("p a n -> p (a n)"), p.vst[g])],
                       writes=[p.b_kT] + p.b_v)

    def ret_group(self, g, full):
        p, S = self, self.S
        sh = p.modT[:, 0, 0:8]
        self.norm_group(p.x, [], g, 0, sh)
        self.rope_tables(g)
        W = p.ret_w_in
        reuse = full and p.relay
        plist = ([0, 1] if full else []) + ([] if reuse else [2, 3])
        if reuse:
            self.kv_load(g)
        for pc in plist:
            wv, wb = self.load_std_piece(W, pc * 512)
            dst, b_dst = (p.qT, p.b_qT) if pc < 2 else (p.kT, p.b_kT)
            for ct in range(4):
                self.proj_A(wv, wb, ct, ct)
            for j in range(2):
                self.rope_pair(2 * (pc % 2) + j, dst, b_dst)
        for hh in ([] if reuse else range(H)):
            wv, wb = self.load_std_piece(W, 2048 + hh * 512)
            for c in range(G):
                self.proj_B(wv, wb, c, c)
                dstv = p.v_all[:, c, hh * 512:(hh + 1) * 512]
                if c % 2 == 0:
                    S.op("act", lambda e: e.activation(out=dstv, in_=p.P[c][:, :], func=AF.Copy), reads=[p.b_P[c]], writes=[p.b_v[c]])
                else:
                    S.op("dve", lambda e: e.tensor_copy(out=dstv, in_=p.P[c][:, :]), reads=[p.b_P[c]], writes=[p.b_v[c]])
        if (not full) and p.relay:
            self.kv_store(g)
        if full:
            for hh in range(H):
                wv, wb = self.load_std_piece(W, 4096 + hh * 512)
                for c in range(G):
                    self.proj_B(wv, wb, c, c)
                    dsts = p.sg_all[:, c, hh * 512:(hh + 1) * 512]
                    S.op("act", lambda e: e.activation(out=dsts, in_=p.P[c][:, :], func=AF.Silu), reads=[p.b_P[c]], writes=[p.b_xg[c]])
        if full and g == 0:
            self.dump("hT", p.hT[:], [p.b_hT])
            self.dump("cos", p.cos[:], [p.b_cos])
            self.dump("sin", p.sin[:], [p.b_sin])
            self.dump("qT", p.qT[:], [p.b_qT])
            self.dump("kT", p.kT[:], [p.b_kT])
            self.dump("v_all", p.v_all[:], p.b_v)
            self.dump("sg_all", p.sg_all, p.b_xg)
        for c in range(G):
            self.ret_chunk(c, full)
            if full and g == 0 and c == 1:
                self.dump("yg", p.yg[:], [p.b_yg])
                self.dump("sm", p.sm[:], [p.b_sm])
                self.dump("wk0", p.wk[0][:], [p.b_wk[0]])
                self.dump("wk3", p.wk[3][:], [p.b_wk[3]])
                self.dump("sTm", p.sTm[:], [p.b_sTm])
                self.dump("kz", p.kz[:], [p.b_kz])
                self.dump("R", p.R[:], p.b_R)
        if full and g == 0:
            self.dump("ynT", p.big_a[:, 0:16 * GT], [p.b_big_a])
        if full:
            self.out_proj(g, p.ret_w_out, p.x, [], p.xa, p.b_xa, 2)
        if full and g == 0:
            self.dump("xg", p.x_g[:], p.b_xg)

    def ret_chunk(self, c, full):
        p, S = self, self.S
        cs = slice(c * CH, (c + 1) * CH)
        self.make_kz(c, lambda hh: (p.ret_zs[:, hh:hh + 1], [p.b_ret_zs]))
        gam = [float(np.exp(128.0 * np.log(1.0 - 2.0 ** (-5.0 - h)))) for h in range(H)]
        if not full:
            for hh in range(H):
                self.state_update(c, hh, gam[hh])
            return
        for hh in range(H):
            for half in range(2):
                S.op("pe", lambda e: e.matmul(p.P[4][:, hh * 128:(hh + 1) * 128], lhsT=p.kT[:, 2 * hh + half, cs], rhs=p.qT[:, 2 * hh + half, cs],
                                              start=(half == 0), stop=(half == 1)),
                     reads=[p.b_kT, p.b_qT], writes=[p.b_P[4]])
        S.op("dve", lambda e: e.tensor_tensor(out=p.sTm[:], in0=p.P[4][:, :], in1=p.ret_dt[:], op=ALU.mult),
             reads=[p.b_P[4], p.b_ret_dt], writes=[p.b_sTm])
        for hh in range(H):
            S.op("pe", lambda e: e.matmul(p.P[0][:, :], lhsT=p.sTm[:, hh * 128:(hh + 1) * 128], rhs=p.v_all[:, c, hh * 512:(hh + 1) * 512],
                                          start=True, stop=True), reads=[p.b_sTm, p.b_v[c]], writes=[p.b_P[0]])
            for half in range(2):
                i = 2 * hh + half
                S.op("pe", lambda e: e.matmul(p.P[1][:, :], lhsT=p.qT[:, i, cs], rhs=p.Rb[:, i, :], start=(half == 0), stop=(half == 1)),
                     reads=[p.b_qT, p.b_Rb[i]], writes=[p.b_P[1]])
            S.op("act", lambda e: e.activation(out=p.wk[4][:], in_=p.P[1][:, :], func=AF.Copy, scale=p.ret_xi[:, hh:hh + 1]),
                 reads=[p.b_P[1], p.b_ret_xi], writes=[p.b_wk[4]])
            S.op("dve", lambda e: e.tensor_tensor(out=p.wk[hh][:], in0=p.wk[4][:], in1=p.P[0][:, :], op=ALU.add),
                 reads=[p.b_wk[4], p.b_P[0]], writes=[p.b_wk[hh]])
            self.state_update(c, hh, gam[hh])
            self.refresh_Rb(hh)
        self.groupnorm_heads([0, 1, 2, 3], c, gate=True)
        self.make_ynT(c, p.ret_gnT, p.b_ret_gnT)

    def ret_layer(self):
        p, S = self, self.S
        for i in range(8):
            S.op("dve", lambda e: e.memset(p.R[:, i, :], 0.0), writes=[p.b_R[i]])
        if p.stop_after == "dbg_ret":
            for hh in range(H):
                self.refresh_Rb(hh)
            self.load_gate(0, 0)
            self.dump("modT", p.modT[:], [p.b_modT])
            self.dump("gsc", p.gsc[:], [p.b_gsc])
            self.dump("gt_bc", p.gt_bc[:], [p.b_gt_bc])
            self.ret_group(0, full=True)
            return
        if p.phase in (None, 1):
            self.ret_part_a()
        if p.phase is None:
            self.exchange(1)
        if p.phase in (None, 2):
            self.ret_part_b()

    def ret_part_a(self):
        p, S = self, self.S
        for g in range(NG):
            self.ret_group(g, full=False)
        self.dma_group("sp", "loc1", [(p.loc[1][i * 128:(i + 1) * 128, :], p.R[:, i, :]) for i in range(8)],
                       reads=p.b_R, writes=[p.b_loc[1]])

    def ret_part_b(self):
        p, S = self, self.S
        self.combine_state(1, p.ret_coef, p.b_ret_coef, nrows=8)
        for hh in range(H):
            self.refresh_Rb(hh)
        self.load_gate(0, 0)
        for g in range(NG):
            self.ret_group(g, full=True)

    def combine_state(self, ex, coef, b_coef, nrows):
        p, S = self, self.S
        gt = p.gath[ex]
        per = nrows * 128 if ex == 1 else 9 * 128
        for i in range(8):
            hh = i // 2
            for cp in range(NCORES):
                tmp, bt = p.wk[cp % 2], p.b_wk[cp % 2]
                r0 = cp * per + i * 128
                S.dma("sp", tmp[:], gt[r0:r0 + 128, :], reads=[p.b_gath[ex]], writes=[bt], key=f"cmb{cp % 2}")
                if cp == 0:
                    S.op("dve", lambda e: e.tensor_scalar(out=p.R[:, i, :], in0=tmp[:], scalar1=coef[:, cp * H + hh: cp * H + hh + 1], scalar2=None,
                                                          op0=ALU.mult), reads=[bt, b_coef], writes=[p.b_R[i]])
                else:
                    S.op("dve", lambda e: e.scalar_tensor_tensor(out=p.R[:, i, :], in0=tmp[:], scalar=coef[:, cp * H + hh: cp * H + hh + 1],
                                                                 in1=p.R[:, i, :], op0=ALU.mult, op1=ALU.add),
                         reads=[bt, b_coef, p.b_R[i]], writes=[p.b_R[i]])

    def ffn_layer(self, l, src, src_bufs, dst, dst_bufs, ex_in, ex_out, final):
        p, S = self, self.S
        S.barrier()
        self.load_gate(l, 1)
        sh = p.modT[:, l, 24:32]
        gidx = 2 * l + 1
        actT = p.big_a[:, :].rearrange("p (k n) -> p k n", k=22)
        cw = p.ffn_cwT
        cb = p.ffn_cbT
        Wup = p.ffn_w_up[l]
        Wdn = p.ffn_w_down[l]
        for g in range(NG):
            if g == 0:
                self.load_halo(src, src_bufs, g, ex_in)
                self.norm_halo(gidx, sh)
            self.norm_group(src, src_bufs, g, gidx, sh)
            srcw = Wup.rearrange("(k p) n -> p k n", p=128)

            def load_up(pc):
                t, b, key = self.ring()
                wv = t[:, 0:4096].rearrange("p (k n) -> p k n", k=8)
                pairs = []
                for kh in range(2):
                    pairs.append((wv[:, 4 * kh:4 * kh + 4, 0:256], srcw[:, 4 * kh:4 * kh + 4, pc * 256:(pc + 1) * 256]))
                    pairs.append((wv[:, 4 * kh:4 * kh + 4, 256:512], srcw[:, 4 * kh:4 * kh + 4, DFF + pc * 256: DFF + (pc + 1) * 256]))
                self.dma_group("pool", key, pairs, writes=[b])
                return wv, b

            loaders = [(lambda pc=pc: load_up(pc)) for pc in range(11)]
            loaders += [(lambda cp=cp, kh=kh: self.load_rows_piece(Wdn, 11 * kh, 11, cp * 256, 256)) for cp in range(4) for kh in range(2)]
            loaded = {}

            def get_piece(i, ahead=2):
                for j in range(i, min(i + ahead + 1, len(loaders))):
                    if j not in loaded:
                        loaded[j] = loaders[j]()
                return loaded.pop(i)

            banks = [0, 1, 2, 3, 5]
            for pc in range(11):
                wv, b = get_piece(pc)
                accs = {}
                for ct in (0, 2, 1, 3):
                    tile_i = 2 * pc + (ct % 2)
                    chan = tile_i if ct < 2 else 22 + tile_i
                    bi = banks[self.rot_bank % len(banks)]
                    self.rot_bank += 1
                    ai = self.rot_acc % 5
                    self.rot_acc += 1
                    ui = self.rot_ub % 2
                    self.rot_ub += 1
                    self.proj_A(wv, b, ct, bi)
                    ub, b_ub = p.ubuf[ui], p.b_ubuf[ui]
                    if g == 0:
                        for kc in range(KC):
                            S.op("pe", lambda e: e.matmul(p.P[4][:, 0:HW], lhsT=wv[:, kc, ct * 128:(ct + 1) * 128], rhs=p.hTh[:, kc, 0:HW],
                                                          start=(kc == 0), stop=(kc == KC - 1)), reads=[b, p.b_hTh], writes=[p.b_P[4]])
                        S.op("dve", lambda e: e.tensor_scalar(out=ub[:, 0:HW], in0=p.P[4][:, 0:HW], scalar1=p.nf[:, 0:1], scalar2=None, op0=ALU.mult),
                             reads=[p.b_P[4], p.b_nf], writes=[b_ub])
                    else:
                        S.op("act", lambda e: e.activation(out=ub[:, 0:HW], in_=p.uhalo[:, chan, :], func=AF.Copy), reads=[p.b_uhalo], writes=[b_ub])
                    S.op("act", lambda e: e.activation(out=ub[:, HW:HW + GT], in_=p.P[bi][:, :], func=AF.Copy), reads=[p.b_P[bi]], writes=[b_ub])
                    if g < NG - 1:
                        S.op("act", lambda e: e.activation(out=p.uhalo[:, chan, :], in_=ub[:, GT:GT + HW], func=AF.Copy), reads=[b_ub], writes=[p.b_uhalo])
                    acc, b_acc = p.wk[ai], p.b_wk[ai]
                    accs[ct] = (acc, b_acc)
                    S.op("act", lambda e: e.activation(out=acc[:], in_=p.P[bi][:, :], func=AF.Identity, scale=cw[:, l, 88 + chan:89 + chan],
                                                       bias=cb[:, l, chan:chan + 1]), reads=[p.b_P[bi], p.b_ffn_cwT, p.b_ffn_cbT], writes=[b_acc])
                    S.op("dve", lambda e: e.scalar_tensor_tensor(out=acc[:], in0=ub[:, HW - 1:HW - 1 + GT], scalar=cw[:, l, 44 + chan:45 + chan], in1=acc[:],
                                                                 op0=ALU.mult, op1=ALU.add), reads=[b_ub, b_acc, p.b_ffn_cwT], writes=[b_acc])
                    S.op("dve", lambda e: e.scalar_tensor_tensor(out=acc[:], in0=ub[:, HW - 2:HW - 2 + GT], scalar=cw[:, l, chan:chan + 1], in1=acc[:],
                                                                 op0=ALU.mult, op1=ALU.add), reads=[b_ub, b_acc, p.b_ffn_cwT], writes=[b_acc])
                    if ct < 2:
                        S.op("act", lambda e: e.activation(out=acc[:], in_=acc[:], func=AF.Silu), reads=[b_acc], writes=[b_acc])
                    else:
                        j = ct - 2
                        (aa, b_aa), (ab, b_ab) = accs[j], accs[ct]
                        S.op("dve", lambda e: e.tensor_tensor(out=actT[:, 2 * pc + j, :], in0=aa[:], in1=ab[:], op=ALU.mult),
                             reads=[b_aa, b_ab], writes=[p.b_actT[2 * pc + j]])
            for cp in range(4):
                for kh in range(2):
                    wv, wb = get_piece(11 + cp * 2 + kh)
                    for c in range(G):
                        for k in range(11):
                            S.op("pe", lambda e: e.matmul(p.P[c][:, 0:256], lhsT=actT[:, 11 * kh + k, c * CH:(c + 1) * CH], rhs=wv[:, k, :],
                                                          start=(kh == 0 and k == 0), stop=(kh == 1 and k == 10)),
                                 reads=[wb, p.b_actT[11 * kh + k]], writes=[p.b_P[c]])
                for c in range(G):
                    ti = self.rot_acc % 5
                    self.rot_acc += 1
                    tmp = p.wk[ti]
                    S.op("dve", lambda e: e.tensor_tensor(out=tmp[:, 0:256], in0=p.P[c][:, 0:256], in1=p.gt_bc[:, cp * 256:(cp + 1) * 256], op=ALU.mult),
                         reads=[p.b_P[c], p.b_gt_bc], writes=[p.b_wk[ti]])
                    S.op("dve", lambda e: e.tensor_tensor(out=p.x_g[:, c, cp * 256:(cp + 1) * 256], in0=p.x_g[:, c, cp * 256:(cp + 1) * 256],
                                                           in1=tmp[:, 0:256], op=ALU.add),
                         reads=[p.b_wk[ti], p.b_xg[c]], writes=[p.b_xg[c]])
            if final:
                self.final_norm_group()
            self.store_group(g, dst, dst_bufs, ex_out)
        S.barrier()

    def final_norm_group(self):
        p, S = self, self.S
        for c in range(G):
            S.op("act", lambda e: e.activation(out=p.xn[:], in_=p.x_g[:, c, :], func=AF.Square, accum_out=p.ss[:, c:c + 1]),
                 reads=[p.b_xg[c]], writes=[p.b_xn, p.b_ss])
        S.op("dve", lambda e: e.tensor_scalar(out=p.ss[:, 0:G], in0=p.ss[:, 0:G], scalar1=1.0 / D, scalar2=EPS, op0=ALU.mult, op1=ALU.add),
             reads=[p.b_ss], writes=[p.b_ss])
        S.op("act", lambda e: e.activation(out=p.ss[:, 0:G], in_=p.ss[:, 0:G], func=AF.Sqrt), reads=[p.b_ss], writes=[p.b_ss])
        S.op("dve", lambda e: e.reciprocal(out=p.rstd[:, 0:G], in_=p.ss[:, 0:G]), reads=[p.b_ss], writes=[p.b_rstd])
        for c in range(G):
            S.op("act", lambda e: e.activation(out=p.x_g[:, c, :], in_=p.x_g[:, c, :], func=AF.Copy, scale=p.rstd[:, c:c + 1]),
                 reads=[p.b_xg[c], p.b_rstd], writes=[p.b_xg[c]])
            for half, (t, b) in enumerate(((p.cos, p.b_cos), (p.sin, p.b_sin))):
                S.op("dve", lambda e: e.tensor_tensor(out=p.x_g[:, c, half * 512:(half + 1) * 512], in0=p.x_g[:, c, half * 512:(half + 1) * 512],
                                                      in1=t[:], op=ALU.mult), reads=[p.b_xg[c], b], writes=[p.b_xg[c]])

    LN16 = math.log(16.0)

    def ml_group(self, g, full):
        p, S = self, self.S
        sh = p.modT[:, 1, 0:8]
        W = p.ml_w_in
        if g == 0:
            self.load_halo(p.xb, p.b_xb, g, 3)
            self.norm_halo(2, sh)
        self.norm_group(p.xb, p.b_xb, g, 2, sh)
        reuse = full and p.relay
        plist = ([0, 1] if full else []) + ([] if reuse else [2, 3])
        if reuse:
            self.kv_load(g)
        for pc in plist:
            wv, wb = self.load_std_piece(W, pc * 512)
            dst, b_dst = (p.qT, p.b_qT) if pc < 2 else (p.kT, p.b_kT)
            for ct in range(4):
                cti = pc * 4 + ct
                self.proj_A(wv, wb, ct, ct)
                ub, b_ub = p.ubuf[ct % 2], p.b_ubuf[ct % 2]
                if g == 0:
                    for kc in range(KC):
                        S.op("pe", lambda e: e.matmul(p.P[4][:, 0:HW], lhsT=wv[:, kc, ct * 128:(ct + 1) * 128], rhs=p.hTh[:, kc, 0:HW],
                                                      start=(kc == 0), stop=(kc == KC - 1)), reads=[wb, p.b_hTh], writes=[p.b_P[4]])
                    S.op("dve", lambda e: e.tensor_scalar(out=ub[:, 0:HW], in0=p.P[4][:, 0:HW], scalar1=p.nf[:, 0:1], scalar2=None, op0=ALU.mult),
                         reads=[p.b_P[4], p.b_nf], writes=[b_ub])
                else:
                    S.op("dve", lambda e: e.tensor_copy(out=ub[:, 0:HW], in_=p.uhalo[:, cti, :]), reads=[p.b_uhalo], writes=[b_ub])
                S.op("act", lambda e: e.activation(out=ub[:, HW:HW + GT], in_=p.P[ct][:, :], func=AF.Copy), reads=[p.b_P[ct]], writes=[b_ub])
                if g < NG - 1:
                    S.op("dve", lambda e: e.tensor_copy(out=p.uhalo[:, cti, :], in_=ub[:, GT:GT + HW]), reads=[b_ub], writes=[p.b_uhalo])
                acc, b_acc = p.wk[ct], p.b_wk[ct]
                S.op("act", lambda e: e.activation(out=acc[:], in_=p.P[ct][:, :], func=AF.Identity, scale=p.ml_cwT[:, 48 + cti:49 + cti],
                                                   bias=p.ml_cbT[:, cti:cti + 1]), reads=[p.b_P[ct], p.b_ml_cwT, p.b_ml_cbT], writes=[b_acc])
                for j in (2, 1, 0):
                    off = HW - (3 - j)
                    S.op("dve", lambda e: e.scalar_tensor_tensor(out=acc[:], in0=ub[:, off:off + GT], scalar=p.ml_cwT[:, j * 16 + cti: j * 16 + cti + 1],
                                                                 in1=acc[:], op0=ALU.mult, op1=ALU.add), reads=[b_ub, b_acc, p.b_ml_cwT], writes=[b_acc])
                S.op("act", lambda e: e.activation(out=dst[:, cti % 8, :], in_=acc[:], func=AF.Silu), reads=[b_acc], writes=[b_dst])
        for hh in ([] if reuse else range(H)):
            wv, wb = self.load_std_piece(W, 2048 + hh * 512)
            for c in range(G):
                self.proj_B(wv, wb, c, c)
                dstv = p.v_all[:, c, hh * 512:(hh + 1) * 512]
                if c % 2 == 0:
                    S.op("act", lambda e: e.activation(out=dstv, in_=p.P[c][:, :], func=AF.Copy), reads=[p.b_P[c]], writes=[p.b_v[c]])
                else:
                    S.op("dve", lambda e: e.tensor_copy(out=dstv, in_=p.P[c][:, :]), reads=[p.b_P[c]], writes=[p.b_v[c]])
        if (not full) and p.relay:
            self.kv_store(g)
        wv, wb = self.load_std_piece(W, 6144, w=8)
        for c in range(G):
            self.proj_B(wv, wb, c, c, w=8)
            S.op("dve", lambda e: e.tensor_tensor(out=p.gat[:, c, :], in0=p.P[c][:, 0:8], in1=p.bg_bc[:], op=ALU.add),
                 reads=[p.b_P[c], p.b_bg_bc], writes=[p.b_gat])
        if full:
            for hh in range(H):
                wv, wb = self.load_std_piece(W, 4096 + hh * 512)
                for c in range(G):
                    self.proj_B(wv, wb, c, c)
                    dsts = p.sg_all[:, c, hh * 512:(hh + 1) * 512]
                    S.op("act", lambda e: e.activation(out=dsts, in_=p.P[c][:, :], func=AF.Sigmoid), reads=[p.b_P[c]], writes=[p.b_xg[c]])
        self.ml_gates_group(full)
        for c in range(G):
            self.ml_chunk(c, full)
        if full:
            self.out_proj(g, p.ml_w_out, p.xb, p.b_xb, p.xa, p.b_xa, 5)

    def ml_gates_group(self, full):
        p, S = self, self.S
        gm, bg = p.gm, p.b_gm
        v3 = lambda k: gm[:, k, :].rearrange("p (c h) -> p c h", h=4)
        z = p.gat[:, :, 4:8]
        li = p.gat[:, :, 0:4]
        S.op("act", lambda e: e.activation(out=v3(0), in_=z, func=AF.Exp, scale=-1.0), reads=[p.b_gat], writes=[bg])
        S.op("dve", lambda e: e.tensor_scalar(out=gm[:, 0, :], in0=gm[:, 0, :], scalar1=1.0, scalar2=None, op0=ALU.add), reads=[bg], writes=[bg])
        S.op("act", lambda e: e.activation(out=gm[:, 0, :], in_=gm[:, 0, :], func=AF.Ln), reads=[bg], writes=[bg])
        S.op("dve", lambda e: e.tensor_scalar(out=gm[:, 0, :], in0=gm[:, 0, :], scalar1=-1.0, scalar2=None, op0=ALU.mult), reads=[bg], writes=[bg])
        n = 4 * G
        S.op("pe", lambda e: e.matmul(p.P[5][:, 0:n], lhsT=p.ut[:, :], rhs=gm[:, 0, :], start=True, stop=True), reads=[p.b_ut, bg], writes=[p.b_P[5]])
        S.op("pe", lambda e: e.matmul(p.P[5][:, n:2 * n], lhsT=p.ones_f[:, :], rhs=gm[:, 0, :], start=True, stop=True), reads=[p.b_ones_f, bg], writes=[p.b_P[5]])
        S.op("dve", lambda e: e.tensor_copy(out=gm[:, 1, :], in_=p.P[5][:, 0:n]), reads=[p.b_P[5]], writes=[bg])
        S.op("dve", lambda e: e.tensor_copy(out=gm[:, 2, :], in_=p.P[5][:, n:2 * n]), reads=[p.b_P[5]], writes=[bg])
        S.op("dve", lambda e: e.tensor_tensor(out=gm[:, 3, :], in0=gm[:, 2, :], in1=gm[:, 1, :], op=ALU.subtract), reads=[bg], writes=[bg])
        S.op("dve", lambda e: e.scalar_tensor_tensor(out=v3(3), in0=v3(3), scalar=-self.LN16, in1=li, op0=ALU.add, op1=ALU.add),
             reads=[bg, p.b_gat], writes=[bg])
        S.op("act", lambda e: e.activation(out=gm[:, 3, :], in_=gm[:, 3, :], func=AF.Exp), reads=[bg], writes=[bg])
        S.op("act", lambda e: e.activation(out=gm[:, 4, :], in_=gm[:, 2, :], func=AF.Exp), reads=[bg], writes=[bg])
        gx = p.gmx[:].rearrange("p c (h t) -> p c h t", t=2)
        for t in range(2):
            S.op("dve", lambda e: e.tensor_copy(out=gx[:, :, :, t], in_=v3(4)), reads=[bg], writes=[p.b_gmx])
        if not full:
            for c in range(G):
                S.op("dve", lambda e: e.tensor_tensor(out=p.fsum[:], in0=p.fsum[:], in1=gm[:, 2, 4 * c:4 * c + 4], op=ALU.add),
                     reads=[bg, p.b_fsum], writes=[p.b_fsum])
        else:
            S.op("dve", lambda e: e.tensor_tensor(out=v3(5), in0=li, in1=v3(1), op=ALU.subtract), reads=[bg, p.b_gat], writes=[bg])
            S.op("dve", lambda e: e.tensor_scalar(out=gm[:, 5, :], in0=gm[:, 5, :], scalar1=-self.LN16, scalar2=None, op0=ALU.add), reads=[bg], writes=[bg])
            S.op("act", lambda e: e.activation(out=gm[:, 6, :], in_=gm[:, 1, :], func=AF.Exp), reads=[bg], writes=[bg])

    def ml_chunk(self, c, full):
        p, S = self, self.S
        cs = slice(c * CH, (c + 1) * CH)
        sm, bs = p.sm, p.b_sm
        gm, bg = p.gm, p.b_gm
        col = 4 * c
        g_b = lambda hh: gm[:, 1, col + hh:col + hh + 1]
        g_ws = lambda hh: gm[:, 3, col + hh:col + hh + 1]
        g_sp = lambda hh: gm[:, 4, col + hh:col + hh + 1]
        g_bj = lambda hh: gm[:, 5, col + hh:col + hh + 1]
        g_wi = lambda hh: gm[:, 6, col + hh:col + hh + 1]
        self.make_kz(c, lambda hh: (g_ws(hh), [bg]))
        if full:
            for hh in range(H):
                S.op("dve", lambda e: e.tensor_scalar(out=p.xn[:, hh * 128:(hh + 1) * 128], in0=p.ident_f[:, :], scalar1=g_b(hh), scalar2=None, op0=ALU.mult),
                     reads=[bg, p.b_ident_f], writes=[p.b_xn])
                S.op("pe", lambda e: e.matmul(p.P[5][:, hh * 128:(hh + 1) * 128], lhsT=p.ones_f[:, :], rhs=p.xn[:, hh * 128:(hh + 1) * 128], start=True, stop=False),
                     reads=[p.b_ones_f, p.b_xn], writes=[p.b_P[5]])
                S.op("pe", lambda e: e.matmul(p.P[5][:, hh * 128:(hh + 1) * 128], lhsT=p.ident_f[:, :], rhs=p.neg[:, :], start=False, stop=True),
                     reads=[p.b_ident_f, p.b_neg], writes=[p.b_P[5]])
            for hh in range(H):
                S.op("act", lambda e: e.activation(out=p.xn2[:, hh * 128:(hh + 1) * 128], in_=p.P[5][:, hh * 128:(hh + 1) * 128], func=AF.Exp,
                                                   bias=g_bj(hh)), reads=[p.b_P[5], bg], writes=[p.b_xn2])
            for hh in range(H):
                for half in range(2):
                    S.op("pe", lambda e: e.matmul(p.P[4][:, hh * 128:(hh + 1) * 128], lhsT=p.kT[:, 2 * hh + half, cs], rhs=p.qT[:, 2 * hh + half, cs],
                                                  start=(half == 0), stop=(half == 1)), reads=[p.b_kT, p.b_qT], writes=[p.b_P[4]])
            S.op("dve", lambda e: e.tensor_tensor(out=p.sTm[:], in0=p.P[4][:, :], in1=p.xn2[:, 0:512], op=ALU.mult),
                 reads=[p.b_P[4], p.b_xn2], writes=[p.b_sTm])
        if full:
            for hh in range(H):
                S.op("pe", lambda e: e.matmul(p.P[5][:, 2 * hh:2 * hh + 1], lhsT=p.sTm[:, hh * 128:(hh + 1) * 128], rhs=p.ones_b[:, 0:1], start=True, stop=True),
                     reads=[p.b_sTm, p.b_ones_b], writes=[p.b_P[5]])
                for half in range(2):
                    i = 2 * hh + half
                    S.op("pe", lambda e: e.matmul(p.P[5][:, 2 * hh + 1:2 * hh + 2], lhsT=p.qT[:, i, cs], rhs=p.nstb[:, i:i + 1], start=(half == 0), stop=(half == 1)),
                         reads=[p.b_qT, p.b_nstb], writes=[p.b_P[5]])
            S.op("dve", lambda e: e.tensor_copy(out=sm[:, 64:72], in_=p.P[5][:, 0:8]), reads=[p.b_P[5]], writes=[bs])
            dv = sm[:, 64:72].rearrange("p (h t) -> p h t", t=2)
            S.op("dve", lambda e: e.tensor_tensor(out=sm[:, 72:76], in0=dv[:, :, 1], in1=gm[:, 6, col:col + 4], op=ALU.mult), reads=[bs, bg], writes=[bs])
            S.op("dve", lambda e: e.tensor_tensor(out=sm[:, 72:76], in0=sm[:, 72:76], in1=dv[:, :, 0], op=ALU.add), reads=[bs], writes=[bs])
            S.op("dve", lambda e: e.scalar_tensor_tensor(out=sm[:, 76:80], in0=sm[:, 72:76], scalar=-1.0, in1=sm[:, 72:76], op0=ALU.mult, op1=ALU.max),
                 reads=[bs], writes=[bs])
            S.op("dve", lambda e: e.tensor_scalar(out=sm[:, 76:80], in0=sm[:, 76:80], scalar1=1.0, scalar2=None, op0=ALU.max), reads=[bs], writes=[bs])
            S.op("dve", lambda e: e.reciprocal(out=sm[:, 80:84], in_=sm[:, 76:80]), reads=[bs], writes=[bs])
        for hh in range(H):
            if full:
                S.op("pe", lambda e: e.matmul(p.P[0][:, :], lhsT=p.sTm[:, hh * 128:(hh + 1) * 128], rhs=p.v_all[:, c, hh * 512:(hh + 1) * 512],
                                              start=True, stop=True), reads=[p.b_sTm, p.b_v[c]], writes=[p.b_P[0]])
                for half in range(2):
                    i = 2 * hh + half
                    S.op("pe", lambda e: e.matmul(p.P[1][:, :], lhsT=p.qT[:, i, cs], rhs=p.Rb[:, i, :], start=(half == 0), stop=(half == 1)),
                         reads=[p.b_qT, p.b_Rb[i]], writes=[p.b_P[1]])
                S.op("act", lambda e: e.activation(out=p.wk[4][:], in_=p.P[1][:, :], func=AF.Copy, scale=g_wi(hh)),
                     reads=[p.b_P[1], bg], writes=[p.b_wk[4]])
                S.op("dve", lambda e: e.tensor_tensor(out=p.wk[hh][:], in0=p.wk[4][:], in1=p.P[0][:, :], op=ALU.add),
                     reads=[p.b_wk[4], p.b_P[0]], writes=[p.b_wk[hh]])
                so = p.sg_all[:, c, hh * 512:(hh + 1) * 512]
                S.op("dve", lambda e: e.scalar_tensor_tensor(out=p.wk[hh][:], in0=p.wk[hh][:], scalar=sm[:, 80 + hh:81 + hh], in1=so, op0=ALU.mult, op1=ALU.mult),
                     reads=[p.b_wk[hh], bs, p.b_xg[c]], writes=[p.b_wk[hh]])
            self.state_update(c, hh, g_sp(hh), [bg])
            if full:
                self.refresh_Rb(hh)
        for i in range(8):
            hh, half = i // 2, i % 2
            S.op("pe", lambda e: e.matmul(p.P[5][:, 16 + i:17 + i], lhsT=p.kz[:, hh * 256 + half * 128: hh * 256 + (half + 1) * 128],
                                          rhs=p.ones_b[:, 0:1], start=True, stop=True), reads=[p.b_kz, p.b_ones_b], writes=[p.b_P[5]])
        S.op("dve", lambda e: e.tensor_tensor(out=p.nst[:], in0=p.nst[:], in1=p.gmx[:, c, :], op=ALU.mult), reads=[p.b_nst, p.b_gmx], writes=[p.b_nst])
        S.op("dve", lambda e: e.tensor_tensor(out=p.nst[:], in0=p.nst[:], in1=p.P[5][:, 16:24], op=ALU.add), reads=[p.b_nst, p.b_P[5]], writes=[p.b_nst])
        if full:
            S.op("act", lambda e: e.activation(out=p.nstb[:], in_=p.nst[:], func=AF.Copy), reads=[p.b_nst], writes=[p.b_nstb])
        if full:
            self.groupnorm_heads([0, 1, 2, 3], c, gate=False)
            self.make_ynT(c, p.ml_gnT, p.b_ml_gnT)

    def ml_layer(self):
        p, S = self, self.S
        if p.phase in (None, 4):
            self.ml_part_a()
        if p.phase is None:
            self.exchange(4)
        if p.phase in (None, 5):
            self.ml_part_b()

    def ml_part_a(self):
        p, S = self, self.S
        for i in range(8):
            S.op("dve", lambda e: e.memset(p.R[:, i, :], 0.0), writes=[p.b_R[i]])
        S.op("dve", lambda e: e.memset(p.nst[:], 0.0), writes=[p.b_nst])
        S.op("dve", lambda e: e.memset(p.fsum[:], 0.0), writes=[p.b_fsum])
        for g in range(NG):
            self.ml_group(g, full=False)
        S.op("dve", lambda e: e.memset(p.wk[4][:], 0.0), writes=[p.b_wk[4]])
        S.op("dve", lambda e: e.tensor_copy(out=p.wk[4][:, 0:8], in_=p.nst[:]), reads=[p.b_nst], writes=[p.b_wk[4]])
        S.op("dve", lambda e: e.tensor_copy(out=p.wk[4][:, 8:12], in_=p.fsum[:]), reads=[p.b_fsum], writes=[p.b_wk[4]])
        pairs = [(p.loc[4][i * 128:(i + 1) * 128, :], p.R[:, i, :]) for i in range(8)] + [(p.loc[4][1024:1152, :], p.wk[4][:])]
        self.dma_group("sp", "loc4", pairs, reads=p.b_R + [p.b_wk[4]], writes=[p.b_loc[4]])

    def ml_part_b(self):
        p, S = self, self.S
        gt = p.gath[4]
        pairs = [(p.stage[cp:cp + 1, 0:4], gt[cp * 1152 + 1024: cp * 1152 + 1025, 8:12]) for cp in range(NCORES)]
        self.dma_group("sp", "stage", pairs, reads=[p.b_gath[4]], writes=[p.b_stage])
        for cp in range(NCORES):
            S.op("dve", lambda e: e.tensor_tensor(out=p.stage[0:8, 32 + cp * 4:36 + cp * 4], in0=p.msel[0:8, cp * 4:cp * 4 + 4], in1=p.stage[0:8, 0:4], op=ALU.mult),
                 reads=[p.b_stage, p.b_msel], writes=[p.b_stage])
        S.op("pe", lambda e: e.matmul(p.P[5][:, 0:32], lhsT=p.ones_f[0:8, :], rhs=p.stage[0:8, 32:64], start=True, stop=True),
             reads=[p.b_ones_f, p.b_stage], writes=[p.b_P[5]])
        S.op("act", lambda e: e.activation(out=p.mcoef[:], in_=p.P[5][:, 0:32], func=AF.Exp), reads=[p.b_P[5]], writes=[p.b_mcoef])
        S.op("dve", lambda e: e.tensor_tensor(out=p.mcoef[:], in0=p.mcoef[:], in1=p.valid[:], op=ALU.mult), reads=[p.b_mcoef, p.b_valid], writes=[p.b_mcoef])
        self.combine_state(4, p.mcoef, p.b_mcoef, nrows=9)
        S.op("dve", lambda e: e.memset(p.nst[:], 0.0), writes=[p.b_nst])
        for cp in range(NCORES):
            tmp, bt = p.wk[cp % 2], p.b_wk[cp % 2]
            r0 = cp * 1152 + 1024
            S.dma("sp", tmp[:, 0:8], gt[r0:r0 + 128, 0:8], reads=[p.b_gath[4]], writes=[bt], key=f"cmb{cp % 2}")
            for hh in range(H):
                S.op("dve", lambda e: e.scalar_tensor_tensor(out=p.nst[:, 2 * hh:2 * hh + 2], in0=tmp[:, 2 * hh:2 * hh + 2],
                                                             scalar=p.mcoef[:, cp * H + hh:cp * H + hh + 1], in1=p.nst[:, 2 * hh:2 * hh + 2],
                                                             op0=ALU.mult, op1=ALU.add), reads=[bt, p.b_mcoef, p.b_nst], writes=[p.b_nst])
        for hh in range(H):
            self.refresh_Rb(hh)
        S.op("act", lambda e: e.activation(out=p.nstb[:], in_=p.nst[:], func=AF.Copy), reads=[p.b_nst], writes=[p.b_nstb])
        self.load_gate(1, 0)
        for g in range(NG):
            self.ml_group(g, full=True)

    def _body(self):
        p, S = self, self.S
        ph = p.phase
        if ph in (None, 1, 2):
            self.ret_layer()
        if p.stop_after == "dbg_ret":
            return
        if p.stop_after == "ret":
            return self.copy_out(p.xa, p.b_xa)
        if ph is None:
            self.exchange(2)
        if ph in (None, 3):
            self.ffn_layer(0, p.xa, p.b_xa, p.xb, p.b_xb, 2, 3, final=False)
        if p.stop_after == "ffn0":
            return self.copy_out(p.xb, p.b_xb)
        if ph is None:
            self.exchange(3)
        if ph in (None, 4, 5):
            self.ml_layer()
        if p.stop_after == "ml":
            return self.copy_out(p.xa, p.b_xa)
        if ph is None:
            self.exchange(5)
        if ph in (None, 6):
            S.dma("sp", p.cos[:], p.final_g[0:1, 0:512].partition_broadcast(128).rearrange("p o n -> p (o n)"), writes=[p.b_cos], key="fg0")
            S.dma("sp", p.sin[:], p.final_g[0:1, 512:1024].partition_broadcast(128).rearrange("p o n -> p (o n)"), writes=[p.b_sin], key="fg1")
            self.ffn_layer(1, p.xa, p.b_xa, p.out, p.b_out, 5, None, final=True)

    def copy_out(self, src, src_bufs):
        p, S = self, self.S
        for g in range(NG):
            for c in range(G):
                r0 = g * GT + c * CH
                S.dma("sp", p.x_g[:, c, :], src[r0:r0 + CH, :], reads=src_bufs, writes=[p.b_xg[c]], key=f"xg{c}")
            pairs = [(p.out[g * GT + c * CH: g * GT + (c + 1) * CH, :], p.x_g[:, c, :]) for c in range(G)]
            self.dma_group("sp", "st_ou", pairs, reads=p.b_xg, writes=[p.b_out[g]])

    def _finish(self):
        p, S = self, self.S
        bufs = list(p.b_out) + [p.b_loc[k] for k in p.b_loc] + p.dbg_bufs + list(p.b_xa) + list(p.b_xb) + [p.b_modscr, p.b_modT_o, p.b_kvst]
        S.wait_bufs("sp", bufs)
        S.barrier()


_PROG_CACHE = {}
MODE = "host6"
STOP_AFTER = None


def _get_prog(mode, stop_after, phase=None):
    key = (mode, stop_after, phase)
    if key not in _PROG_CACHE:
        pr = Prog("host" if mode.startswith("host") else mode, stop_after, phase)
        pr.build()
        _PROG_CACHE[key] = pr
    return _PROG_CACHE[key]


def _in_maps(inputs):
    f = lambda a: np.ascontiguousarray(np.asarray(a), dtype=np.float32)
    tabs, lg = _const_tables()
    x = f(inputs["x"]).reshape(SEQ, D)
    pos = np.ascontiguousarray(np.asarray(inputs["positions"]).astype(np.int32)).reshape(SEQ)
    shared = {
        "cT": np.ascontiguousarray(f(inputs["c"]).reshape(KC, 128).T),
        "ada_w": f(inputs["ada_w"]),
        "ada_bT": np.ascontiguousarray(f(inputs["ada_b"]).reshape(2, 48, 128).transpose(2, 0, 1)),
        "ntgT": np.ascontiguousarray(f(inputs["norm_tok_g"]).reshape(2, KC, 128).transpose(2, 0, 1)),
        "nfgT": np.ascontiguousarray(f(inputs["norm_ffn_g"]).reshape(2, KC, 128).transpose(2, 0, 1)),
        "ret_w_in": f(inputs["ret_w_in"]).reshape(D, 6144),
        "ret_gnT": np.ascontiguousarray(f(inputs["ret_gn_g"]).reshape(16, 128).T),
        "ret_w_out": f(inputs["ret_w_out"]).reshape(2048, D),
        "ml_w_in": f(inputs["ml_w_in"]).reshape(D, 6152),
        "ml_b_gate": f(inputs["ml_b_gate"]).reshape(1, 8),
        "ml_cwT": np.ascontiguousarray(f(inputs["ml_conv_w"]).reshape(64, 128).T),
        "ml_cbT": np.ascontiguousarray(f(inputs["ml_conv_b"]).reshape(16, 128).T),
        "ml_gnT": np.ascontiguousarray(f(inputs["ml_gn_g"]).reshape(16, 128).T),
        "ml_w_out": f(inputs["ml_w_out"]).reshape(2048, D),
        "ffn_w_up": f(inputs["ffn_w_up"]),
        "ffn_cwT": np.ascontiguousarray(f(inputs["ffn_conv_w"]).reshape(2, 132, 128).transpose(2, 0, 1)),
        "ffn_cbT": np.ascontiguousarray(f(inputs["ffn_conv_b"]).reshape(2, 44, 128).transpose(2, 0, 1)),
        "ffn_w_down": f(inputs["ffn_w_down"]),
        "final_g": f(inputs["final_g"]).reshape(1, D),
    }
    shared.update(tabs)
    maps = []
    for c in range(NCORES):
        m = dict(shared)
        m["x"] = x[c * T:(c + 1) * T]
        m["pos"] = pos[c * T:(c + 1) * T].reshape(1, T)
        m.update(_core_tables(c, lg))
        maps.append(m)
    return maps


def _launch(pr, maps, extra):
    ms = []
    for c, m in enumerate(maps):
        mm = {k: v for k, v in m.items() if k in pr.in_names}
        for k, v in extra.items():
            if k in pr.in_names:
                mm[k] = v[c] if isinstance(v, list) else v
        ms.append(mm)
    return run_bass_kernel_spmd(pr.nc, ms, core_ids=list(range(NCORES))).results


def kernel(**inputs):
    maps = _in_maps(inputs)
    if MODE == "host6":
        extra = {}
        res = None
        for ph in range(1, 7):
            pr = _get_prog("host", None, ph)
            res = _launch(pr, maps, extra)
            for k in (1, 2, 3, 4, 5):
                if f"loc{k}" in res[0]:
                    extra[f"gath{k}"] = np.concatenate([res[c][f"loc{k}"] for c in range(NCORES)], axis=0)
            for nm in ("xa", "xb"):
                if nm in res[0]:
                    extra[nm] = [res[c][nm] for c in range(NCORES)]
            if "kst_o" in res[0]:
                extra["kst_i"] = [res[c]["kst_o"] for c in range(NCORES)]
                extra["vst_i"] = [res[c]["vst_o"] for c in range(NCORES)]
            if "modT_o" in res[0]:
                extra["modT_i"] = [res[c]["modT_o"] for c in range(NCORES)]
                extra["modscr_i"] = [res[c]["modscr_o"] for c in range(NCORES)]
    elif MODE == "host":
        pr = _get_prog("host", STOP_AFTER)
        extra = {f"gath{k}": np.zeros((NCORES * r, cdim), np.float32) for k, (r, cdim) in EX_SIZES.items()}
        order = {None: [1, 2, 3, 4, 5], "ret": [1], "ffn0": [1, 2], "ml": [1, 2, 3, 4]}[STOP_AFTER]
        res = None
        for step in range(len(order) + 1):
            res = _launch(pr, maps, extra)
            if step < len(order):
                k = order[step]
                extra[f"gath{k}"] = np.concatenate([res[c][f"loc{k}"] for c in range(NCORES)], axis=0)
    else:
        pr = _get_prog("cc", None)
        res = _launch(pr, maps, {})
    out = np.concatenate([res[c]["out"] for c in range(NCORES)], axis=0)
    return out.reshape(1, SEQ, D).astype(np.float32)
```

```python
import contextlib
import math
import numpy as np
import concourse.bass as bass
import concourse.mybir as mybir
from concourse.bass_utils import run_bass_kernel_spmd

F32 = mybir.dt.float32
BF16 = mybir.dt.bfloat16
I32 = mybir.dt.int32
AF = mybir.ActivationFunctionType
ALU = mybir.AluOpType

NCORES = 8
SEQ = 16384
D = 1024
T = SEQ // NCORES
CH = 128
G = 4
GT = G * CH
NG = T // GT
KC = D // 128
H = 4
DK = 256
DV = 512
DFF = 2816
EPS = 1e-6
HW = 3
TWO_PI = 2.0 * math.pi
C1 = 6.28125
C2 = TWO_PI - C1
PI_SAFE = 3.1415925


class Buf:
    __slots__ = ("name", "w", "r")

    def __init__(self, name):
        self.name = name
        self.w = None
        self.r = {}


class Sched:
    ENGS = ("pe", "dve", "act", "pool", "sp")

    def __init__(self, nc, stack):
        self.nc = nc
        self.stack = stack
        self.eng = {"pe": nc.tensor, "dve": nc.vector, "act": nc.scalar, "pool": nc.gpsimd, "sp": nc.sync}
        self.sems = {}
        self.cnt = {}
        self.seen = {e: {} for e in self.ENGS}
        for e in self.ENGS:
            self.sems[e] = stack.enter_context(nc.semaphore("s_" + e))
            self.cnt[e] = 0
        self.ninst = 0

    def buf(self, name):
        return Buf(name)

    def _dma_sem(self, key):
        k = "dma_" + key
        if k not in self.sems:
            self.sems[k] = self.stack.enter_context(self.nc.semaphore("s_" + k))
            self.cnt[k] = 0
        return k

    def _wait(self, e, deps):
        need = {}
        for d in deps:
            if d is None:
                continue
            k, v = d
            if k == e and e == "pe":
                continue
            if need.get(k, 0) < v:
                need[k] = v
        for k, v in need.items():
            if self.seen[e].get(k, 0) < v:
                self.eng[e].wait_ge(self.sems[k], v)
                self.seen[e][k] = v

    @staticmethod
    def _deps(reads, writes):
        deps = []
        for b in reads:
            deps.append(b.w)
        for b in writes:
            deps.append(b.w)
            deps.extend(b.r.items())
        return deps

    @staticmethod
    def _record(ev, reads, writes):
        k, v = ev
        for b in reads:
            if b.r.get(k, 0) < v:
                b.r[k] = v
        for b in writes:
            b.w = ev
            b.r = {}

    def op(self, e, fn, reads=(), writes=()):
        self._wait(e, self._deps(reads, writes))
        ins = fn(self.eng[e])
        self.cnt[e] += 1
        ins.then_inc(self.sems[e], 1)
        self.ninst += 1
        self._record((e, self.cnt[e]), reads, writes)
        return ins

    def dma(self, q, out, in_, reads=(), writes=(), key=None, **kw):
        k = self._dma_sem(key)
        self._wait(q, self._deps(reads, writes))
        ins = self.eng[q].dma_start(out=out, in_=in_, **kw)
        self.cnt[k] += 16
        ins.then_inc(self.sems[k], 16)
        self.ninst += 1
        self._record((k, self.cnt[k]), reads, writes)
        return ins

    def wait_bufs(self, e, bufs):
        deps = []
        for b in bufs:
            deps.append(b.w)
            deps.extend(b.r.items())
        self._wait(e, deps)

    def barrier(self):
        for e in self.ENGS:
            deps = [(k, v) for k, v in self.cnt.items() if v > 0]
            self._wait(e, deps)


def _const_tables():
    t = {}
    n = np.arange(128, dtype=np.float32)
    inv_freq = (10000.0 ** (-(np.arange(0, DK, 2, dtype=np.float32)) / DK)).astype(np.float32)
    t["inv_freq"] = inv_freq.reshape(128, 1).astype(np.float32)
    lg = np.log(1.0 - 2.0 ** (-5.0 - np.arange(H, dtype=np.float64)))
    i = np.arange(128)[None, :]
    j = np.arange(128)[:, None]
    dt = np.zeros((128, H, 128), np.float64)
    for h in range(H):
        dt[:, h, :] = np.where(i >= j, np.exp((i - j) * lg[h]), 0.0) * (DK ** -0.5)
    t["ret_dt"] = dt.reshape(128, H * 128).astype(np.float32)
    t["ret_xi"] = np.exp((np.arange(128)[:, None] + 1.0) * lg[None, :]).astype(np.float32)
    t["ret_zs"] = (np.exp((127.0 - np.arange(128)[:, None]) * lg[None, :]) * (DK ** -0.5)).astype(np.float32)
    t["neg"] = np.where(j <= i, 0.0, -30000.0).astype(np.float32)
    t["ut"] = np.where(j <= i, 1.0, 0.0).astype(np.float32)
    t["ident"] = np.eye(128, dtype=np.float32)
    return t, lg


def _core_tables(c, lg):
    sel = np.zeros((128, NCORES), np.float32)
    if c > 0:
        sel[:, c - 1] = 1.0
    nf = np.full((128, 1), 0.0 if c == 0 else 1.0, np.float32)
    rc = np.zeros((128, NCORES * H), np.float32)
    for cp in range(c):
        for h in range(H):
            rc[:, cp * H + h] = np.exp(T * (c - 1 - cp) * lg[h])
    valid = np.zeros((128, NCORES * H), np.float32)
    for cp in range(c):
        valid[:, cp * H:(cp + 1) * H] = 1.0
    msel = np.zeros((NCORES, NCORES, H), np.float32)
    for cpp in range(NCORES):
        for cp in range(NCORES):
            if cp < cpp < c:
                msel[cpp, cp, :] = 1.0
    selmat = np.zeros((NCORES * HW, HW), np.float32)
    if c > 0:
        for r in range(HW):
            selmat[(c - 1) * HW + r, r] = 1.0
    return {"selmat": selmat, "sel": sel, "nf": nf, "ret_coef": rc, "valid": valid, "msel": msel.reshape(NCORES, NCORES * H)}


EX_SIZES = {1: (8 * 128, 512), 2: (HW, D), 3: (HW, D), 4: (9 * 128, 512), 5: (HW, D)}


class Prog:
    def __init__(self, mode="host", stop_after=None, phase=None):
        self.mode = mode
        self.stop_after = stop_after
        self.phase = phase
        self.in_names = []
        self.layers = [0, 1] if phase in (None, 1) else ([0] if phase <= 3 else [1])
        self.mod_layers = [0, 1] if phase in (None, 1) else []
        self.nc = bass.Bass("TRN2", target_bir_lowering=False)
        self.st = contextlib.ExitStack()
        self.debug = stop_after is not None and stop_after.startswith("dbg")
        self.dbg_bufs = []

    def din(self, name, shape, dt=F32):
        self.in_names.append(name)
        return self.nc.dram_tensor(name, list(shape), dt, kind="ExternalInput").ap()

    def dout(self, name, shape, dt=F32):
        return self.nc.dram_tensor(name, list(shape), dt, kind="ExternalOutput").ap()

    def dint(self, name, shape, dt=F32):
        return self.nc.dram_tensor(name, list(shape), dt, kind="Internal").ap()

    def sb(self, name, shape, dt=F32):
        t = self.st.enter_context(self.nc.sbuf_tensor("sb_" + name, list(shape), dt))
        b = Buf(name)
        return t, b

    def ps(self, name, shape, dt=F32):
        t = self.st.enter_context(self.nc.psum_tensor("ps_" + name, list(shape), dt))
        return t

    def build(self):
        with self.st:
            self.S = Sched(self.nc, self.st)
            self._declare()
            self._setup()
            self._body()
            self._finish()
        return self.nc

    def _declare(self):
        p = self
        p.x = p.din("x", [T, D])
        p.pos = p.din("pos", [1, T], I32)
        p.c_in = p.din("cT", [128, KC])
        p.ada_w = p.din("ada_w", [2, D, 6 * D])
        p.ada_b = p.din("ada_bT", [128, 2, 48])
        p.ntg = p.din("ntgT", [128, 2, KC])
        p.nfg = p.din("nfgT", [128, 2, KC])
        p.ret_w_in = p.din("ret_w_in", [D, 6144])
        p.ret_gn = p.din("ret_gnT", [128, 16])
        p.ret_w_out = p.din("ret_w_out", [2048, D])
        p.ml_w_in = p.din("ml_w_in", [D, 6152])
        p.ml_bg = p.din("ml_b_gate", [1, 8])
        p.ml_cw = p.din("ml_cwT", [128, 64])
        p.ml_cb = p.din("ml_cbT", [128, 16])
        p.ml_gn = p.din("ml_gnT", [128, 16])
        p.ml_w_out = p.din("ml_w_out", [2048, D])
        p.ffn_w_up = p.din("ffn_w_up", [2, D, 2 * DFF])
        p.ffn_cw = p.din("ffn_cwT", [128, 2, 132])
        p.ffn_cb = p.din("ffn_cbT", [128, 2, 44])
        p.ffn_w_down = p.din("ffn_w_down", [2, DFF, D])
        p.final_g = p.din("final_g", [1, D])
        p.t_inv_freq = p.din("inv_freq", [128, 1])
        p.t_ret_dt = p.din("ret_dt", [128, 512])
        p.t_ret_xi = p.din("ret_xi", [128, 4])
        p.t_ret_zs = p.din("ret_zs", [128, 4])
        p.t_neg = p.din("neg", [128, 128])
        p.t_ut = p.din("ut", [128, 128])
        p.t_ident = p.din("ident", [128, 128])
        p.t_sel = p.din("sel", [128, NCORES])
        p.t_selmat = p.din("selmat", [NCORES * HW, HW])
        p.t_nf = p.din("nf", [128, 1])
        p.t_ret_coef = p.din("ret_coef", [128, NCORES * H])
        p.t_valid = p.din("valid", [128, NCORES * H])
        p.t_msel = p.din("msel", [NCORES, NCORES * H])
        ph = p.phase
        p.out = p.dout("out", [T, D]) if ph in (None, 6) or p.stop_after else None
        if ph is None:
            p.modscr = p.dint("modscr", [2, 48, 128])
        elif ph == 1:
            p.modscr = p.dout("modscr_o", [2, 48, 128])
            p.modT_o = p.dout("modT_o", [128, 96])
        else:
            p.modscr = p.din("modscr_i", [2, 48, 128])
            p.modT_i = p.din("modT_i", [128, 96])
        p.b_modscr = Buf("modscr")
        p.b_modT_o = Buf("modT_o")
        kinds = {None: ("int", "int"), 1: (None, None), 2: ("out", None), 3: ("in", "out"), 4: (None, "in"), 5: ("out", "in"), 6: ("in", None)}[ph]
        mk = {"int": p.dint, "in": p.din, "out": p.dout, None: (lambda *a: None)}
        p.xa = mk[kinds[0]]("xa", [T, D])
        p.xb = mk[kinds[1]]("xb", [T, D])
        p.relay = ph in (1, 2, 4, 5)
        if ph in (1, 4):
            p.kst = p.dout("kst_o", [NG, 128, 8 * GT], BF16)
            p.vst = p.dout("vst_o", [NG, 128, G * 2048], BF16)
        elif ph in (2, 5):
            p.kst = p.din("kst_i", [NG, 128, 8 * GT], BF16)
            p.vst = p.din("vst_i", [NG, 128, G * 2048], BF16)
        p.b_kvst = Buf("kvst")
        sizes = dict(EX_SIZES)
        loc_ph = {1: 1, 2: 2, 3: 3, 4: 4, 5: 5}
        gath_ph = {1: (2,), 2: (3,), 3: (4, 5), 4: (5,), 5: (6,)}
        p.loc = {}
        p.gath = {}
        for k, (r, cdim) in sizes.items():
            if p.mode == "host":
                if ph is None or loc_ph[k] == ph:
                    p.loc[k] = p.dout(f"loc{k}", [r, cdim])
                if ph is None or ph in gath_ph[k]:
                    p.gath[k] = p.din(f"gath{k}", [NCORES * r, cdim])
            else:
                p.loc[k] = p.dint(f"loc{k}", [r, cdim])
                p.gath[k] = p.dint(f"gath{k}", [NCORES * r, cdim])
        p.b_loc = {k: Buf(f"loc{k}") for k in sizes}
        p.b_gath = {k: Buf(f"gath{k}") for k in sizes}
        p.b_xa = [Buf(f"xa{g}") for g in range(NG)]
        p.b_xb = [Buf(f"xb{g}") for g in range(NG)]
        p.b_out = [Buf(f"out{g}") for g in range(NG)]

        p.ident_f, p.b_ident_f = p.sb("ident_f", [128, 128])
        p.ident_b, p.b_ident_b = p.sb("ident_b", [128, 128], BF16)
        p.ones_f, p.b_ones_f = p.sb("ones_f", [128, 128])
        p.ones_b, p.b_ones_b = p.sb("ones_b", [128, 8], BF16)
        p.ret_dt, p.b_ret_dt = p.sb("ret_dt", [128, 512])
        p.ret_xi, p.b_ret_xi = p.sb("ret_xi", [128, 4])
        p.ret_zs, p.b_ret_zs = p.sb("ret_zs", [128, 4])
        p.neg, p.b_neg = p.sb("negm", [128, 128])
        p.ut, p.b_ut = p.sb("utm", [128, 128])
        p.inv_freq, p.b_inv_freq = p.sb("inv_freq_s", [128, 1])
        p.sel, p.b_sel = p.sb("sel_s", [128, NCORES])
        p.selmat, p.b_selmat = p.sb("selmat_s", [NCORES * HW, HW])
        p.adabT, p.b_adabT = p.sb("adabT", [128, 2, 48])
        p.cT_f, p.b_cT_f = p.sb("cT_f", [128, KC])
        p.nf, p.b_nf = p.sb("nf_s", [128, 1])
        p.ret_coef, p.b_ret_coef = p.sb("ret_coef_s", [128, NCORES * H])
        p.valid, p.b_valid = p.sb("valid_s", [128, NCORES * H])
        p.msel, p.b_msel = p.sb("msel_s", [NCORES, NCORES * H])
        p.consts = [p.b_ident_f, p.b_ident_b, p.b_ones_f, p.b_ones_b]
        p.modT, p.b_modT = p.sb("modT", [128, 2, 48])
        p.ntgT, p.b_ntgT = p.sb("ntgT", [128, 2, KC])
        p.nfgT, p.b_nfgT = p.sb("nfgT", [128, 2, KC])
        p.gsc, p.b_gsc = p.sb("gsc", [128, 4, KC])
        p.ret_gnT, p.b_ret_gnT = p.sb("ret_gnT", [128, 16])
        p.ml_gnT, p.b_ml_gnT = p.sb("ml_gnT", [128, 16])
        p.ml_cwT, p.b_ml_cwT = p.sb("ml_cwT", [128, 64])
        p.ml_cbT, p.b_ml_cbT = p.sb("ml_cbT", [128, 16])
        p.ffn_cwT, p.b_ffn_cwT = p.sb("ffn_cwT", [128, 2, 132])
        p.ffn_cbT, p.b_ffn_cbT = p.sb("ffn_cbT", [128, 2, 44])
        p.bg_bc, p.b_bg_bc = p.sb("bg_bc", [128, 8])
        p.gt_bc, p.b_gt_bc = p.sb("gt_bc", [128, D])
        p.cT_b, p.b_cT_b = p.sb("cT_b", [128, KC], BF16)
        p.stage, p.b_stage = p.sb("stage", [128, 128])
        p.NR = 3
        p.wt = []
        p.b_wt = []
        for i in range(p.NR):
            t, b = p.sb(f"wt{i}", [128, 4096], BF16)
            p.wt.append(t)
            p.b_wt.append(b)
        p.ring_pos = 0
        p.rot_bank = 0
        p.rot_acc = 0
        p.rot_ub = 0
        p.b_actT = [Buf(f"actT{i}") for i in range(22)]
        p.x_g, _ = p.sb("x_g", [128, G, D])
        p.b_xg = [Buf(f"xg{c}") for c in range(G)]
        p.sg_all = p.x_g[:].bitcast(BF16)
        p.xn, p.b_xn = p.sb("xn", [128, D])
        p.xn2, p.b_xn2 = p.sb("xn2", [128, D])
        p.junk, p.b_junk = p.sb("junk", [128, D], BF16)
        p.uhalo, p.b_uhalo = p.sb("uhalo", [128, 44, HW])
        p.ss, p.b_ss = p.sb("ss", [128, 8])
        p.rstd, p.b_rstd = p.sb("rstd", [128, 8])
        p.hT, p.b_hT = p.sb("hT", [128, KC, GT], BF16)
        p.xh, p.b_xh = p.sb("xh", [32, D])
        p.hTh, p.b_hTh = p.sb("hTh", [128, KC, 32], BF16)
        p.big_a, p.b_big_a = p.sb("big_a", [128, 22 * GT], BF16)
        p.qT, p.b_qT = p.sb("qT", [128, 8, GT], BF16)
        p.kT, p.b_kT = p.sb("kT", [128, 8, GT], BF16)
        p.v_all, _ = p.sb("v_all", [128, G, 2048], BF16)
        p.b_v = [Buf(f"v{c}") for c in range(G)]
        p.R, _ = p.sb("R", [128, 8, 512])
        p.b_R = [Buf(f"R{i}") for i in range(8)]
        p.Rb, _ = p.sb("Rb", [128, 8, 512], BF16)
        p.b_Rb = [Buf(f"Rb{i}") for i in range(8)]
        p.nst, p.b_nst = p.sb("nst", [128, 8])
        p.nstb, p.b_nstb = p.sb("nstb", [128, 8], BF16)
        p.fsum, p.b_fsum = p.sb("fsum", [128, 4])
        p.wk = []
        p.b_wk = []
        for i in range(5):
            t, b = p.sb(f"wk{i}", [128, 512])
            p.wk.append(t)
            p.b_wk.append(b)
        p.cos, p.b_cos = p.sb("cos", [128, GT])
        p.sin, p.b_sin = p.sb("sin", [128, GT])
        p.yg, p.b_yg = p.sb("yg", [128, 2048], BF16)
        p.sTm, p.b_sTm = p.sb("sTm", [128, 512], BF16)
        p.kz, p.b_kz = p.sb("kz", [128, 1024], BF16)
        p.sm, p.b_sm = p.sb("sm", [128, 96])
        p.mcoef, p.b_mcoef = p.sb("mcoef", [128, NCORES * H])
        p.gat, p.b_gat = p.sb("gat", [128, G, 8])
        p.gmx, p.b_gmx = p.sb("gmx", [128, G, 8])
        p.gm, p.b_gm = p.sb("gm", [128, 7, 4 * G])
        p.ubuf = []
        p.b_ubuf = []
        for i in range(2):
            t, b = p.sb(f"ubuf{i}", [128, HW + GT])
            p.ubuf.append(t)
            p.b_ubuf.append(b)
        p.P = [p.ps(f"P{i}", [128, 512]) for i in range(6)]
        p.b_P = [Buf(f"P{i}") for i in range(6)]
        p.Pb = [p.ps(f"Pb{i}", [128, 1024], BF16) for i in range(2)]
        p.b_Pb = [Buf(f"Pb{i}") for i in range(2)]

    def dma_group(self, q, key, pairs, reads=(), writes=()):
        S = self.S
        k = S._dma_sem(key)
        S._wait(q, S._deps(reads, writes))
        for out, in_ in pairs:
            ins = S.eng[q].dma_start(out=out, in_=in_)
            S.cnt[k] += 16
            ins.then_inc(S.sems[k], 16)
            S.ninst += 1
        S._record((k, S.cnt[k]), reads, writes)

    def dump(self, name, ap, bufs):
        if not self.debug:
            return
        shape = list(ap.shape)
        d = self.nc.dram_tensor("dbg_" + name, shape, ap.dtype, kind="ExternalOutput").ap()
        b = Buf("dbg_" + name)
        self.dbg_bufs.append(b)
        self.S.dma("sp", d, ap, reads=bufs, writes=[b], key="dbg_" + name)

    def load_T(self, src_rows, n, dst_ap, dst_buf):
        p, S = self, self.S
        S.dma("sp", p.stage[0:n, :], src_rows, writes=[p.b_stage], key="stage")
        S.op("pe", lambda e: e.transpose(p.P[5][:, 0:n], p.stage[0:n, :], p.ident_f[0:n, 0:n]),
             reads=[p.b_stage, p.b_ident_f], writes=[p.b_P[5]])
        S.op("dve", lambda e: e.tensor_copy(out=dst_ap, in_=p.P[5][:, 0:n]), reads=[p.b_P[5]], writes=[dst_buf])

    def ring(self):
        i = self.ring_pos % self.NR
        self.ring_pos += 1
        return self.wt[i], self.b_wt[i], f"w{i}"

    def load_std_piece(self, W2d, c0, w=512):
        t, b, key = self.ring()
        view = t[:, 0:8 * w].rearrange("p (k n) -> p k n", k=8)
        src = W2d.rearrange("(k p) n -> p k n", p=128)
        pairs = [(view[:, 0:4, :], src[:, 0:4, c0:c0 + w]), (view[:, 4:8, :], src[:, 4:8, c0:c0 + w])]
        self.dma_group("pool", key, pairs, writes=[b])
        return view, b

    def load_rows_piece(self, W2d, k0, nk, c0, w):
        t, b, key = self.ring()
        view = t[:, 0:nk * w].rearrange("p (k n) -> p k n", k=nk)
        src = W2d.rearrange("(k p) n -> p k n", p=128)
        pairs = []
        step = 4
        for a in range(0, nk, step):
            e = min(nk, a + step)
            pairs.append((view[:, a:e, :], src[:, k0 + a:k0 + e, c0:c0 + w]))
        self.dma_group("pool", key, pairs, writes=[b])
        return view, b

    def _setup(self):
        p, S = self, self.S
        loads = [(p.ident_f, p.b_ident_f, p.t_ident), (p.ret_dt, p.b_ret_dt, p.t_ret_dt),
                 (p.ret_xi, p.b_ret_xi, p.t_ret_xi), (p.ret_zs, p.b_ret_zs, p.t_ret_zs),
                 (p.neg, p.b_neg, p.t_neg), (p.ut, p.b_ut, p.t_ut), (p.inv_freq, p.b_inv_freq, p.t_inv_freq),
                 (p.sel, p.b_sel, p.t_sel), (p.nf, p.b_nf, p.t_nf), (p.ret_coef, p.b_ret_coef, p.t_ret_coef),
                 (p.valid, p.b_valid, p.t_valid), (p.msel, p.b_msel, p.t_msel)]
        self.dma_group("sp", "setup", [(t[:], src) for t, b, src in loads], writes=[b for t, b, s in loads])
        S.dma("sp", p.bg_bc[:], p.ml_bg[0:1, :].partition_broadcast(128).rearrange("p o n -> p (o n)"),
              writes=[p.b_bg_bc], key="setup2")
        S.op("dve", lambda e: e.memset(p.ones_f[:], 1.0), writes=[p.b_ones_f])
        S.op("dve", lambda e: e.memset(p.ones_b[:], 1.0), writes=[p.b_ones_b])
        S.op("dve", lambda e: e.memset(p.xh[:], 0.0), writes=[p.b_xh])
        S.op("dve", lambda e: e.tensor_copy(out=p.ident_b[:], in_=p.ident_f[:]), reads=[p.b_ident_f], writes=[p.b_ident_b])
        vec = [(p.ntgT, p.b_ntgT, p.ntg), (p.nfgT, p.b_nfgT, p.nfg), (p.ffn_cwT, p.b_ffn_cwT, p.ffn_cw), (p.ffn_cbT, p.b_ffn_cbT, p.ffn_cb),
               (p.ret_gnT, p.b_ret_gnT, p.ret_gn), (p.ml_gnT, p.b_ml_gnT, p.ml_gn), (p.ml_cwT, p.b_ml_cwT, p.ml_cw), (p.ml_cbT, p.b_ml_cbT, p.ml_cb),
               (p.adabT, p.b_adabT, p.ada_b), (p.cT_f, p.b_cT_f, p.c_in), (p.selmat, p.b_selmat, p.t_selmat)]
        self.dma_group("sp", "setup3", [(t[:], s) for t, b, s in vec], writes=[b for t, b, s in vec])
        if p.mod_layers:
            S.op("act", lambda e: e.activation(out=p.cT_b[:], in_=p.cT_f[:], func=AF.Silu), reads=[p.b_cT_f], writes=[p.b_cT_b])
            for l in p.mod_layers:
                for pc in range(12):
                    wv, wb = self.load_std_piece(p.ada_w[l], pc * 512)
                    for ct in range(4):
                        col = l * 48 + pc * 4 + ct
                        for kc in range(KC):
                            S.op("pe", lambda e: e.matmul(p.P[4][:, col:col + 1], lhsT=wv[:, kc, ct * 128:(ct + 1) * 128],
                                                          rhs=p.cT_b[:, kc:kc + 1], start=(kc == 0), stop=(kc == KC - 1)),
                                 reads=[wb, p.b_cT_b], writes=[p.b_P[4]])
            for l in p.mod_layers:
                S.op("dve", lambda e: e.tensor_tensor(out=p.modT[:, l, :], in0=p.adabT[:, l, :], in1=p.P[4][:, l * 48:(l + 1) * 48], op=ALU.add),
                     reads=[p.b_P[4], p.b_adabT], writes=[p.b_modT])
                S.op("pe", lambda e: e.transpose(p.P[5][0:48, 0:128], p.modT[:, l, :], p.ident_f[:, :]),
                     reads=[p.b_modT, p.b_ident_f], writes=[p.b_P[5]])
                S.op("dve", lambda e: e.tensor_copy(out=p.stage[0:48, :], in_=p.P[5][0:48, 0:128]), reads=[p.b_P[5]], writes=[p.b_stage])
                S.dma("sp", p.modscr[l], p.stage[0:48, :], reads=[p.b_stage], writes=[p.b_modscr], key="modscr")
            if p.phase == 1:
                S.dma("sp", p.modT_o[:, :], p.modT[:].rearrange("p l n -> p (l n)"), reads=[p.b_modT], writes=[p.b_modT_o], key="modT_o")
        else:
            S.dma("sp", p.modT[:].rearrange("p l n -> p (l n)"), p.modT_i[:, :], writes=[p.b_modT], key="modT_i")
        for l in p.layers:
            S.op("dve", lambda e: e.scalar_tensor_tensor(out=p.gsc[:, 2 * l, :], in0=p.modT[:, l, 8:16], scalar=1.0,
                                                         in1=p.ntgT[:, l, :], op0=ALU.add, op1=ALU.mult),
                 reads=[p.b_modT, p.b_ntgT], writes=[p.b_gsc])
            S.op("dve", lambda e: e.scalar_tensor_tensor(out=p.gsc[:, 2 * l + 1, :], in0=p.modT[:, l, 32:40], scalar=1.0,
                                                         in1=p.nfgT[:, l, :], op0=ALU.add, op1=ALU.mult),
                 reads=[p.b_modT, p.b_nfgT], writes=[p.b_gsc])

    def load_gate(self, l, which):
        p, S = self, self.S
        r0 = 16 if which == 0 else 40
        src = p.modscr[l, r0:r0 + 8, :].rearrange("(o a) b -> o (a b)", o=1).partition_broadcast(128).rearrange("p o n -> p (o n)")
        S.dma("sp", p.gt_bc[:], src, reads=[p.b_modscr], writes=[p.b_gt_bc], key="gt")

    def norm_rows(self, xt, bx, npart, gidx, sh, dst_fn, dst_buf, col):
        p, S = self, self.S
        S.op("act", lambda e: e.activation(out=p.xn[0:npart, :], in_=xt, func=AF.Square, accum_out=p.ss[0:npart, col:col + 1]),
             reads=[bx], writes=[p.b_xn, p.b_ss])
        S.op("dve", lambda e: e.tensor_scalar(out=p.ss[0:npart, col:col + 1], in0=p.ss[0:npart, col:col + 1], scalar1=1.0 / D, scalar2=EPS,
                                              op0=ALU.mult, op1=ALU.add), reads=[p.b_ss], writes=[p.b_ss])
        S.op("act", lambda e: e.activation(out=p.ss[0:npart, col:col + 1], in_=p.ss[0:npart, col:col + 1], func=AF.Sqrt),
             reads=[p.b_ss], writes=[p.b_ss])
        S.op("dve", lambda e: e.reciprocal(out=p.rstd[0:npart, col:col + 1], in_=p.ss[0:npart, col:col + 1]),
             reads=[p.b_ss], writes=[p.b_rstd])
        S.op("act", lambda e: e.activation(out=p.xn[0:npart, :], in_=xt, func=AF.Copy, scale=p.rstd[0:npart, col:col + 1]),
             reads=[bx, p.b_rstd], writes=[p.b_xn])
        for half in range(2):
            bank = p.P[half]
            for k4 in range(4):
                kc = half * 4 + k4
                S.op("pe", lambda e: e.transpose(bank[:, k4 * 128:k4 * 128 + npart], p.xn[0:npart, kc * 128:(kc + 1) * 128],
                                                 p.ident_f[0:npart, 0:npart]),
                     reads=[p.b_xn, p.b_ident_f], writes=[p.b_P[half]])
            for k4 in range(4):
                kc = half * 4 + k4
                src = bank[:, k4 * 128:k4 * 128 + npart]
                if kc % 2 == 0:
                    S.op("act", lambda e: e.activation(out=dst_fn(kc), in_=src, func=AF.Identity,
                                                       scale=p.gsc[:, gidx, kc:kc + 1], bias=sh[:, kc:kc + 1]),
                         reads=[p.b_P[half], p.b_gsc, p.b_modT], writes=[dst_buf])
                else:
                    S.op("dve", lambda e: e.tensor_scalar(out=dst_fn(kc), in0=src, scalar1=p.gsc[:, gidx, kc:kc + 1],
                                                          scalar2=sh[:, kc:kc + 1], op0=ALU.mult, op1=ALU.add),
                         reads=[p.b_P[half], p.b_gsc, p.b_modT], writes=[dst_buf])

    def norm_group(self, src, src_bufs, g, gidx, sh):
        p, S = self, self.S
        for c in range(G):
            r0 = g * GT + c * CH
            S.dma("sp", p.x_g[:, c, :], src[r0:r0 + CH, :], reads=src_bufs, writes=[p.b_xg[c]], key=f"xg{c}")
        for c in range(G):
            S.op("act", lambda e: e.activation(out=p.junk[:], in_=p.x_g[:, c, :], func=AF.Square, accum_out=p.ss[:, c:c + 1]),
                 reads=[p.b_xg[c]], writes=[p.b_junk, p.b_ss])
        S.op("dve", lambda e: e.tensor_scalar(out=p.ss[:, 0:G], in0=p.ss[:, 0:G], scalar1=1.0 / D, scalar2=EPS, op0=ALU.mult, op1=ALU.add),
             reads=[p.b_ss], writes=[p.b_ss])
        S.op("act", lambda e: e.activation(out=p.ss[:, 0:G], in_=p.ss[:, 0:G], func=AF.Sqrt), reads=[p.b_ss], writes=[p.b_ss])
        S.op("dve", lambda e: e.reciprocal(out=p.rstd[:, 0:G], in_=p.ss[:, 0:G]), reads=[p.b_ss], writes=[p.b_rstd])
        for c in range(G):
            xn, b_xn = (p.xn, p.b_xn) if c % 2 == 0 else (p.xn2, p.b_xn2)
            S.op("act", lambda e: e.activation(out=xn[:], in_=p.x_g[:, c, :], func=AF.Copy, scale=p.rstd[:, c:c + 1]),
                 reads=[p.b_xg[c], p.b_rstd], writes=[b_xn])
            for half in range(2):
                bi = 2 * (c % 2) + half
                bank = p.P[bi]
                for k4 in range(4):
                    kc = half * 4 + k4
                    S.op("pe", lambda e: e.transpose(bank[:, k4 * 128:(k4 + 1) * 128], xn[:, kc * 128:(kc + 1) * 128], p.ident_f[:, :]),
                         reads=[b_xn, p.b_ident_f], writes=[p.b_P[bi]])
                for k4 in range(4):
                    kc = half * 4 + k4
                    srcp = bank[:, k4 * 128:(k4 + 1) * 128]
                    dst = p.hT[:, kc, c * CH:(c + 1) * CH]
                    if kc % 2 == 0:
                        S.op("act", lambda e: e.activation(out=dst, in_=srcp, func=AF.Identity,
                                                           scale=p.gsc[:, gidx, kc:kc + 1], bias=sh[:, kc:kc + 1]),
                             reads=[p.b_P[bi], p.b_gsc, p.b_modT], writes=[p.b_hT])
                    else:
                        S.op("dve", lambda e: e.tensor_scalar(out=dst, in0=srcp, scalar1=p.gsc[:, gidx, kc:kc + 1],
                                                              scalar2=sh[:, kc:kc + 1], op0=ALU.mult, op1=ALU.add),
                             reads=[p.b_P[bi], p.b_gsc, p.b_modT], writes=[p.b_hT])

    def load_halo(self, src, src_bufs, g, ex):
        p, S = self, self.S
        if g > 0:
            r0 = g * GT - HW
            S.dma("sp", p.xh[0:HW, :], src[r0:r0 + HW, :], reads=src_bufs, writes=[p.b_xh], key="xh")
        else:
            gt = p.gath[ex]
            nr = NCORES * HW
            S.dma("sp", p.xn[0:nr, :], gt[:, :], reads=[p.b_gath[ex]], writes=[p.b_xn], key="xnh")
            for half in range(2):
                S.op("pe", lambda e: e.matmul(p.P[half][0:HW, :], lhsT=p.selmat[0:nr, 0:HW], rhs=p.xn[0:nr, half * 512:(half + 1) * 512],
                                              start=True, stop=True), reads=[p.b_selmat, p.b_xn], writes=[p.b_P[half]])
                S.op("dve", lambda e: e.tensor_copy(out=p.xh[0:HW, half * 512:(half + 1) * 512], in_=p.P[half][0:HW, :]),
                     reads=[p.b_P[half]], writes=[p.b_xh])

    def norm_halo(self, gidx, sh):
        p = self
        self.norm_rows(p.xh[0:32, :], p.b_xh, 32, gidx, sh, lambda kc: p.hTh[:, kc, :], p.b_hTh, 4)

    def rope_tables(self, g):
        p, S = self, self.S
        posi = p.wk[4][:].bitcast(I32)
        src = p.pos[0:1, g * GT:(g + 1) * GT].partition_broadcast(128).rearrange("p o n -> p (o n)")
        S.dma("sp", posi, src, writes=[p.b_wk[4]], key="posi")
        ang, b_ang = p.wk[0], p.b_wk[0]
        S.op("dve", lambda e: e.tensor_copy(out=p.wk[1][:], in_=posi), reads=[p.b_wk[4]], writes=[p.b_wk[1]])
        S.op("dve", lambda e: e.tensor_scalar(out=ang[:], in0=p.wk[1][:], scalar1=p.inv_freq[:, 0:1], scalar2=None, op0=ALU.mult),
             reads=[p.b_wk[1], p.b_inv_freq], writes=[b_ang])
        for dst, b_dst, shift in ((p.sin, p.b_sin, 0.0), (p.cos, p.b_cos, 0.5 * math.pi)):
            xs, b_xs = p.wk[1], p.b_wk[1]
            kf, b_kf = p.wk[2], p.b_wk[2]
            ki = p.wk[3][:].bitcast(I32)
            b_ki = p.b_wk[3]
            S.op("dve", lambda e: e.tensor_scalar(out=xs[:], in0=ang[:], scalar1=shift, scalar2=None, op0=ALU.add),
                 reads=[b_ang], writes=[b_xs])
            S.op("dve", lambda e: e.tensor_scalar(out=kf[:], in0=xs[:], scalar1=1.0 / TWO_PI, scalar2=None, op0=ALU.mult),
                 reads=[b_xs], writes=[b_kf])
            S.op("dve", lambda e: e.tensor_copy(out=ki, in_=kf[:]), reads=[b_kf], writes=[b_ki])
            S.op("dve", lambda e: e.tensor_copy(out=kf[:], in_=ki), reads=[b_ki], writes=[b_kf])
            S.op("dve", lambda e: e.scalar_tensor_tensor(out=xs[:], in0=kf[:], scalar=-C1, in1=xs[:], op0=ALU.mult, op1=ALU.add),
                 reads=[b_kf, b_xs], writes=[b_xs])
            S.op("dve", lambda e: e.scalar_tensor_tensor(out=xs[:], in0=kf[:], scalar=-C2, in1=xs[:], op0=ALU.mult, op1=ALU.add),
                 reads=[b_kf, b_xs], writes=[b_xs])
            S.op("dve", lambda e: e.tensor_scalar(out=kf[:], in0=xs[:], scalar1=-math.pi, scalar2=TWO_PI, op0=ALU.is_lt, op1=ALU.mult),
                 reads=[b_xs], writes=[b_kf])
            S.op("dve", lambda e: e.tensor_tensor(out=xs[:], in0=xs[:], in1=kf[:], op=ALU.add), reads=[b_xs, b_kf], writes=[b_xs])
            S.op("dve", lambda e: e.tensor_scalar(out=kf[:], in0=xs[:], scalar1=math.pi, scalar2=-TWO_PI, op0=ALU.is_gt, op1=ALU.mult),
                 reads=[b_xs], writes=[b_kf])
            S.op("dve", lambda e: e.tensor_tensor(out=xs[:], in0=xs[:], in1=kf[:], op=ALU.add), reads=[b_xs, b_kf], writes=[b_xs])
            S.op("dve", lambda e: e.tensor_scalar(out=xs[:], in0=xs[:], scalar1=-PI_SAFE, scalar2=PI_SAFE, op0=ALU.max, op1=ALU.min),
                 reads=[b_xs], writes=[b_xs])
            S.op("act", lambda e: e.activation(out=dst[:], in_=xs[:], func=AF.Sin), reads=[b_xs], writes=[b_dst])

    def rope_pair(self, hh, dst, b_dst):
        p, S = self, self.S
        i1, i2 = 2 * (hh % 2), 2 * (hh % 2) + 1
        b1, b2 = p.P[i1], p.P[i2]
        A, Bm, C_, Dm = p.wk[0], p.wk[1], p.wk[2], p.wk[3]
        S.op("dve", lambda e: e.tensor_tensor(out=A[:], in0=b1[:], in1=p.cos[:], op=ALU.mult), reads=[p.b_P[i1], p.b_cos], writes=[p.b_wk[0]])
        S.op("dve", lambda e: e.tensor_tensor(out=Bm[:], in0=b2[:], in1=p.sin[:], op=ALU.mult), reads=[p.b_P[i2], p.b_sin], writes=[p.b_wk[1]])
        S.op("dve", lambda e: e.tensor_tensor(out=C_[:], in0=b1[:], in1=p.sin[:], op=ALU.mult), reads=[p.b_P[i1], p.b_sin], writes=[p.b_wk[2]])
        S.op("dve", lambda e: e.tensor_tensor(out=Dm[:], in0=b2[:], in1=p.cos[:], op=ALU.mult), reads=[p.b_P[i2], p.b_cos], writes=[p.b_wk[3]])
        S.op("dve", lambda e: e.tensor_tensor(out=dst[:, 2 * hh, :], in0=A[:], in1=Bm[:], op=ALU.subtract),
             reads=[p.b_wk[0], p.b_wk[1]], writes=[b_dst])
        S.op("dve", lambda e: e.tensor_tensor(out=dst[:, 2 * hh + 1, :], in0=C_[:], in1=Dm[:], op=ALU.add),
             reads=[p.b_wk[2], p.b_wk[3]], writes=[b_dst])

    def proj_A(self, wv, wb, ct, bank_i, hT=None, b_hT=None, n=GT):
        p, S = self, self.S
        hT = p.hT if hT is None else hT
        b_hT = p.b_hT if b_hT is None else b_hT
        for kc in range(KC):
            S.op("pe", lambda e: e.matmul(p.P[bank_i][:, 0:n], lhsT=wv[:, kc, ct * 128:(ct + 1) * 128], rhs=hT[:, kc, 0:n],
                                          start=(kc == 0), stop=(kc == KC - 1)),
                 reads=[wb, b_hT], writes=[p.b_P[bank_i]])

    def proj_B(self, wv, wb, c, bank_i, w=512):
        p, S = self, self.S
        for kc in range(KC):
            S.op("pe", lambda e: e.matmul(p.P[bank_i][:, 0:w], lhsT=p.hT[:, kc, c * CH:(c + 1) * CH], rhs=wv[:, kc, 0:w],
                                          start=(kc == 0), stop=(kc == KC - 1)),
                 reads=[wb, p.b_hT], writes=[p.b_P[bank_i]])

    def exchange(self, k):
        p, S = self, self.S
        if p.mode == "host":
            return
        S.wait_bufs("pool", [p.b_loc[k], p.b_gath[k]])
        ins = p.nc.gpsimd.collective_compute("AllGather", ALU.bypass, replica_groups=[list(range(NCORES))],
                                             ins=[p.loc[k][:, :]], outs=[p.gath[k][:, :]])
        key = S._dma_sem(f"cc{k}")
        S.cnt[key] += 16
        ins.then_inc(S.sems[key], 16)
        S._record((key, S.cnt[key]), [p.b_loc[k]], [p.b_gath[k]])

    def make_kz(self, c, scale_ap_fn):
        p, S = self, self.S
        cs = slice(c * CH, (c + 1) * CH)
        for i in range(8):
            S.op("pe", lambda e: e.transpose(p.Pb[0][:, i * 128:(i + 1) * 128], p.kT[:, i, cs], p.ident_b[:, :]),
                 reads=[p.b_kT, p.b_ident_b], writes=[p.b_Pb[0]])
        for hh in range(H):
            sc, sbufs = scale_ap_fn(hh)
            S.op("act", lambda e: e.activation(out=p.kz[:, hh * 256:(hh + 1) * 256], in_=p.Pb[0][:, hh * 256:(hh + 1) * 256],
                                               func=AF.Copy, scale=sc),
                 reads=[p.b_Pb[0]] + sbufs, writes=[p.b_kz])

    def state_update(self, c, hh, decay, dbufs=()):
        p, S = self, self.S
        dbufs = list(dbufs)
        for half in range(2):
            i = 2 * hh + half
            S.op("pe", lambda e: e.matmul(p.P[2 + half][:, :], lhsT=p.kz[:, hh * 256 + half * 128: hh * 256 + (half + 1) * 128],
                                          rhs=p.v_all[:, c, hh * 512:(hh + 1) * 512], start=True, stop=True),
                 reads=[p.b_kz, p.b_v[c]], writes=[p.b_P[2 + half]])
            S.op("dve", lambda e: e.scalar_tensor_tensor(out=p.R[:, i, :], in0=p.R[:, i, :], scalar=decay, in1=p.P[2 + half][:, :],
                                                         op0=ALU.mult, op1=ALU.add),
                 reads=[p.b_R[i], p.b_P[2 + half]] + dbufs, writes=[p.b_R[i]])

    def refresh_Rb(self, hh):
        p, S = self, self.S
        for half in range(2):
            i = 2 * hh + half
            S.op("act", lambda e: e.activation(out=p.Rb[:, i, :], in_=p.R[:, i, :], func=AF.Copy),
                 reads=[p.b_R[i]], writes=[p.b_Rb[i]])

    def groupnorm_heads(self, ywk, c, gate):
        p, S = self, self.S
        for hh in range(H):
            S.op("dve", lambda e: e.bn_stats(out=p.sm[:, 6 * hh:6 * hh + 6], in_=p.wk[ywk[hh]][:]), reads=[p.b_wk[ywk[hh]]], writes=[p.b_sm])
            S.op("dve", lambda e: e.bn_aggr(out=p.sm[:, 24 + 2 * hh:26 + 2 * hh], in_=p.sm[:, 6 * hh:6 * hh + 6]), reads=[p.b_sm], writes=[p.b_sm])
        var = p.sm[:, 24:32].rearrange("p (h t) -> p h t", t=2)[:, :, 1]
        S.op("dve", lambda e: e.tensor_scalar(out=p.sm[:, 32:36], in0=var, scalar1=EPS, scalar2=None, op0=ALU.add), reads=[p.b_sm], writes=[p.b_sm])
        S.op("act", lambda e: e.activation(out=p.sm[:, 32:36], in_=p.sm[:, 32:36], func=AF.Ln), reads=[p.b_sm], writes=[p.b_sm])
        S.op("act", lambda e: e.activation(out=p.sm[:, 32:36], in_=p.sm[:, 32:36], func=AF.Exp, scale=-0.5), reads=[p.b_sm], writes=[p.b_sm])
        for hh in range(H):
            w = p.wk[ywk[hh]]
            if gate:
                S.op("dve", lambda e: e.tensor_scalar(out=w[:], in0=w[:], scalar1=p.sm[:, 24 + 2 * hh:25 + 2 * hh], scalar2=p.sm[:, 32 + hh:33 + hh],
                                                      op0=ALU.subtract, op1=ALU.mult), reads=[p.b_wk[ywk[hh]], p.b_sm], writes=[p.b_wk[ywk[hh]]])
                sgv = p.sg_all[:, c, hh * 512:(hh + 1) * 512]
                S.op("dve", lambda e: e.tensor_tensor(out=p.yg[:, hh * 512:(hh + 1) * 512], in0=w[:], in1=sgv, op=ALU.mult),
                     reads=[p.b_wk[ywk[hh]], p.b_xg[c]], writes=[p.b_yg])
            else:
                S.op("dve", lambda e: e.tensor_scalar(out=p.yg[:, hh * 512:(hh + 1) * 512], in0=w[:], scalar1=p.sm[:, 24 + 2 * hh:25 + 2 * hh],
                                                      scalar2=p.sm[:, 32 + hh:33 + hh], op0=ALU.subtract, op1=ALU.mult),
                     reads=[p.b_wk[ywk[hh]], p.b_sm], writes=[p.b_yg])

    def make_ynT(self, c, gnT, b_gnT):
        p, S = self, self.S
        ynT = p.big_a[:, 0:16 * GT].rearrange("p (k n) -> p k n", k=16)
        for kc in range(16):
            bi = kc // 8
            S.op("pe", lambda e: e.transpose(p.Pb[bi][:, (kc % 8) * 128:(kc % 8 + 1) * 128], p.yg[:, kc * 128:(kc + 1) * 128], p.ident_b[:, :]),
                 reads=[p.b_yg, p.b_ident_b], writes=[p.b_Pb[bi]])
        for kc in range(16):
            bi = kc // 8
            src = p.Pb[bi][:, (kc % 8) * 128:(kc % 8 + 1) * 128]
            dst = ynT[:, kc, c * CH:(c + 1) * CH]
            if kc % 2 == 0:
                S.op("act", lambda e: e.activation(out=dst, in_=src, func=AF.Copy, scale=gnT[:, kc:kc + 1]),
                     reads=[p.b_Pb[bi], b_gnT], writes=[p.b_big_a])
            else:
                S.op("dve", lambda e: e.tensor_scalar(out=dst, in0=src, scalar1=gnT[:, kc:kc + 1], scalar2=None, op0=ALU.mult),
                     reads=[p.b_Pb[bi], b_gnT], writes=[p.b_big_a])

    def out_proj(self, g, W_out, src, src_bufs, dst, dst_bufs, halo_ex):
        p, S = self, self.S
        ynT = p.big_a[:, 0:16 * GT].rearrange("p (k n) -> p k n", k=16)
        for c in range(G):
            r0 = g * GT + c * CH
            S.dma("sp", p.x_g[:, c, :], src[r0:r0 + CH, :], reads=src_bufs, writes=[p.b_xg[c]], key=f"xg{c}")
        for cp in range(4):
            wv, wb = self.load_rows_piece(W_out, 0, 16, cp * 256, 256)
            for c in range(G):
                bi = (cp * G + c) % 6
                for kc in range(16):
                    S.op("pe", lambda e: e.matmul(p.P[bi][:, 0:256], lhsT=ynT[:, kc, c * CH:(c + 1) * CH], rhs=wv[:, kc, :],
                                                  start=(kc == 0), stop=(kc == 15)),
                         reads=[wb, p.b_big_a], writes=[p.b_P[bi]])
                ti = (cp * G + c) % 5
                tmp = p.wk[ti]
                S.op("dve", lambda e: e.tensor_tensor(out=tmp[:, 0:256], in0=p.P[bi][:, 0:256], in1=p.gt_bc[:, cp * 256:(cp + 1) * 256], op=ALU.mult),
                     reads=[p.b_P[bi], p.b_gt_bc], writes=[p.b_wk[ti]])
                S.op("dve", lambda e: e.tensor_tensor(out=p.x_g[:, c, cp * 256:(cp + 1) * 256], in0=p.x_g[:, c, cp * 256:(cp + 1) * 256],
                                                      in1=tmp[:, 0:256], op=ALU.add),
                     reads=[p.b_wk[ti], p.b_xg[c]], writes=[p.b_xg[c]])
        self.store_group(g, dst, dst_bufs, halo_ex)

    def store_group(self, g, dst, dst_bufs, halo_ex):
        p, S = self, self.S
        pairs = [(dst[g * GT + c * CH: g * GT + (c + 1) * CH, :], p.x_g[:, c, :]) for c in range(G)]
        self.dma_group("sp", f"st_{dst_bufs[0].name[:2]}", pairs, reads=p.b_xg, writes=[dst_bufs[g]])
        if g == NG - 1 and halo_ex is not None:
            S.dma("sp", p.loc[halo_ex][:, :], p.x_g[128 - HW:128, G - 1, :], reads=[p.b_xg[G - 1]], writes=[p.b_loc[halo_ex]], key=f"loc{halo_ex}")

    def kv_store(self, g):
        p = self
        self.dma_group("sp", "kvst", [(p.kst[g], p.kT[:].rearrange("p a n -> p (a n)")), (p.vst[g], p.v_all[:].rearrange("p a n -> p (a n)"))],
                       reads=[p.b_kT] + p.b_v, writes=[p.b_kvst])

    def kv_load(self, g):
        p = self
        self.dma_group("sp", "kvld", [(p.kT[:].rearrange("p a n -> p (a n)"), p.kst[g]), (p.v_all[:].rearrange("p a n -> p (a n)"), p.vst[g])],
                       writes=[p.b_kT] + p.b_v)

    def ret_group(self, g, full):
        p, S = self, self.S
        sh = p.modT[:, 0, 0:8]
        self.norm_group(p.x, [], g, 0, sh)
        self.rope_tables(g)
        W = p.ret_w_in
        reuse = full and p.relay
        plist = ([0, 1] if full else []) + ([] if reuse else [2, 3])
        if reuse:
            self.kv_load(g)
        for pc in plist:
            wv, wb = self.load_std_piece(W, pc * 512)
            dst, b_dst = (p.qT, p.b_qT) if pc < 2 else (p.kT, p.b_kT)
            for ct in range(4):
                self.proj_A(wv, wb, ct, ct)
            for j in range(2):
                self.rope_pair(2 * (pc % 2) + j, dst, b_dst)
        for hh in ([] if reuse else range(H)):
            wv, wb = self.load_std_piece(W, 2048 + hh * 512)
            for c in range(G):
                self.proj_B(wv, wb, c, c)
                dstv = p.v_all[:, c, hh * 512:(hh + 1) * 512]
                if c % 2 == 0:
                    S.op("act", lambda e: e.activation(out=dstv, in_=p.P[c][:, :], func=AF.Copy), reads=[p.b_P[c]], writes=[p.b_v[c]])
                else:
                    S.op("dve", lambda e: e.tensor_copy(out=dstv, in_=p.P[c][:, :]), reads=[p.b_P[c]], writes=[p.b_v[c]])
        if (not full) and p.relay:
            self.kv_store(g)
        if full:
            for hh in range(H):
                wv, wb = self.load_std_piece(W, 4096 + hh * 512)
                for c in range(G):
                    self.proj_B(wv, wb, c, c)
                    dsts = p.sg_all[:, c, hh * 512:(hh + 1) * 512]
                    S.op("act", lambda e: e.activation(out=dsts, in_=p.P[c][:, :], func=AF.Silu), reads=[p.b_P[c]], writes=[p.b_xg[c]])
        if full and g == 0:
            self.dump("hT", p.hT[:], [p.b_hT])
            self.dump("cos", p.cos[:], [p.b_cos])
            self.dump("sin", p.sin[:], [p.b_sin])
            self.dump("qT", p.qT[:], [p.b_qT])
            self.dump("kT", p.kT[:], [p.b_kT])
            self.dump("v_all", p.v_all[:], p.b_v)
            self.dump("sg_all", p.sg_all, p.b_xg)
        for c in range(G):
            self.ret_chunk(c, full)
            if full and g == 0 and c == 1:
                self.dump("yg", p.yg[:], [p.b_yg])
                self.dump("sm", p.sm[:], [p.b_sm])
                self.dump("wk0", p.wk[0][:], [p.b_wk[0]])
                self.dump("wk3", p.wk[3][:], [p.b_wk[3]])
                self.dump("sTm", p.sTm[:], [p.b_sTm])
                self.dump("kz", p.kz[:], [p.b_kz])
                self.dump("R", p.R[:], p.b_R)
        if full and g == 0:
            self.dump("ynT", p.big_a[:, 0:16 * GT], [p.b_big_a])
        if full:
            self.out_proj(g, p.ret_w_out, p.x, [], p.xa, p.b_xa, 2)
        if full and g == 0:
            self.dump("xg", p.x_g[:], p.b_xg)

    def ret_chunk(self, c, full):
        p, S = self, self.S
        cs = slice(c * CH, (c + 1) * CH)
        self.make_kz(c, lambda hh: (p.ret_zs[:, hh:hh + 1], [p.b_ret_zs]))
        gam = [float(np.exp(128.0 * np.log(1.0 - 2.0 ** (-5.0 - h)))) for h in range(H)]
        if not full:
            for hh in range(H):
                self.state_update(c, hh, gam[hh])
            return
        for hh in range(H):
            for half in range(2):
                S.op("pe", lambda e: e.matmul(p.P[4][:, hh * 128:(hh + 1) * 128], lhsT=p.kT[:, 2 * hh + half, cs], rhs=p.qT[:, 2 * hh + half, cs],
                                              start=(half == 0), stop=(half == 1)),
                     reads=[p.b_kT, p.b_qT], writes=[p.b_P[4]])
        S.op("dve", lambda e: e.tensor_tensor(out=p.sTm[:], in0=p.P[4][:, :], in1=p.ret_dt[:], op=ALU.mult),
             reads=[p.b_P[4], p.b_ret_dt], writes=[p.b_sTm])
        for hh in range(H):
            S.op("pe", lambda e: e.matmul(p.P[0][:, :], lhsT=p.sTm[:, hh * 128:(hh + 1) * 128], rhs=p.v_all[:, c, hh * 512:(hh + 1) * 512],
                                          start=True, stop=True), reads=[p.b_sTm, p.b_v[c]], writes=[p.b_P[0]])
            for half in range(2):
                i = 2 * hh + half
                S.op("pe", lambda e: e.matmul(p.P[1][:, :], lhsT=p.qT[:, i, cs], rhs=p.Rb[:, i, :], start=(half == 0), stop=(half == 1)),
                     reads=[p.b_qT, p.b_Rb[i]], writes=[p.b_P[1]])
            S.op("act", lambda e: e.activation(out=p.wk[4][:], in_=p.P[1][:, :], func=AF.Copy, scale=p.ret_xi[:, hh:hh + 1]),
                 reads=[p.b_P[1], p.b_ret_xi], writes=[p.b_wk[4]])
            S.op("dve", lambda e: e.tensor_tensor(out=p.wk[hh][:], in0=p.wk[4][:], in1=p.P[0][:, :], op=ALU.add),
                 reads=[p.b_wk[4], p.b_P[0]], writes=[p.b_wk[hh]])
            self.state_update(c, hh, gam[hh])
            self.refresh_Rb(hh)
        self.groupnorm_heads([0, 1, 2, 3], c, gate=True)
        self.make_ynT(c, p.ret_gnT, p.b_ret_gnT)

    def ret_layer(self):
        p, S = self, self.S
        for i in range(8):
            S.op("dve", lambda e: e.memset(p.R[:, i, :], 0.0), writes=[p.b_R[i]])
        if p.stop_after == "dbg_ret":
            for hh in range(H):
                self.refresh_Rb(hh)
            self.load_gate(0, 0)
            self.dump("modT", p.modT[:], [p.b_modT])
            self.dump("gsc", p.gsc[:], [p.b_gsc])
            self.dump("gt_bc", p.gt_bc[:], [p.b_gt_bc])
            self.ret_group(0, full=True)
            return
        if p.phase in (None, 1):
            self.ret_part_a()
        if p.phase is None:
            self.exchange(1)
        if p.phase in (None, 2):
            self.ret_part_b()

    def ret_part_a(self):
        p, S = self, self.S
        for g in range(NG):
            self.ret_group(g, full=False)
        self.dma_group("sp", "loc1", [(p.loc[1][i * 128:(i + 1) * 128, :], p.R[:, i, :]) for i in range(8)],
                       reads=p.b_R, writes=[p.b_loc[1]])

    def ret_part_b(self):
        p, S = self, self.S
        self.combine_state(1, p.ret_coef, p.b_ret_coef, nrows=8)
        for hh in range(H):
            self.refresh_Rb(hh)
        self.load_gate(0, 0)
        for g in range(NG):
            self.ret_group(g, full=True)

    def combine_state(self, ex, coef, b_coef, nrows):
        p, S = self, self.S
        gt = p.gath[ex]
        per = nrows * 128 if ex == 1 else 9 * 128
        for i in range(8):
            hh = i // 2
            for cp in range(NCORES):
                tmp, bt = p.wk[cp % 2], p.b_wk[cp % 2]
                r0 = cp * per + i * 128
                S.dma("sp", tmp[:], gt[r0:r0 + 128, :], reads=[p.b_gath[ex]], writes=[bt], key=f"cmb{cp % 2}")
                if cp == 0:
                    S.op("dve", lambda e: e.tensor_scalar(out=p.R[:, i, :], in0=tmp[:], scalar1=coef[:, cp * H + hh: cp * H + hh + 1], scalar2=None,
                                                          op0=ALU.mult), reads=[bt, b_coef], writes=[p.b_R[i]])
                else:
                    S.op("dve", lambda e: e.scalar_tensor_tensor(out=p.R[:, i, :], in0=tmp[:], scalar=coef[:, cp * H + hh: cp * H + hh + 1],
                                                                 in1=p.R[:, i, :], op0=ALU.mult, op1=ALU.add),
                         reads=[bt, b_coef, p.b_R[i]], writes=[p.b_R[i]])

    def ffn_layer(self, l, src, src_bufs, dst, dst_bufs, ex_in, ex_out, final):
        p, S = self, self.S
        S.barrier()
        self.load_gate(l, 1)
        sh = p.modT[:, l, 24:32]
        gidx = 2 * l + 1
        actT = p.big_a[:, :].rearrange("p (k n) -> p k n", k=22)
        cw = p.ffn_cwT
        cb = p.ffn_cbT
        Wup = p.ffn_w_up[l]
        Wdn = p.ffn_w_down[l]
        for g in range(NG):
            if g == 0:
                self.load_halo(src, src_bufs, g, ex_in)
                self.norm_halo(gidx, sh)
            self.norm_group(src, src_bufs, g, gidx, sh)
            srcw = Wup.rearrange("(k p) n -> p k n", p=128)

            def load_up(pc):
                t, b, key = self.ring()
                wv = t[:, 0:4096].rearrange("p (k n) -> p k n", k=8)
                pairs = []
                for kh in range(2):
                    pairs.append((wv[:, 4 * kh:4 * kh + 4, 0:256], srcw[:, 4 * kh:4 * kh + 4, pc * 256:(pc + 1) * 256]))
                    pairs.append((wv[:, 4 * kh:4 * kh + 4, 256:512], srcw[:, 4 * kh:4 * kh + 4, DFF + pc * 256: DFF + (pc + 1) * 256]))
                self.dma_group("pool", key, pairs, writes=[b])
                return wv, b

            loaders = [(lambda pc=pc: load_up(pc)) for pc in range(11)]
            loaders += [(lambda cp=cp, kh=kh: self.load_rows_piece(Wdn, 11 * kh, 11, cp * 256, 256)) for cp in range(4) for kh in range(2)]
            loaded = {}

            def get_piece(i, ahead=2):
                for j in range(i, min(i + ahead + 1, len(loaders))):
                    if j not in loaded:
                        loaded[j] = loaders[j]()
                return loaded.pop(i)

            banks = [0, 1, 2, 3, 5]
            for pc in range(11):
                wv, b = get_piece(pc)
                for j in range(2):
                    tl = []
                    for ct in (j, 2 + j):
                        tile_i = 2 * pc + j
                        chan = tile_i if ct < 2 else 22 + tile_i
                        bi = banks[self.rot_bank % len(banks)]
                        self.rot_bank += 1
                        ai = self.rot_acc % 5
                        self.rot_acc += 1
                        ui = self.rot_ub % 2
                        self.rot_ub += 1
                        tl.append((ct, chan, bi, p.wk[ai], p.b_wk[ai], p.ubuf[ui], p.b_ubuf[ui]))
                    for ct, chan, bi, acc, b_acc, ub, b_ub in tl:
                        self.proj_A(wv, b, ct, bi)
                        if g == 0:
                            for kc in range(KC):
                                S.op("pe", lambda e: e.matmul(p.P[4][:, 0:HW], lhsT=wv[:, kc, ct * 128:(ct + 1) * 128], rhs=p.hTh[:, kc, 0:HW],
                                                              start=(kc == 0), stop=(kc == KC - 1)), reads=[b, p.b_hTh], writes=[p.b_P[4]])
                            S.op("dve", lambda e: e.tensor_scalar(out=ub[:, 0:HW], in0=p.P[4][:, 0:HW], scalar1=p.nf[:, 0:1], scalar2=None, op0=ALU.mult),
                                 reads=[p.b_P[4], p.b_nf], writes=[b_ub])
                        else:
                            S.op("act", lambda e: e.activation(out=ub[:, 0:HW], in_=p.uhalo[:, chan, :], func=AF.Copy), reads=[p.b_uhalo], writes=[b_ub])
                    for ct, chan, bi, acc, b_acc, ub, b_ub in tl:
                        S.op("act", lambda e: e.activation(out=ub[:, HW:HW + GT], in_=p.P[bi][:, :], func=AF.Copy), reads=[p.b_P[bi]], writes=[b_ub])
                        S.op("act", lambda e: e.activation(out=acc[:], in_=p.P[bi][:, :], func=AF.Identity, scale=cw[:, l, 88 + chan:89 + chan],
                                                           bias=cb[:, l, chan:chan + 1]), reads=[p.b_P[bi], p.b_ffn_cwT, p.b_ffn_cbT], writes=[b_acc])
                    if g < NG - 1:
                        for ct, chan, bi, acc, b_acc, ub, b_ub in tl:
                            S.op("act", lambda e: e.activation(out=p.uhalo[:, chan, :], in_=ub[:, GT:GT + HW], func=AF.Copy), reads=[b_ub], writes=[p.b_uhalo])
                    for tap, off in ((1, HW - 1), (0, HW - 2)):
                        for ct, chan, bi, acc, b_acc, ub, b_ub in tl:
                            S.op("dve", lambda e: e.scalar_tensor_tensor(out=acc[:], in0=ub[:, off:off + GT], scalar=cw[:, l, tap * 44 + chan:tap * 44 + chan + 1],
                                                                         in1=acc[:], op0=ALU.mult, op1=ALU.add), reads=[b_ub, b_acc, p.b_ffn_cwT], writes=[b_acc])
                    (_, _, _, aa, b_aa, _, _), (_, _, _, ab, b_ab, _, _) = tl
                    S.op("act", lambda e: e.activation(out=aa[:], in_=aa[:], func=AF.Silu), reads=[b_aa], writes=[b_aa])
                    S.op("dve", lambda e: e.tensor_tensor(out=actT[:, 2 * pc + j, :], in0=aa[:], in1=ab[:], op=ALU.mult),
                         reads=[b_aa, b_ab], writes=[p.b_actT[2 * pc + j]])
            for cp in range(4):
                for kh in range(2):
                    wv, wb = get_piece(11 + cp * 2 + kh)
                    for c in range(G):
                        for k in range(11):
                            S.op("pe", lambda e: e.matmul(p.P[c][:, 0:256], lhsT=actT[:, 11 * kh + k, c * CH:(c + 1) * CH], rhs=wv[:, k, :],
                                                          start=(kh == 0 and k == 0), stop=(kh == 1 and k == 10)),
                                 reads=[wb, p.b_actT[11 * kh + k]], writes=[p.b_P[c]])
                for c in range(G):
                    ti = self.rot_acc % 5
                    self.rot_acc += 1
                    tmp = p.wk[ti]
                    S.op("dve", lambda e: e.tensor_tensor(out=tmp[:, 0:256], in0=p.P[c][:, 0:256], in1=p.gt_bc[:, cp * 256:(cp + 1) * 256], op=ALU.mult),
                         reads=[p.b_P[c], p.b_gt_bc], writes=[p.b_wk[ti]])
                    S.op("dve", lambda e: e.tensor_tensor(out=p.x_g[:, c, cp * 256:(cp + 1) * 256], in0=p.x_g[:, c, cp * 256:(cp + 1) * 256],
                                                           in1=tmp[:, 0:256], op=ALU.add),
                         reads=[p.b_wk[ti], p.b_xg[c]], writes=[p.b_xg[c]])
            if final:
                self.final_norm_group()
            self.store_group(g, dst, dst_bufs, ex_out)
        S.barrier()

    def final_norm_group(self):
        p, S = self, self.S
        for c in range(G):
            S.op("act", lambda e: e.activation(out=p.xn[:], in_=p.x_g[:, c, :], func=AF.Square, accum_out=p.ss[:, c:c + 1]),
                 reads=[p.b_xg[c]], writes=[p.b_xn, p.b_ss])
        S.op("dve", lambda e: e.tensor_scalar(out=p.ss[:, 0:G], in0=p.ss[:, 0:G], scalar1=1.0 / D, scalar2=EPS, op0=ALU.mult, op1=ALU.add),
             reads=[p.b_ss], writes=[p.b_ss])
        S.op("act", lambda e: e.activation(out=p.ss[:, 0:G], in_=p.ss[:, 0:G], func=AF.Sqrt), reads=[p.b_ss], writes=[p.b_ss])
        S.op("dve", lambda e: e.reciprocal(out=p.rstd[:, 0:G], in_=p.ss[:, 0:G]), reads=[p.b_ss], writes=[p.b_rstd])
        for c in range(G):
            S.op("act", lambda e: e.activation(out=p.x_g[:, c, :], in_=p.x_g[:, c, :], func=AF.Copy, scale=p.rstd[:, c:c + 1]),
                 reads=[p.b_xg[c], p.b_rstd], writes=[p.b_xg[c]])
            for half, (t, b) in enumerate(((p.cos, p.b_cos), (p.sin, p.b_sin))):
                S.op("dve", lambda e: e.tensor_tensor(out=p.x_g[:, c, half * 512:(half + 1) * 512], in0=p.x_g[:, c, half * 512:(half + 1) * 512],
                                                      in1=t[:], op=ALU.mult), reads=[p.b_xg[c], b], writes=[p.b_xg[c]])

    LN16 = math.log(16.0)

    def ml_group(self, g, full):
        p, S = self, self.S
        sh = p.modT[:, 1, 0:8]
        W = p.ml_w_in
        if g == 0:
            self.load_halo(p.xb, p.b_xb, g, 3)
            self.norm_halo(2, sh)
        self.norm_group(p.xb, p.b_xb, g, 2, sh)
        reuse = full and p.relay
        plist = ([0, 1] if full else []) + ([] if reuse else [2, 3])
        if reuse:
            self.kv_load(g)
        for pc in plist:
            wv, wb = self.load_std_piece(W, pc * 512)
            dst, b_dst = (p.qT, p.b_qT) if pc < 2 else (p.kT, p.b_kT)
            for ct in range(4):
                cti = pc * 4 + ct
                self.proj_A(wv, wb, ct, ct)
                ub, b_ub = p.ubuf[ct % 2], p.b_ubuf[ct % 2]
                if g == 0:
                    for kc in range(KC):
                        S.op("pe", lambda e: e.matmul(p.P[4][:, 0:HW], lhsT=wv[:, kc, ct * 128:(ct + 1) * 128], rhs=p.hTh[:, kc, 0:HW],
                                                      start=(kc == 0), stop=(kc == KC - 1)), reads=[wb, p.b_hTh], writes=[p.b_P[4]])
                    S.op("dve", lambda e: e.tensor_scalar(out=ub[:, 0:HW], in0=p.P[4][:, 0:HW], scalar1=p.nf[:, 0:1], scalar2=None, op0=ALU.mult),
                         reads=[p.b_P[4], p.b_nf], writes=[b_ub])
                else:
                    S.op("dve", lambda e: e.tensor_copy(out=ub[:, 0:HW], in_=p.uhalo[:, cti, :]), reads=[p.b_uhalo], writes=[b_ub])
                S.op("act", lambda e: e.activation(out=ub[:, HW:HW + GT], in_=p.P[ct][:, :], func=AF.Copy), reads=[p.b_P[ct]], writes=[b_ub])
                if g < NG - 1:
                    S.op("dve", lambda e: e.tensor_copy(out=p.uhalo[:, cti, :], in_=ub[:, GT:GT + HW]), reads=[b_ub], writes=[p.b_uhalo])
                acc, b_acc = p.wk[ct], p.b_wk[ct]
                S.op("act", lambda e: e.activation(out=acc[:], in_=p.P[ct][:, :], func=AF.Identity, scale=p.ml_cwT[:, 48 + cti:49 + cti],
                                                   bias=p.ml_cbT[:, cti:cti + 1]), reads=[p.b_P[ct], p.b_ml_cwT, p.b_ml_cbT], writes=[b_acc])
                for j in (2, 1, 0):
                    off = HW - (3 - j)
                    S.op("dve", lambda e: e.scalar_tensor_tensor(out=acc[:], in0=ub[:, off:off + GT], scalar=p.ml_cwT[:, j * 16 + cti: j * 16 + cti + 1],
                                                                 in1=acc[:], op0=ALU.mult, op1=ALU.add), reads=[b_ub, b_acc, p.b_ml_cwT], writes=[b_acc])
                S.op("act", lambda e: e.activation(out=dst[:, cti % 8, :], in_=acc[:], func=AF.Silu), reads=[b_acc], writes=[b_dst])
        for hh in ([] if reuse else range(H)):
            wv, wb = self.load_std_piece(W, 2048 + hh * 512)
            for c in range(G):
                self.proj_B(wv, wb, c, c)
                dstv = p.v_all[:, c, hh * 512:(hh + 1) * 512]
                if c % 2 == 0:
                    S.op("act", lambda e: e.activation(out=dstv, in_=p.P[c][:, :], func=AF.Copy), reads=[p.b_P[c]], writes=[p.b_v[c]])
                else:
                    S.op("dve", lambda e: e.tensor_copy(out=dstv, in_=p.P[c][:, :]), reads=[p.b_P[c]], writes=[p.b_v[c]])
        if (not full) and p.relay:
            self.kv_store(g)
        wv, wb = self.load_std_piece(W, 6144, w=8)
        for c in range(G):
            self.proj_B(wv, wb, c, c, w=8)
            S.op("dve", lambda e: e.tensor_tensor(out=p.gat[:, c, :], in0=p.P[c][:, 0:8], in1=p.bg_bc[:], op=ALU.add),
                 reads=[p.b_P[c], p.b_bg_bc], writes=[p.b_gat])
        if full:
            for hh in range(H):
                wv, wb = self.load_std_piece(W, 4096 + hh * 512)
                for c in range(G):
                    self.proj_B(wv, wb, c, c)
                    dsts = p.sg_all[:, c, hh * 512:(hh + 1) * 512]
                    S.op("act", lambda e: e.activation(out=dsts, in_=p.P[c][:, :], func=AF.Sigmoid), reads=[p.b_P[c]], writes=[p.b_xg[c]])
        self.ml_gates_group(full)
        for c in range(G):
            self.ml_chunk(c, full)
        if full:
            self.out_proj(g, p.ml_w_out, p.xb, p.b_xb, p.xa, p.b_xa, 5)

    def ml_gates_group(self, full):
        p, S = self, self.S
        gm, bg = p.gm, p.b_gm
        v3 = lambda k: gm[:, k, :].rearrange("p (c h) -> p c h", h=4)
        z = p.gat[:, :, 4:8]
        li = p.gat[:, :, 0:4]
        S.op("act", lambda e: e.activation(out=v3(0), in_=z, func=AF.Exp, scale=-1.0), reads=[p.b_gat], writes=[bg])
        S.op("dve", lambda e: e.tensor_scalar(out=gm[:, 0, :], in0=gm[:, 0, :], scalar1=1.0, scalar2=None, op0=ALU.add), reads=[bg], writes=[bg])
        S.op("act", lambda e: e.activation(out=gm[:, 0, :], in_=gm[:, 0, :], func=AF.Ln), reads=[bg], writes=[bg])
        S.op("dve", lambda e: e.tensor_scalar(out=gm[:, 0, :], in0=gm[:, 0, :], scalar1=-1.0, scalar2=None, op0=ALU.mult), reads=[bg], writes=[bg])
        n = 4 * G
        S.op("pe", lambda e: e.matmul(p.P[5][:, 0:n], lhsT=p.ut[:, :], rhs=gm[:, 0, :], start=True, stop=True), reads=[p.b_ut, bg], writes=[p.b_P[5]])
        S.op("pe", lambda e: e.matmul(p.P[5][:, n:2 * n], lhsT=p.ones_f[:, :], rhs=gm[:, 0, :], start=True, stop=True), reads=[p.b_ones_f, bg], writes=[p.b_P[5]])
        S.op("dve", lambda e: e.tensor_copy(out=gm[:, 1, :], in_=p.P[5][:, 0:n]), reads=[p.b_P[5]], writes=[bg])
        S.op("dve", lambda e: e.tensor_copy(out=gm[:, 2, :], in_=p.P[5][:, n:2 * n]), reads=[p.b_P[5]], writes=[bg])
        S.op("dve", lambda e: e.tensor_tensor(out=gm[:, 3, :], in0=gm[:, 2, :], in1=gm[:, 1, :], op=ALU.subtract), reads=[bg], writes=[bg])
        S.op("dve", lambda e: e.scalar_tensor_tensor(out=v3(3), in0=v3(3), scalar=-self.LN16, in1=li, op0=ALU.add, op1=ALU.add),
             reads=[bg, p.b_gat], writes=[bg])
        S.op("act", lambda e: e.activation(out=gm[:, 3, :], in_=gm[:, 3, :], func=AF.Exp), reads=[bg], writes=[bg])
        S.op("act", lambda e: e.activation(out=gm[:, 4, :], in_=gm[:, 2, :], func=AF.Exp), reads=[bg], writes=[bg])
        gx = p.gmx[:].rearrange("p c (h t) -> p c h t", t=2)
        for t in range(2):
            S.op("dve", lambda e: e.tensor_copy(out=gx[:, :, :, t], in_=v3(4)), reads=[bg], writes=[p.b_gmx])
        if not full:
            for c in range(G):
                S.op("dve", lambda e: e.tensor_tensor(out=p.fsum[:], in0=p.fsum[:], in1=gm[:, 2, 4 * c:4 * c + 4], op=ALU.add),
                     reads=[bg, p.b_fsum], writes=[p.b_fsum])
        else:
            S.op("dve", lambda e: e.tensor_tensor(out=v3(5), in0=li, in1=v3(1), op=ALU.subtract), reads=[bg, p.b_gat], writes=[bg])
            S.op("dve", lambda e: e.tensor_scalar(out=gm[:, 5, :], in0=gm[:, 5, :], scalar1=-self.LN16, scalar2=None, op0=ALU.add), reads=[bg], writes=[bg])
            S.op("act", lambda e: e.activation(out=gm[:, 6, :], in_=gm[:, 1, :], func=AF.Exp), reads=[bg], writes=[bg])

    def ml_chunk(self, c, full):
        p, S = self, self.S
        cs = slice(c * CH, (c + 1) * CH)
        sm, bs = p.sm, p.b_sm
        gm, bg = p.gm, p.b_gm
        col = 4 * c
        g_b = lambda hh: gm[:, 1, col + hh:col + hh + 1]
        g_ws = lambda hh: gm[:, 3, col + hh:col + hh + 1]
        g_sp = lambda hh: gm[:, 4, col + hh:col + hh + 1]
        g_bj = lambda hh: gm[:, 5, col + hh:col + hh + 1]
        g_wi = lambda hh: gm[:, 6, col + hh:col + hh + 1]
        self.make_kz(c, lambda hh: (g_ws(hh), [bg]))
        if full:
            for hh in range(H):
                S.op("dve", lambda e: e.tensor_scalar(out=p.xn[:, hh * 128:(hh + 1) * 128], in0=p.ident_f[:, :], scalar1=g_b(hh), scalar2=None, op0=ALU.mult),
                     reads=[bg, p.b_ident_f], writes=[p.b_xn])
                S.op("pe", lambda e: e.matmul(p.P[5][:, hh * 128:(hh + 1) * 128], lhsT=p.ones_f[:, :], rhs=p.xn[:, hh * 128:(hh + 1) * 128], start=True, stop=False),
                     reads=[p.b_ones_f, p.b_xn], writes=[p.b_P[5]])
                S.op("pe", lambda e: e.matmul(p.P[5][:, hh * 128:(hh + 1) * 128], lhsT=p.ident_f[:, :], rhs=p.neg[:, :], start=False, stop=True),
                     reads=[p.b_ident_f, p.b_neg], writes=[p.b_P[5]])
            for hh in range(H):
                S.op("act", lambda e: e.activation(out=p.xn2[:, hh * 128:(hh + 1) * 128], in_=p.P[5][:, hh * 128:(hh + 1) * 128], func=AF.Exp,
                                                   bias=g_bj(hh)), reads=[p.b_P[5], bg], writes=[p.b_xn2])
            for hh in range(H):
                for half in range(2):
                    S.op("pe", lambda e: e.matmul(p.P[4][:, hh * 128:(hh + 1) * 128], lhsT=p.kT[:, 2 * hh + half, cs], rhs=p.qT[:, 2 * hh + half, cs],
                                                  start=(half == 0), stop=(half == 1)), reads=[p.b_kT, p.b_qT], writes=[p.b_P[4]])
            S.op("dve", lambda e: e.tensor_tensor(out=p.sTm[:], in0=p.P[4][:, :], in1=p.xn2[:, 0:512], op=ALU.mult),
                 reads=[p.b_P[4], p.b_xn2], writes=[p.b_sTm])
        if full:
            for hh in range(H):
                S.op("pe", lambda e: e.matmul(p.P[5][:, 2 * hh:2 * hh + 1], lhsT=p.sTm[:, hh * 128:(hh + 1) * 128], rhs=p.ones_b[:, 0:1], start=True, stop=True),
                     reads=[p.b_sTm, p.b_ones_b], writes=[p.b_P[5]])
                for half in range(2):
                    i = 2 * hh + half
                    S.op("pe", lambda e: e.matmul(p.P[5][:, 2 * hh + 1:2 * hh + 2], lhsT=p.qT[:, i, cs], rhs=p.nstb[:, i:i + 1], start=(half == 0), stop=(half == 1)),
                         reads=[p.b_qT, p.b_nstb], writes=[p.b_P[5]])
            S.op("dve", lambda e: e.tensor_copy(out=sm[:, 64:72], in_=p.P[5][:, 0:8]), reads=[p.b_P[5]], writes=[bs])
            dv = sm[:, 64:72].rearrange("p (h t) -> p h t", t=2)
            S.op("dve", lambda e: e.tensor_tensor(out=sm[:, 72:76], in0=dv[:, :, 1], in1=gm[:, 6, col:col + 4], op=ALU.mult), reads=[bs, bg], writes=[bs])
            S.op("dve", lambda e: e.tensor_tensor(out=sm[:, 72:76], in0=sm[:, 72:76], in1=dv[:, :, 0], op=ALU.add), reads=[bs], writes=[bs])
            S.op("dve", lambda e: e.scalar_tensor_tensor(out=sm[:, 76:80], in0=sm[:, 72:76], scalar=-1.0, in1=sm[:, 72:76], op0=ALU.mult, op1=ALU.max),
                 reads=[bs], writes=[bs])
            S.op("dve", lambda e: e.tensor_scalar(out=sm[:, 76:80], in0=sm[:, 76:80], scalar1=1.0, scalar2=None, op0=ALU.max), reads=[bs], writes=[bs])
            S.op("dve", lambda e: e.reciprocal(out=sm[:, 80:84], in_=sm[:, 76:80]), reads=[bs], writes=[bs])
        for hh in range(H):
            if full:
                S.op("pe", lambda e: e.matmul(p.P[0][:, :], lhsT=p.sTm[:, hh * 128:(hh + 1) * 128], rhs=p.v_all[:, c, hh * 512:(hh + 1) * 512],
                                              start=True, stop=True), reads=[p.b_sTm, p.b_v[c]], writes=[p.b_P[0]])
                for half in range(2):
                    i = 2 * hh + half
                    S.op("pe", lambda e: e.matmul(p.P[1][:, :], lhsT=p.qT[:, i, cs], rhs=p.Rb[:, i, :], start=(half == 0), stop=(half == 1)),
                         reads=[p.b_qT, p.b_Rb[i]], writes=[p.b_P[1]])
                S.op("act", lambda e: e.activation(out=p.wk[4][:], in_=p.P[1][:, :], func=AF.Copy, scale=g_wi(hh)),
                     reads=[p.b_P[1], bg], writes=[p.b_wk[4]])
                S.op("dve", lambda e: e.tensor_tensor(out=p.wk[hh][:], in0=p.wk[4][:], in1=p.P[0][:, :], op=ALU.add),
                     reads=[p.b_wk[4], p.b_P[0]], writes=[p.b_wk[hh]])
                so = p.sg_all[:, c, hh * 512:(hh + 1) * 512]
                S.op("dve", lambda e: e.scalar_tensor_tensor(out=p.wk[hh][:], in0=p.wk[hh][:], scalar=sm[:, 80 + hh:81 + hh], in1=so, op0=ALU.mult, op1=ALU.mult),
                     reads=[p.b_wk[hh], bs, p.b_xg[c]], writes=[p.b_wk[hh]])
            self.state_update(c, hh, g_sp(hh), [bg])
            if full:
                self.refresh_Rb(hh)
        for i in range(8):
            hh, half = i // 2, i % 2
            S.op("pe", lambda e: e.matmul(p.P[5][:, 16 + i:17 + i], lhsT=p.kz[:, hh * 256 + half * 128: hh * 256 + (half + 1) * 128],
                                          rhs=p.ones_b[:, 0:1], start=True, stop=True), reads=[p.b_kz, p.b_ones_b], writes=[p.b_P[5]])
        S.op("dve", lambda e: e.tensor_tensor(out=p.nst[:], in0=p.nst[:], in1=p.gmx[:, c, :], op=ALU.mult), reads=[p.b_nst, p.b_gmx], writes=[p.b_nst])
        S.op("dve", lambda e: e.tensor_tensor(out=p.nst[:], in0=p.nst[:], in1=p.P[5][:, 16:24], op=ALU.add), reads=[p.b_nst, p.b_P[5]], writes=[p.b_nst])
        if full:
            S.op("act", lambda e: e.activation(out=p.nstb[:], in_=p.nst[:], func=AF.Copy), reads=[p.b_nst], writes=[p.b_nstb])
        if full:
            self.groupnorm_heads([0, 1, 2, 3], c, gate=False)
            self.make_ynT(c, p.ml_gnT, p.b_ml_gnT)

    def ml_layer(self):
        p, S = self, self.S
        if p.phase in (None, 4):
            self.ml_part_a()
        if p.phase is None:
            self.exchange(4)
        if p.phase in (None, 5):
            self.ml_part_b()

    def ml_part_a(self):
        p, S = self, self.S
        for i in range(8):
            S.op("dve", lambda e: e.memset(p.R[:, i, :], 0.0), writes=[p.b_R[i]])
        S.op("dve", lambda e: e.memset(p.nst[:], 0.0), writes=[p.b_nst])
        S.op("dve", lambda e: e.memset(p.fsum[:], 0.0), writes=[p.b_fsum])
        for g in range(NG):
            self.ml_group(g, full=False)
        S.op("dve", lambda e: e.memset(p.wk[4][:], 0.0), writes=[p.b_wk[4]])
        S.op("dve", lambda e: e.tensor_copy(out=p.wk[4][:, 0:8], in_=p.nst[:]), reads=[p.b_nst], writes=[p.b_wk[4]])
        S.op("dve", lambda e: e.tensor_copy(out=p.wk[4][:, 8:12], in_=p.fsum[:]), reads=[p.b_fsum], writes=[p.b_wk[4]])
        pairs = [(p.loc[4][i * 128:(i + 1) * 128, :], p.R[:, i, :]) for i in range(8)] + [(p.loc[4][1024:1152, :], p.wk[4][:])]
        self.dma_group("sp", "loc4", pairs, reads=p.b_R + [p.b_wk[4]], writes=[p.b_loc[4]])

    def ml_part_b(self):
        p, S = self, self.S
        gt = p.gath[4]
        pairs = [(p.stage[cp:cp + 1, 0:4], gt[cp * 1152 + 1024: cp * 1152 + 1025, 8:12]) for cp in range(NCORES)]
        self.dma_group("sp", "stage", pairs, reads=[p.b_gath[4]], writes=[p.b_stage])
        for cp in range(NCORES):
            S.op("dve", lambda e: e.tensor_tensor(out=p.stage[0:8, 32 + cp * 4:36 + cp * 4], in0=p.msel[0:8, cp * 4:cp * 4 + 4], in1=p.stage[0:8, 0:4], op=ALU.mult),
                 reads=[p.b_stage, p.b_msel], writes=[p.b_stage])
        S.op("pe", lambda e: e.matmul(p.P[5][:, 0:32], lhsT=p.ones_f[0:8, :], rhs=p.stage[0:8, 32:64], start=True, stop=True),
             reads=[p.b_ones_f, p.b_stage], writes=[p.b_P[5]])
        S.op("act", lambda e: e.activation(out=p.mcoef[:], in_=p.P[5][:, 0:32], func=AF.Exp), reads=[p.b_P[5]], writes=[p.b_mcoef])
        S.op("dve", lambda e: e.tensor_tensor(out=p.mcoef[:], in0=p.mcoef[:], in1=p.valid[:], op=ALU.mult), reads=[p.b_mcoef, p.b_valid], writes=[p.b_mcoef])
        self.combine_state(4, p.mcoef, p.b_mcoef, nrows=9)
        S.op("dve", lambda e: e.memset(p.nst[:], 0.0), writes=[p.b_nst])
        for cp in range(NCORES):
            tmp, bt = p.wk[cp % 2], p.b_wk[cp % 2]
            r0 = cp * 1152 + 1024
            S.dma("sp", tmp[:, 0:8], gt[r0:r0 + 128, 0:8], reads=[p.b_gath[4]], writes=[bt], key=f"cmb{cp % 2}")
            for hh in range(H):
                S.op("dve", lambda e: e.scalar_tensor_tensor(out=p.nst[:, 2 * hh:2 * hh + 2], in0=tmp[:, 2 * hh:2 * hh + 2],
                                                             scalar=p.mcoef[:, cp * H + hh:cp * H + hh + 1], in1=p.nst[:, 2 * hh:2 * hh + 2],
                                                             op0=ALU.mult, op1=ALU.add), reads=[bt, p.b_mcoef, p.b_nst], writes=[p.b_nst])
        for hh in range(H):
            self.refresh_Rb(hh)
        S.op("act", lambda e: e.activation(out=p.nstb[:], in_=p.nst[:], func=AF.Copy), reads=[p.b_nst], writes=[p.b_nstb])
        self.load_gate(1, 0)
        for g in range(NG):
            self.ml_group(g, full=True)

    def _body(self):
        p, S = self, self.S
        ph = p.phase
        if ph in (None, 1, 2):
            self.ret_layer()
        if p.stop_after == "dbg_ret":
            return
        if p.stop_after == "ret":
            return self.copy_out(p.xa, p.b_xa)
        if ph is None:
            self.exchange(2)
        if ph in (None, 3):
            self.ffn_layer(0, p.xa, p.b_xa, p.xb, p.b_xb, 2, 3, final=False)
        if p.stop_after == "ffn0":
            return self.copy_out(p.xb, p.b_xb)
        if ph is None:
            self.exchange(3)
        if ph in (None, 4, 5):
            self.ml_layer()
        if p.stop_after == "ml":
            return self.copy_out(p.xa, p.b_xa)
        if ph is None:
            self.exchange(5)
        if ph in (None, 6):
            S.dma("sp", p.cos[:], p.final_g[0:1, 0:512].partition_broadcast(128).rearrange("p o n -> p (o n)"), writes=[p.b_cos], key="fg0")
            S.dma("sp", p.sin[:], p.final_g[0:1, 512:1024].partition_broadcast(128).rearrange("p o n -> p (o n)"), writes=[p.b_sin], key="fg1")
            self.ffn_layer(1, p.xa, p.b_xa, p.out, p.b_out, 5, None, final=True)

    def copy_out(self, src, src_bufs):
        p, S = self, self.S
        for g in range(NG):
            for c in range(G):
                r0 = g * GT + c * CH
                S.dma("sp", p.x_g[:, c, :], src[r0:r0 + CH, :], reads=src_bufs, writes=[p.b_xg[c]], key=f"xg{c}")
            pairs = [(p.out[g * GT + c * CH: g * GT + (c + 1) * CH, :], p.x_g[:, c, :]) for c in range(G)]
            self.dma_group("sp", "st_ou", pairs, reads=p.b_xg, writes=[p.b_out[g]])

    def _finish(self):
        p, S = self, self.S
        bufs = list(p.b_out) + [p.b_loc[k] for k in p.b_loc] + p.dbg_bufs + list(p.b_xa) + list(p.b_xb) + [p.b_modscr, p.b_modT_o, p.b_kvst]
        S.wait_bufs("sp", bufs)
        S.barrier()


_PROG_CACHE = {}
MODE = "host6"
STOP_AFTER = None


def _get_prog(mode, stop_after, phase=None):
    key = (mode, stop_after, phase)
    if key not in _PROG_CACHE:
        pr = Prog("host" if mode.startswith("host") else mode, stop_after, phase)
        pr.build()
        _PROG_CACHE[key] = pr
    return _PROG_CACHE[key]


def _in_maps(inputs):
    f = lambda a: np.ascontiguousarray(np.asarray(a), dtype=np.float32)
    tabs, lg = _const_tables()
    x = f(inputs["x"]).reshape(SEQ, D)
    pos = np.ascontiguousarray(np.asarray(inputs["positions"]).astype(np.int32)).reshape(SEQ)
    shared = {
        "cT": np.ascontiguousarray(f(inputs["c"]).reshape(KC, 128).T),
        "ada_w": f(inputs["ada_w"]),
        "ada_bT": np.ascontiguousarray(f(inputs["ada_b"]).reshape(2, 48, 128).transpose(2, 0, 1)),
        "ntgT": np.ascontiguousarray(f(inputs["norm_tok_g"]).reshape(2, KC, 128).transpose(2, 0, 1)),
        "nfgT": np.ascontiguousarray(f(inputs["norm_ffn_g"]).reshape(2, KC, 128).transpose(2, 0, 1)),
        "ret_w_in": f(inputs["ret_w_in"]).reshape(D, 6144),
        "ret_gnT": np.ascontiguousarray(f(inputs["ret_gn_g"]).reshape(16, 128).T),
        "ret_w_out": f(inputs["ret_w_out"]).reshape(2048, D),
        "ml_w_in": f(inputs["ml_w_in"]).reshape(D, 6152),
        "ml_b_gate": f(inputs["ml_b_gate"]).reshape(1, 8),
        "ml_cwT": np.ascontiguousarray(f(inputs["ml_conv_w"]).reshape(64, 128).T),
        "ml_cbT": np.ascontiguousarray(f(inputs["ml_conv_b"]).reshape(16, 128).T),
        "ml_gnT": np.ascontiguousarray(f(inputs["ml_gn_g"]).reshape(16, 128).T),
        "ml_w_out": f(inputs["ml_w_out"]).reshape(2048, D),
        "ffn_w_up": f(inputs["ffn_w_up"]),
        "ffn_cwT": np.ascontiguousarray(f(inputs["ffn_conv_w"]).reshape(2, 132, 128).transpose(2, 0, 1)),
        "ffn_cbT": np.ascontiguousarray(f(inputs["ffn_conv_b"]).reshape(2, 44, 128).transpose(2, 0, 1)),
        "ffn_w_down": f(inputs["ffn_w_down"]),
        "final_g": f(inputs["final_g"]).reshape(1, D),
    }
    shared.update(tabs)
    maps = []
    for c in range(NCORES):
        m = dict(shared)
        m["x"] = x[c * T:(c + 1) * T]
        m["pos"] = pos[c * T:(c + 1) * T].reshape(1, T)
        m.update(_core_tables(c, lg))
        maps.append(m)
    return maps


def _launch(pr, maps, extra):
    ms = []
    for c, m in enumerate(maps):
        mm = {k: v for k, v in m.items() if k in pr.in_names}
        for k, v in extra.items():
            if k in pr.in_names:
                mm[k] = v[c] if isinstance(v, list) else v
        ms.append(mm)
    return run_bass_kernel_spmd(pr.nc, ms, core_ids=list(range(NCORES))).results


def kernel(**inputs):
    maps = _in_maps(inputs)
    if MODE == "host6":
        extra = {}
        res = None
        for ph in range(1, 7):
            pr = _get_prog("host", None, ph)
            res = _launch(pr, maps, extra)
            for k in (1, 2, 3, 4, 5):
                if f"loc{k}" in res[0]:
                    extra[f"gath{k}"] = np.concatenate([res[c][f"loc{k}"] for c in range(NCORES)], axis=0)
            for nm in ("xa", "xb"):
                if nm in res[0]:
                    extra[nm] = [res[c][nm] for c in range(NCORES)]
            if "kst_o" in res[0]:
                extra["kst_i"] = [res[c]["kst_o"] for c in range(NCORES)]
                extra["vst_i"] = [res[c]["vst_o"] for c in range(NCORES)]
            if "modT_o" in res[0]:
                extra["modT_i"] = [res[c]["modT_o"] for c in range(NCORES)]
                extra["modscr_i"] = [res[c]["modscr_o"] for c in range(NCORES)]
    elif MODE == "host":
        pr = _get_prog("host", STOP_AFTER)
        extra = {f"gath{k}": np.zeros((NCORES * r, cdim), np.float32) for k, (r, cdim) in EX_SIZES.items()}
        order = {None: [1, 2, 3, 4, 5], "ret": [1], "ffn0": [1, 2], "ml": [1, 2, 3, 4]}[STOP_AFTER]
        res = None
        for step in range(len(order) + 1):
            res = _launch(pr, maps, extra)
            if step < len(order):
                k = order[step]
                extra[f"gath{k}"] = np.concatenate([res[c][f"loc{k}"] for c in range(NCORES)], axis=0)
    else:
        pr = _get_prog("cc", None)
        res = _launch(pr, maps, {})
    out = np.concatenate([res[c]["out"] for c in range(NCORES)], axis=0)
    return out.reshape(1, SEQ, D).astype(np.float32)
```

```python
import contextlib
import math
import numpy as np
import concourse.bass as bass
import concourse.mybir as mybir
from concourse.bass_utils import run_bass_kernel_spmd

F32 = mybir.dt.float32
BF16 = mybir.dt.bfloat16
I32 = mybir.dt.int32
AF = mybir.ActivationFunctionType
ALU = mybir.AluOpType

NCORES = 8
SEQ = 16384
D = 1024
T = SEQ // NCORES
CH = 128
G = 4
GT = G * CH
NG = T // GT
KC = D // 128
H = 4
DK = 256
DV = 512
DFF = 2816
EPS = 1e-6
HW = 3
TWO_PI = 2.0 * math.pi
C1 = 6.28125
C2 = TWO_PI - C1
PI_SAFE = 3.1415925


class Buf:
    __slots__ = ("name", "w", "r")

    def __init__(self, name):
        self.name = name
        self.w = None
        self.r = {}


class Sched:
    ENGS = ("pe", "dve", "act", "pool", "sp")

    def __init__(self, nc, stack):
        self.nc = nc
        self.stack = stack
        self.eng = {"pe": nc.tensor, "dve": nc.vector, "act": nc.scalar, "pool": nc.gpsimd, "sp": nc.sync}
        self.sems = {}
        self.cnt = {}
        self.seen = {e: {} for e in self.ENGS}
        for e in self.ENGS:
            self.sems[e] = stack.enter_context(nc.semaphore("s_" + e))
            self.cnt[e] = 0
        self.ninst = 0

    def buf(self, name):
        return Buf(name)

    def _dma_sem(self, key):
        k = "dma_" + key
        if k not in self.sems:
            self.sems[k] = self.stack.enter_context(self.nc.semaphore("s_" + k))
            self.cnt[k] = 0
        return k

    def _wait(self, e, deps):
        need = {}
        for d in deps:
            if d is None:
                continue
            k, v = d
            if k == e and e == "pe":
                continue
            if need.get(k, 0) < v:
                need[k] = v
        for k, v in need.items():
            if self.seen[e].get(k, 0) < v:
                self.eng[e].wait_ge(self.sems[k], v)
                self.seen[e][k] = v

    @staticmethod
    def _deps(reads, writes):
        deps = []
        for b in reads:
            deps.append(b.w)
        for b in writes:
            deps.append(b.w)
            deps.extend(b.r.items())
        return deps

    @staticmethod
    def _record(ev, reads, writes):
        k, v = ev
        for b in reads:
            if b.r.get(k, 0) < v:
                b.r[k] = v
        for b in writes:
            b.w = ev
            b.r = {}

    def op(self, e, fn, reads=(), writes=()):
        self._wait(e, self._deps(reads, writes))
        ins = fn(self.eng[e])
        self.cnt[e] += 1
        ins.then_inc(self.sems[e], 1)
        self.ninst += 1
        self._record((e, self.cnt[e]), reads, writes)
        return ins

    def dma(self, q, out, in_, reads=(), writes=(), key=None, **kw):
        k = self._dma_sem(key)
        self._wait(q, self._deps(reads, writes))
        ins = self.eng[q].dma_start(out=out, in_=in_, **kw)
        self.cnt[k] += 16
        ins.then_inc(self.sems[k], 16)
        self.ninst += 1
        self._record((k, self.cnt[k]), reads, writes)
        return ins

    def wait_bufs(self, e, bufs):
        deps = []
        for b in bufs:
            deps.append(b.w)
            deps.extend(b.r.items())
        self._wait(e, deps)

    def barrier(self):
        for e in self.ENGS:
            deps = [(k, v) for k, v in self.cnt.items() if v > 0]
            self._wait(e, deps)


def _const_tables():
    t = {}
    n = np.arange(128, dtype=np.float32)
    inv_freq = (10000.0 ** (-(np.arange(0, DK, 2, dtype=np.float32)) / DK)).astype(np.float32)
    t["inv_freq"] = inv_freq.reshape(128, 1).astype(np.float32)
    lg = np.log(1.0 - 2.0 ** (-5.0 - np.arange(H, dtype=np.float64)))
    i = np.arange(128)[None, :]
    j = np.arange(128)[:, None]
    dt = np.zeros((128, H, 128), np.float64)
    for h in range(H):
        dt[:, h, :] = np.where(i >= j, np.exp((i - j) * lg[h]), 0.0) * (DK ** -0.5)
    t["ret_dt"] = dt.reshape(128, H * 128).astype(np.float32)
    t["ret_xi"] = np.exp((np.arange(128)[:, None] + 1.0) * lg[None, :]).astype(np.float32)
    t["ret_zs"] = (np.exp((127.0 - np.arange(128)[:, None]) * lg[None, :]) * (DK ** -0.5)).astype(np.float32)
    t["neg"] = np.where(j <= i, 0.0, -30000.0).astype(np.float32)
    t["ut"] = np.where(j <= i, 1.0, 0.0).astype(np.float32)
    t["ident"] = np.eye(128, dtype=np.float32)
    return t, lg


def _core_tables(c, lg):
    sel = np.zeros((128, NCORES), np.float32)
    if c > 0:
        sel[:, c - 1] = 1.0
    nf = np.full((128, 1), 0.0 if c == 0 else 1.0, np.float32)
    rc = np.zeros((128, NCORES * H), np.float32)
    for cp in range(c):
        for h in range(H):
            rc[:, cp * H + h] = np.exp(T * (c - 1 - cp) * lg[h])
    valid = np.zeros((128, NCORES * H), np.float32)
    for cp in range(c):
        valid[:, cp * H:(cp + 1) * H] = 1.0
    msel = np.zeros((NCORES, NCORES, H), np.float32)
    for cpp in range(NCORES):
        for cp in range(NCORES):
            if cp < cpp < c:
                msel[cpp, cp, :] = 1.0
    selmat = np.zeros((NCORES * HW, HW), np.float32)
    if c > 0:
        for r in range(HW):
            selmat[(c - 1) * HW + r, r] = 1.0
    return {"selmat": selmat, "sel": sel, "nf": nf, "ret_coef": rc, "valid": valid, "msel": msel.reshape(NCORES, NCORES * H)}


EX_SIZES = {1: (8 * 128, 512), 2: (HW, D), 3: (HW, D), 4: (9 * 128, 512), 5: (HW, D)}


class Prog:
    def __init__(self, mode="host", stop_after=None, phase=None):
        self.mode = mode
        self.stop_after = stop_after
        self.phase = phase
        self.in_names = []
        self.layers = [0, 1] if phase in (None, 1) else ([0] if phase <= 3 else [1])
        self.mod_layers = [0, 1] if phase in (None, 1) else []
        self.nc = bass.Bass("TRN2", target_bir_lowering=False)
        self.st = contextlib.ExitStack()
        self.debug = stop_after is not None and stop_after.startswith("dbg")
        self.dbg_bufs = []

    def din(self, name, shape, dt=F32):
        self.in_names.append(name)
        return self.nc.dram_tensor(name, list(shape), dt, kind="ExternalInput").ap()

    def dout(self, name, shape, dt=F32):
        return self.nc.dram_tensor(name, list(shape), dt, kind="ExternalOutput").ap()

    def dint(self, name, shape, dt=F32):
        return self.nc.dram_tensor(name, list(shape), dt, kind="Internal").ap()

    def sb(self, name, shape, dt=F32):
        t = self.st.enter_context(self.nc.sbuf_tensor("sb_" + name, list(shape), dt))
        b = Buf(name)
        return t, b

    def ps(self, name, shape, dt=F32):
        t = self.st.enter_context(self.nc.psum_tensor("ps_" + name, list(shape), dt))
        return t

    def build(self):
        with self.st:
            self.S = Sched(self.nc, self.st)
            self._declare()
            self._setup()
            self._body()
            self._finish()
        return self.nc

    def _declare(self):
        p = self
        p.x = p.din("x", [T, D])
        p.pos = p.din("pos", [1, T], I32)
        p.c_in = p.din("cT", [128, KC])
        p.ada_w = p.din("ada_w", [2, D, 6 * D])
        p.ada_b = p.din("ada_bT", [128, 2, 48])
        p.ntg = p.din("ntgT", [128, 2, KC])
        p.nfg = p.din("nfgT", [128, 2, KC])
        p.ret_w_in = p.din("ret_w_in", [D, 6144])
        p.ret_gn = p.din("ret_gnT", [128, 16])
        p.ret_w_out = p.din("ret_w_out", [2048, D])
        p.ml_w_in = p.din("ml_w_in", [D, 6152])
        p.ml_bg = p.din("ml_b_gate", [1, 8])
        p.ml_cw = p.din("ml_cwT", [128, 64])
        p.ml_cb = p.din("ml_cbT", [128, 16])
        p.ml_gn = p.din("ml_gnT", [128, 16])
        p.ml_w_out = p.din("ml_w_out", [2048, D])
        p.ffn_w_up = p.din("ffn_w_up", [2, D, 2 * DFF])
        p.ffn_cw = p.din("ffn_cwT", [128, 2, 132])
        p.ffn_cb = p.din("ffn_cbT", [128, 2, 44])
        p.ffn_w_down = p.din("ffn_w_down", [2, DFF, D])
        p.final_g = p.din("final_g", [1, D])
        p.t_inv_freq = p.din("inv_freq", [128, 1])
        p.t_ret_dt = p.din("ret_dt", [128, 512])
        p.t_ret_xi = p.din("ret_xi", [128, 4])
        p.t_ret_zs = p.din("ret_zs", [128, 4])
        p.t_neg = p.din("neg", [128, 128])
        p.t_ut = p.din("ut", [128, 128])
        p.t_ident = p.din("ident", [128, 128])
        p.t_sel = p.din("sel", [128, NCORES])
        p.t_selmat = p.din("selmat", [NCORES * HW, HW])
        p.t_nf = p.din("nf", [128, 1])
        p.t_ret_coef = p.din("ret_coef", [128, NCORES * H])
        p.t_valid = p.din("valid", [128, NCORES * H])
        p.t_msel = p.din("msel", [NCORES, NCORES * H])
        ph = p.phase
        p.out = p.dout("out", [T, D]) if ph in (None, 6) or p.stop_after else None
        if ph is None:
            p.modscr = p.dint("modscr", [2, 48, 128])
        elif ph == 1:
            p.modscr = p.dout("modscr_o", [2, 48, 128])
            p.modT_o = p.dout("modT_o", [128, 96])
        else:
            p.modscr = p.din("modscr_i", [2, 48, 128])
            p.modT_i = p.din("modT_i", [128, 96])
        p.b_modscr = Buf("modscr")
        p.b_modT_o = Buf("modT_o")
        kinds = {None: ("int", "int"), 1: (None, None), 2: ("out", None), 3: ("in", "out"), 4: (None, "in"), 5: ("out", "in"), 6: ("in", None)}[ph]
        mk = {"int": p.dint, "in": p.din, "out": p.dout, None: (lambda *a: None)}
        p.xa = mk[kinds[0]]("xa", [T, D])
        p.xb = mk[kinds[1]]("xb", [T, D])
        p.relay = ph in (1, 2, 4, 5)
        if ph in (1, 4):
            p.kst = p.dout("kst_o", [NG, 128, 8 * GT], BF16)
            p.vst = p.dout("vst_o", [NG, 128, G * 2048], BF16)
        elif ph in (2, 5):
            p.kst = p.din("kst_i", [NG, 128, 8 * GT], BF16)
            p.vst = p.din("vst_i", [NG, 128, G * 2048], BF16)
        p.b_kvst = Buf("kvst")
        sizes = dict(EX_SIZES)
        loc_ph = {1: 1, 2: 2, 3: 3, 4: 4, 5: 5}
        gath_ph = {1: (2,), 2: (3,), 3: (4, 5), 4: (5,), 5: (6,)}
        p.loc = {}
        p.gath = {}
        for k, (r, cdim) in sizes.items():
            if p.mode == "host":
                if ph is None or loc_ph[k] == ph:
                    p.loc[k] = p.dout(f"loc{k}", [r, cdim])
                if ph is None or ph in gath_ph[k]:
                    p.gath[k] = p.din(f"gath{k}", [NCORES * r, cdim])
            else:
                p.loc[k] = p.dint(f"loc{k}", [r, cdim])
                p.gath[k] = p.dint(f"gath{k}", [NCORES * r, cdim])
        p.b_loc = {k: Buf(f"loc{k}") for k in sizes}
        p.b_gath = {k: Buf(f"gath{k}") for k in sizes}
        p.b_xa = [Buf(f"xa{g}") for g in range(NG)]
        p.b_xb = [Buf(f"xb{g}") for g in range(NG)]
        p.b_out = [Buf(f"out{g}") for g in range(NG)]

        p.ident_f, p.b_ident_f = p.sb("ident_f", [128, 128])
        p.ident_b, p.b_ident_b = p.sb("ident_b", [128, 128], BF16)
        p.ones_f, p.b_ones_f = p.sb("ones_f", [128, 128])
        p.ones_b, p.b_ones_b = p.sb("ones_b", [128, 8], BF16)
        p.ret_dt, p.b_ret_dt = p.sb("ret_dt", [128, 512])
        p.ret_xi, p.b_ret_xi = p.sb("ret_xi", [128, 4])
        p.ret_zs, p.b_ret_zs = p.sb("ret_zs", [128, 4])
        p.neg, p.b_neg = p.sb("negm", [128, 128])
        p.ut, p.b_ut = p.sb("utm", [128, 128])
        p.inv_freq, p.b_inv_freq = p.sb("inv_freq_s", [128, 1])
        p.sel, p.b_sel = p.sb("sel_s", [128, NCORES])
        p.selmat, p.b_selmat = p.sb("selmat_s", [NCORES * HW, HW])
        p.adabT, p.b_adabT = p.sb("adabT", [128, 2, 48])
        p.cT_f, p.b_cT_f = p.sb("cT_f", [128, KC])
        p.nf, p.b_nf = p.sb("nf_s", [128, 1])
        p.ret_coef, p.b_ret_coef = p.sb("ret_coef_s", [128, NCORES * H])
        p.valid, p.b_valid = p.sb("valid_s", [128, NCORES * H])
        p.msel, p.b_msel = p.sb("msel_s", [NCORES, NCORES * H])
        p.consts = [p.b_ident_f, p.b_ident_b, p.b_ones_f, p.b_ones_b]
        p.modT, p.b_modT = p.sb("modT", [128, 2, 48])
        p.ntgT, p.b_ntgT = p.sb("ntgT", [128, 2, KC])
        p.nfgT, p.b_nfgT = p.sb("nfgT", [128, 2, KC])
        p.gsc, p.b_gsc = p.sb("gsc", [128, 4, KC])
        p.ret_gnT, p.b_ret_gnT = p.sb("ret_gnT", [128, 16])
        p.ml_gnT, p.b_ml_gnT = p.sb("ml_gnT", [128, 16])
        p.ml_cwT, p.b_ml_cwT = p.sb("ml_cwT", [128, 64])
        p.ml_cbT, p.b_ml_cbT = p.sb("ml_cbT", [128, 16])
        p.ffn_cwT, p.b_ffn_cwT = p.sb("ffn_cwT", [128, 2, 132])
        p.ffn_cbT, p.b_ffn_cbT = p.sb("ffn_cbT", [128, 2, 44])
        p.bg_bc, p.b_bg_bc = p.sb("bg_bc", [128, 8])
        p.gt_bc, p.b_gt_bc = p.sb("gt_bc", [128, D])
        p.cT_b, p.b_cT_b = p.sb("cT_b", [128, KC], BF16)
        p.stage, p.b_stage = p.sb("stage", [128, 128])
        p.NR = 3
        p.wt = []
        p.b_wt = []
        for i in range(p.NR):
            t, b = p.sb(f"wt{i}", [128, 4096], BF16)
            p.wt.append(t)
            p.b_wt.append(b)
        p.ring_pos = 0
        p.rot_bank = 0
        p.rot_acc = 0
        p.rot_ub = 0
        p.b_actT = [Buf(f"actT{i}") for i in range(22)]
        p.x_g, _ = p.sb("x_g", [128, G, D])
        p.b_xg = [Buf(f"xg{c}") for c in range(G)]
        p.sg_all = p.x_g[:].bitcast(BF16)
        p.xn, p.b_xn = p.sb("xn", [128, D])
        p.xn2, p.b_xn2 = p.sb("xn2", [128, D])
        p.junk, p.b_junk = p.sb("junk", [128, D], BF16)
        p.uhalo, p.b_uhalo = p.sb("uhalo", [128, 44, HW])
        p.ss, p.b_ss = p.sb("ss", [128, 8])
        p.rstd, p.b_rstd = p.sb("rstd", [128, 8])
        p.hT, p.b_hT = p.sb("hT", [128, KC, GT], BF16)
        p.xh, p.b_xh = p.sb("xh", [32, D])
        p.hTh, p.b_hTh = p.sb("hTh", [128, KC, 32], BF16)
        p.big_a, p.b_big_a = p.sb("big_a", [128, 22 * GT], BF16)
        p.qT, p.b_qT = p.sb("qT", [128, 8, GT], BF16)
        p.kT, p.b_kT = p.sb("kT", [128, 8, GT], BF16)
        p.v_all, _ = p.sb("v_all", [128, G, 2048], BF16)
        p.b_v = [Buf(f"v{c}") for c in range(G)]
        p.R, _ = p.sb("R", [128, 8, 512])
        p.b_R = [Buf(f"R{i}") for i in range(8)]
        p.Rb, _ = p.sb("Rb", [128, 8, 512], BF16)
        p.b_Rb = [Buf(f"Rb{i}") for i in range(8)]
        p.nst, p.b_nst = p.sb("nst", [128, 8])
        p.nstb, p.b_nstb = p.sb("nstb", [128, 8], BF16)
        p.fsum, p.b_fsum = p.sb("fsum", [128, 4])
        p.wk = []
        p.b_wk = []
        for i in range(5):
            t, b = p.sb(f"wk{i}", [128, 512])
            p.wk.append(t)
            p.b_wk.append(b)
        p.cos, p.b_cos = p.sb("cos", [128, GT])
        p.sin, p.b_sin = p.sb("sin", [128, GT])
        p.yg, p.b_yg = p.sb("yg", [128, 2048], BF16)
        p.sTm, p.b_sTm = p.sb("sTm", [128, 512], BF16)
        p.kz, p.b_kz = p.sb("kz", [128, 1024], BF16)
        p.sm, p.b_sm = p.sb("sm", [128, 96])
        p.mcoef, p.b_mcoef = p.sb("mcoef", [128, NCORES * H])
        p.gat, p.b_gat = p.sb("gat", [128, G, 8])
        p.gmx, p.b_gmx = p.sb("gmx", [128, G, 8])
        p.gm, p.b_gm = p.sb("gm", [128, 7, 4 * G])
        p.ubuf = []
        p.b_ubuf = []
        for i in range(2):
            t, b = p.sb(f"ubuf{i}", [128, HW + GT])
            p.ubuf.append(t)
            p.b_ubuf.append(b)
        p.P = [p.ps(f"P{i}", [128, 512]) for i in range(6)]
        p.b_P = [Buf(f"P{i}") for i in range(6)]
        p.Pb = [p.ps(f"Pb{i}", [128, 1024], BF16) for i in range(2)]
        p.b_Pb = [Buf(f"Pb{i}") for i in range(2)]

    def dma_group(self, q, key, pairs, reads=(), writes=()):
        S = self.S
        k = S._dma_sem(key)
        S._wait(q, S._deps(reads, writes))
        for out, in_ in pairs:
            ins = S.eng[q].dma_start(out=out, in_=in_)
            S.cnt[k] += 16
            ins.then_inc(S.sems[k], 16)
            S.ninst += 1
        S._record((k, S.cnt[k]), reads, writes)

    def dump(self, name, ap, bufs):
        if not self.debug:
            return
        shape = list(ap.shape)
        d = self.nc.dram_tensor("dbg_" + name, shape, ap.dtype, kind="ExternalOutput").ap()
        b = Buf("dbg_" + name)
        self.dbg_bufs.append(b)
        self.S.dma("sp", d, ap, reads=bufs, writes=[b], key="dbg_" + name)

    def load_T(self, src_rows, n, dst_ap, dst_buf):
        p, S = self, self.S
        S.dma("sp", p.stage[0:n, :], src_rows, writes=[p.b_stage], key="stage")
        S.op("pe", lambda e: e.transpose(p.P[5][:, 0:n], p.stage[0:n, :], p.ident_f[0:n, 0:n]),
             reads=[p.b_stage, p.b_ident_f], writes=[p.b_P[5]])
        S.op("dve", lambda e: e.tensor_copy(out=dst_ap, in_=p.P[5][:, 0:n]), reads=[p.b_P[5]], writes=[dst_buf])

    def ring(self):
        i = self.ring_pos % self.NR
        self.ring_pos += 1
        return self.wt[i], self.b_wt[i], f"w{i}"

    def load_std_piece(self, W2d, c0, w=512):
        t, b, key = self.ring()
        view = t[:, 0:8 * w].rearrange("p (k n) -> p k n", k=8)
        src = W2d.rearrange("(k p) n -> p k n", p=128)
        pairs = [(view[:, 0:4, :], src[:, 0:4, c0:c0 + w]), (view[:, 4:8, :], src[:, 4:8, c0:c0 + w])]
        self.dma_group("pool", key, pairs, writes=[b])
        return view, b

    def load_rows_piece(self, W2d, k0, nk, c0, w):
        t, b, key = self.ring()
        view = t[:, 0:nk * w].rearrange("p (k n) -> p k n", k=nk)
        src = W2d.rearrange("(k p) n -> p k n", p=128)
        pairs = []
        step = 4
        for a in range(0, nk, step):
            e = min(nk, a + step)
            pairs.append((view[:, a:e, :], src[:, k0 + a:k0 + e, c0:c0 + w]))
        self.dma_group("pool", key, pairs, writes=[b])
        return view, b

    def _setup(self):
        p, S = self, self.S
        loads = [(p.ident_f, p.b_ident_f, p.t_ident), (p.ret_dt, p.b_ret_dt, p.t_ret_dt),
                 (p.ret_xi, p.b_ret_xi, p.t_ret_xi), (p.ret_zs, p.b_ret_zs, p.t_ret_zs),
                 (p.neg, p.b_neg, p.t_neg), (p.ut, p.b_ut, p.t_ut), (p.inv_freq, p.b_inv_freq, p.t_inv_freq),
                 (p.sel, p.b_sel, p.t_sel), (p.nf, p.b_nf, p.t_nf), (p.ret_coef, p.b_ret_coef, p.t_ret_coef),
                 (p.valid, p.b_valid, p.t_valid), (p.msel, p.b_msel, p.t_msel)]
        self.dma_group("sp", "setup", [(t[:], src) for t, b, src in loads], writes=[b for t, b, s in loads])
        S.dma("sp", p.bg_bc[:], p.ml_bg[0:1, :].partition_broadcast(128).rearrange("p o n -> p (o n)"),
              writes=[p.b_bg_bc], key="setup2")
        S.op("dve", lambda e: e.memset(p.ones_f[:], 1.0), writes=[p.b_ones_f])
        S.op("dve", lambda e: e.memset(p.ones_b[:], 1.0), writes=[p.b_ones_b])
        S.op("dve", lambda e: e.memset(p.xh[:], 0.0), writes=[p.b_xh])
        S.op("dve", lambda e: e.tensor_copy(out=p.ident_b[:], in_=p.ident_f[:]), reads=[p.b_ident_f], writes=[p.b_ident_b])
        vec = [(p.ntgT, p.b_ntgT, p.ntg), (p.nfgT, p.b_nfgT, p.nfg), (p.ffn_cwT, p.b_ffn_cwT, p.ffn_cw), (p.ffn_cbT, p.b_ffn_cbT, p.ffn_cb),
               (p.ret_gnT, p.b_ret_gnT, p.ret_gn), (p.ml_gnT, p.b_ml_gnT, p.ml_gn), (p.ml_cwT, p.b_ml_cwT, p.ml_cw), (p.ml_cbT, p.b_ml_cbT, p.ml_cb),
               (p.adabT, p.b_adabT, p.ada_b), (p.cT_f, p.b_cT_f, p.c_in), (p.selmat, p.b_selmat, p.t_selmat)]
        self.dma_group("sp", "setup3", [(t[:], s) for t, b, s in vec], writes=[b for t, b, s in vec])
        if p.mod_layers:
            S.op("act", lambda e: e.activation(out=p.cT_b[:], in_=p.cT_f[:], func=AF.Silu), reads=[p.b_cT_f], writes=[p.b_cT_b])
            for l in p.mod_layers:
                for pc in range(12):
                    wv, wb = self.load_std_piece(p.ada_w[l], pc * 512)
                    for ct in range(4):
                        col = l * 48 + pc * 4 + ct
                        for kc in range(KC):
                            S.op("pe", lambda e: e.matmul(p.P[4][:, col:col + 1], lhsT=wv[:, kc, ct * 128:(ct + 1) * 128],
                                                          rhs=p.cT_b[:, kc:kc + 1], start=(kc == 0), stop=(kc == KC - 1)),
                                 reads=[wb, p.b_cT_b], writes=[p.b_P[4]])
            for l in p.mod_layers:
                S.op("dve", lambda e: e.tensor_tensor(out=p.modT[:, l, :], in0=p.adabT[:, l, :], in1=p.P[4][:, l * 48:(l + 1) * 48], op=ALU.add),
                     reads=[p.b_P[4], p.b_adabT], writes=[p.b_modT])
                S.op("pe", lambda e: e.transpose(p.P[5][0:48, 0:128], p.modT[:, l, :], p.ident_f[:, :]),
                     reads=[p.b_modT, p.b_ident_f], writes=[p.b_P[5]])
                S.op("dve", lambda e: e.tensor_copy(out=p.stage[0:48, :], in_=p.P[5][0:48, 0:128]), reads=[p.b_P[5]], writes=[p.b_stage])
                S.dma("sp", p.modscr[l], p.stage[0:48, :], reads=[p.b_stage], writes=[p.b_modscr], key="modscr")
            if p.phase == 1:
                S.dma("sp", p.modT_o[:, :], p.modT[:].rearrange("p l n -> p (l n)"), reads=[p.b_modT], writes=[p.b_modT_o], key="modT_o")
        else:
            S.dma("sp", p.modT[:].rearrange("p l n -> p (l n)"), p.modT_i[:, :], writes=[p.b_modT], key="modT_i")
        for l in p.layers:
            S.op("dve", lambda e: e.scalar_tensor_tensor(out=p.gsc[:, 2 * l, :], in0=p.modT[:, l, 8:16], scalar=1.0,
                                                         in1=p.ntgT[:, l, :], op0=ALU.add, op1=ALU.mult),
                 reads=[p.b_modT, p.b_ntgT], writes=[p.b_gsc])
            S.op("dve", lambda e: e.scalar_tensor_tensor(out=p.gsc[:, 2 * l + 1, :], in0=p.modT[:, l, 32:40], scalar=1.0,
                                                         in1=p.nfgT[:, l, :], op0=ALU.add, op1=ALU.mult),
                 reads=[p.b_modT, p.b_nfgT], writes=[p.b_gsc])

    def load_gate(self, l, which):
        p, S = self, self.S
        r0 = 16 if which == 0 else 40
        src = p.modscr[l, r0:r0 + 8, :].rearrange("(o a) b -> o (a b)", o=1).partition_broadcast(128).rearrange("p o n -> p (o n)")
        S.dma("sp", p.gt_bc[:], src, reads=[p.b_modscr], writes=[p.b_gt_bc], key="gt")

    def norm_rows(self, xt, bx, npart, gidx, sh, dst_fn, dst_buf, col):
        p, S = self, self.S
        S.op("act", lambda e: e.activation(out=p.xn[0:npart, :], in_=xt, func=AF.Square, accum_out=p.ss[0:npart, col:col + 1]),
             reads=[bx], writes=[p.b_xn, p.b_ss])
        S.op("dve", lambda e: e.tensor_scalar(out=p.ss[0:npart, col:col + 1], in0=p.ss[0:npart, col:col + 1], scalar1=1.0 / D, scalar2=EPS,
                                              op0=ALU.mult, op1=ALU.add), reads=[p.b_ss], writes=[p.b_ss])
        S.op("act", lambda e: e.activation(out=p.ss[0:npart, col:col + 1], in_=p.ss[0:npart, col:col + 1], func=AF.Sqrt),
             reads=[p.b_ss], writes=[p.b_ss])
        S.op("dve", lambda e: e.reciprocal(out=p.rstd[0:npart, col:col + 1], in_=p.ss[0:npart, col:col + 1]),
             reads=[p.b_ss], writes=[p.b_rstd])
        S.op("act", lambda e: e.activation(out=p.xn[0:npart, :], in_=xt, func=AF.Copy, scale=p.rstd[0:npart, col:col + 1]),
             reads=[bx, p.b_rstd], writes=[p.b_xn])
        for half in range(2):
            bank = p.P[half]
            for k4 in range(4):
                kc = half * 4 + k4
                S.op("pe", lambda e: e.transpose(bank[:, k4 * 128:k4 * 128 + npart], p.xn[0:npart, kc * 128:(kc + 1) * 128],
                                                 p.ident_f[0:npart, 0:npart]),
                     reads=[p.b_xn, p.b_ident_f], writes=[p.b_P[half]])
            for k4 in range(4):
                kc = half * 4 + k4
                src = bank[:, k4 * 128:k4 * 128 + npart]
                if kc % 2 == 0:
                    S.op("act", lambda e: e.activation(out=dst_fn(kc), in_=src, func=AF.Identity,
                                                       scale=p.gsc[:, gidx, kc:kc + 1], bias=sh[:, kc:kc + 1]),
                         reads=[p.b_P[half], p.b_gsc, p.b_modT], writes=[dst_buf])
                else:
                    S.op("dve", lambda e: e.tensor_scalar(out=dst_fn(kc), in0=src, scalar1=p.gsc[:, gidx, kc:kc + 1],
                                                          scalar2=sh[:, kc:kc + 1], op0=ALU.mult, op1=ALU.add),
                         reads=[p.b_P[half], p.b_gsc, p.b_modT], writes=[dst_buf])

    def norm_group(self, src, src_bufs, g, gidx, sh):
        p, S = self, self.S
        for c in range(G):
            r0 = g * GT + c * CH
            S.dma("sp", p.x_g[:, c, :], src[r0:r0 + CH, :], reads=src_bufs, writes=[p.b_xg[c]], key=f"xg{c}")
        for c in range(G):
            S.op("act", lambda e: e.activation(out=p.junk[:], in_=p.x_g[:, c, :], func=AF.Square, accum_out=p.ss[:, c:c + 1]),
                 reads=[p.b_xg[c]], writes=[p.b_junk, p.b_ss])
        S.op("dve", lambda e: e.tensor_scalar(out=p.ss[:, 0:G], in0=p.ss[:, 0:G], scalar1=1.0 / D, scalar2=EPS, op0=ALU.mult, op1=ALU.add),
             reads=[p.b_ss], writes=[p.b_ss])
        S.op("act", lambda e: e.activation(out=p.ss[:, 0:G], in_=p.ss[:, 0:G], func=AF.Sqrt), reads=[p.b_ss], writes=[p.b_ss])
        S.op("dve", lambda e: e.reciprocal(out=p.rstd[:, 0:G], in_=p.ss[:, 0:G]), reads=[p.b_ss], writes=[p.b_rstd])
        for c in range(G):
            xn, b_xn = (p.xn, p.b_xn) if c % 2 == 0 else (p.xn2, p.b_xn2)
            S.op("act", lambda e: e.activation(out=xn[:], in_=p.x_g[:, c, :], func=AF.Copy, scale=p.rstd[:, c:c + 1]),
                 reads=[p.b_xg[c], p.b_rstd], writes=[b_xn])
            for half in range(2):
                bi = 2 * (c % 2) + half
                bank = p.P[bi]
                for k4 in range(4):
                    kc = half * 4 + k4
                    S.op("pe", lambda e: e.transpose(bank[:, k4 * 128:(k4 + 1) * 128], xn[:, kc * 128:(kc + 1) * 128], p.ident_f[:, :]),
                         reads=[b_xn, p.b_ident_f], writes=[p.b_P[bi]])
                for k4 in range(4):
                    kc = half * 4 + k4
                    srcp = bank[:, k4 * 128:(k4 + 1) * 128]
                    dst = p.hT[:, kc, c * CH:(c + 1) * CH]
                    if kc % 2 == 0:
                        S.op("act", lambda e: e.activation(out=dst, in_=srcp, func=AF.Identity,
                                                           scale=p.gsc[:, gidx, kc:kc + 1], bias=sh[:, kc:kc + 1]),
                             reads=[p.b_P[bi], p.b_gsc, p.b_modT], writes=[p.b_hT])
                    else:
                        S.op("dve", lambda e: e.tensor_scalar(out=dst, in0=srcp, scalar1=p.gsc[:, gidx, kc:kc + 1],
                                                              scalar2=sh[:, kc:kc + 1], op0=ALU.mult, op1=ALU.add),
                             reads=[p.b_P[bi], p.b_gsc, p.b_modT], writes=[p.b_hT])

    def load_halo(self, src, src_bufs, g, ex):
        p, S = self, self.S
        if g > 0:
            r0 = g * GT - HW
            S.dma("sp", p.xh[0:HW, :], src[r0:r0 + HW, :], reads=src_bufs, writes=[p.b_xh], key="xh")
        else:
            gt = p.gath[ex]
            nr = NCORES * HW
            S.dma("sp", p.xn[0:nr, :], gt[:, :], reads=[p.b_gath[ex]], writes=[p.b_xn], key="xnh")
            for half in range(2):
                S.op("pe", lambda e: e.matmul(p.P[half][0:HW, :], lhsT=p.selmat[0:nr, 0:HW], rhs=p.xn[0:nr, half * 512:(half + 1) * 512],
                                              start=True, stop=True), reads=[p.b_selmat, p.b_xn], writes=[p.b_P[half]])
                S.op("dve", lambda e: e.tensor_copy(out=p.xh[0:HW, half * 512:(half + 1) * 512], in_=p.P[half][0:HW, :]),
                     reads=[p.b_P[half]], writes=[p.b_xh])

    def norm_halo(self, gidx, sh):
        p = self
        self.norm_rows(p.xh[0:32, :], p.b_xh, 32, gidx, sh, lambda kc: p.hTh[:, kc, :], p.b_hTh, 4)

    def rope_tables(self, g):
        p, S = self, self.S
        posi = p.wk[4][:].bitcast(I32)
        src = p.pos[0:1, g * GT:(g + 1) * GT].partition_broadcast(128).rearrange("p o n -> p (o n)")
        S.dma("sp", posi, src, writes=[p.b_wk[4]], key="posi")
        ang, b_ang = p.wk[0], p.b_wk[0]
        S.op("dve", lambda e: e.tensor_copy(out=p.wk[1][:], in_=posi), reads=[p.b_wk[4]], writes=[p.b_wk[1]])
        S.op("dve", lambda e: e.tensor_scalar(out=ang[:], in0=p.wk[1][:], scalar1=p.inv_freq[:, 0:1], scalar2=None, op0=ALU.mult),
             reads=[p.b_wk[1], p.b_inv_freq], writes=[b_ang])
        for dst, b_dst, shift in ((p.sin, p.b_sin, 0.0), (p.cos, p.b_cos, 0.5 * math.pi)):
            xs, b_xs = p.wk[1], p.b_wk[1]
            kf, b_kf = p.wk[2], p.b_wk[2]
            ki = p.wk[3][:].bitcast(I32)
            b_ki = p.b_wk[3]
            S.op("dve", lambda e: e.tensor_scalar(out=xs[:], in0=ang[:], scalar1=shift, scalar2=None, op0=ALU.add),
                 reads=[b_ang], writes=[b_xs])
            S.op("dve", lambda e: e.tensor_scalar(out=kf[:], in0=xs[:], scalar1=1.0 / TWO_PI, scalar2=None, op0=ALU.mult),
                 reads=[b_xs], writes=[b_kf])
            S.op("dve", lambda e: e.tensor_copy(out=ki, in_=kf[:]), reads=[b_kf], writes=[b_ki])
            S.op("dve", lambda e: e.tensor_copy(out=kf[:], in_=ki), reads=[b_ki], writes=[b_kf])
            S.op("dve", lambda e: e.scalar_tensor_tensor(out=xs[:], in0=kf[:], scalar=-C1, in1=xs[:], op0=ALU.mult, op1=ALU.add),
                 reads=[b_kf, b_xs], writes=[b_xs])
            S.op("dve", lambda e: e.scalar_tensor_tensor(out=xs[:], in0=kf[:], scalar=-C2, in1=xs[:], op0=ALU.mult, op1=ALU.add),
                 reads=[b_kf, b_xs], writes=[b_xs])
            S.op("dve", lambda e: e.tensor_scalar(out=kf[:], in0=xs[:], scalar1=-math.pi, scalar2=TWO_PI, op0=ALU.is_lt, op1=ALU.mult),
                 reads=[b_xs], writes=[b_kf])
            S.op("dve", lambda e: e.tensor_tensor(out=xs[:], in0=xs[:], in1=kf[:], op=ALU.add), reads=[b_xs, b_kf], writes=[b_xs])
            S.op("dve", lambda e: e.tensor_scalar(out=kf[:], in0=xs[:], scalar1=math.pi, scalar2=-TWO_PI, op0=ALU.is_gt, op1=ALU.mult),
                 reads=[b_xs], writes=[b_kf])
            S.op("dve", lambda e: e.tensor_tensor(out=xs[:], in0=xs[:], in1=kf[:], op=ALU.add), reads=[b_xs, b_kf], writes=[b_xs])
            S.op("dve", lambda e: e.tensor_scalar(out=xs[:], in0=xs[:], scalar1=-PI_SAFE, scalar2=PI_SAFE, op0=ALU.max, op1=ALU.min),
                 reads=[b_xs], writes=[b_xs])
            S.op("act", lambda e: e.activation(out=dst[:], in_=xs[:], func=AF.Sin), reads=[b_xs], writes=[b_dst])

    def rope_pair(self, hh, dst, b_dst):
        p, S = self, self.S
        i1, i2 = 2 * (hh % 2), 2 * (hh % 2) + 1
        b1, b2 = p.P[i1], p.P[i2]
        A, Bm, C_, Dm = p.wk[0], p.wk[1], p.wk[2], p.wk[3]
        S.op("dve", lambda e: e.tensor_tensor(out=A[:], in0=b1[:], in1=p.cos[:], op=ALU.mult), reads=[p.b_P[i1], p.b_cos], writes=[p.b_wk[0]])
        S.op("dve", lambda e: e.tensor_tensor(out=Bm[:], in0=b2[:], in1=p.sin[:], op=ALU.mult), reads=[p.b_P[i2], p.b_sin], writes=[p.b_wk[1]])
        S.op("dve", lambda e: e.tensor_tensor(out=C_[:], in0=b1[:], in1=p.sin[:], op=ALU.mult), reads=[p.b_P[i1], p.b_sin], writes=[p.b_wk[2]])
        S.op("dve", lambda e: e.tensor_tensor(out=Dm[:], in0=b2[:], in1=p.cos[:], op=ALU.mult), reads=[p.b_P[i2], p.b_cos], writes=[p.b_wk[3]])
        S.op("dve", lambda e: e.tensor_tensor(out=dst[:, 2 * hh, :], in0=A[:], in1=Bm[:], op=ALU.subtract),
             reads=[p.b_wk[0], p.b_wk[1]], writes=[b_dst])
        S.op("dve", lambda e: e.tensor_tensor(out=dst[:, 2 * hh + 1, :], in0=C_[:], in1=Dm[:], op=ALU.add),
             reads=[p.b_wk[2], p.b_wk[3]], writes=[b_dst])

    def proj_A(self, wv, wb, ct, bank_i, hT=None, b_hT=None, n=GT):
        p, S = self, self.S
        hT = p.hT if hT is None else hT
        b_hT = p.b_hT if b_hT is None else b_hT
        for kc in range(KC):
            S.op("pe", lambda e: e.matmul(p.P[bank_i][:, 0:n], lhsT=wv[:, kc, ct * 128:(ct + 1) * 128], rhs=hT[:, kc, 0:n],
                                          start=(kc == 0), stop=(kc == KC - 1)),
                 reads=[wb, b_hT], writes=[p.b_P[bank_i]])

    def proj_B(self, wv, wb, c, bank_i, w=512):
        p, S = self, self.S
        for kc in range(KC):
            S.op("pe", lambda e: e.matmul(p.P[bank_i][:, 0:w], lhsT=p.hT[:, kc, c * CH:(c + 1) * CH], rhs=wv[:, kc, 0:w],
                                          start=(kc == 0), stop=(kc == KC - 1)),
                 reads=[wb, p.b_hT], writes=[p.b_P[bank_i]])

    def exchange(self, k):
        p, S = self, self.S
        if p.mode == "host":
            return
        S.wait_bufs("pool", [p.b_loc[k], p.b_gath[k]])
        ins = p.nc.gpsimd.collective_compute("AllGather", ALU.bypass, replica_groups=[list(range(NCORES))],
                                             ins=[p.loc[k][:, :]], outs=[p.gath[k][:, :]])
        key = S._dma_sem(f"cc{k}")
        S.cnt[key] += 16
        ins.then_inc(S.sems[key], 16)
        S._record((key, S.cnt[key]), [p.b_loc[k]], [p.b_gath[k]])

    def make_kz(self, c, scale_ap_fn):
        p, S = self, self.S
        cs = slice(c * CH, (c + 1) * CH)
        for i in range(8):
            S.op("pe", lambda e: e.transpose(p.Pb[0][:, i * 128:(i + 1) * 128], p.kT[:, i, cs], p.ident_b[:, :]),
                 reads=[p.b_kT, p.b_ident_b], writes=[p.b_Pb[0]])
        for hh in range(H):
            sc, sbufs = scale_ap_fn(hh)
            S.op("act", lambda e: e.activation(out=p.kz[:, hh * 256:(hh + 1) * 256], in_=p.Pb[0][:, hh * 256:(hh + 1) * 256],
                                               func=AF.Copy, scale=sc),
                 reads=[p.b_Pb[0]] + sbufs, writes=[p.b_kz])

    def state_update(self, c, hh, decay, dbufs=()):
        p, S = self, self.S
        dbufs = list(dbufs)
        for half in range(2):
            i = 2 * hh + half
            S.op("pe", lambda e: e.matmul(p.P[2 + half][:, :], lhsT=p.kz[:, hh * 256 + half * 128: hh * 256 + (half + 1) * 128],
                                          rhs=p.v_all[:, c, hh * 512:(hh + 1) * 512], start=True, stop=True),
                 reads=[p.b_kz, p.b_v[c]], writes=[p.b_P[2 + half]])
            S.op("dve", lambda e: e.scalar_tensor_tensor(out=p.R[:, i, :], in0=p.R[:, i, :], scalar=decay, in1=p.P[2 + half][:, :],
                                                         op0=ALU.mult, op1=ALU.add),
                 reads=[p.b_R[i], p.b_P[2 + half]] + dbufs, writes=[p.b_R[i]])

    def refresh_Rb(self, hh):
        p, S = self, self.S
        for half in range(2):
            i = 2 * hh + half
            S.op("act", lambda e: e.activation(out=p.Rb[:, i, :], in_=p.R[:, i, :], func=AF.Copy),
                 reads=[p.b_R[i]], writes=[p.b_Rb[i]])

    def groupnorm_heads(self, ywk, c, gate):
        p, S = self, self.S
        for hh in range(H):
            S.op("dve", lambda e: e.bn_stats(out=p.sm[:, 6 * hh:6 * hh + 6], in_=p.wk[ywk[hh]][:]), reads=[p.b_wk[ywk[hh]]], writes=[p.b_sm])
            S.op("dve", lambda e: e.bn_aggr(out=p.sm[:, 24 + 2 * hh:26 + 2 * hh], in_=p.sm[:, 6 * hh:6 * hh + 6]), reads=[p.b_sm], writes=[p.b_sm])
        var = p.sm[:, 24:32].rearrange("p (h t) -> p h t", t=2)[:, :, 1]
        S.op("dve", lambda e: e.tensor_scalar(out=p.sm[:, 32:36], in0=var, scalar1=EPS, scalar2=None, op0=ALU.add), reads=[p.b_sm], writes=[p.b_sm])
        S.op("act", lambda e: e.activation(out=p.sm[:, 32:36], in_=p.sm[:, 32:36], func=AF.Ln), reads=[p.b_sm], writes=[p.b_sm])
        S.op("act", lambda e: e.activation(out=p.sm[:, 32:36], in_=p.sm[:, 32:36], func=AF.Exp, scale=-0.5), reads=[p.b_sm], writes=[p.b_sm])
        for hh in range(H):
            w = p.wk[ywk[hh]]
            if gate:
                S.op("dve", lambda e: e.tensor_scalar(out=w[:], in0=w[:], scalar1=p.sm[:, 24 + 2 * hh:25 + 2 * hh], scalar2=p.sm[:, 32 + hh:33 + hh],
                                                      op0=ALU.subtract, op1=ALU.mult), reads=[p.b_wk[ywk[hh]], p.b_sm], writes=[p.b_wk[ywk[hh]]])
                sgv = p.sg_all[:, c, hh * 512:(hh + 1) * 512]
                S.op("dve", lambda e: e.tensor_tensor(out=p.yg[:, hh * 512:(hh + 1) * 512], in0=w[:], in1=sgv, op=ALU.mult),
                     reads=[p.b_wk[ywk[hh]], p.b_xg[c]], writes=[p.b_yg])
            else:
                S.op("dve", lambda e: e.tensor_scalar(out=p.yg[:, hh * 512:(hh + 1) * 512], in0=w[:], scalar1=p.sm[:, 24 + 2 * hh:25 + 2 * hh],
                                                      scalar2=p.sm[:, 32 + hh:33 + hh], op0=ALU.subtract, op1=ALU.mult),
                     reads=[p.b_wk[ywk[hh]], p.b_sm], writes=[p.b_yg])

    def make_ynT(self, c, gnT, b_gnT):
        p, S = self, self.S
        ynT = p.big_a[:, 0:16 * GT].rearrange("p (k n) -> p k n", k=16)
        for kc in range(16):
            bi = kc // 8
            S.op("pe", lambda e: e.transpose(p.Pb[bi][:, (kc % 8) * 128:(kc % 8 + 1) * 128], p.yg[:, kc * 128:(kc + 1) * 128], p.ident_b[:, :]),
                 reads=[p.b_yg, p.b_ident_b], writes=[p.b_Pb[bi]])
        for kc in range(16):
            bi = kc // 8
            src = p.Pb[bi][:, (kc % 8) * 128:(kc % 8 + 1) * 128]
            dst = ynT[:, kc, c * CH:(c + 1) * CH]
            if kc % 2 == 0:
                S.op("act", lambda e: e.activation(out=dst, in_=src, func=AF.Copy, scale=gnT[:, kc:kc + 1]),
                     reads=[p.b_Pb[bi], b_gnT], writes=[p.b_big_a])
            else:
                S.op("dve", lambda e: e.tensor_scalar(out=dst, in0=src, scalar1=gnT[:, kc:kc + 1], scalar2=None, op0=ALU.mult),
                     reads=[p.b_Pb[bi], b_gnT], writes=[p.b_big_a])

    def out_proj(self, g, W_out, src, src_bufs, dst, dst_bufs, halo_ex):
        p, S = self, self.S
        ynT = p.big_a[:, 0:16 * GT].rearrange("p (k n) -> p k n", k=16)
        for c in range(G):
            r0 = g * GT + c * CH
            S.dma("sp", p.x_g[:, c, :], src[r0:r0 + CH, :], reads=src_bufs, writes=[p.b_xg[c]], key=f"xg{c}")
        for cp in range(4):
            wv, wb = self.load_rows_piece(W_out, 0, 16, cp * 256, 256)
            for c in range(G):
                bi = (cp * G + c) % 6
                for kc in range(16):
                    S.op("pe", lambda e: e.matmul(p.P[bi][:, 0:256], lhsT=ynT[:, kc, c * CH:(c + 1) * CH], rhs=wv[:, kc, :],
                                                  start=(kc == 0), stop=(kc == 15)),
                         reads=[wb, p.b_big_a], writes=[p.b_P[bi]])
                S.op("dve", lambda e: e.tensor_tensor(out=p.wk[c][:, 0:256], in0=p.P[bi][:, 0:256], in1=p.gt_bc[:, cp * 256:(cp + 1) * 256], op=ALU.mult),
                     reads=[p.b_P[bi], p.b_gt_bc], writes=[p.b_wk[c]])
            for c in range(G):
                S.op("dve", lambda e: e.tensor_tensor(out=p.x_g[:, c, cp * 256:(cp + 1) * 256], in0=p.x_g[:, c, cp * 256:(cp + 1) * 256],
                                                      in1=p.wk[c][:, 0:256], op=ALU.add),
                     reads=[p.b_wk[c], p.b_xg[c]], writes=[p.b_xg[c]])
        self.store_group(g, dst, dst_bufs, halo_ex)

    def store_group(self, g, dst, dst_bufs, halo_ex):
        p, S = self, self.S
        pairs = [(dst[g * GT + c * CH: g * GT + (c + 1) * CH, :], p.x_g[:, c, :]) for c in range(G)]
        self.dma_group("sp", f"st_{dst_bufs[0].name[:2]}", pairs, reads=p.b_xg, writes=[dst_bufs[g]])
        if g == NG - 1 and halo_ex is not None:
            S.dma("sp", p.loc[halo_ex][:, :], p.x_g[128 - HW:128, G - 1, :], reads=[p.b_xg[G - 1]], writes=[p.b_loc[halo_ex]], key=f"loc{halo_ex}")

    def kv_store(self, g):
        p = self
        self.dma_group("sp", "kvst", [(p.kst[g], p.kT[:].rearrange("p a n -> p (a n)")), (p.vst[g], p.v_all[:].rearrange("p a n -> p (a n)"))],
                       reads=[p.b_kT] + p.b_v, writes=[p.b_kvst])

    def kv_load(self, g):
        p = self
        self.dma_group("sp", "kvld", [(p.kT[:].rearrange("p a n -> p (a n)"), p.kst[g]), (p.v_all[:].rearrange("p a n -> p (a n)"), p.vst[g])],
                       writes=[p.b_kT] + p.b_v)

    def ret_group(self, g, full):
        p, S = self, self.S
        sh = p.modT[:, 0, 0:8]
        self.norm_group(p.x, [], g, 0, sh)
        self.rope_tables(g)
        W = p.ret_w_in
        reuse = full and p.relay
        plist = ([0, 1] if full else []) + ([] if reuse else [2, 3])
        if reuse:
            self.kv_load(g)
        for pc in plist:
            wv, wb = self.load_std_piece(W, pc * 512)
            dst, b_dst = (p.qT, p.b_qT) if pc < 2 else (p.kT, p.b_kT)
            for ct in range(4):
                self.proj_A(wv, wb, ct, ct)
            for j in range(2):
                self.rope_pair(2 * (pc % 2) + j, dst, b_dst)
        for hh in ([] if reuse else range(H)):
            wv, wb = self.load_std_piece(W, 2048 + hh * 512)
            for c in range(G):
                self.proj_B(wv, wb, c, c)
                dstv = p.v_all[:, c, hh * 512:(hh + 1) * 512]
                if c % 2 == 0:
                    S.op("act", lambda e: e.activation(out=dstv, in_=p.P[c][:, :], func=AF.Copy), reads=[p.b_P[c]], writes=[p.b_v[c]])
                else:
                    S.op("dve", lambda e: e.tensor_copy(out=dstv, in_=p.P[c][:, :]), reads=[p.b_P[c]], writes=[p.b_v[c]])
        if (not full) and p.relay:
            self.kv_store(g)
        if full:
            for hh in range(H):
                wv, wb = self.load_std_piece(W, 4096 + hh * 512)
                for c in range(G):
                    self.proj_B(wv, wb, c, c)
                    dsts = p.sg_all[:, c, hh * 512:(hh + 1) * 512]
                    S.op("act", lambda e: e.activation(out=dsts, in_=p.P[c][:, :], func=AF.Silu), reads=[p.b_P[c]], writes=[p.b_xg[c]])
        if full and g == 0:
            self.dump("hT", p.hT[:], [p.b_hT])
            self.dump("cos", p.cos[:], [p.b_cos])
            self.dump("sin", p.sin[:], [p.b_sin])
            self.dump("qT", p.qT[:], [p.b_qT])
            self.dump("kT", p.kT[:], [p.b_kT])
            self.dump("v_all", p.v_all[:], p.b_v)
            self.dump("sg_all", p.sg_all, p.b_xg)
        for c in range(G):
            self.ret_chunk(c, full)
            if full and g == 0 and c == 1:
                self.dump("yg", p.yg[:], [p.b_yg])
                self.dump("sm", p.sm[:], [p.b_sm])
                self.dump("wk0", p.wk[0][:], [p.b_wk[0]])
                self.dump("wk3", p.wk[3][:], [p.b_wk[3]])
                self.dump("sTm", p.sTm[:], [p.b_sTm])
                self.dump("kz", p.kz[:], [p.b_kz])
                self.dump("R", p.R[:], p.b_R)
        if full and g == 0:
            self.dump("ynT", p.big_a[:, 0:16 * GT], [p.b_big_a])
        if full:
            self.out_proj(g, p.ret_w_out, p.x, [], p.xa, p.b_xa, 2)
        if full and g == 0:
            self.dump("xg", p.x_g[:], p.b_xg)

    def ret_chunk(self, c, full):
        p, S = self, self.S
        cs = slice(c * CH, (c + 1) * CH)
        self.make_kz(c, lambda hh: (p.ret_zs[:, hh:hh + 1], [p.b_ret_zs]))
        gam = [float(np.exp(128.0 * np.log(1.0 - 2.0 ** (-5.0 - h)))) for h in range(H)]
        if not full:
            for hh in range(H):
                self.state_update(c, hh, gam[hh])
            return
        for hh in range(H):
            for half in range(2):
                S.op("pe", lambda e: e.matmul(p.P[4][:, hh * 128:(hh + 1) * 128], lhsT=p.kT[:, 2 * hh + half, cs], rhs=p.qT[:, 2 * hh + half, cs],
                                              start=(half == 0), stop=(half == 1)),
                     reads=[p.b_kT, p.b_qT], writes=[p.b_P[4]])
        S.op("dve", lambda e: e.tensor_tensor(out=p.sTm[:], in0=p.P[4][:, :], in1=p.ret_dt[:], op=ALU.mult),
             reads=[p.b_P[4], p.b_ret_dt], writes=[p.b_sTm])
        for hh in range(H):
            S.op("pe", lambda e: e.matmul(p.P[0][:, :], lhsT=p.sTm[:, hh * 128:(hh + 1) * 128], rhs=p.v_all[:, c, hh * 512:(hh + 1) * 512],
                                          start=True, stop=True), reads=[p.b_sTm, p.b_v[c]], writes=[p.b_P[0]])
            for half in range(2):
                i = 2 * hh + half
                S.op("pe", lambda e: e.matmul(p.P[1][:, :], lhsT=p.qT[:, i, cs], rhs=p.Rb[:, i, :], start=(half == 0), stop=(half == 1)),
                     reads=[p.b_qT, p.b_Rb[i]], writes=[p.b_P[1]])
            S.op("act", lambda e: e.activation(out=p.wk[4][:], in_=p.P[1][:, :], func=AF.Copy, scale=p.ret_xi[:, hh:hh + 1]),
                 reads=[p.b_P[1], p.b_ret_xi], writes=[p.b_wk[4]])
            S.op("dve", lambda e: e.tensor_tensor(out=p.wk[hh][:], in0=p.wk[4][:], in1=p.P[0][:, :], op=ALU.add),
                 reads=[p.b_wk[4], p.b_P[0]], writes=[p.b_wk[hh]])
            self.state_update(c, hh, gam[hh])
            self.refresh_Rb(hh)
        self.groupnorm_heads([0, 1, 2, 3], c, gate=True)
        self.make_ynT(c, p.ret_gnT, p.b_ret_gnT)

    def ret_layer(self):
        p, S = self, self.S
        for i in range(8):
            S.op("dve", lambda e: e.memset(p.R[:, i, :], 0.0), writes=[p.b_R[i]])
        if p.stop_after == "dbg_ret":
            for hh in range(H):
                self.refresh_Rb(hh)
            self.load_gate(0, 0)
            self.dump("modT", p.modT[:], [p.b_modT])
            self.dump("gsc", p.gsc[:], [p.b_gsc])
            self.dump("gt_bc", p.gt_bc[:], [p.b_gt_bc])
            self.ret_group(0, full=True)
            return
        if p.phase in (None, 1):
            self.ret_part_a()
        if p.phase is None:
            self.exchange(1)
        if p.phase in (None, 2):
            self.ret_part_b()

    def ret_part_a(self):
        p, S = self, self.S
        for g in range(NG):
            self.ret_group(g, full=False)
        self.dma_group("sp", "loc1", [(p.loc[1][i * 128:(i + 1) * 128, :], p.R[:, i, :]) for i in range(8)],
                       reads=p.b_R, writes=[p.b_loc[1]])

    def ret_part_b(self):
        p, S = self, self.S
        self.combine_state(1, p.ret_coef, p.b_ret_coef, nrows=8)
        for hh in range(H):
            self.refresh_Rb(hh)
        self.load_gate(0, 0)
        for g in range(NG):
            self.ret_group(g, full=True)

    def combine_state(self, ex, coef, b_coef, nrows):
        p, S = self, self.S
        gt = p.gath[ex]
        per = nrows * 128 if ex == 1 else 9 * 128
        for i in range(8):
            hh = i // 2
            for cp in range(NCORES):
                tmp, bt = p.wk[cp % 2], p.b_wk[cp % 2]
                r0 = cp * per + i * 128
                S.dma("sp", tmp[:], gt[r0:r0 + 128, :], reads=[p.b_gath[ex]], writes=[bt], key=f"cmb{cp % 2}")
                if cp == 0:
                    S.op("dve", lambda e: e.tensor_scalar(out=p.R[:, i, :], in0=tmp[:], scalar1=coef[:, cp * H + hh: cp * H + hh + 1], scalar2=None,
                                                          op0=ALU.mult), reads=[bt, b_coef], writes=[p.b_R[i]])
                else:
                    S.op("dve", lambda e: e.scalar_tensor_tensor(out=p.R[:, i, :], in0=tmp[:], scalar=coef[:, cp * H + hh: cp * H + hh + 1],
                                                                 in1=p.R[:, i, :], op0=ALU.mult, op1=ALU.add),
                         reads=[bt, b_coef, p.b_R[i]], writes=[p.b_R[i]])

    def ffn_layer(self, l, src, src_bufs, dst, dst_bufs, ex_in, ex_out, final):
        p, S = self, self.S
        S.barrier()
        self.load_gate(l, 1)
        sh = p.modT[:, l, 24:32]
        gidx = 2 * l + 1
        actT = p.big_a[:, :].rearrange("p (k n) -> p k n", k=22)
        cw = p.ffn_cwT
        cb = p.ffn_cbT
        Wup = p.ffn_w_up[l]
        Wdn = p.ffn_w_down[l]
        for g in range(NG):
            if g == 0:
                self.load_halo(src, src_bufs, g, ex_in)
                self.norm_halo(gidx, sh)
            self.norm_group(src, src_bufs, g, gidx, sh)
            srcw = Wup.rearrange("(k p) n -> p k n", p=128)

            def load_up(pc):
                t, b, key = self.ring()
                wv = t[:, 0:4096].rearrange("p (k n) -> p k n", k=8)
                pairs = []
                for kh in range(2):
                    pairs.append((wv[:, 4 * kh:4 * kh + 4, 0:256], srcw[:, 4 * kh:4 * kh + 4, pc * 256:(pc + 1) * 256]))
                    pairs.append((wv[:, 4 * kh:4 * kh + 4, 256:512], srcw[:, 4 * kh:4 * kh + 4, DFF + pc * 256: DFF + (pc + 1) * 256]))
                self.dma_group("pool", key, pairs, writes=[b])
                return wv, b

            loaders = [(lambda pc=pc: load_up(pc)) for pc in range(11)]
            loaders += [(lambda cp=cp, kh=kh: self.load_rows_piece(Wdn, 11 * kh, 11, cp * 256, 256)) for cp in range(4) for kh in range(2)]
            loaded = {}

            def get_piece(i, ahead=2):
                for j in range(i, min(i + ahead + 1, len(loaders))):
                    if j not in loaded:
                        loaded[j] = loaders[j]()
                return loaded.pop(i)

            banks = [0, 1, 2, 3, 5]
            for pc in range(11):
                wv, b = get_piece(pc)
                for j in range(2):
                    tl = []
                    for ct in (j, 2 + j):
                        tile_i = 2 * pc + j
                        chan = tile_i if ct < 2 else 22 + tile_i
                        bi = banks[self.rot_bank % len(banks)]
                        self.rot_bank += 1
                        ai = self.rot_acc % 5
                        self.rot_acc += 1
                        ui = self.rot_ub % 2
                        self.rot_ub += 1
                        tl.append((ct, chan, bi, p.wk[ai], p.b_wk[ai], p.ubuf[ui], p.b_ubuf[ui]))
                    for ct, chan, bi, acc, b_acc, ub, b_ub in tl:
                        self.proj_A(wv, b, ct, bi)
                        if g == 0:
                            for kc in range(KC):
                                S.op("pe", lambda e: e.matmul(p.P[4][:, 0:HW], lhsT=wv[:, kc, ct * 128:(ct + 1) * 128], rhs=p.hTh[:, kc, 0:HW],
                                                              start=(kc == 0), stop=(kc == KC - 1)), reads=[b, p.b_hTh], writes=[p.b_P[4]])
                            S.op("dve", lambda e: e.tensor_scalar(out=ub[:, 0:HW], in0=p.P[4][:, 0:HW], scalar1=p.nf[:, 0:1], scalar2=None, op0=ALU.mult),
                                 reads=[p.b_P[4], p.b_nf], writes=[b_ub])
                        else:
                            S.op("act", lambda e: e.activation(out=ub[:, 0:HW], in_=p.uhalo[:, chan, :], func=AF.Copy), reads=[p.b_uhalo], writes=[b_ub])
                    for ct, chan, bi, acc, b_acc, ub, b_ub in tl:
                        S.op("act", lambda e: e.activation(out=ub[:, HW:HW + GT], in_=p.P[bi][:, :], func=AF.Copy), reads=[p.b_P[bi]], writes=[b_ub])
                        S.op("act", lambda e: e.activation(out=acc[:], in_=p.P[bi][:, :], func=AF.Identity, scale=cw[:, l, 88 + chan:89 + chan],
                                                           bias=cb[:, l, chan:chan + 1]), reads=[p.b_P[bi], p.b_ffn_cwT, p.b_ffn_cbT], writes=[b_acc])
                    if g < NG - 1:
                        for ct, chan, bi, acc, b_acc, ub, b_ub in tl:
                            S.op("act", lambda e: e.activation(out=p.uhalo[:, chan, :], in_=ub[:, GT:GT + HW], func=AF.Copy), reads=[b_ub], writes=[p.b_uhalo])
                    for tap, off in ((1, HW - 1), (0, HW - 2)):
                        for ct, chan, bi, acc, b_acc, ub, b_ub in tl:
                            S.op("dve", lambda e: e.scalar_tensor_tensor(out=acc[:], in0=ub[:, off:off + GT], scalar=cw[:, l, tap * 44 + chan:tap * 44 + chan + 1],
                                                                         in1=acc[:], op0=ALU.mult, op1=ALU.add), reads=[b_ub, b_acc, p.b_ffn_cwT], writes=[b_acc])
                    (_, _, _, aa, b_aa, _, _), (_, _, _, ab, b_ab, _, _) = tl
                    S.op("act", lambda e: e.activation(out=aa[:], in_=aa[:], func=AF.Silu), reads=[b_aa], writes=[b_aa])
                    S.op("dve", lambda e: e.tensor_tensor(out=actT[:, 2 * pc + j, :], in0=aa[:], in1=ab[:], op=ALU.mult),
                         reads=[b_aa, b_ab], writes=[p.b_actT[2 * pc + j]])
            for cp in range(4):
                for kh in range(2):
                    wv, wb = get_piece(11 + cp * 2 + kh)
                    for c in range(G):
                        for k in range(11):
                            S.op("pe", lambda e: e.matmul(p.P[c][:, 0:256], lhsT=actT[:, 11 * kh + k, c * CH:(c + 1) * CH], rhs=wv[:, k, :],
                                                          start=(kh == 0 and k == 0), stop=(kh == 1 and k == 10)),
                                 reads=[wb, p.b_actT[11 * kh + k]], writes=[p.b_P[c]])
                for c in range(G):
                    S.op("dve", lambda e: e.tensor_tensor(out=p.wk[c][:, 0:256], in0=p.P[c][:, 0:256], in1=p.gt_bc[:, cp * 256:(cp + 1) * 256], op=ALU.mult),
                         reads=[p.b_P[c], p.b_gt_bc], writes=[p.b_wk[c]])
                for c in range(G):
                    S.op("dve", lambda e: e.tensor_tensor(out=p.x_g[:, c, cp * 256:(cp + 1) * 256], in0=p.x_g[:, c, cp * 256:(cp + 1) * 256],
                                                          in1=p.wk[c][:, 0:256], op=ALU.add),
                         reads=[p.b_wk[c], p.b_xg[c]], writes=[p.b_xg[c]])
            if final:
                self.final_norm_group()
            self.store_group(g, dst, dst_bufs, ex_out)
        S.barrier()

    def final_norm_group(self):
        p, S = self, self.S
        for c in range(G):
            S.op("act", lambda e: e.activation(out=p.xn[:], in_=p.x_g[:, c, :], func=AF.Square, accum_out=p.ss[:, c:c + 1]),
                 reads=[p.b_xg[c]], writes=[p.b_xn, p.b_ss])
        S.op("dve", lambda e: e.tensor_scalar(out=p.ss[:, 0:G], in0=p.ss[:, 0:G], scalar1=1.0 / D, scalar2=EPS, op0=ALU.mult, op1=ALU.add),
             reads=[p.b_ss], writes=[p.b_ss])
        S.op("act", lambda e: e.activation(out=p.ss[:, 0:G], in_=p.ss[:, 0:G], func=AF.Sqrt), reads=[p.b_ss], writes=[p.b_ss])
        S.op("dve", lambda e: e.reciprocal(out=p.rstd[:, 0:G], in_=p.ss[:, 0:G]), reads=[p.b_ss], writes=[p.b_rstd])
        for c in range(G):
            S.op("act", lambda e: e.activation(out=p.x_g[:, c, :], in_=p.x_g[:, c, :], func=AF.Copy, scale=p.rstd[:, c:c + 1]),
                 reads=[p.b_xg[c], p.b_rstd], writes=[p.b_xg[c]])
            for half, (t, b) in enumerate(((p.cos, p.b_cos), (p.sin, p.b_sin))):
                S.op("dve", lambda e: e.tensor_tensor(out=p.x_g[:, c, half * 512:(half + 1) * 512], in0=p.x_g[:, c, half * 512:(half + 1) * 512],
                                                      in1=t[:], op=ALU.mult), reads=[p.b_xg[c], b], writes=[p.b_xg[c]])

    LN16 = math.log(16.0)

    def ml_group(self, g, full):
        p, S = self, self.S
        sh = p.modT[:, 1, 0:8]
        W = p.ml_w_in
        if g == 0:
            self.load_halo(p.xb, p.b_xb, g, 3)
            self.norm_halo(2, sh)
        self.norm_group(p.xb, p.b_xb, g, 2, sh)
        reuse = full and p.relay
        plist = ([0, 1] if full else []) + ([] if reuse else [2, 3])
        if reuse:
            self.kv_load(g)
        for pc in plist:
            wv, wb = self.load_std_piece(W, pc * 512)
            dst, b_dst = (p.qT, p.b_qT) if pc < 2 else (p.kT, p.b_kT)
            for pr in range(2):
                tl = []
                for ct in (2 * pr, 2 * pr + 1):
                    tl.append((ct, pc * 4 + ct, p.wk[ct], p.b_wk[ct], p.ubuf[ct % 2], p.b_ubuf[ct % 2]))
                for ct, cti, acc, b_acc, ub, b_ub in tl:
                    self.proj_A(wv, wb, ct, ct)
                    if g == 0:
                        for kc in range(KC):
                            S.op("pe", lambda e: e.matmul(p.P[4][:, 0:HW], lhsT=wv[:, kc, ct * 128:(ct + 1) * 128], rhs=p.hTh[:, kc, 0:HW],
                                                          start=(kc == 0), stop=(kc == KC - 1)), reads=[wb, p.b_hTh], writes=[p.b_P[4]])
                        S.op("dve", lambda e: e.tensor_scalar(out=ub[:, 0:HW], in0=p.P[4][:, 0:HW], scalar1=p.nf[:, 0:1], scalar2=None, op0=ALU.mult),
                             reads=[p.b_P[4], p.b_nf], writes=[b_ub])
                    else:
                        S.op("act", lambda e: e.activation(out=ub[:, 0:HW], in_=p.uhalo[:, cti, :], func=AF.Copy), reads=[p.b_uhalo], writes=[b_ub])
                for ct, cti, acc, b_acc, ub, b_ub in tl:
                    S.op("act", lambda e: e.activation(out=ub[:, HW:HW + GT], in_=p.P[ct][:, :], func=AF.Copy), reads=[p.b_P[ct]], writes=[b_ub])
                    S.op("act", lambda e: e.activation(out=acc[:], in_=p.P[ct][:, :], func=AF.Identity, scale=p.ml_cwT[:, 48 + cti:49 + cti],
                                                       bias=p.ml_cbT[:, cti:cti + 1]), reads=[p.b_P[ct], p.b_ml_cwT, p.b_ml_cbT], writes=[b_acc])
                if g < NG - 1:
                    for ct, cti, acc, b_acc, ub, b_ub in tl:
                        S.op("act", lambda e: e.activation(out=p.uhalo[:, cti, :], in_=ub[:, GT:GT + HW], func=AF.Copy), reads=[b_ub], writes=[p.b_uhalo])
                for j in (2, 1, 0):
                    off = HW - (3 - j)
                    for ct, cti, acc, b_acc, ub, b_ub in tl:
                        S.op("dve", lambda e: e.scalar_tensor_tensor(out=acc[:], in0=ub[:, off:off + GT], scalar=p.ml_cwT[:, j * 16 + cti: j * 16 + cti + 1],
                                                                     in1=acc[:], op0=ALU.mult, op1=ALU.add), reads=[b_ub, b_acc, p.b_ml_cwT], writes=[b_acc])
                for ct, cti, acc, b_acc, ub, b_ub in tl:
                    S.op("act", lambda e: e.activation(out=dst[:, cti % 8, :], in_=acc[:], func=AF.Silu), reads=[b_acc], writes=[b_dst])
        for hh in ([] if reuse else range(H)):
            wv, wb = self.load_std_piece(W, 2048 + hh * 512)
            for c in range(G):
                self.proj_B(wv, wb, c, c)
                dstv = p.v_all[:, c, hh * 512:(hh + 1) * 512]
                if c % 2 == 0:
                    S.op("act", lambda e: e.activation(out=dstv, in_=p.P[c][:, :], func=AF.Copy), reads=[p.b_P[c]], writes=[p.b_v[c]])
                else:
                    S.op("dve", lambda e: e.tensor_copy(out=dstv, in_=p.P[c][:, :]), reads=[p.b_P[c]], writes=[p.b_v[c]])
        if (not full) and p.relay:
            self.kv_store(g)
        wv, wb = self.load_std_piece(W, 6144, w=8)
        for c in range(G):
            self.proj_B(wv, wb, c, c, w=8)
            S.op("dve", lambda e: e.tensor_tensor(out=p.gat[:, c, :], in0=p.P[c][:, 0:8], in1=p.bg_bc[:], op=ALU.add),
                 reads=[p.b_P[c], p.b_bg_bc], writes=[p.b_gat])
        if full:
            for hh in range(H):
                wv, wb = self.load_std_piece(W, 4096 + hh * 512)
                for c in range(G):
                    self.proj_B(wv, wb, c, c)
                    dsts = p.sg_all[:, c, hh * 512:(hh + 1) * 512]
                    S.op("act", lambda e: e.activation(out=dsts, in_=p.P[c][:, :], func=AF.Sigmoid), reads=[p.b_P[c]], writes=[p.b_xg[c]])
        self.ml_gates_group(full)
        for c in range(G):
            self.ml_chunk(c, full)
        if full:
            self.out_proj(g, p.ml_w_out, p.xb, p.b_xb, p.xa, p.b_xa, 5)

    def ml_gates_group(self, full):
        p, S = self, self.S
        gm, bg = p.gm, p.b_gm
        v3 = lambda k: gm[:, k, :].rearrange("p (c h) -> p c h", h=4)
        z = p.gat[:, :, 4:8]
        li = p.gat[:, :, 0:4]
        S.op("act", lambda e: e.activation(out=v3(0), in_=z, func=AF.Exp, scale=-1.0), reads=[p.b_gat], writes=[bg])
        S.op("dve", lambda e: e.tensor_scalar(out=gm[:, 0, :], in0=gm[:, 0, :], scalar1=1.0, scalar2=None, op0=ALU.add), reads=[bg], writes=[bg])
        S.op("act", lambda e: e.activation(out=gm[:, 0, :], in_=gm[:, 0, :], func=AF.Ln), reads=[bg], writes=[bg])
        S.op("dve", lambda e: e.tensor_scalar(out=gm[:, 0, :], in0=gm[:, 0, :], scalar1=-1.0, scalar2=None, op0=ALU.mult), reads=[bg], writes=[bg])
        n = 4 * G
        S.op("pe", lambda e: e.matmul(p.P[5][:, 0:n], lhsT=p.ut[:, :], rhs=gm[:, 0, :], start=True, stop=True), reads=[p.b_ut, bg], writes=[p.b_P[5]])
        S.op("pe", lambda e: e.matmul(p.P[5][:, n:2 * n], lhsT=p.ones_f[:, :], rhs=gm[:, 0, :], start=True, stop=True), reads=[p.b_ones_f, bg], writes=[p.b_P[5]])
        S.op("dve", lambda e: e.tensor_copy(out=gm[:, 1, :], in_=p.P[5][:, 0:n]), reads=[p.b_P[5]], writes=[bg])
        S.op("dve", lambda e: e.tensor_copy(out=gm[:, 2, :], in_=p.P[5][:, n:2 * n]), reads=[p.b_P[5]], writes=[bg])
        S.op("dve", lambda e: e.tensor_tensor(out=gm[:, 3, :], in0=gm[:, 2, :], in1=gm[:, 1, :], op=ALU.subtract), reads=[bg], writes=[bg])
        S.op("dve", lambda e: e.scalar_tensor_tensor(out=v3(3), in0=v3(3), scalar=-self.LN16, in1=li, op0=ALU.add, op1=ALU.add),
             reads=[bg, p.b_gat], writes=[bg])
        S.op("act", lambda e: e.activation(out=gm[:, 3, :], in_=gm[:, 3, :], func=AF.Exp), reads=[bg], writes=[bg])
        S.op("act", lambda e: e.activation(out=gm[:, 4, :], in_=gm[:, 2, :], func=AF.Exp), reads=[bg], writes=[bg])
        gx = p.gmx[:].rearrange("p c (h t) -> p c h t", t=2)
        for t in range(2):
            S.op("dve", lambda e: e.tensor_copy(out=gx[:, :, :, t], in_=v3(4)), reads=[bg], writes=[p.b_gmx])
        if not full:
            for c in range(G):
                S.op("dve", lambda e: e.tensor_tensor(out=p.fsum[:], in0=p.fsum[:], in1=gm[:, 2, 4 * c:4 * c + 4], op=ALU.add),
                     reads=[bg, p.b_fsum], writes=[p.b_fsum])
        else:
            S.op("dve", lambda e: e.tensor_tensor(out=v3(5), in0=li, in1=v3(1), op=ALU.subtract), reads=[bg, p.b_gat], writes=[bg])
            S.op("dve", lambda e: e.tensor_scalar(out=gm[:, 5, :], in0=gm[:, 5, :], scalar1=-self.LN16, scalar2=None, op0=ALU.add), reads=[bg], writes=[bg])
            S.op("act", lambda e: e.activation(out=gm[:, 6, :], in_=gm[:, 1, :], func=AF.Exp), reads=[bg], writes=[bg])

    def ml_chunk(self, c, full):
        p, S = self, self.S
        cs = slice(c * CH, (c + 1) * CH)
        sm, bs = p.sm, p.b_sm
        gm, bg = p.gm, p.b_gm
        col = 4 * c
        g_b = lambda hh: gm[:, 1, col + hh:col + hh + 1]
        g_ws = lambda hh: gm[:, 3, col + hh:col + hh + 1]
        g_sp = lambda hh: gm[:, 4, col + hh:col + hh + 1]
        g_bj = lambda hh: gm[:, 5, col + hh:col + hh + 1]
        g_wi = lambda hh: gm[:, 6, col + hh:col + hh + 1]
        self.make_kz(c, lambda hh: (g_ws(hh), [bg]))
        if full:
            for hh in range(H):
                S.op("dve", lambda e: e.tensor_scalar(out=p.xn[:, hh * 128:(hh + 1) * 128], in0=p.ident_f[:, :], scalar1=g_b(hh), scalar2=None, op0=ALU.mult),
                     reads=[bg, p.b_ident_f], writes=[p.b_xn])
                S.op("pe", lambda e: e.matmul(p.P[5][:, hh * 128:(hh + 1) * 128], lhsT=p.ones_f[:, :], rhs=p.xn[:, hh * 128:(hh + 1) * 128], start=True, stop=False),
                     reads=[p.b_ones_f, p.b_xn], writes=[p.b_P[5]])
                S.op("pe", lambda e: e.matmul(p.P[5][:, hh * 128:(hh + 1) * 128], lhsT=p.ident_f[:, :], rhs=p.neg[:, :], start=False, stop=True),
                     reads=[p.b_ident_f, p.b_neg], writes=[p.b_P[5]])
            for hh in range(H):
                S.op("act", lambda e: e.activation(out=p.xn2[:, hh * 128:(hh + 1) * 128], in_=p.P[5][:, hh * 128:(hh + 1) * 128], func=AF.Exp,
                                                   bias=g_bj(hh)), reads=[p.b_P[5], bg], writes=[p.b_xn2])
            for hh in range(H):
                for half in range(2):
                    S.op("pe", lambda e: e.matmul(p.P[4][:, hh * 128:(hh + 1) * 128], lhsT=p.kT[:, 2 * hh + half, cs], rhs=p.qT[:, 2 * hh + half, cs],
                                                  start=(half == 0), stop=(half == 1)), reads=[p.b_kT, p.b_qT], writes=[p.b_P[4]])
            S.op("dve", lambda e: e.tensor_tensor(out=p.sTm[:], in0=p.P[4][:, :], in1=p.xn2[:, 0:512], op=ALU.mult),
                 reads=[p.b_P[4], p.b_xn2], writes=[p.b_sTm])
        if full:
            for hh in range(H):
                S.op("pe", lambda e: e.matmul(p.P[5][:, 2 * hh:2 * hh + 1], lhsT=p.sTm[:, hh * 128:(hh + 1) * 128], rhs=p.ones_b[:, 0:1], start=True, stop=True),
                     reads=[p.b_sTm, p.b_ones_b], writes=[p.b_P[5]])
                for half in range(2):
                    i = 2 * hh + half
                    S.op("pe", lambda e: e.matmul(p.P[5][:, 2 * hh + 1:2 * hh + 2], lhsT=p.qT[:, i, cs], rhs=p.nstb[:, i:i + 1], start=(half == 0), stop=(half == 1)),
                         reads=[p.b_qT, p.b_nstb], writes=[p.b_P[5]])
            S.op("dve", lambda e: e.tensor_copy(out=sm[:, 64:72], in_=p.P[5][:, 0:8]), reads=[p.b_P[5]], writes=[bs])
            dv = sm[:, 64:72].rearrange("p (h t) -> p h t", t=2)
            S.op("dve", lambda e: e.tensor_tensor(out=sm[:, 72:76], in0=dv[:, :, 1], in1=gm[:, 6, col:col + 4], op=ALU.mult), reads=[bs, bg], writes=[bs])
            S.op("dve", lambda e: e.tensor_tensor(out=sm[:, 72:76], in0=sm[:, 72:76], in1=dv[:, :, 0], op=ALU.add), reads=[bs], writes=[bs])
            S.op("dve", lambda e: e.scalar_tensor_tensor(out=sm[:, 76:80], in0=sm[:, 72:76], scalar=-1.0, in1=sm[:, 72:76], op0=ALU.mult, op1=ALU.max),
                 reads=[bs], writes=[bs])
            S.op("dve", lambda e: e.tensor_scalar(out=sm[:, 76:80], in0=sm[:, 76:80], scalar1=1.0, scalar2=None, op0=ALU.max), reads=[bs], writes=[bs])
            S.op("dve", lambda e: e.reciprocal(out=sm[:, 80:84], in_=sm[:, 76:80]), reads=[bs], writes=[bs])
        for hh in range(H):
            if full:
                S.op("pe", lambda e: e.matmul(p.P[0][:, :], lhsT=p.sTm[:, hh * 128:(hh + 1) * 128], rhs=p.v_all[:, c, hh * 512:(hh + 1) * 512],
                                              start=True, stop=True), reads=[p.b_sTm, p.b_v[c]], writes=[p.b_P[0]])
                for half in range(2):
                    i = 2 * hh + half
                    S.op("pe", lambda e: e.matmul(p.P[1][:, :], lhsT=p.qT[:, i, cs], rhs=p.Rb[:, i, :], start=(half == 0), stop=(half == 1)),
                         reads=[p.b_qT, p.b_Rb[i]], writes=[p.b_P[1]])
                S.op("act", lambda e: e.activation(out=p.wk[4][:], in_=p.P[1][:, :], func=AF.Copy, scale=g_wi(hh)),
                     reads=[p.b_P[1], bg], writes=[p.b_wk[4]])
                S.op("dve", lambda e: e.tensor_tensor(out=p.wk[hh][:], in0=p.wk[4][:], in1=p.P[0][:, :], op=ALU.add),
                     reads=[p.b_wk[4], p.b_P[0]], writes=[p.b_wk[hh]])
                so = p.sg_all[:, c, hh * 512:(hh + 1) * 512]
                S.op("dve", lambda e: e.scalar_tensor_tensor(out=p.wk[hh][:], in0=p.wk[hh][:], scalar=sm[:, 80 + hh:81 + hh], in1=so, op0=ALU.mult, op1=ALU.mult),
                     reads=[p.b_wk[hh], bs, p.b_xg[c]], writes=[p.b_wk[hh]])
            self.state_update(c, hh, g_sp(hh), [bg])
            if full:
                self.refresh_Rb(hh)
        for i in range(8):
            hh, half = i // 2, i % 2
            S.op("pe", lambda e: e.matmul(p.P[5][:, 16 + i:17 + i], lhsT=p.kz[:, hh * 256 + half * 128: hh * 256 + (half + 1) * 128],
                                          rhs=p.ones_b[:, 0:1], start=True, stop=True), reads=[p.b_kz, p.b_ones_b], writes=[p.b_P[5]])
        S.op("dve", lambda e: e.tensor_tensor(out=p.nst[:], in0=p.nst[:], in1=p.gmx[:, c, :], op=ALU.mult), reads=[p.b_nst, p.b_gmx], writes=[p.b_nst])
        S.op("dve", lambda e: e.tensor_tensor(out=p.nst[:], in0=p.nst[:], in1=p.P[5][:, 16:24], op=ALU.add), reads=[p.b_nst, p.b_P[5]], writes=[p.b_nst])
        if full:
            S.op("act", lambda e: e.activation(out=p.nstb[:], in_=p.nst[:], func=AF.Copy), reads=[p.b_nst], writes=[p.b_nstb])
        if full:
            self.groupnorm_heads([0, 1, 2, 3], c, gate=False)
            self.make_ynT(c, p.ml_gnT, p.b_ml_gnT)

    def ml_layer(self):
        p, S = self, self.S
        if p.phase in (None, 4):
            self.ml_part_a()
        if p.phase is None:
            self.exchange(4)
        if p.phase in (None, 5):
            self.ml_part_b()

    def ml_part_a(self):
        p, S = self, self.S
        for i in range(8):
            S.op("dve", lambda e: e.memset(p.R[:, i, :], 0.0), writes=[p.b_R[i]])
        S.op("dve", lambda e: e.memset(p.nst[:], 0.0), writes=[p.b_nst])
        S.op("dve", lambda e: e.memset(p.fsum[:], 0.0), writes=[p.b_fsum])
        for g in range(NG):
            self.ml_group(g, full=False)
        S.op("dve", lambda e: e.memset(p.wk[4][:], 0.0), writes=[p.b_wk[4]])
        S.op("dve", lambda e: e.tensor_copy(out=p.wk[4][:, 0:8], in_=p.nst[:]), reads=[p.b_nst], writes=[p.b_wk[4]])
        S.op("dve", lambda e: e.tensor_copy(out=p.wk[4][:, 8:12], in_=p.fsum[:]), reads=[p.b_fsum], writes=[p.b_wk[4]])
        pairs = [(p.loc[4][i * 128:(i + 1) * 128, :], p.R[:, i, :]) for i in range(8)] + [(p.loc[4][1024:1152, :], p.wk[4][:])]
        self.dma_group("sp", "loc4", pairs, reads=p.b_R + [p.b_wk[4]], writes=[p.b_loc[4]])

    def ml_part_b(self):
        p, S = self, self.S
        gt = p.gath[4]
        pairs = [(p.stage[cp:cp + 1, 0:4], gt[cp * 1152 + 1024: cp * 1152 + 1025, 8:12]) for cp in range(NCORES)]
        self.dma_group("sp", "stage", pairs, reads=[p.b_gath[4]], writes=[p.b_stage])
        for cp in range(NCORES):
            S.op("dve", lambda e: e.tensor_tensor(out=p.stage[0:8, 32 + cp * 4:36 + cp * 4], in0=p.msel[0:8, cp * 4:cp * 4 + 4], in1=p.stage[0:8, 0:4], op=ALU.mult),
                 reads=[p.b_stage, p.b_msel], writes=[p.b_stage])
        S.op("pe", lambda e: e.matmul(p.P[5][:, 0:32], lhsT=p.ones_f[0:8, :], rhs=p.stage[0:8, 32:64], start=True, stop=True),
             reads=[p.b_ones_f, p.b_stage], writes=[p.b_P[5]])
        S.op("act", lambda e: e.activation(out=p.mcoef[:], in_=p.P[5][:, 0:32], func=AF.Exp), reads=[p.b_P[5]], writes=[p.b_mcoef])
        S.op("dve", lambda e: e.tensor_tensor(out=p.mcoef[:], in0=p.mcoef[:], in1=p.valid[:], op=ALU.mult), reads=[p.b_mcoef, p.b_valid], writes=[p.b_mcoef])
        self.combine_state(4, p.mcoef, p.b_mcoef, nrows=9)
        S.op("dve", lambda e: e.memset(p.nst[:], 0.0), writes=[p.b_nst])
        for cp in range(NCORES):
            tmp, bt = p.wk[cp % 2], p.b_wk[cp % 2]
            r0 = cp * 1152 + 1024
            S.dma("sp", tmp[:, 0:8], gt[r0:r0 + 128, 0:8], reads=[p.b_gath[4]], writes=[bt], key=f"cmb{cp % 2}")
            for hh in range(H):
                S.op("dve", lambda e: e.scalar_tensor_tensor(out=p.nst[:, 2 * hh:2 * hh + 2], in0=tmp[:, 2 * hh:2 * hh + 2],
                                                             scalar=p.mcoef[:, cp * H + hh:cp * H + hh + 1], in1=p.nst[:, 2 * hh:2 * hh + 2],
                                                             op0=ALU.mult, op1=ALU.add), reads=[bt, p.b_mcoef, p.b_nst], writes=[p.b_nst])
        for hh in range(H):
            self.refresh_Rb(hh)
        S.op("act", lambda e: e.activation(out=p.nstb[:], in_=p.nst[:], func=AF.Copy), reads=[p.b_nst], writes=[p.b_nstb])
        self.load_gate(1, 0)
        for g in range(NG):
            self.ml_group(g, full=True)

    def _body(self):
        p, S = self, self.S
        ph = p.phase
        if ph in (None, 1, 2):
            self.ret_layer()
        if p.stop_after == "dbg_ret":
            return
        if p.stop_after == "ret":
            return self.copy_out(p.xa, p.b_xa)
        if ph is None:
            self.exchange(2)
        if ph in (None, 3):
            self.ffn_layer(0, p.xa, p.b_xa, p.xb, p.b_xb, 2, 3, final=False)
        if p.stop_after == "ffn0":
            return self.copy_out(p.xb, p.b_xb)
        if ph is None:
            self.exchange(3)
        if ph in (None, 4, 5):
            self.ml_layer()
        if p.stop_after == "ml":
            return self.copy_out(p.xa, p.b_xa)
        if ph is None:
            self.exchange(5)
        if ph in (None, 6):
            S.dma("sp", p.cos[:], p.final_g[0:1, 0:512].partition_broadcast(128).rearrange("p o n -> p (o n)"), writes=[p.b_cos], key="fg0")
            S.dma("sp", p.sin[:], p.final_g[0:1, 512:1024].partition_broadcast(128).rearrange("p o n -> p (o n)"), writes=[p.b_sin], key="fg1")
            self.ffn_layer(1, p.xa, p.b_xa, p.out, p.b_out, 5, None, final=True)

    def copy_out(self, src, src_bufs):
        p, S = self, self.S
        for g in range(NG):
            for c in range(G):
                r0 = g * GT + c * CH
                S.dma("sp", p.x_g[:, c, :], src[r0:r0 + CH, :], reads=src_bufs, writes=[p.b_xg[c]], key=f"xg{c}")
            pairs = [(p.out[g * GT + c * CH: g * GT + (c + 1) * CH, :], p.x_g[:, c, :]) for c in range(G)]
            self.dma_group("sp", "st_ou", pairs, reads=p.b_xg, writes=[p.b_out[g]])

    def _finish(self):
        p, S = self, self.S
        bufs = list(p.b_out) + [p.b_loc[k] for k in p.b_loc] + p.dbg_bufs + list(p.b_xa) + list(p.b_xb) + [p.b_modscr, p.b_modT_o, p.b_kvst]
        S.wait_bufs("sp", bufs)
        S.barrier()


_PROG_CACHE = {}
MODE = "host6"
STOP_AFTER = None


def _get_prog(mode, stop_after, phase=None):
    key = (mode, stop_after, phase)
    if key not in _PROG_CACHE:
        pr = Prog("host" if mode.startswith("host") else mode, stop_after, phase)
        pr.build()
        _PROG_CACHE[key] = pr
    return _PROG_CACHE[key]


def _in_maps(inputs):
    f = lambda a: np.ascontiguousarray(np.asarray(a), dtype=np.float32)
    tabs, lg = _const_tables()
    x = f(inputs["x"]).reshape(SEQ, D)
    pos = np.ascontiguousarray(np.asarray(inputs["positions"]).astype(np.int32)).reshape(SEQ)
    shared = {
        "cT": np.ascontiguousarray(f(inputs["c"]).reshape(KC, 128).T),
        "ada_w": f(inputs["ada_w"]),
        "ada_bT": np.ascontiguousarray(f(inputs["ada_b"]).reshape(2, 48, 128).transpose(2, 0, 1)),
        "ntgT": np.ascontiguousarray(f(inputs["norm_tok_g"]).reshape(2, KC, 128).transpose(2, 0, 1)),
        "nfgT": np.ascontiguousarray(f(inputs["norm_ffn_g"]).reshape(2, KC, 128).transpose(2, 0, 1)),
        "ret_w_in": f(inputs["ret_w_in"]).reshape(D, 6144),
        "ret_gnT": np.ascontiguousarray(f(inputs["ret_gn_g"]).reshape(16, 128).T),
        "ret_w_out": f(inputs["ret_w_out"]).reshape(2048, D),
        "ml_w_in": f(inputs["ml_w_in"]).reshape(D, 6152),
        "ml_b_gate": f(inputs["ml_b_gate"]).reshape(1, 8),
        "ml_cwT": np.ascontiguousarray(f(inputs["ml_conv_w"]).reshape(64, 128).T),
        "ml_cbT": np.ascontiguousarray(f(inputs["ml_conv_b"]).reshape(16, 128).T),
        "ml_gnT": np.ascontiguousarray(f(inputs["ml_gn_g"]).reshape(16, 128).T),
        "ml_w_out": f(inputs["ml_w_out"]).reshape(2048, D),
        "ffn_w_up": f(inputs["ffn_w_up"]),
        "ffn_cwT": np.ascontiguousarray(f(inputs["ffn_conv_w"]).reshape(2, 132, 128).transpose(2, 0, 1)),
        "ffn_cbT": np.ascontiguousarray(f(inputs["ffn_conv_b"]).reshape(2, 44, 128).transpose(2, 0, 1)),
        "ffn_w_down": f(inputs["ffn_w_down"]),
        "final_g": f(inputs["final_g"]).reshape(1, D),
    }
    shared.update(tabs)
    maps = []
    for c in range(NCORES):
        m = dict(shared)
        m["x"] = x[c * T:(c + 1) * T]
        m["pos"] = pos[c * T:(c + 1) * T].reshape(1, T)
        m.update(_core_tables(c, lg))
        maps.append(m)
    return maps


def _launch(pr, maps, extra):
    ms = []
    for c, m in enumerate(maps):
        mm = {k: v for k, v in m.items() if k in pr.in_names}
        for k, v in extra.items():
            if k in pr.in_names:
                mm[k] = v[c] if isinstance(v, list) else v
        ms.append(mm)
    return run_bass_kernel_spmd(pr.nc, ms, core_ids=list(range(NCORES))).results


def kernel(**inputs):
    maps = _in_maps(inputs)
    if MODE == "host6":
        extra = {}
        res = None
        for ph in range(1, 7):
            pr = _get_prog("host", None, ph)
            res = _launch(pr, maps, extra)
            for k in (1, 2, 3, 4, 5):
                if f"loc{k}" in res[0]:
                    extra[f"gath{k}"] = np.concatenate([res[c][f"loc{k}"] for c in range(NCORES)], axis=0)
            for nm in ("xa", "xb"):
                if nm in res[0]:
                    extra[nm] = [res[c][nm] for c in range(NCORES)]
            if "kst_o" in res[0]:
                extra["kst_i"] = [res[c]["kst_o"] for c in range(NCORES)]
                extra["vst_i"] = [res[c]["vst_o"] for c in range(NCORES)]
            if "modT_o" in res[0]:
                extra["modT_i"] = [res[c]["modT_o"] for c in range(NCORES)]
                extra["modscr_i"] = [res[c]["modscr_o"] for c in range(NCORES)]
    elif MODE == "host":
        pr = _get_prog("host", STOP_AFTER)
        extra = {f"gath{k}": np.zeros((NCORES * r, cdim), np.float32) for k, (r, cdim) in EX_SIZES.items()}
        order = {None: [1, 2, 3, 4, 5], "ret": [1], "ffn0": [1, 2], "ml": [1, 2, 3, 4]}[STOP_AFTER]
        res = None
        for step in range(len(order) + 1):
            res = _launch(pr, maps, extra)
            if step < len(order):
                k = order[step]
                extra[f"gath{k}"] = np.concatenate([res[c][f"loc{k}"] for c in range(NCORES)], axis=0)
    else:
        pr = _get_prog("cc", None)
        res = _launch(pr, maps, {})
    out = np.concatenate([res[c]["out"] for c in range(NCORES)], axis=0)
    return out.reshape(1, SEQ, D).astype(np.float32)
```

```python
import contextlib
import math
import numpy as np
import concourse.bass as bass
import concourse.mybir as mybir
from concourse.bass_utils import run_bass_kernel_spmd

F32 = mybir.dt.float32
BF16 = mybir.dt.bfloat16
I32 = mybir.dt.int32
AF = mybir.ActivationFunctionType
ALU = mybir.AluOpType

NCORES = 8
SEQ = 16384
D = 1024
T = SEQ // NCORES
CH = 128
G = 4
GT = G * CH
NG = T // GT
KC = D // 128
H = 4
DK = 256
DV = 512
DFF = 2816
EPS = 1e-6
HW = 3
TWO_PI = 2.0 * math.pi
C1 = 6.28125
C2 = TWO_PI - C1
PI_SAFE = 3.1415925


class Buf:
    __slots__ = ("name", "w", "r")

    def __init__(self, name):
        self.name = name
        self.w = None
        self.r = {}


class Sched:
    ENGS = ("pe", "dve", "act", "pool", "sp")

    def __init__(self, nc, stack):
        self.nc = nc
        self.stack = stack
        self.eng = {"pe": nc.tensor, "dve": nc.vector, "act": nc.scalar, "pool": nc.gpsimd, "sp": nc.sync}
        self.sems = {}
        self.cnt = {}
        self.seen = {e: {} for e in self.ENGS}
        for e in self.ENGS:
            self.sems[e] = stack.enter_context(nc.semaphore("s_" + e))
            self.cnt[e] = 0
        self.ninst = 0

    def buf(self, name):
        return Buf(name)

    def _dma_sem(self, key):
        k = "dma_" + key
        if k not in self.sems:
            self.sems[k] = self.stack.enter_context(self.nc.semaphore("s_" + k))
            self.cnt[k] = 0
        return k

    def _wait(self, e, deps, wdeps=()):
        need = {}
        for d in deps:
            if d is None:
                continue
            k, v = d
            if k == e and e == "pe":
                continue
            if need.get(k, 0) < v:
                need[k] = v
        for d in wdeps:
            if d is None:
                continue
            k, v = d
            if k == e:
                continue
            if need.get(k, 0) < v:
                need[k] = v
        for k, v in need.items():
            if self.seen[e].get(k, 0) < v:
                self.eng[e].wait_ge(self.sems[k], v)
                self.seen[e][k] = v

    @staticmethod
    def _deps(reads, writes):
        deps = []
        for b in reads:
            deps.append(b.w)
        for b in writes:
            deps.append(b.w)
            deps.extend(b.r.items())
        return deps

    @staticmethod
    def _deps2(reads, writes):
        rd = [b.w for b in reads]
        wd = []
        for b in writes:
            wd.append(b.w)
            wd.extend(b.r.items())
        return rd, wd

    @staticmethod
    def _record(ev, reads, writes):
        k, v = ev
        for b in reads:
            if b.r.get(k, 0) < v:
                b.r[k] = v
        for b in writes:
            b.w = ev
            b.r = {}

    def op(self, e, fn, reads=(), writes=()):
        rd, wd = self._deps2(reads, writes)
        self._wait(e, rd, wd)
        ins = fn(self.eng[e])
        self.cnt[e] += 1
        ins.then_inc(self.sems[e], 1)
        self.ninst += 1
        self._record((e, self.cnt[e]), reads, writes)
        return ins

    def dma(self, q, out, in_, reads=(), writes=(), key=None, **kw):
        k = self._dma_sem(key)
        self._wait(q, self._deps(reads, writes))
        ins = self.eng[q].dma_start(out=out, in_=in_, **kw)
        self.cnt[k] += 16
        ins.then_inc(self.sems[k], 16)
        self.ninst += 1
        self._record((k, self.cnt[k]), reads, writes)
        return ins

    def wait_bufs(self, e, bufs):
        deps = []
        for b in bufs:
            deps.append(b.w)
            deps.extend(b.r.items())
        self._wait(e, deps)

    def barrier(self):
        for e in self.ENGS:
            deps = [(k, v) for k, v in self.cnt.items() if v > 0]
            self._wait(e, deps)


def _const_tables():
    t = {}
    n = np.arange(128, dtype=np.float32)
    inv_freq = (10000.0 ** (-(np.arange(0, DK, 2, dtype=np.float32)) / DK)).astype(np.float32)
    t["inv_freq"] = inv_freq.reshape(128, 1).astype(np.float32)
    lg = np.log(1.0 - 2.0 ** (-5.0 - np.arange(H, dtype=np.float64)))
    i = np.arange(128)[None, :]
    j = np.arange(128)[:, None]
    dt = np.zeros((128, H, 128), np.float64)
    for h in range(H):
        dt[:, h, :] = np.where(i >= j, np.exp((i - j) * lg[h]), 0.0) * (DK ** -0.5)
    t["ret_dt"] = dt.reshape(128, H * 128).astype(np.float32)
    t["ret_xi"] = np.exp((np.arange(128)[:, None] + 1.0) * lg[None, :]).astype(np.float32)
    t["ret_zs"] = (np.exp((127.0 - np.arange(128)[:, None]) * lg[None, :]) * (DK ** -0.5)).astype(np.float32)
    t["neg"] = np.where(j <= i, 0.0, -30000.0).astype(np.float32)
    t["ut"] = np.where(j <= i, 1.0, 0.0).astype(np.float32)
    t["ident"] = np.eye(128, dtype=np.float32)
    return t, lg


def _core_tables(c, lg):
    sel = np.zeros((128, NCORES), np.float32)
    if c > 0:
        sel[:, c - 1] = 1.0
    nf = np.full((128, 1), 0.0 if c == 0 else 1.0, np.float32)
    rc = np.zeros((128, NCORES * H), np.float32)
    for cp in range(c):
        for h in range(H):
            rc[:, cp * H + h] = np.exp(T * (c - 1 - cp) * lg[h])
    valid = np.zeros((128, NCORES * H), np.float32)
    for cp in range(c):
        valid[:, cp * H:(cp + 1) * H] = 1.0
    msel = np.zeros((NCORES, NCORES, H), np.float32)
    for cpp in range(NCORES):
        for cp in range(NCORES):
            if cp < cpp < c:
                msel[cpp, cp, :] = 1.0
    selmat = np.zeros((NCORES * HW, HW), np.float32)
    if c > 0:
        for r in range(HW):
            selmat[(c - 1) * HW + r, r] = 1.0
    return {"selmat": selmat, "sel": sel, "nf": nf, "ret_coef": rc, "valid": valid, "msel": msel.reshape(NCORES, NCORES * H)}


EX_SIZES = {1: (8 * 128, 512), 2: (HW, D), 3: (HW, D), 4: (9 * 128, 512), 5: (HW, D)}


class Prog:
    def __init__(self, mode="host", stop_after=None, phase=None):
        self.mode = mode
        self.stop_after = stop_after
        self.phase = phase
        self.in_names = []
        self.layers = [0, 1] if phase in (None, 1) else ([0] if phase <= 3 else [1])
        self.mod_layers = [0, 1] if phase in (None, 1) else []
        self.nc = bass.Bass("TRN2", target_bir_lowering=False)
        self.st = contextlib.ExitStack()
        self.debug = stop_after is not None and stop_after.startswith("dbg")
        self.dbg_bufs = []

    def din(self, name, shape, dt=F32):
        self.in_names.append(name)
        return self.nc.dram_tensor(name, list(shape), dt, kind="ExternalInput").ap()

    def dout(self, name, shape, dt=F32):
        return self.nc.dram_tensor(name, list(shape), dt, kind="ExternalOutput").ap()

    def dint(self, name, shape, dt=F32):
        return self.nc.dram_tensor(name, list(shape), dt, kind="Internal").ap()

    def sb(self, name, shape, dt=F32):
        t = self.st.enter_context(self.nc.sbuf_tensor("sb_" + name, list(shape), dt))
        b = Buf(name)
        return t, b

    def ps(self, name, shape, dt=F32):
        t = self.st.enter_context(self.nc.psum_tensor("ps_" + name, list(shape), dt))
        return t

    def build(self):
        with self.st:
            self.S = Sched(self.nc, self.st)
            self._declare()
            self._setup()
            self._body()
            self._finish()
        return self.nc

    def _declare(self):
        p = self
        p.x = p.din("x", [T, D])
        p.pos = p.din("pos", [1, T], I32)
        p.c_in = p.din("cT", [128, KC])
        p.ada_w = p.din("ada_w", [2, D, 6 * D])
        p.ada_b = p.din("ada_bT", [128, 2, 48])
        p.ntg = p.din("ntgT", [128, 2, KC])
        p.nfg = p.din("nfgT", [128, 2, KC])
        p.ret_w_in = p.din("ret_w_in", [D, 6144])
        p.ret_gn = p.din("ret_gnT", [128, 16])
        p.ret_w_out = p.din("ret_w_out", [2048, D])
        p.ml_w_in = p.din("ml_w_in", [D, 6152])
        p.ml_bg = p.din("ml_b_gate", [1, 8])
        p.ml_cw = p.din("ml_cwT", [128, 64])
        p.ml_cb = p.din("ml_cbT", [128, 16])
        p.ml_gn = p.din("ml_gnT", [128, 16])
        p.ml_w_out = p.din("ml_w_out", [2048, D])
        p.ffn_w_up = p.din("ffn_w_up", [2, D, 2 * DFF])
        p.ffn_cw = p.din("ffn_cwT", [128, 2, 132])
        p.ffn_cb = p.din("ffn_cbT", [128, 2, 44])
        p.ffn_w_down = p.din("ffn_w_down", [2, DFF, D])
        p.final_g = p.din("final_g", [1, D])
        p.t_inv_freq = p.din("inv_freq", [128, 1])
        p.t_ret_dt = p.din("ret_dt", [128, 512])
        p.t_ret_xi = p.din("ret_xi", [128, 4])
        p.t_ret_zs = p.din("ret_zs", [128, 4])
        p.t_neg = p.din("neg", [128, 128])
        p.t_ut = p.din("ut", [128, 128])
        p.t_ident = p.din("ident", [128, 128])
        p.t_sel = p.din("sel", [128, NCORES])
        p.t_selmat = p.din("selmat", [NCORES * HW, HW])
        p.t_nf = p.din("nf", [128, 1])
        p.t_ret_coef = p.din("ret_coef", [128, NCORES * H])
        p.t_valid = p.din("valid", [128, NCORES * H])
        p.t_msel = p.din("msel", [NCORES, NCORES * H])
        ph = p.phase
        p.out = p.dout("out", [T, D]) if ph in (None, 6) or p.stop_after else None
        if ph is None:
            p.modscr = p.dint("modscr", [2, 48, 128])
        elif ph == 1:
            p.modscr = p.dout("modscr_o", [2, 48, 128])
            p.modT_o = p.dout("modT_o", [128, 96])
        else:
            p.modscr = p.din("modscr_i", [2, 48, 128])
            p.modT_i = p.din("modT_i", [128, 96])
        p.b_modscr = Buf("modscr")
        p.b_modT_o = Buf("modT_o")
        kinds = {None: ("int", "int"), 1: (None, None), 2: ("out", None), 3: ("in", "out"), 4: (None, "in"), 5: ("out", "in"), 6: ("in", None)}[ph]
        mk = {"int": p.dint, "in": p.din, "out": p.dout, None: (lambda *a: None)}
        p.xa = mk[kinds[0]]("xa", [T, D])
        p.xb = mk[kinds[1]]("xb", [T, D])
        p.relay = ph in (1, 2, 4, 5)
        if ph in (1, 4):
            p.kst = p.dout("kst_o", [NG, 128, 8 * GT], BF16)
            p.vst = p.dout("vst_o", [NG, 128, G * 2048], BF16)
        elif ph in (2, 5):
            p.kst = p.din("kst_i", [NG, 128, 8 * GT], BF16)
            p.vst = p.din("vst_i", [NG, 128, G * 2048], BF16)
        p.b_kvst = Buf("kvst")
        sizes = dict(EX_SIZES)
        loc_ph = {1: 1, 2: 2, 3: 3, 4: 4, 5: 5}
        gath_ph = {1: (2,), 2: (3,), 3: (4, 5), 4: (5,), 5: (6,)}
        p.loc = {}
        p.gath = {}
        for k, (r, cdim) in sizes.items():
            if p.mode == "host":
                if ph is None or loc_ph[k] == ph:
                    p.loc[k] = p.dout(f"loc{k}", [r, cdim])
                if ph is None or ph in gath_ph[k]:
                    p.gath[k] = p.din(f"gath{k}", [NCORES * r, cdim])
            else:
                p.loc[k] = p.dint(f"loc{k}", [r, cdim])
                p.gath[k] = p.dint(f"gath{k}", [NCORES * r, cdim])
        p.b_loc = {k: Buf(f"loc{k}") for k in sizes}
        p.b_gath = {k: Buf(f"gath{k}") for k in sizes}
        p.b_xa = [Buf(f"xa{g}") for g in range(NG)]
        p.b_xb = [Buf(f"xb{g}") for g in range(NG)]
        p.b_out = [Buf(f"out{g}") for g in range(NG)]

        p.ident_f, p.b_ident_f = p.sb("ident_f", [128, 128])
        p.ident_b, p.b_ident_b = p.sb("ident_b", [128, 128], BF16)
        p.ones_f, p.b_ones_f = p.sb("ones_f", [128, 128])
        p.ones_b, p.b_ones_b = p.sb("ones_b", [128, 8], BF16)
        p.ret_dt, p.b_ret_dt = p.sb("ret_dt", [128, 512])
        p.ret_xi, p.b_ret_xi = p.sb("ret_xi", [128, 4])
        p.ret_zs, p.b_ret_zs = p.sb("ret_zs", [128, 4])
        p.neg, p.b_neg = p.sb("negm", [128, 128])
        p.ut, p.b_ut = p.sb("utm", [128, 128])
        p.inv_freq, p.b_inv_freq = p.sb("inv_freq_s", [128, 1])
        p.sel, p.b_sel = p.sb("sel_s", [128, NCORES])
        p.selmat, p.b_selmat = p.sb("selmat_s", [NCORES * HW, HW])
        p.adabT, p.b_adabT = p.sb("adabT", [128, 2, 48])
        p.cT_f, p.b_cT_f = p.sb("cT_f", [128, KC])
        p.nf, p.b_nf = p.sb("nf_s", [128, 1])
        p.ret_coef, p.b_ret_coef = p.sb("ret_coef_s", [128, NCORES * H])
        p.valid, p.b_valid = p.sb("valid_s", [128, NCORES * H])
        p.msel, p.b_msel = p.sb("msel_s", [NCORES, NCORES * H])
        p.consts = [p.b_ident_f, p.b_ident_b, p.b_ones_f, p.b_ones_b]
        p.modT, p.b_modT = p.sb("modT", [128, 2, 48])
        p.ntgT, p.b_ntgT = p.sb("ntgT", [128, 2, KC])
        p.nfgT, p.b_nfgT = p.sb("nfgT", [128, 2, KC])
        p.gsc, p.b_gsc = p.sb("gsc", [128, 4, KC])
        p.ret_gnT, p.b_ret_gnT = p.sb("ret_gnT", [128, 16])
        p.ml_gnT, p.b_ml_gnT = p.sb("ml_gnT", [128, 16])
        p.ml_cwT, p.b_ml_cwT = p.sb("ml_cwT", [128, 64])
        p.ml_cbT, p.b_ml_cbT = p.sb("ml_cbT", [128, 16])
        p.ffn_cwT, p.b_ffn_cwT = p.sb("ffn_cwT", [128, 2, 132])
        p.ffn_cbT, p.b_ffn_cbT = p.sb("ffn_cbT", [128, 2, 44])
        p.bg_bc, p.b_bg_bc = p.sb("bg_bc", [128, 8])
        p.gt_bc, p.b_gt_bc = p.sb("gt_bc", [128, D])
        p.cT_b, p.b_cT_b = p.sb("cT_b", [128, KC], BF16)
        p.stage, p.b_stage = p.sb("stage", [128, 128])
        p.NR = 3
        p.wt = []
        p.b_wt = []
        for i in range(p.NR):
            t, b = p.sb(f"wt{i}", [128, 4096], BF16)
            p.wt.append(t)
            p.b_wt.append(b)
        p.ring_pos = 0
        p.rot_bank = 0
        p.rot_acc = 0
        p.rot_ub = 0
        p.b_actT = [Buf(f"actT{i}") for i in range(22)]
        p.x_g, _ = p.sb("x_g", [128, G, D])
        p.b_xg = [Buf(f"xg{c}") for c in range(G)]
        p.sg_all = p.x_g[:].bitcast(BF16)
        p.xn, p.b_xn = p.sb("xn", [128, D])
        p.xn2, p.b_xn2 = p.sb("xn2", [128, D])
        p.junk, p.b_junk = p.sb("junk", [128, D], BF16)
        p.uhalo, p.b_uhalo = p.sb("uhalo", [128, 44, HW])
        p.ss, p.b_ss = p.sb("ss", [128, 8])
        p.rstd, p.b_rstd = p.sb("rstd", [128, 8])
        p.hT, p.b_hT = p.sb("hT", [128, KC, GT], BF16)
        p.xh, p.b_xh = p.sb("xh", [32, D])
        p.hTh, p.b_hTh = p.sb("hTh", [128, KC, 32], BF16)
        p.big_a, p.b_big_a = p.sb("big_a", [128, 22 * GT], BF16)
        p.qT, p.b_qT = p.sb("qT", [128, 8, GT], BF16)
        p.kT, p.b_kT = p.sb("kT", [128, 8, GT], BF16)
        p.v_all, _ = p.sb("v_all", [128, G, 2048], BF16)
        p.b_v = [Buf(f"v{c}") for c in range(G)]
        p.R, _ = p.sb("R", [128, 8, 512])
        p.b_R = [Buf(f"R{i}") for i in range(8)]
        p.Rb, _ = p.sb("Rb", [128, 8, 512], BF16)
        p.b_Rb = [Buf(f"Rb{i}") for i in range(8)]
        p.nst, p.b_nst = p.sb("nst", [128, 8])
        p.nstb, p.b_nstb = p.sb("nstb", [128, 8], BF16)
        p.fsum, p.b_fsum = p.sb("fsum", [128, 4])
        p.wk = []
        p.b_wk = []
        for i in range(5):
            t, b = p.sb(f"wk{i}", [128, 512])
            p.wk.append(t)
            p.b_wk.append(b)
        p.cos, p.b_cos = p.sb("cos", [128, GT])
        p.sin, p.b_sin = p.sb("sin", [128, GT])
        p.yg, p.b_yg = p.sb("yg", [128, 2048], BF16)
        p.sTm, p.b_sTm = p.sb("sTm", [128, 512], BF16)
        p.kz, p.b_kz = p.sb("kz", [128, 1024], BF16)
        p.sm, p.b_sm = p.sb("sm", [128, 96])
        p.mcoef, p.b_mcoef = p.sb("mcoef", [128, NCORES * H])
        p.gat, p.b_gat = p.sb("gat", [128, G, 8])
        p.gmx, p.b_gmx = p.sb("gmx", [128, G, 8])
        p.gm, p.b_gm = p.sb("gm", [128, 7, 4 * G])
        p.ubuf = []
        p.b_ubuf = []
        for i in range(2):
            t, b = p.sb(f"ubuf{i}", [128, HW + GT])
            p.ubuf.append(t)
            p.b_ubuf.append(b)
        p.P = [p.ps(f"P{i}", [128, 512]) for i in range(6)]
        p.b_P = [Buf(f"P{i}") for i in range(6)]
        p.Pb = [p.ps(f"Pb{i}", [128, 1024], BF16) for i in range(2)]
        p.b_Pb = [Buf(f"Pb{i}") for i in range(2)]

    def dma_group(self, q, key, pairs, reads=(), writes=()):
        S = self.S
        k = S._dma_sem(key)
        S._wait(q, S._deps(reads, writes))
        for out, in_ in pairs:
            ins = S.eng[q].dma_start(out=out, in_=in_)
            S.cnt[k] += 16
            ins.then_inc(S.sems[k], 16)
            S.ninst += 1
        S._record((k, S.cnt[k]), reads, writes)

    def dump(self, name, ap, bufs):
        if not self.debug:
            return
        shape = list(ap.shape)
        d = self.nc.dram_tensor("dbg_" + name, shape, ap.dtype, kind="ExternalOutput").ap()
        b = Buf("dbg_" + name)
        self.dbg_bufs.append(b)
        self.S.dma("sp", d, ap, reads=bufs, writes=[b], key="dbg_" + name)

    def load_T(self, src_rows, n, dst_ap, dst_buf):
        p, S = self, self.S
        S.dma("sp", p.stage[0:n, :], src_rows, writes=[p.b_stage], key="stage")
        S.op("pe", lambda e: e.transpose(p.P[5][:, 0:n], p.stage[0:n, :], p.ident_f[0:n, 0:n]),
             reads=[p.b_stage, p.b_ident_f], writes=[p.b_P[5]])
        S.op("dve", lambda e: e.tensor_copy(out=dst_ap, in_=p.P[5][:, 0:n]), reads=[p.b_P[5]], writes=[dst_buf])

    def ring(self):
        i = self.ring_pos % self.NR
        self.ring_pos += 1
        return self.wt[i], self.b_wt[i], f"w{i}"

    def load_std_piece(self, W2d, c0, w=512):
        t, b, key = self.ring()
        view = t[:, 0:8 * w].rearrange("p (k n) -> p k n", k=8)
        src = W2d.rearrange("(k p) n -> p k n", p=128)
        pairs = [(view[:, 0:4, :], src[:, 0:4, c0:c0 + w]), (view[:, 4:8, :], src[:, 4:8, c0:c0 + w])]
        self.dma_group("pool", key, pairs, writes=[b])
        return view, b

    def load_rows_piece(self, W2d, k0, nk, c0, w):
        t, b, key = self.ring()
        view = t[:, 0:nk * w].rearrange("p (k n) -> p k n", k=nk)
        src = W2d.rearrange("(k p) n -> p k n", p=128)
        pairs = []
        step = 4
        for a in range(0, nk, step):
            e = min(nk, a + step)
            pairs.append((view[:, a:e, :], src[:, k0 + a:k0 + e, c0:c0 + w]))
        self.dma_group("pool", key, pairs, writes=[b])
        return view, b

    def _setup(self):
        p, S = self, self.S
        loads = [(p.ident_f, p.b_ident_f, p.t_ident), (p.ret_dt, p.b_ret_dt, p.t_ret_dt),
                 (p.ret_xi, p.b_ret_xi, p.t_ret_xi), (p.ret_zs, p.b_ret_zs, p.t_ret_zs),
                 (p.neg, p.b_neg, p.t_neg), (p.ut, p.b_ut, p.t_ut), (p.inv_freq, p.b_inv_freq, p.t_inv_freq),
                 (p.sel, p.b_sel, p.t_sel), (p.nf, p.b_nf, p.t_nf), (p.ret_coef, p.b_ret_coef, p.t_ret_coef),
                 (p.valid, p.b_valid, p.t_valid), (p.msel, p.b_msel, p.t_msel)]
        self.dma_group("sp", "setup", [(t[:], src) for t, b, src in loads], writes=[b for t, b, s in loads])
        S.dma("sp", p.bg_bc[:], p.ml_bg[0:1, :].partition_broadcast(128).rearrange("p o n -> p (o n)"),
              writes=[p.b_bg_bc], key="setup2")
        S.op("dve", lambda e: e.memset(p.ones_f[:], 1.0), writes=[p.b_ones_f])
        S.op("dve", lambda e: e.memset(p.ones_b[:], 1.0), writes=[p.b_ones_b])
        S.op("dve", lambda e: e.memset(p.xh[:], 0.0), writes=[p.b_xh])
        S.op("dve", lambda e: e.tensor_copy(out=p.ident_b[:], in_=p.ident_f[:]), reads=[p.b_ident_f], writes=[p.b_ident_b])
        vec = [(p.ntgT, p.b_ntgT, p.ntg), (p.nfgT, p.b_nfgT, p.nfg), (p.ffn_cwT, p.b_ffn_cwT, p.ffn_cw), (p.ffn_cbT, p.b_ffn_cbT, p.ffn_cb),
               (p.ret_gnT, p.b_ret_gnT, p.ret_gn), (p.ml_gnT, p.b_ml_gnT, p.ml_gn), (p.ml_cwT, p.b_ml_cwT, p.ml_cw), (p.ml_cbT, p.b_ml_cbT, p.ml_cb),
               (p.adabT, p.b_adabT, p.ada_b), (p.cT_f, p.b_cT_f, p.c_in), (p.selmat, p.b_selmat, p.t_selmat)]
        self.dma_group("sp", "setup3", [(t[:], s) for t, b, s in vec], writes=[b for t, b, s in vec])
        if p.mod_layers:
            S.op("act", lambda e: e.activation(out=p.cT_b[:], in_=p.cT_f[:], func=AF.Silu), reads=[p.b_cT_f], writes=[p.b_cT_b])
            for l in p.mod_layers:
                for pc in range(12):
                    wv, wb = self.load_std_piece(p.ada_w[l], pc * 512)
                    for ct in range(4):
                        col = l * 48 + pc * 4 + ct
                        for kc in range(KC):
                            S.op("pe", lambda e: e.matmul(p.P[4][:, col:col + 1], lhsT=wv[:, kc, ct * 128:(ct + 1) * 128],
                                                          rhs=p.cT_b[:, kc:kc + 1], start=(kc == 0), stop=(kc == KC - 1)),
                                 reads=[wb, p.b_cT_b], writes=[p.b_P[4]])
            for l in p.mod_layers:
                S.op("dve", lambda e: e.tensor_tensor(out=p.modT[:, l, :], in0=p.adabT[:, l, :], in1=p.P[4][:, l * 48:(l + 1) * 48], op=ALU.add),
                     reads=[p.b_P[4], p.b_adabT], writes=[p.b_modT])
                S.op("pe", lambda e: e.transpose(p.P[5][0:48, 0:128], p.modT[:, l, :], p.ident_f[:, :]),
                     reads=[p.b_modT, p.b_ident_f], writes=[p.b_P[5]])
                S.op("dve", lambda e: e.tensor_copy(out=p.stage[0:48, :], in_=p.P[5][0:48, 0:128]), reads=[p.b_P[5]], writes=[p.b_stage])
                S.dma("sp", p.modscr[l], p.stage[0:48, :], reads=[p.b_stage], writes=[p.b_modscr], key="modscr")
            if p.phase == 1:
                S.dma("sp", p.modT_o[:, :], p.modT[:].rearrange("p l n -> p (l n)"), reads=[p.b_modT], writes=[p.b_modT_o], key="modT_o")
        else:
            S.dma("sp", p.modT[:].rearrange("p l n -> p (l n)"), p.modT_i[:, :], writes=[p.b_modT], key="modT_i")
        for l in p.layers:
            S.op("dve", lambda e: e.scalar_tensor_tensor(out=p.gsc[:, 2 * l, :], in0=p.modT[:, l, 8:16], scalar=1.0,
                                                         in1=p.ntgT[:, l, :], op0=ALU.add, op1=ALU.mult),
                 reads=[p.b_modT, p.b_ntgT], writes=[p.b_gsc])
            S.op("dve", lambda e: e.scalar_tensor_tensor(out=p.gsc[:, 2 * l + 1, :], in0=p.modT[:, l, 32:40], scalar=1.0,
                                                         in1=p.nfgT[:, l, :], op0=ALU.add, op1=ALU.mult),
                 reads=[p.b_modT, p.b_nfgT], writes=[p.b_gsc])

    def load_gate(self, l, which):
        p, S = self, self.S
        r0 = 16 if which == 0 else 40
        src = p.modscr[l, r0:r0 + 8, :].rearrange("(o a) b -> o (a b)", o=1).partition_broadcast(128).rearrange("p o n -> p (o n)")
        S.dma("sp", p.gt_bc[:], src, reads=[p.b_modscr], writes=[p.b_gt_bc], key="gt")

    def norm_rows(self, xt, bx, npart, gidx, sh, dst_fn, dst_buf, col):
        p, S = self, self.S
        S.op("act", lambda e: e.activation(out=p.xn[0:npart, :], in_=xt, func=AF.Square, accum_out=p.ss[0:npart, col:col + 1]),
             reads=[bx], writes=[p.b_xn, p.b_ss])
        S.op("dve", lambda e: e.tensor_scalar(out=p.ss[0:npart, col:col + 1], in0=p.ss[0:npart, col:col + 1], scalar1=1.0 / D, scalar2=EPS,
                                              op0=ALU.mult, op1=ALU.add), reads=[p.b_ss], writes=[p.b_ss])
        S.op("act", lambda e: e.activation(out=p.ss[0:npart, col:col + 1], in_=p.ss[0:npart, col:col + 1], func=AF.Sqrt),
             reads=[p.b_ss], writes=[p.b_ss])
        S.op("dve", lambda e: e.reciprocal(out=p.rstd[0:npart, col:col + 1], in_=p.ss[0:npart, col:col + 1]),
             reads=[p.b_ss], writes=[p.b_rstd])
        S.op("act", lambda e: e.activation(out=p.xn[0:npart, :], in_=xt, func=AF.Copy, scale=p.rstd[0:npart, col:col + 1]),
             reads=[bx, p.b_rstd], writes=[p.b_xn])
        for half in range(2):
            bank = p.P[half]
            for k4 in range(4):
                kc = half * 4 + k4
                S.op("pe", lambda e: e.transpose(bank[:, k4 * 128:k4 * 128 + npart], p.xn[0:npart, kc * 128:(kc + 1) * 128],
                                                 p.ident_f[0:npart, 0:npart]),
                     reads=[p.b_xn, p.b_ident_f], writes=[p.b_P[half]])
            for k4 in range(4):
                kc = half * 4 + k4
                src = bank[:, k4 * 128:k4 * 128 + npart]
                if kc % 2 == 0:
                    S.op("act", lambda e: e.activation(out=dst_fn(kc), in_=src, func=AF.Identity,
                                                       scale=p.gsc[:, gidx, kc:kc + 1], bias=sh[:, kc:kc + 1]),
                         reads=[p.b_P[half], p.b_gsc, p.b_modT], writes=[dst_buf])
                else:
                    S.op("dve", lambda e: e.tensor_scalar(out=dst_fn(kc), in0=src, scalar1=p.gsc[:, gidx, kc:kc + 1],
                                                          scalar2=sh[:, kc:kc + 1], op0=ALU.mult, op1=ALU.add),
                         reads=[p.b_P[half], p.b_gsc, p.b_modT], writes=[dst_buf])

    def norm_group(self, src, src_bufs, g, gidx, sh):
        p, S = self, self.S
        for c in range(G):
            r0 = g * GT + c * CH
            S.dma("sp", p.x_g[:, c, :], src[r0:r0 + CH, :], reads=src_bufs, writes=[p.b_xg[c]], key=f"xg{c}")
        for c in range(G):
            S.op("act", lambda e: e.activation(out=p.junk[:], in_=p.x_g[:, c, :], func=AF.Square, accum_out=p.ss[:, c:c + 1]),
                 reads=[p.b_xg[c]], writes=[p.b_junk, p.b_ss])
        S.op("dve", lambda e: e.tensor_scalar(out=p.ss[:, 0:G], in0=p.ss[:, 0:G], scalar1=1.0 / D, scalar2=EPS, op0=ALU.mult, op1=ALU.add),
             reads=[p.b_ss], writes=[p.b_ss])
        S.op("act", lambda e: e.activation(out=p.ss[:, 0:G], in_=p.ss[:, 0:G], func=AF.Sqrt), reads=[p.b_ss], writes=[p.b_ss])
        S.op("dve", lambda e: e.reciprocal(out=p.rstd[:, 0:G], in_=p.ss[:, 0:G]), reads=[p.b_ss], writes=[p.b_rstd])
        for c in range(G):
            xn, b_xn = (p.xn, p.b_xn) if c % 2 == 0 else (p.xn2, p.b_xn2)
            S.op("act", lambda e: e.activation(out=xn[:], in_=p.x_g[:, c, :], func=AF.Copy, scale=p.rstd[:, c:c + 1]),
                 reads=[p.b_xg[c], p.b_rstd], writes=[b_xn])
            for half in range(2):
                bi = 2 * (c % 2) + half
                bank = p.P[bi]
                for k4 in range(4):
                    kc = half * 4 + k4
                    S.op("pe", lambda e: e.transpose(bank[:, k4 * 128:(k4 + 1) * 128], xn[:, kc * 128:(kc + 1) * 128], p.ident_f[:, :]),
                         reads=[b_xn, p.b_ident_f], writes=[p.b_P[bi]])
                for k4 in range(4):
                    kc = half * 4 + k4
                    srcp = bank[:, k4 * 128:(k4 + 1) * 128]
                    dst = p.hT[:, kc, c * CH:(c + 1) * CH]
                    if half == 0:
                        S.op("act", lambda e: e.activation(out=dst, in_=srcp, func=AF.Identity,
                                                           scale=p.gsc[:, gidx, kc:kc + 1], bias=sh[:, kc:kc + 1]),
                             reads=[p.b_P[bi], p.b_gsc, p.b_modT], writes=[p.b_hT])
                    else:
                        S.op("dve", lambda e: e.tensor_scalar(out=dst, in0=srcp, scalar1=p.gsc[:, gidx, kc:kc + 1],
                                                              scalar2=sh[:, kc:kc + 1], op0=ALU.mult, op1=ALU.add),
                             reads=[p.b_P[bi], p.b_gsc, p.b_modT], writes=[p.b_hT])

    def load_halo(self, src, src_bufs, g, ex):
        p, S = self, self.S
        if g > 0:
            r0 = g * GT - HW
            S.dma("sp", p.xh[0:HW, :], src[r0:r0 + HW, :], reads=src_bufs, writes=[p.b_xh], key="xh")
        else:
            gt = p.gath[ex]
            nr = NCORES * HW
            S.dma("sp", p.xn[0:nr, :], gt[:, :], reads=[p.b_gath[ex]], writes=[p.b_xn], key="xnh")
            for half in range(2):
                S.op("pe", lambda e: e.matmul(p.P[half][0:HW, :], lhsT=p.selmat[0:nr, 0:HW], rhs=p.xn[0:nr, half * 512:(half + 1) * 512],
                                              start=True, stop=True), reads=[p.b_selmat, p.b_xn], writes=[p.b_P[half]])
                S.op("dve", lambda e: e.tensor_copy(out=p.xh[0:HW, half * 512:(half + 1) * 512], in_=p.P[half][0:HW, :]),
                     reads=[p.b_P[half]], writes=[p.b_xh])

    def norm_halo(self, gidx, sh):
        p = self
        self.norm_rows(p.xh[0:32, :], p.b_xh, 32, gidx, sh, lambda kc: p.hTh[:, kc, :], p.b_hTh, 4)

    def rope_tables(self, g):
        p, S = self, self.S
        posi = p.wk[4][:].bitcast(I32)
        src = p.pos[0:1, g * GT:(g + 1) * GT].partition_broadcast(128).rearrange("p o n -> p (o n)")
        S.dma("sp", posi, src, writes=[p.b_wk[4]], key="posi")
        ang, b_ang = p.wk[0], p.b_wk[0]
        S.op("dve", lambda e: e.tensor_copy(out=p.wk[1][:], in_=posi), reads=[p.b_wk[4]], writes=[p.b_wk[1]])
        S.op("dve", lambda e: e.tensor_scalar(out=ang[:], in0=p.wk[1][:], scalar1=p.inv_freq[:, 0:1], scalar2=None, op0=ALU.mult),
             reads=[p.b_wk[1], p.b_inv_freq], writes=[b_ang])
        for dst, b_dst, shift in ((p.sin, p.b_sin, 0.0), (p.cos, p.b_cos, 0.5 * math.pi)):
            xs, b_xs = p.wk[1], p.b_wk[1]
            kf, b_kf = p.wk[2], p.b_wk[2]
            ki = p.wk[3][:].bitcast(I32)
            b_ki = p.b_wk[3]
            S.op("dve", lambda e: e.tensor_scalar(out=xs[:], in0=ang[:], scalar1=shift, scalar2=None, op0=ALU.add),
                 reads=[b_ang], writes=[b_xs])
            S.op("dve", lambda e: e.tensor_scalar(out=kf[:], in0=xs[:], scalar1=1.0 / TWO_PI, scalar2=None, op0=ALU.mult),
                 reads=[b_xs], writes=[b_kf])
            S.op("dve", lambda e: e.tensor_copy(out=ki, in_=kf[:]), reads=[b_kf], writes=[b_ki])
            S.op("dve", lambda e: e.tensor_copy(out=kf[:], in_=ki), reads=[b_ki], writes=[b_kf])
            S.op("dve", lambda e: e.scalar_tensor_tensor(out=xs[:], in0=kf[:], scalar=-C1, in1=xs[:], op0=ALU.mult, op1=ALU.add),
                 reads=[b_kf, b_xs], writes=[b_xs])
            S.op("dve", lambda e: e.scalar_tensor_tensor(out=xs[:], in0=kf[:], scalar=-C2, in1=xs[:], op0=ALU.mult, op1=ALU.add),
                 reads=[b_kf, b_xs], writes=[b_xs])
            S.op("dve", lambda e: e.tensor_scalar(out=kf[:], in0=xs[:], scalar1=-math.pi, scalar2=TWO_PI, op0=ALU.is_lt, op1=ALU.mult),
                 reads=[b_xs], writes=[b_kf])
            S.op("dve", lambda e: e.tensor_tensor(out=xs[:], in0=xs[:], in1=kf[:], op=ALU.add), reads=[b_xs, b_kf], writes=[b_xs])
            S.op("dve", lambda e: e.tensor_scalar(out=kf[:], in0=xs[:], scalar1=math.pi, scalar2=-TWO_PI, op0=ALU.is_gt, op1=ALU.mult),
                 reads=[b_xs], writes=[b_kf])
            S.op("dve", lambda e: e.tensor_tensor(out=xs[:], in0=xs[:], in1=kf[:], op=ALU.add), reads=[b_xs, b_kf], writes=[b_xs])
            S.op("dve", lambda e: e.tensor_scalar(out=xs[:], in0=xs[:], scalar1=-PI_SAFE, scalar2=PI_SAFE, op0=ALU.max, op1=ALU.min),
                 reads=[b_xs], writes=[b_xs])
            S.op("act", lambda e: e.activation(out=dst[:], in_=xs[:], func=AF.Sin), reads=[b_xs], writes=[b_dst])

    def rope_pair(self, hh, dst, b_dst):
        p, S = self, self.S
        i1, i2 = 2 * (hh % 2), 2 * (hh % 2) + 1
        b1, b2 = p.P[i1], p.P[i2]
        A, Bm, C_, Dm = p.wk[0], p.wk[1], p.wk[2], p.wk[3]
        S.op("dve", lambda e: e.tensor_tensor(out=A[:], in0=b1[:], in1=p.cos[:], op=ALU.mult), reads=[p.b_P[i1], p.b_cos], writes=[p.b_wk[0]])
        S.op("dve", lambda e: e.tensor_tensor(out=Bm[:], in0=b2[:], in1=p.sin[:], op=ALU.mult), reads=[p.b_P[i2], p.b_sin], writes=[p.b_wk[1]])
        S.op("dve", lambda e: e.tensor_tensor(out=C_[:], in0=b1[:], in1=p.sin[:], op=ALU.mult), reads=[p.b_P[i1], p.b_sin], writes=[p.b_wk[2]])
        S.op("dve", lambda e: e.tensor_tensor(out=Dm[:], in0=b2[:], in1=p.cos[:], op=ALU.mult), reads=[p.b_P[i2], p.b_cos], writes=[p.b_wk[3]])
        S.op("dve", lambda e: e.tensor_tensor(out=dst[:, 2 * hh, :], in0=A[:], in1=Bm[:], op=ALU.subtract),
             reads=[p.b_wk[0], p.b_wk[1]], writes=[b_dst])
        S.op("dve", lambda e: e.tensor_tensor(out=dst[:, 2 * hh + 1, :], in0=C_[:], in1=Dm[:], op=ALU.add),
             reads=[p.b_wk[2], p.b_wk[3]], writes=[b_dst])

    def proj_A(self, wv, wb, ct, bank_i, hT=None, b_hT=None, n=GT):
        p, S = self, self.S
        hT = p.hT if hT is None else hT
        b_hT = p.b_hT if b_hT is None else b_hT
        for kc in range(KC):
            S.op("pe", lambda e: e.matmul(p.P[bank_i][:, 0:n], lhsT=wv[:, kc, ct * 128:(ct + 1) * 128], rhs=hT[:, kc, 0:n],
                                          start=(kc == 0), stop=(kc == KC - 1)),
                 reads=[wb, b_hT], writes=[p.b_P[bank_i]])

    def proj_B(self, wv, wb, c, bank_i, w=512):
        p, S = self, self.S
        for kc in range(KC):
            S.op("pe", lambda e: e.matmul(p.P[bank_i][:, 0:w], lhsT=p.hT[:, kc, c * CH:(c + 1) * CH], rhs=wv[:, kc, 0:w],
                                          start=(kc == 0), stop=(kc == KC - 1)),
                 reads=[wb, p.b_hT], writes=[p.b_P[bank_i]])

    def exchange(self, k):
        p, S = self, self.S
        if p.mode == "host":
            return
        S.wait_bufs("pool", [p.b_loc[k], p.b_gath[k]])
        ins = p.nc.gpsimd.collective_compute("AllGather", ALU.bypass, replica_groups=[list(range(NCORES))],
                                             ins=[p.loc[k][:, :]], outs=[p.gath[k][:, :]])
        key = S._dma_sem(f"cc{k}")
        S.cnt[key] += 16
        ins.then_inc(S.sems[key], 16)
        S._record((key, S.cnt[key]), [p.b_loc[k]], [p.b_gath[k]])

    def make_kz(self, c, scale_ap_fn):
        p, S = self, self.S
        cs = slice(c * CH, (c + 1) * CH)
        for i in range(8):
            S.op("pe", lambda e: e.transpose(p.Pb[0][:, i * 128:(i + 1) * 128], p.kT[:, i, cs], p.ident_b[:, :]),
                 reads=[p.b_kT, p.b_ident_b], writes=[p.b_Pb[0]])
        for hh in range(H):
            sc, sbufs = scale_ap_fn(hh)
            S.op("act", lambda e: e.activation(out=p.kz[:, hh * 256:(hh + 1) * 256], in_=p.Pb[0][:, hh * 256:(hh + 1) * 256],
                                               func=AF.Copy, scale=sc),
                 reads=[p.b_Pb[0]] + sbufs, writes=[p.b_kz])

    def state_update(self, c, hh, decay, dbufs=()):
        p, S = self, self.S
        dbufs = list(dbufs)
        for half in range(2):
            i = 2 * hh + half
            S.op("pe", lambda e: e.matmul(p.P[2 + half][:, :], lhsT=p.kz[:, hh * 256 + half * 128: hh * 256 + (half + 1) * 128],
                                          rhs=p.v_all[:, c, hh * 512:(hh + 1) * 512], start=True, stop=True),
                 reads=[p.b_kz, p.b_v[c]], writes=[p.b_P[2 + half]])
            S.op("dve", lambda e: e.scalar_tensor_tensor(out=p.R[:, i, :], in0=p.R[:, i, :], scalar=decay, in1=p.P[2 + half][:, :],
                                                         op0=ALU.mult, op1=ALU.add),
                 reads=[p.b_R[i], p.b_P[2 + half]] + dbufs, writes=[p.b_R[i]])

    def refresh_Rb(self, hh):
        p, S = self, self.S
        for half in range(2):
            i = 2 * hh + half
            S.op("act", lambda e: e.activation(out=p.Rb[:, i, :], in_=p.R[:, i, :], func=AF.Copy),
                 reads=[p.b_R[i]], writes=[p.b_Rb[i]])

    def groupnorm_heads(self, ywk, c, gate):
        p, S = self, self.S
        for hh in range(H):
            S.op("dve", lambda e: e.bn_stats(out=p.sm[:, 6 * hh:6 * hh + 6], in_=p.wk[ywk[hh]][:]), reads=[p.b_wk[ywk[hh]]], writes=[p.b_sm])
            S.op("dve", lambda e: e.bn_aggr(out=p.sm[:, 24 + 2 * hh:26 + 2 * hh], in_=p.sm[:, 6 * hh:6 * hh + 6]), reads=[p.b_sm], writes=[p.b_sm])
        var = p.sm[:, 24:32].rearrange("p (h t) -> p h t", t=2)[:, :, 1]
        S.op("dve", lambda e: e.tensor_scalar(out=p.sm[:, 32:36], in0=var, scalar1=EPS, scalar2=None, op0=ALU.add), reads=[p.b_sm], writes=[p.b_sm])
        S.op("act", lambda e: e.activation(out=p.sm[:, 32:36], in_=p.sm[:, 32:36], func=AF.Ln), reads=[p.b_sm], writes=[p.b_sm])
        S.op("act", lambda e: e.activation(out=p.sm[:, 32:36], in_=p.sm[:, 32:36], func=AF.Exp, scale=-0.5), reads=[p.b_sm], writes=[p.b_sm])
        for hh in range(H):
            w = p.wk[ywk[hh]]
            if gate:
                S.op("dve", lambda e: e.tensor_scalar(out=w[:], in0=w[:], scalar1=p.sm[:, 24 + 2 * hh:25 + 2 * hh], scalar2=p.sm[:, 32 + hh:33 + hh],
                                                      op0=ALU.subtract, op1=ALU.mult), reads=[p.b_wk[ywk[hh]], p.b_sm], writes=[p.b_wk[ywk[hh]]])
                sgv = p.sg_all[:, c, hh * 512:(hh + 1) * 512]
                S.op("dve", lambda e: e.tensor_tensor(out=p.yg[:, hh * 512:(hh + 1) * 512], in0=w[:], in1=sgv, op=ALU.mult),
                     reads=[p.b_wk[ywk[hh]], p.b_xg[c]], writes=[p.b_yg])
            else:
                S.op("dve", lambda e: e.tensor_scalar(out=p.yg[:, hh * 512:(hh + 1) * 512], in0=w[:], scalar1=p.sm[:, 24 + 2 * hh:25 + 2 * hh],
                                                      scalar2=p.sm[:, 32 + hh:33 + hh], op0=ALU.subtract, op1=ALU.mult),
                     reads=[p.b_wk[ywk[hh]], p.b_sm], writes=[p.b_yg])

    def make_ynT(self, c, gnT, b_gnT):
        p, S = self, self.S
        ynT = p.big_a[:, 0:16 * GT].rearrange("p (k n) -> p k n", k=16)
        for kc in range(16):
            bi = kc // 8
            S.op("pe", lambda e: e.transpose(p.Pb[bi][:, (kc % 8) * 128:(kc % 8 + 1) * 128], p.yg[:, kc * 128:(kc + 1) * 128], p.ident_b[:, :]),
                 reads=[p.b_yg, p.b_ident_b], writes=[p.b_Pb[bi]])
        for kc in range(16):
            bi = kc // 8
            src = p.Pb[bi][:, (kc % 8) * 128:(kc % 8 + 1) * 128]
            dst = ynT[:, kc, c * CH:(c + 1) * CH]
            if bi == 0:
                S.op("act", lambda e: e.activation(out=dst, in_=src, func=AF.Copy, scale=gnT[:, kc:kc + 1]),
                     reads=[p.b_Pb[bi], b_gnT], writes=[p.b_big_a])
            else:
                S.op("dve", lambda e: e.tensor_scalar(out=dst, in0=src, scalar1=gnT[:, kc:kc + 1], scalar2=None, op0=ALU.mult),
                     reads=[p.b_Pb[bi], b_gnT], writes=[p.b_big_a])

    def out_proj(self, g, W_out, src, src_bufs, dst, dst_bufs, halo_ex):
        p, S = self, self.S
        ynT = p.big_a[:, 0:16 * GT].rearrange("p (k n) -> p k n", k=16)
        for c in range(G):
            r0 = g * GT + c * CH
            S.dma("sp", p.x_g[:, c, :], src[r0:r0 + CH, :], reads=src_bufs, writes=[p.b_xg[c]], key=f"xg{c}")
        for cp in range(4):
            wv, wb = self.load_rows_piece(W_out, 0, 16, cp * 256, 256)
            for c in range(G):
                bi = (cp * G + c) % 6
                for kc in range(16):
                    S.op("pe", lambda e: e.matmul(p.P[bi][:, 0:256], lhsT=ynT[:, kc, c * CH:(c + 1) * CH], rhs=wv[:, kc, :],
                                                  start=(kc == 0), stop=(kc == 15)),
                         reads=[wb, p.b_big_a], writes=[p.b_P[bi]])
                S.op("dve", lambda e: e.tensor_tensor(out=p.wk[c][:, 0:256], in0=p.P[bi][:, 0:256], in1=p.gt_bc[:, cp * 256:(cp + 1) * 256], op=ALU.mult),
                     reads=[p.b_P[bi], p.b_gt_bc], writes=[p.b_wk[c]])
            for c in range(G):
                S.op("dve", lambda e: e.tensor_tensor(out=p.x_g[:, c, cp * 256:(cp + 1) * 256], in0=p.x_g[:, c, cp * 256:(cp + 1) * 256],
                                                      in1=p.wk[c][:, 0:256], op=ALU.add),
                     reads=[p.b_wk[c], p.b_xg[c]], writes=[p.b_xg[c]])
        self.store_group(g, dst, dst_bufs, halo_ex)

    def store_group(self, g, dst, dst_bufs, halo_ex):
        p, S = self, self.S
        pairs = [(dst[g * GT + c * CH: g * GT + (c + 1) * CH, :], p.x_g[:, c, :]) for c in range(G)]
        self.dma_group("sp", f"st_{dst_bufs[0].name[:2]}", pairs, reads=p.b_xg, writes=[dst_bufs[g]])
        if g == NG - 1 and halo_ex is not None:
            S.dma("sp", p.loc[halo_ex][:, :], p.x_g[128 - HW:128, G - 1, :], reads=[p.b_xg[G - 1]], writes=[p.b_loc[halo_ex]], key=f"loc{halo_ex}")

    def kv_store(self, g):
        p = self
        self.dma_group("sp", "kvst", [(p.kst[g], p.kT[:].rearrange("p a n -> p (a n)")), (p.vst[g], p.v_all[:].rearrange("p a n -> p (a n)"))],
                       reads=[p.b_kT] + p.b_v, writes=[p.b_kvst])

    def kv_load(self, g):
        p = self
        self.dma_group("sp", "kvld", [(p.kT[:].rearrange("p a n -> p (a n)"), p.kst[g]), (p.v_all[:].rearrange("p a n -> p (a n)"), p.vst[g])],
                       writes=[p.b_kT] + p.b_v)

    def ret_group(self, g, full):
        p, S = self, self.S
        sh = p.modT[:, 0, 0:8]
        self.norm_group(p.x, [], g, 0, sh)
        self.rope_tables(g)
        W = p.ret_w_in
        reuse = full and p.relay
        plist = ([0, 1] if full else []) + ([] if reuse else [2, 3])
        if reuse:
            self.kv_load(g)
        for pc in plist:
            wv, wb = self.load_std_piece(W, pc * 512)
            dst, b_dst = (p.qT, p.b_qT) if pc < 2 else (p.kT, p.b_kT)
            for ct in range(4):
                self.proj_A(wv, wb, ct, ct)
            for j in range(2):
                self.rope_pair(2 * (pc % 2) + j, dst, b_dst)
        for hh in ([] if reuse else range(H)):
            wv, wb = self.load_std_piece(W, 2048 + hh * 512)
            for c in range(G):
                self.proj_B(wv, wb, c, c)
                dstv = p.v_all[:, c, hh * 512:(hh + 1) * 512]
                if c % 2 == 0:
                    S.op("act", lambda e: e.activation(out=dstv, in_=p.P[c][:, :], func=AF.Copy), reads=[p.b_P[c]], writes=[p.b_v[c]])
                else:
                    S.op("dve", lambda e: e.tensor_copy(out=dstv, in_=p.P[c][:, :]), reads=[p.b_P[c]], writes=[p.b_v[c]])
        if (not full) and p.relay:
            self.kv_store(g)
        if full:
            for hh in range(H):
                wv, wb = self.load_std_piece(W, 4096 + hh * 512)
                for c in range(G):
                    self.proj_B(wv, wb, c, c)
                    dsts = p.sg_all[:, c, hh * 512:(hh + 1) * 512]
                    S.op("act", lambda e: e.activation(out=dsts, in_=p.P[c][:, :], func=AF.Silu), reads=[p.b_P[c]], writes=[p.b_xg[c]])
        if full and g == 0:
            self.dump("hT", p.hT[:], [p.b_hT])
            self.dump("cos", p.cos[:], [p.b_cos])
            self.dump("sin", p.sin[:], [p.b_sin])
            self.dump("qT", p.qT[:], [p.b_qT])
            self.dump("kT", p.kT[:], [p.b_kT])
            self.dump("v_all", p.v_all[:], p.b_v)
            self.dump("sg_all", p.sg_all, p.b_xg)
        for c in range(G):
            self.ret_chunk(c, full)
            if full and g == 0 and c == 1:
                self.dump("yg", p.yg[:], [p.b_yg])
                self.dump("sm", p.sm[:], [p.b_sm])
                self.dump("wk0", p.wk[0][:], [p.b_wk[0]])
                self.dump("wk3", p.wk[3][:], [p.b_wk[3]])
                self.dump("sTm", p.sTm[:], [p.b_sTm])
                self.dump("kz", p.kz[:], [p.b_kz])
                self.dump("R", p.R[:], p.b_R)
        if full and g == 0:
            self.dump("ynT", p.big_a[:, 0:16 * GT], [p.b_big_a])
        if full:
            self.out_proj(g, p.ret_w_out, p.x, [], p.xa, p.b_xa, 2)
        if full and g == 0:
            self.dump("xg", p.x_g[:], p.b_xg)

    def ret_chunk(self, c, full):
        p, S = self, self.S
        cs = slice(c * CH, (c + 1) * CH)
        self.make_kz(c, lambda hh: (p.ret_zs[:, hh:hh + 1], [p.b_ret_zs]))
        gam = [float(np.exp(128.0 * np.log(1.0 - 2.0 ** (-5.0 - h)))) for h in range(H)]
        if not full:
            for hh in range(H):
                self.state_update(c, hh, gam[hh])
            return
        for hh in range(H):
            for half in range(2):
                S.op("pe", lambda e: e.matmul(p.P[4][:, hh * 128:(hh + 1) * 128], lhsT=p.kT[:, 2 * hh + half, cs], rhs=p.qT[:, 2 * hh + half, cs],
                                              start=(half == 0), stop=(half == 1)),
                     reads=[p.b_kT, p.b_qT], writes=[p.b_P[4]])
        S.op("dve", lambda e: e.tensor_tensor(out=p.sTm[:], in0=p.P[4][:, :], in1=p.ret_dt[:], op=ALU.mult),
             reads=[p.b_P[4], p.b_ret_dt], writes=[p.b_sTm])
        for hh in range(H):
            S.op("pe", lambda e: e.matmul(p.P[0][:, :], lhsT=p.sTm[:, hh * 128:(hh + 1) * 128], rhs=p.v_all[:, c, hh * 512:(hh + 1) * 512],
                                          start=True, stop=True), reads=[p.b_sTm, p.b_v[c]], writes=[p.b_P[0]])
            for half in range(2):
                i = 2 * hh + half
                S.op("pe", lambda e: e.matmul(p.P[1][:, :], lhsT=p.qT[:, i, cs], rhs=p.Rb[:, i, :], start=(half == 0), stop=(half == 1)),
                     reads=[p.b_qT, p.b_Rb[i]], writes=[p.b_P[1]])
            S.op("act", lambda e: e.activation(out=p.wk[4][:], in_=p.P[1][:, :], func=AF.Copy, scale=p.ret_xi[:, hh:hh + 1]),
                 reads=[p.b_P[1], p.b_ret_xi], writes=[p.b_wk[4]])
            S.op("dve", lambda e: e.tensor_tensor(out=p.wk[hh][:], in0=p.wk[4][:], in1=p.P[0][:, :], op=ALU.add),
                 reads=[p.b_wk[4], p.b_P[0]], writes=[p.b_wk[hh]])
            self.state_update(c, hh, gam[hh])
            self.refresh_Rb(hh)
        self.groupnorm_heads([0, 1, 2, 3], c, gate=True)
        self.make_ynT(c, p.ret_gnT, p.b_ret_gnT)

    def ret_layer(self):
        p, S = self, self.S
        for i in range(8):
            S.op("dve", lambda e: e.memset(p.R[:, i, :], 0.0), writes=[p.b_R[i]])
        if p.stop_after == "dbg_ret":
            for hh in range(H):
                self.refresh_Rb(hh)
            self.load_gate(0, 0)
            self.dump("modT", p.modT[:], [p.b_modT])
            self.dump("gsc", p.gsc[:], [p.b_gsc])
            self.dump("gt_bc", p.gt_bc[:], [p.b_gt_bc])
            self.ret_group(0, full=True)
            return
        if p.phase in (None, 1):
            self.ret_part_a()
        if p.phase is None:
            self.exchange(1)
        if p.phase in (None, 2):
            self.ret_part_b()

    def ret_part_a(self):
        p, S = self, self.S
        for g in range(NG):
            self.ret_group(g, full=False)
        self.dma_group("sp", "loc1", [(p.loc[1][i * 128:(i + 1) * 128, :], p.R[:, i, :]) for i in range(8)],
                       reads=p.b_R, writes=[p.b_loc[1]])

    def ret_part_b(self):
        p, S = self, self.S
        self.combine_state(1, p.ret_coef, p.b_ret_coef, nrows=8)
        for hh in range(H):
            self.refresh_Rb(hh)
        self.load_gate(0, 0)
        for g in range(NG):
            self.ret_group(g, full=True)

    def combine_state(self, ex, coef, b_coef, nrows):
        p, S = self, self.S
        gt = p.gath[ex]
        per = nrows * 128 if ex == 1 else 9 * 128
        for i in range(8):
            hh = i // 2
            for cp in range(NCORES):
                tmp, bt = p.wk[cp % 2], p.b_wk[cp % 2]
                r0 = cp * per + i * 128
                S.dma("sp", tmp[:], gt[r0:r0 + 128, :], reads=[p.b_gath[ex]], writes=[bt], key=f"cmb{cp % 2}")
                if cp == 0:
                    S.op("dve", lambda e: e.tensor_scalar(out=p.R[:, i, :], in0=tmp[:], scalar1=coef[:, cp * H + hh: cp * H + hh + 1], scalar2=None,
                                                          op0=ALU.mult), reads=[bt, b_coef], writes=[p.b_R[i]])
                else:
                    S.op("dve", lambda e: e.scalar_tensor_tensor(out=p.R[:, i, :], in0=tmp[:], scalar=coef[:, cp * H + hh: cp * H + hh + 1],
                                                                 in1=p.R[:, i, :], op0=ALU.mult, op1=ALU.add),
                         reads=[bt, b_coef, p.b_R[i]], writes=[p.b_R[i]])

    def ffn_layer(self, l, src, src_bufs, dst, dst_bufs, ex_in, ex_out, final):
        p, S = self, self.S
        S.barrier()
        self.load_gate(l, 1)
        sh = p.modT[:, l, 24:32]
        gidx = 2 * l + 1
        actT = p.big_a[:, :].rearrange("p (k n) -> p k n", k=22)
        cw = p.ffn_cwT
        cb = p.ffn_cbT
        Wup = p.ffn_w_up[l]
        Wdn = p.ffn_w_down[l]
        for g in range(NG):
            if g == 0:
                self.load_halo(src, src_bufs, g, ex_in)
                self.norm_halo(gidx, sh)
            self.norm_group(src, src_bufs, g, gidx, sh)
            srcw = Wup.rearrange("(k p) n -> p k n", p=128)

            def load_up(pc):
                t, b, key = self.ring()
                wv = t[:, 0:4096].rearrange("p (k n) -> p k n", k=8)
                pairs = []
                for kh in range(2):
                    pairs.append((wv[:, 4 * kh:4 * kh + 4, 0:256], srcw[:, 4 * kh:4 * kh + 4, pc * 256:(pc + 1) * 256]))
                    pairs.append((wv[:, 4 * kh:4 * kh + 4, 256:512], srcw[:, 4 * kh:4 * kh + 4, DFF + pc * 256: DFF + (pc + 1) * 256]))
                self.dma_group("pool", key, pairs, writes=[b])
                return wv, b

            loaders = [(lambda pc=pc: load_up(pc)) for pc in range(11)]
            loaders += [(lambda cp=cp, kh=kh: self.load_rows_piece(Wdn, 11 * kh, 11, cp * 256, 256)) for cp in range(4) for kh in range(2)]
            loaded = {}

            def get_piece(i, ahead=2):
                for j in range(i, min(i + ahead + 1, len(loaders))):
                    if j not in loaded:
                        loaded[j] = loaders[j]()
                return loaded.pop(i)

            banks = [0, 1, 2, 3, 5]
            for pc in range(11):
                wv, b = get_piece(pc)
                for j in range(2):
                    tl = []
                    for ct in (j, 2 + j):
                        tile_i = 2 * pc + j
                        chan = tile_i if ct < 2 else 22 + tile_i
                        bi = banks[self.rot_bank % len(banks)]
                        self.rot_bank += 1
                        ai = self.rot_acc % 5
                        self.rot_acc += 1
                        ui = self.rot_ub % 2
                        self.rot_ub += 1
                        tl.append((ct, chan, bi, p.wk[ai], p.b_wk[ai], p.ubuf[ui], p.b_ubuf[ui]))
                    for ct, chan, bi, acc, b_acc, ub, b_ub in tl:
                        self.proj_A(wv, b, ct, bi)
                        if g == 0:
                            for kc in range(KC):
                                S.op("pe", lambda e: e.matmul(p.P[4][:, 0:HW], lhsT=wv[:, kc, ct * 128:(ct + 1) * 128], rhs=p.hTh[:, kc, 0:HW],
                                                              start=(kc == 0), stop=(kc == KC - 1)), reads=[b, p.b_hTh], writes=[p.b_P[4]])
                            S.op("dve", lambda e: e.tensor_scalar(out=ub[:, 0:HW], in0=p.P[4][:, 0:HW], scalar1=p.nf[:, 0:1], scalar2=None, op0=ALU.mult),
                                 reads=[p.b_P[4], p.b_nf], writes=[b_ub])
                        else:
                            S.op("act", lambda e: e.activation(out=ub[:, 0:HW], in_=p.uhalo[:, chan, :], func=AF.Copy), reads=[p.b_uhalo], writes=[b_ub])
                    for ct, chan, bi, acc, b_acc, ub, b_ub in tl:
                        S.op("act", lambda e: e.activation(out=ub[:, HW:HW + GT], in_=p.P[bi][:, :], func=AF.Copy), reads=[p.b_P[bi]], writes=[b_ub])
                        S.op("act", lambda e: e.activation(out=acc[:], in_=p.P[bi][:, :], func=AF.Identity, scale=cw[:, l, 88 + chan:89 + chan],
                                                           bias=cb[:, l, chan:chan + 1]), reads=[p.b_P[bi], p.b_ffn_cwT, p.b_ffn_cbT], writes=[b_acc])
                    if g < NG - 1:
                        for ct, chan, bi, acc, b_acc, ub, b_ub in tl:
                            S.op("act", lambda e: e.activation(out=p.uhalo[:, chan, :], in_=ub[:, GT:GT + HW], func=AF.Copy), reads=[b_ub], writes=[p.b_uhalo])
                    for tap, off in ((1, HW - 1), (0, HW - 2)):
                        for ct, chan, bi, acc, b_acc, ub, b_ub in tl:
                            S.op("dve", lambda e: e.scalar_tensor_tensor(out=acc[:], in0=ub[:, off:off + GT], scalar=cw[:, l, tap * 44 + chan:tap * 44 + chan + 1],
                                                                         in1=acc[:], op0=ALU.mult, op1=ALU.add), reads=[b_ub, b_acc, p.b_ffn_cwT], writes=[b_acc])
                    (_, _, _, aa, b_aa, _, _), (_, _, _, ab, b_ab, _, _) = tl
                    S.op("act", lambda e: e.activation(out=aa[:], in_=aa[:], func=AF.Silu), reads=[b_aa], writes=[b_aa])
                    S.op("dve", lambda e: e.tensor_tensor(out=actT[:, 2 * pc + j, :], in0=aa[:], in1=ab[:], op=ALU.mult),
                         reads=[b_aa, b_ab], writes=[p.b_actT[2 * pc + j]])
            for cp in range(4):
                for kh in range(2):
                    wv, wb = get_piece(11 + cp * 2 + kh)
                    for c in range(G):
                        for k in range(11):
                            S.op("pe", lambda e: e.matmul(p.P[c][:, 0:256], lhsT=actT[:, 11 * kh + k, c * CH:(c + 1) * CH], rhs=wv[:, k, :],
                                                          start=(kh == 0 and k == 0), stop=(kh == 1 and k == 10)),
                                 reads=[wb, p.b_actT[11 * kh + k]], writes=[p.b_P[c]])
                for c in range(G):
                    S.op("dve", lambda e: e.tensor_tensor(out=p.wk[c][:, 0:256], in0=p.P[c][:, 0:256], in1=p.gt_bc[:, cp * 256:(cp + 1) * 256], op=ALU.mult),
                         reads=[p.b_P[c], p.b_gt_bc], writes=[p.b_wk[c]])
                for c in range(G):
                    S.op("dve", lambda e: e.tensor_tensor(out=p.x_g[:, c, cp * 256:(cp + 1) * 256], in0=p.x_g[:, c, cp * 256:(cp + 1) * 256],
                                                          in1=p.wk[c][:, 0:256], op=ALU.add),
                         reads=[p.b_wk[c], p.b_xg[c]], writes=[p.b_xg[c]])
            if final:
                self.final_norm_group()
            self.store_group(g, dst, dst_bufs, ex_out)
        S.barrier()

    def final_norm_group(self):
        p, S = self, self.S
        for c in range(G):
            S.op("act", lambda e: e.activation(out=p.xn[:], in_=p.x_g[:, c, :], func=AF.Square, accum_out=p.ss[:, c:c + 1]),
                 reads=[p.b_xg[c]], writes=[p.b_xn, p.b_ss])
        S.op("dve", lambda e: e.tensor_scalar(out=p.ss[:, 0:G], in0=p.ss[:, 0:G], scalar1=1.0 / D, scalar2=EPS, op0=ALU.mult, op1=ALU.add),
             reads=[p.b_ss], writes=[p.b_ss])
        S.op("act", lambda e: e.activation(out=p.ss[:, 0:G], in_=p.ss[:, 0:G], func=AF.Sqrt), reads=[p.b_ss], writes=[p.b_ss])
        S.op("dve", lambda e: e.reciprocal(out=p.rstd[:, 0:G], in_=p.ss[:, 0:G]), reads=[p.b_ss], writes=[p.b_rstd])
        for c in range(G):
            S.op("act", lambda e: e.activation(out=p.x_g[:, c, :], in_=p.x_g[:, c, :], func=AF.Copy, scale=p.rstd[:, c:c + 1]),
                 reads=[p.b_xg[c], p.b_rstd], writes=[p.b_xg[c]])
            for half, (t, b) in enumerate(((p.cos, p.b_cos), (p.sin, p.b_sin))):
                S.op("dve", lambda e: e.tensor_tensor(out=p.x_g[:, c, half * 512:(half + 1) * 512], in0=p.x_g[:, c, half * 512:(half + 1) * 512],
                                                      in1=t[:], op=ALU.mult), reads=[p.b_xg[c], b], writes=[p.b_xg[c]])

    LN16 = math.log(16.0)

    def ml_group(self, g, full):
        p, S = self, self.S
        sh = p.modT[:, 1, 0:8]
        W = p.ml_w_in
        if g == 0:
            self.load_halo(p.xb, p.b_xb, g, 3)
            self.norm_halo(2, sh)
        self.norm_group(p.xb, p.b_xb, g, 2, sh)
        reuse = full and p.relay
        plist = ([0, 1] if full else []) + ([] if reuse else [2, 3])
        if reuse:
            self.kv_load(g)
        for pc in plist:
            wv, wb = self.load_std_piece(W, pc * 512)
            dst, b_dst = (p.qT, p.b_qT) if pc < 2 else (p.kT, p.b_kT)
            for pr in range(2):
                tl = []
                for ct in (2 * pr, 2 * pr + 1):
                    tl.append((ct, pc * 4 + ct, p.wk[ct], p.b_wk[ct], p.ubuf[ct % 2], p.b_ubuf[ct % 2]))
                for ct, cti, acc, b_acc, ub, b_ub in tl:
                    self.proj_A(wv, wb, ct, ct)
                    if g == 0:
                        for kc in range(KC):
                            S.op("pe", lambda e: e.matmul(p.P[4][:, 0:HW], lhsT=wv[:, kc, ct * 128:(ct + 1) * 128], rhs=p.hTh[:, kc, 0:HW],
                                                          start=(kc == 0), stop=(kc == KC - 1)), reads=[wb, p.b_hTh], writes=[p.b_P[4]])
                        S.op("dve", lambda e: e.tensor_scalar(out=ub[:, 0:HW], in0=p.P[4][:, 0:HW], scalar1=p.nf[:, 0:1], scalar2=None, op0=ALU.mult),
                             reads=[p.b_P[4], p.b_nf], writes=[b_ub])
                    else:
                        S.op("act", lambda e: e.activation(out=ub[:, 0:HW], in_=p.uhalo[:, cti, :], func=AF.Copy), reads=[p.b_uhalo], writes=[b_ub])
                for ct, cti, acc, b_acc, ub, b_ub in tl:
                    S.op("act", lambda e: e.activation(out=ub[:, HW:HW + GT], in_=p.P[ct][:, :], func=AF.Copy), reads=[p.b_P[ct]], writes=[b_ub])
                    S.op("act", lambda e: e.activation(out=acc[:], in_=p.P[ct][:, :], func=AF.Identity, scale=p.ml_cwT[:, 48 + cti:49 + cti],
                                                       bias=p.ml_cbT[:, cti:cti + 1]), reads=[p.b_P[ct], p.b_ml_cwT, p.b_ml_cbT], writes=[b_acc])
                if g < NG - 1:
                    for ct, cti, acc, b_acc, ub, b_ub in tl:
                        S.op("act", lambda e: e.activation(out=p.uhalo[:, cti, :], in_=ub[:, GT:GT + HW], func=AF.Copy), reads=[b_ub], writes=[p.b_uhalo])
                for j in (2, 1, 0):
                    off = HW - (3 - j)
                    for ct, cti, acc, b_acc, ub, b_ub in tl:
                        S.op("dve", lambda e: e.scalar_tensor_tensor(out=acc[:], in0=ub[:, off:off + GT], scalar=p.ml_cwT[:, j * 16 + cti: j * 16 + cti + 1],
                                                                     in1=acc[:], op0=ALU.mult, op1=ALU.add), reads=[b_ub, b_acc, p.b_ml_cwT], writes=[b_acc])
                for ct, cti, acc, b_acc, ub, b_ub in tl:
                    S.op("act", lambda e: e.activation(out=dst[:, cti % 8, :], in_=acc[:], func=AF.Silu), reads=[b_acc], writes=[b_dst])
        for hh in ([] if reuse else range(H)):
            wv, wb = self.load_std_piece(W, 2048 + hh * 512)
            for c in range(G):
                self.proj_B(wv, wb, c, c)
                dstv = p.v_all[:, c, hh * 512:(hh + 1) * 512]
                if c % 2 == 0:
                    S.op("act", lambda e: e.activation(out=dstv, in_=p.P[c][:, :], func=AF.Copy), reads=[p.b_P[c]], writes=[p.b_v[c]])
                else:
                    S.op("dve", lambda e: e.tensor_copy(out=dstv, in_=p.P[c][:, :]), reads=[p.b_P[c]], writes=[p.b_v[c]])
        if (not full) and p.relay:
            self.kv_store(g)
        wv, wb = self.load_std_piece(W, 6144, w=8)
        for c in range(G):
            self.proj_B(wv, wb, c, c, w=8)
            S.op("dve", lambda e: e.tensor_tensor(out=p.gat[:, c, :], in0=p.P[c][:, 0:8], in1=p.bg_bc[:], op=ALU.add),
                 reads=[p.b_P[c], p.b_bg_bc], writes=[p.b_gat])
        if full:
            for hh in range(H):
                wv, wb = self.load_std_piece(W, 4096 + hh * 512)
                for c in range(G):
                    self.proj_B(wv, wb, c, c)
                    dsts = p.sg_all[:, c, hh * 512:(hh + 1) * 512]
                    S.op("act", lambda e: e.activation(out=dsts, in_=p.P[c][:, :], func=AF.Sigmoid), reads=[p.b_P[c]], writes=[p.b_xg[c]])
        self.ml_gates_group(full)
        for c in range(G):
            self.ml_chunk(c, full)
        if full:
            self.out_proj(g, p.ml_w_out, p.xb, p.b_xb, p.xa, p.b_xa, 5)

    def ml_gates_group(self, full):
        p, S = self, self.S
        gm, bg = p.gm, p.b_gm
        v3 = lambda k: gm[:, k, :].rearrange("p (c h) -> p c h", h=4)
        z = p.gat[:, :, 4:8]
        li = p.gat[:, :, 0:4]
        S.op("act", lambda e: e.activation(out=v3(0), in_=z, func=AF.Exp, scale=-1.0), reads=[p.b_gat], writes=[bg])
        S.op("dve", lambda e: e.tensor_scalar(out=gm[:, 0, :], in0=gm[:, 0, :], scalar1=1.0, scalar2=None, op0=ALU.add), reads=[bg], writes=[bg])
        S.op("act", lambda e: e.activation(out=gm[:, 0, :], in_=gm[:, 0, :], func=AF.Ln), reads=[bg], writes=[bg])
        S.op("dve", lambda e: e.tensor_scalar(out=gm[:, 0, :], in0=gm[:, 0, :], scalar1=-1.0, scalar2=None, op0=ALU.mult), reads=[bg], writes=[bg])
        n = 4 * G
        S.op("pe", lambda e: e.matmul(p.P[5][:, 0:n], lhsT=p.ut[:, :], rhs=gm[:, 0, :], start=True, stop=True), reads=[p.b_ut, bg], writes=[p.b_P[5]])
        S.op("pe", lambda e: e.matmul(p.P[5][:, n:2 * n], lhsT=p.ones_f[:, :], rhs=gm[:, 0, :], start=True, stop=True), reads=[p.b_ones_f, bg], writes=[p.b_P[5]])
        S.op("dve", lambda e: e.tensor_copy(out=gm[:, 1, :], in_=p.P[5][:, 0:n]), reads=[p.b_P[5]], writes=[bg])
        S.op("dve", lambda e: e.tensor_copy(out=gm[:, 2, :], in_=p.P[5][:, n:2 * n]), reads=[p.b_P[5]], writes=[bg])
        S.op("dve", lambda e: e.tensor_tensor(out=gm[:, 3, :], in0=gm[:, 2, :], in1=gm[:, 1, :], op=ALU.subtract), reads=[bg], writes=[bg])
        S.op("dve", lambda e: e.scalar_tensor_tensor(out=v3(3), in0=v3(3), scalar=-self.LN16, in1=li, op0=ALU.add, op1=ALU.add),
             reads=[bg, p.b_gat], writes=[bg])
        S.op("act", lambda e: e.activation(out=gm[:, 3, :], in_=gm[:, 3, :], func=AF.Exp), reads=[bg], writes=[bg])
        S.op("act", lambda e: e.activation(out=gm[:, 4, :], in_=gm[:, 2, :], func=AF.Exp), reads=[bg], writes=[bg])
        gx = p.gmx[:].rearrange("p c (h t) -> p c h t", t=2)
        for t in range(2):
            S.op("dve", lambda e: e.tensor_copy(out=gx[:, :, :, t], in_=v3(4)), reads=[bg], writes=[p.b_gmx])
        if not full:
            for c in range(G):
                S.op("dve", lambda e: e.tensor_tensor(out=p.fsum[:], in0=p.fsum[:], in1=gm[:, 2, 4 * c:4 * c + 4], op=ALU.add),
                     reads=[bg, p.b_fsum], writes=[p.b_fsum])
        else:
            S.op("dve", lambda e: e.tensor_tensor(out=v3(5), in0=li, in1=v3(1), op=ALU.subtract), reads=[bg, p.b_gat], writes=[bg])
            S.op("dve", lambda e: e.tensor_scalar(out=gm[:, 5, :], in0=gm[:, 5, :], scalar1=-self.LN16, scalar2=None, op0=ALU.add), reads=[bg], writes=[bg])
            S.op("act", lambda e: e.activation(out=gm[:, 6, :], in_=gm[:, 1, :], func=AF.Exp), reads=[bg], writes=[bg])

    def ml_chunk(self, c, full):
        p, S = self, self.S
        cs = slice(c * CH, (c + 1) * CH)
        sm, bs = p.sm, p.b_sm
        gm, bg = p.gm, p.b_gm
        col = 4 * c
        g_b = lambda hh: gm[:, 1, col + hh:col + hh + 1]
        g_ws = lambda hh: gm[:, 3, col + hh:col + hh + 1]
        g_sp = lambda hh: gm[:, 4, col + hh:col + hh + 1]
        g_bj = lambda hh: gm[:, 5, col + hh:col + hh + 1]
        g_wi = lambda hh: gm[:, 6, col + hh:col + hh + 1]
        self.make_kz(c, lambda hh: (g_ws(hh), [bg]))
        if full:
            for hh in range(H):
                S.op("dve", lambda e: e.tensor_scalar(out=p.xn[:, hh * 128:(hh + 1) * 128], in0=p.ident_f[:, :], scalar1=g_b(hh), scalar2=None, op0=ALU.mult),
                     reads=[bg, p.b_ident_f], writes=[p.b_xn])
                S.op("pe", lambda e: e.matmul(p.P[5][:, hh * 128:(hh + 1) * 128], lhsT=p.ones_f[:, :], rhs=p.xn[:, hh * 128:(hh + 1) * 128], start=True, stop=False),
                     reads=[p.b_ones_f, p.b_xn], writes=[p.b_P[5]])
                S.op("pe", lambda e: e.matmul(p.P[5][:, hh * 128:(hh + 1) * 128], lhsT=p.ident_f[:, :], rhs=p.neg[:, :], start=False, stop=True),
                     reads=[p.b_ident_f, p.b_neg], writes=[p.b_P[5]])
            for hh in range(H):
                S.op("act", lambda e: e.activation(out=p.xn2[:, hh * 128:(hh + 1) * 128], in_=p.P[5][:, hh * 128:(hh + 1) * 128], func=AF.Exp,
                                                   bias=g_bj(hh)), reads=[p.b_P[5], bg], writes=[p.b_xn2])
            for hh in range(H):
                for half in range(2):
                    S.op("pe", lambda e: e.matmul(p.P[4][:, hh * 128:(hh + 1) * 128], lhsT=p.kT[:, 2 * hh + half, cs], rhs=p.qT[:, 2 * hh + half, cs],
                                                  start=(half == 0), stop=(half == 1)), reads=[p.b_kT, p.b_qT], writes=[p.b_P[4]])
            S.op("dve", lambda e: e.tensor_tensor(out=p.sTm[:], in0=p.P[4][:, :], in1=p.xn2[:, 0:512], op=ALU.mult),
                 reads=[p.b_P[4], p.b_xn2], writes=[p.b_sTm])
        if full:
            for hh in range(H):
                S.op("pe", lambda e: e.matmul(p.P[5][:, 2 * hh:2 * hh + 1], lhsT=p.sTm[:, hh * 128:(hh + 1) * 128], rhs=p.ones_b[:, 0:1], start=True, stop=True),
                     reads=[p.b_sTm, p.b_ones_b], writes=[p.b_P[5]])
                for half in range(2):
                    i = 2 * hh + half
                    S.op("pe", lambda e: e.matmul(p.P[5][:, 2 * hh + 1:2 * hh + 2], lhsT=p.qT[:, i, cs], rhs=p.nstb[:, i:i + 1], start=(half == 0), stop=(half == 1)),
                         reads=[p.b_qT, p.b_nstb], writes=[p.b_P[5]])
            S.op("dve", lambda e: e.tensor_copy(out=sm[:, 64:72], in_=p.P[5][:, 0:8]), reads=[p.b_P[5]], writes=[bs])
            dv = sm[:, 64:72].rearrange("p (h t) -> p h t", t=2)
            S.op("dve", lambda e: e.tensor_tensor(out=sm[:, 72:76], in0=dv[:, :, 1], in1=gm[:, 6, col:col + 4], op=ALU.mult), reads=[bs, bg], writes=[bs])
            S.op("dve", lambda e: e.tensor_tensor(out=sm[:, 72:76], in0=sm[:, 72:76], in1=dv[:, :, 0], op=ALU.add), reads=[bs], writes=[bs])
            S.op("dve", lambda e: e.scalar_tensor_tensor(out=sm[:, 76:80], in0=sm[:, 72:76], scalar=-1.0, in1=sm[:, 72:76], op0=ALU.mult, op1=ALU.max),
                 reads=[bs], writes=[bs])
            S.op("dve", lambda e: e.tensor_scalar(out=sm[:, 76:80], in0=sm[:, 76:80], scalar1=1.0, scalar2=None, op0=ALU.max), reads=[bs], writes=[bs])
            S.op("dve", lambda e: e.reciprocal(out=sm[:, 80:84], in_=sm[:, 76:80]), reads=[bs], writes=[bs])
        for hh in range(H):
            if full:
                S.op("pe", lambda e: e.matmul(p.P[0][:, :], lhsT=p.sTm[:, hh * 128:(hh + 1) * 128], rhs=p.v_all[:, c, hh * 512:(hh + 1) * 512],
                                              start=True, stop=True), reads=[p.b_sTm, p.b_v[c]], writes=[p.b_P[0]])
                for half in range(2):
                    i = 2 * hh + half
                    S.op("pe", lambda e: e.matmul(p.P[1][:, :], lhsT=p.qT[:, i, cs], rhs=p.Rb[:, i, :], start=(half == 0), stop=(half == 1)),
                         reads=[p.b_qT, p.b_Rb[i]], writes=[p.b_P[1]])
                S.op("act", lambda e: e.activation(out=p.wk[4][:], in_=p.P[1][:, :], func=AF.Copy, scale=g_wi(hh)),
                     reads=[p.b_P[1], bg], writes=[p.b_wk[4]])
                S.op("dve", lambda e: e.tensor_tensor(out=p.wk[hh][:], in0=p.wk[4][:], in1=p.P[0][:, :], op=ALU.add),
                     reads=[p.b_wk[4], p.b_P[0]], writes=[p.b_wk[hh]])
                so = p.sg_all[:, c, hh * 512:(hh + 1) * 512]
                S.op("dve", lambda e: e.scalar_tensor_tensor(out=p.wk[hh][:], in0=p.wk[hh][:], scalar=sm[:, 80 + hh:81 + hh], in1=so, op0=ALU.mult, op1=ALU.mult),
                     reads=[p.b_wk[hh], bs, p.b_xg[c]], writes=[p.b_wk[hh]])
            self.state_update(c, hh, g_sp(hh), [bg])
            if full:
                self.refresh_Rb(hh)
        for i in range(8):
            hh, half = i // 2, i % 2
            S.op("pe", lambda e: e.matmul(p.P[5][:, 16 + i:17 + i], lhsT=p.kz[:, hh * 256 + half * 128: hh * 256 + (half + 1) * 128],
                                          rhs=p.ones_b[:, 0:1], start=True, stop=True), reads=[p.b_kz, p.b_ones_b], writes=[p.b_P[5]])
        S.op("dve", lambda e: e.tensor_tensor(out=p.nst[:], in0=p.nst[:], in1=p.gmx[:, c, :], op=ALU.mult), reads=[p.b_nst, p.b_gmx], writes=[p.b_nst])
        S.op("dve", lambda e: e.tensor_tensor(out=p.nst[:], in0=p.nst[:], in1=p.P[5][:, 16:24], op=ALU.add), reads=[p.b_nst, p.b_P[5]], writes=[p.b_nst])
        if full:
            S.op("act", lambda e: e.activation(out=p.nstb[:], in_=p.nst[:], func=AF.Copy), reads=[p.b_nst], writes=[p.b_nstb])
        if full:
            self.groupnorm_heads([0, 1, 2, 3], c, gate=False)
            self.make_ynT(c, p.ml_gnT, p.b_ml_gnT)

    def ml_layer(self):
        p, S = self, self.S
        if p.phase in (None, 4):
            self.ml_part_a()
        if p.phase is None:
            self.exchange(4)
        if p.phase in (None, 5):
            self.ml_part_b()

    def ml_part_a(self):
        p, S = self, self.S
        for i in range(8):
            S.op("dve", lambda e: e.memset(p.R[:, i, :], 0.0), writes=[p.b_R[i]])
        S.op("dve", lambda e: e.memset(p.nst[:], 0.0), writes=[p.b_nst])
        S.op("dve", lambda e: e.memset(p.fsum[:], 0.0), writes=[p.b_fsum])
        for g in range(NG):
            self.ml_group(g, full=False)
        S.op("dve", lambda e: e.memset(p.wk[4][:], 0.0), writes=[p.b_wk[4]])
        S.op("dve", lambda e: e.tensor_copy(out=p.wk[4][:, 0:8], in_=p.nst[:]), reads=[p.b_nst], writes=[p.b_wk[4]])
        S.op("dve", lambda e: e.tensor_copy(out=p.wk[4][:, 8:12], in_=p.fsum[:]), reads=[p.b_fsum], writes=[p.b_wk[4]])
        pairs = [(p.loc[4][i * 128:(i + 1) * 128, :], p.R[:, i, :]) for i in range(8)] + [(p.loc[4][1024:1152, :], p.wk[4][:])]
        self.dma_group("sp", "loc4", pairs, reads=p.b_R + [p.b_wk[4]], writes=[p.b_loc[4]])

    def ml_part_b(self):
        p, S = self, self.S
        gt = p.gath[4]
        pairs = [(p.stage[cp:cp + 1, 0:4], gt[cp * 1152 + 1024: cp * 1152 + 1025, 8:12]) for cp in range(NCORES)]
        self.dma_group("sp", "stage", pairs, reads=[p.b_gath[4]], writes=[p.b_stage])
        for cp in range(NCORES):
            S.op("dve", lambda e: e.tensor_tensor(out=p.stage[0:8, 32 + cp * 4:36 + cp * 4], in0=p.msel[0:8, cp * 4:cp * 4 + 4], in1=p.stage[0:8, 0:4], op=ALU.mult),
                 reads=[p.b_stage, p.b_msel], writes=[p.b_stage])
        S.op("pe", lambda e: e.matmul(p.P[5][:, 0:32], lhsT=p.ones_f[0:8, :], rhs=p.stage[0:8, 32:64], start=True, stop=True),
             reads=[p.b_ones_f, p.b_stage], writes=[p.b_P[5]])
        S.op("act", lambda e: e.activation(out=p.mcoef[:], in_=p.P[5][:, 0:32], func=AF.Exp), reads=[p.b_P[5]], writes=[p.b_mcoef])
        S.op("dve", lambda e: e.tensor_tensor(out=p.mcoef[:], in0=p.mcoef[:], in1=p.valid[:], op=ALU.mult), reads=[p.b_mcoef, p.b_valid], writes=[p.b_mcoef])
        self.combine_state(4, p.mcoef, p.b_mcoef, nrows=9)
        S.op("dve", lambda e: e.memset(p.nst[:], 0.0), writes=[p.b_nst])
        for cp in range(NCORES):
            tmp, bt = p.wk[cp % 2], p.b_wk[cp % 2]
            r0 = cp * 1152 + 1024
            S.dma("sp", tmp[:, 0:8], gt[r0:r0 + 128, 0:8], reads=[p.b_gath[4]], writes=[bt], key=f"cmb{cp % 2}")
            for hh in range(H):
                S.op("dve", lambda e: e.scalar_tensor_tensor(out=p.nst[:, 2 * hh:2 * hh + 2], in0=tmp[:, 2 * hh:2 * hh + 2],
                                                             scalar=p.mcoef[:, cp * H + hh:cp * H + hh + 1], in1=p.nst[:, 2 * hh:2 * hh + 2],
                                                             op0=ALU.mult, op1=ALU.add), reads=[bt, p.b_mcoef, p.b_nst], writes=[p.b_nst])
        for hh in range(H):
            self.refresh_Rb(hh)
        S.op("act", lambda e: e.activation(out=p.nstb[:], in_=p.nst[:], func=AF.Copy), reads=[p.b_nst], writes=[p.b_nstb])
        self.load_gate(1, 0)
        for g in range(NG):
            self.ml_group(g, full=True)

    def _body(self):
        p, S = self, self.S
        ph = p.phase
        if ph in (None, 1, 2):
            self.ret_layer()
        if p.stop_after == "dbg_ret":
            return
        if p.stop_after == "ret":
            return self.copy_out(p.xa, p.b_xa)
        if ph is None:
            self.exchange(2)
        if ph in (None, 3):
            self.ffn_layer(0, p.xa, p.b_xa, p.xb, p.b_xb, 2, 3, final=False)
        if p.stop_after == "ffn0":
            return self.copy_out(p.xb, p.b_xb)
        if ph is None:
            self.exchange(3)
        if ph in (None, 4, 5):
            self.ml_layer()
        if p.stop_after == "ml":
            return self.copy_out(p.xa, p.b_xa)
        if ph is None:
            self.exchange(5)
        if ph in (None, 6):
            S.dma("sp", p.cos[:], p.final_g[0:1, 0:512].partition_broadcast(128).rearrange("p o n -> p (o n)"), writes=[p.b_cos], key="fg0")
            S.dma("sp", p.sin[:], p.final_g[0:1, 512:1024].partition_broadcast(128).rearrange("p o n -> p (o n)"), writes=[p.b_sin], key="fg1")
            self.ffn_layer(1, p.xa, p.b_xa, p.out, p.b_out, 5, None, final=True)

    def copy_out(self, src, src_bufs):
        p, S = self, self.S
        for g in range(NG):
            for c in range(G):
                r0 = g * GT + c * CH
                S.dma("sp", p.x_g[:, c, :], src[r0:r0 + CH, :], reads=src_bufs, writes=[p.b_xg[c]], key=f"xg{c}")
            pairs = [(p.out[g * GT + c * CH: g * GT + (c + 1) * CH, :], p.x_g[:, c, :]) for c in range(G)]
            self.dma_group("sp", "st_ou", pairs, reads=p.b_xg, writes=[p.b_out[g]])

    def _finish(self):
        p, S = self, self.S
        bufs = list(p.b_out) + [p.b_loc[k] for k in p.b_loc] + p.dbg_bufs + list(p.b_xa) + list(p.b_xb) + [p.b_modscr, p.b_modT_o, p.b_kvst]
        S.wait_bufs("sp", bufs)
        S.barrier()


_PROG_CACHE = {}
MODE = "host6"
STOP_AFTER = None


def _get_prog(mode, stop_after, phase=None):
    key = (mode, stop_after, phase)
    if key not in _PROG_CACHE:
        pr = Prog("host" if mode.startswith("host") else mode, stop_after, phase)
        pr.build()
        _PROG_CACHE[key] = pr
    return _PROG_CACHE[key]


def _in_maps(inputs):
    f = lambda a: np.ascontiguousarray(np.asarray(a), dtype=np.float32)
    tabs, lg = _const_tables()
    x = f(inputs["x"]).reshape(SEQ, D)
    pos = np.ascontiguousarray(np.asarray(inputs["positions"]).astype(np.int32)).reshape(SEQ)
    shared = {
        "cT": np.ascontiguousarray(f(inputs["c"]).reshape(KC, 128).T),
        "ada_w": f(inputs["ada_w"]),
        "ada_bT": np.ascontiguousarray(f(inputs["ada_b"]).reshape(2, 48, 128).transpose(2, 0, 1)),
        "ntgT": np.ascontiguousarray(f(inputs["norm_tok_g"]).reshape(2, KC, 128).transpose(2, 0, 1)),
        "nfgT": np.ascontiguousarray(f(inputs["norm_ffn_g"]).reshape(2, KC, 128).transpose(2, 0, 1)),
        "ret_w_in": f(inputs["ret_w_in"]).reshape(D, 6144),
        "ret_gnT": np.ascontiguousarray(f(inputs["ret_gn_g"]).reshape(16, 128).T),
        "ret_w_out": f(inputs["ret_w_out"]).reshape(2048, D),
        "ml_w_in": f(inputs["ml_w_in"]).reshape(D, 6152),
        "ml_b_gate": f(inputs["ml_b_gate"]).reshape(1, 8),
        "ml_cwT": np.ascontiguousarray(f(inputs["ml_conv_w"]).reshape(64, 128).T),
        "ml_cbT": np.ascontiguousarray(f(inputs["ml_conv_b"]).reshape(16, 128).T),
        "ml_gnT": np.ascontiguousarray(f(inputs["ml_gn_g"]).reshape(16, 128).T),
        "ml_w_out": f(inputs["ml_w_out"]).reshape(2048, D),
        "ffn_w_up": f(inputs["ffn_w_up"]),
        "ffn_cwT": np.ascontiguousarray(f(inputs["ffn_conv_w"]).reshape(2, 132, 128).transpose(2, 0, 1)),
        "ffn_cbT": np.ascontiguousarray(f(inputs["ffn_conv_b"]).reshape(2, 44, 128).transpose(2, 0, 1)),
        "ffn_w_down": f(inputs["ffn_w_down"]),
        "final_g": f(inputs["final_g"]).reshape(1, D),
    }
    shared.update(tabs)
    maps = []
    for c in range(NCORES):
        m = dict(shared)
        m["x"] = x[c * T:(c + 1) * T]
        m["pos"] = pos[c * T:(c + 1) * T].reshape(1, T)
        m.update(_core_tables(c, lg))
        maps.append(m)
    return maps


def _launch(pr, maps, extra):
    ms = []
    for c, m in enumerate(maps):
        mm = {k: v for k, v in m.items() if k in pr.in_names}
        for k, v in extra.items():
            if k in pr.in_names:
                mm[k] = v[c] if isinstance(v, list) else v
        ms.append(mm)
    return run_bass_kernel_spmd(pr.nc, ms, core_ids=list(range(NCORES))).results


def kernel(**inputs):
    maps = _in_maps(inputs)
    if MODE == "host6":
        extra = {}
        res = None
        for ph in range(1, 7):
            pr = _get_prog("host", None, ph)
            res = _launch(pr, maps, extra)
            for k in (1, 2, 3, 4, 5):
                if f"loc{k}" in res[0]:
                    extra[f"gath{k}"] = np.concatenate([res[c][f"loc{k}"] for c in range(NCORES)], axis=0)
            for nm in ("xa", "xb"):
                if nm in res[0]:
                    extra[nm] = [res[c][nm] for c in range(NCORES)]
            if "kst_o" in res[0]:
                extra["kst_i"] = [res[c]["kst_o"] for c in range(NCORES)]
                extra["vst_i"] = [res[c]["vst_o"] for c in range(NCORES)]
            if "modT_o" in res[0]:
                extra["modT_i"] = [res[c]["modT_o"] for c in range(NCORES)]
                extra["modscr_i"] = [res[c]["modscr_o"] for c in range(NCORES)]
    elif MODE == "host":
        pr = _get_prog("host", STOP_AFTER)
        extra = {f"gath{k}": np.zeros((NCORES * r, cdim), np.float32) for k, (r, cdim) in EX_SIZES.items()}
        order = {None: [1, 2, 3, 4, 5], "ret": [1], "ffn0": [1, 2], "ml": [1, 2, 3, 4]}[STOP_AFTER]
        res = None
        for step in range(len(order) + 1):
            res = _launch(pr, maps, extra)
            if step < len(order):
                k = order[step]
                extra[f"gath{k}"] = np.concatenate([res[c][f"loc{k}"] for c in range(NCORES)], axis=0)
    else:
        pr = _get_prog("cc", None)
        res = _launch(pr, maps, {})
    out = np.concatenate([res[c]["out"] for c in range(NCORES)], axis=0)
    return out.reshape(1, SEQ, D).astype(np.float32)
```

```python
import contextlib
import math
import numpy as np
import concourse.bass as bass
import concourse.mybir as mybir
from concourse.bass_utils import run_bass_kernel_spmd

F32 = mybir.dt.float32
BF16 = mybir.dt.bfloat16
I32 = mybir.dt.int32
AF = mybir.ActivationFunctionType
ALU = mybir.AluOpType

NCORES = 8
SEQ = 16384
D = 1024
T = SEQ // NCORES
CH = 128
G = 4
GT = G * CH
NG = T // GT
KC = D // 128
H = 4
DK = 256
DV = 512
DFF = 2816
EPS = 1e-6
HW = 3
TWO_PI = 2.0 * math.pi
C1 = 6.28125
C2 = TWO_PI - C1
PI_SAFE = 3.1415925


class Buf:
    __slots__ = ("name", "w", "r")

    def __init__(self, name):
        self.name = name
        self.w = None
        self.r = {}


class Sched:
    ENGS = ("pe", "dve", "act", "pool", "sp")

    def __init__(self, nc, stack):
        self.nc = nc
        self.stack = stack
        self.eng = {"pe": nc.tensor, "dve": nc.vector, "act": nc.scalar, "pool": nc.gpsimd, "sp": nc.sync}
        self.sems = {}
        self.cnt = {}
        self.seen = {e: {} for e in self.ENGS}
        for e in self.ENGS:
            self.sems[e] = stack.enter_context(nc.semaphore("s_" + e))
            self.cnt[e] = 0
        self.ninst = 0

    def buf(self, name):
        return Buf(name)

    def _dma_sem(self, key):
        k = "dma_" + key
        if k not in self.sems:
            self.sems[k] = self.stack.enter_context(self.nc.semaphore("s_" + k))
            self.cnt[k] = 0
        return k

    def _wait(self, e, deps, wdeps=()):
        need = {}
        for d in deps:
            if d is None:
                continue
            k, v = d
            if k == e and e == "pe":
                continue
            if need.get(k, 0) < v:
                need[k] = v
        for d in wdeps:
            if d is None:
                continue
            k, v = d
            if k == e:
                continue
            if need.get(k, 0) < v:
                need[k] = v
        for k, v in need.items():
            if self.seen[e].get(k, 0) < v:
                self.eng[e].wait_ge(self.sems[k], v)
                self.seen[e][k] = v

    @staticmethod
    def _deps(reads, writes):
        deps = []
        for b in reads:
            deps.append(b.w)
        for b in writes:
            deps.append(b.w)
            deps.extend(b.r.items())
        return deps

    @staticmethod
    def _deps2(reads, writes):
        rd = [b.w for b in reads]
        wd = []
        for b in writes:
            wd.append(b.w)
            wd.extend(b.r.items())
        return rd, wd

    @staticmethod
    def _record(ev, reads, writes):
        k, v = ev
        for b in reads:
            if b.r.get(k, 0) < v:
                b.r[k] = v
        for b in writes:
            b.w = ev
            b.r = {}

    def op(self, e, fn, reads=(), writes=()):
        rd, wd = self._deps2(reads, writes)
        self._wait(e, rd, wd)
        ins = fn(self.eng[e])
        self.cnt[e] += 1
        ins.then_inc(self.sems[e], 1)
        self.ninst += 1
        self._record((e, self.cnt[e]), reads, writes)
        return ins

    def dma(self, q, out, in_, reads=(), writes=(), key=None, **kw):
        k = self._dma_sem(key)
        self._wait(q, self._deps(reads, writes))
        ins = self.eng[q].dma_start(out=out, in_=in_, **kw)
        self.cnt[k] += 16
        ins.then_inc(self.sems[k], 16)
        self.ninst += 1
        self._record((k, self.cnt[k]), reads, writes)
        return ins

    def wait_bufs(self, e, bufs):
        deps = []
        for b in bufs:
            deps.append(b.w)
            deps.extend(b.r.items())
        self._wait(e, deps)

    def barrier(self):
        for e in self.ENGS:
            deps = [(k, v) for k, v in self.cnt.items() if v > 0]
            self._wait(e, deps)


def _const_tables():
    t = {}
    n = np.arange(128, dtype=np.float32)
    inv_freq = (10000.0 ** (-(np.arange(0, DK, 2, dtype=np.float32)) / DK)).astype(np.float32)
    t["inv_freq"] = inv_freq.reshape(128, 1).astype(np.float32)
    lg = np.log(1.0 - 2.0 ** (-5.0 - np.arange(H, dtype=np.float64)))
    i = np.arange(128)[None, :]
    j = np.arange(128)[:, None]
    dt = np.zeros((128, H, 128), np.float64)
    for h in range(H):
        dt[:, h, :] = np.where(i >= j, np.exp((i - j) * lg[h]), 0.0) * (DK ** -0.5)
    t["ret_dt"] = dt.reshape(128, H * 128).astype(np.float32)
    t["ret_xi"] = np.exp((np.arange(128)[:, None] + 1.0) * lg[None, :]).astype(np.float32)
    t["ret_zs"] = (np.exp((127.0 - np.arange(128)[:, None]) * lg[None, :]) * (DK ** -0.5)).astype(np.float32)
    t["neg"] = np.where(j <= i, 0.0, -30000.0).astype(np.float32)
    t["ut"] = np.where(j <= i, 1.0, 0.0).astype(np.float32)
    t["ident"] = np.eye(128, dtype=np.float32)
    return t, lg


def _core_tables(c, lg):
    sel = np.zeros((128, NCORES), np.float32)
    if c > 0:
        sel[:, c - 1] = 1.0
    nf = np.full((128, 1), 0.0 if c == 0 else 1.0, np.float32)
    rc = np.zeros((128, NCORES * H), np.float32)
    for cp in range(c):
        for h in range(H):
            rc[:, cp * H + h] = np.exp(T * (c - 1 - cp) * lg[h])
    valid = np.zeros((128, NCORES * H), np.float32)
    for cp in range(c):
        valid[:, cp * H:(cp + 1) * H] = 1.0
    msel = np.zeros((NCORES, NCORES, H), np.float32)
    for cpp in range(NCORES):
        for cp in range(NCORES):
            if cp < cpp < c:
                msel[cpp, cp, :] = 1.0
    selmat = np.zeros((NCORES * HW, HW), np.float32)
    if c > 0:
        for r in range(HW):
            selmat[(c - 1) * HW + r, r] = 1.0
    return {"selmat": selmat, "sel": sel, "nf": nf, "ret_coef": rc, "valid": valid, "msel": msel.reshape(NCORES, NCORES * H)}


EX_SIZES = {1: (8 * 128, 512), 2: (HW, D), 3: (HW, D), 4: (9 * 128, 512), 5: (HW, D)}


class Prog:
    def __init__(self, mode="host", stop_after=None, phase=None):
        self.mode = mode
        self.stop_after = stop_after
        self.phase = phase
        self.in_names = []
        self.layers = [0, 1] if phase in (None, 1) else ([0] if phase <= 3 else [1])
        self.mod_layers = [0, 1] if phase in (None, 1) else []
        self.nc = bass.Bass("TRN2", target_bir_lowering=False)
        self.st = contextlib.ExitStack()
        self.debug = stop_after is not None and stop_after.startswith("dbg")
        self.dbg_bufs = []

    def din(self, name, shape, dt=F32):
        self.in_names.append(name)
        return self.nc.dram_tensor(name, list(shape), dt, kind="ExternalInput").ap()

    def dout(self, name, shape, dt=F32):
        return self.nc.dram_tensor(name, list(shape), dt, kind="ExternalOutput").ap()

    def dint(self, name, shape, dt=F32):
        return self.nc.dram_tensor(name, list(shape), dt, kind="Internal").ap()

    def sb(self, name, shape, dt=F32):
        t = self.st.enter_context(self.nc.sbuf_tensor("sb_" + name, list(shape), dt))
        b = Buf(name)
        return t, b

    def ps(self, name, shape, dt=F32):
        t = self.st.enter_context(self.nc.psum_tensor("ps_" + name, list(shape), dt))
        return t

    def build(self):
        with self.st:
            self.S = Sched(self.nc, self.st)
            self._declare()
            self._setup()
            self._body()
            self._finish()
        return self.nc

    def _declare(self):
        p = self
        p.x = p.din("x", [T, D])
        p.pos = p.din("pos", [1, T], I32)
        p.c_in = p.din("cT", [128, KC])
        p.ada_w = p.din("ada_w", [2, D, 6 * D])
        p.ada_b = p.din("ada_bT", [128, 2, 48])
        p.ntg = p.din("ntgT", [128, 2, KC])
        p.nfg = p.din("nfgT", [128, 2, KC])
        p.ret_w_in = p.din("ret_w_in", [D, 6144])
        p.ret_gn = p.din("ret_gnT", [128, 16])
        p.ret_w_out = p.din("ret_w_out", [2048, D])
        p.ml_w_in = p.din("ml_w_in", [D, 6152])
        p.ml_bg = p.din("ml_b_gate", [1, 8])
        p.ml_cw = p.din("ml_cwT", [128, 64])
        p.ml_cb = p.din("ml_cbT", [128, 16])
        p.ml_gn = p.din("ml_gnT", [128, 16])
        p.ml_w_out = p.din("ml_w_out", [2048, D])
        p.ffn_w_up = p.din("ffn_w_up", [2, D, 2 * DFF])
        p.ffn_cw = p.din("ffn_cwT", [128, 2, 132])
        p.ffn_cb = p.din("ffn_cbT", [128, 2, 44])
        p.ffn_w_down = p.din("ffn_w_down", [2, DFF, D])
        p.final_g = p.din("final_g", [1, D])
        p.t_inv_freq = p.din("inv_freq", [128, 1])
        p.t_ret_dt = p.din("ret_dt", [128, 512])
        p.t_ret_xi = p.din("ret_xi", [128, 4])
        p.t_ret_zs = p.din("ret_zs", [128, 4])
        p.t_neg = p.din("neg", [128, 128])
        p.t_ut = p.din("ut", [128, 128])
        p.t_ident = p.din("ident", [128, 128])
        p.t_sel = p.din("sel", [128, NCORES])
        p.t_selmat = p.din("selmat", [NCORES * HW, HW])
        p.t_nf = p.din("nf", [128, 1])
        p.t_ret_coef = p.din("ret_coef", [128, NCORES * H])
        p.t_valid = p.din("valid", [128, NCORES * H])
        p.t_msel = p.din("msel", [NCORES, NCORES * H])
        ph = p.phase
        p.out = p.dout("out", [T, D]) if ph in (None, 6) or p.stop_after else None
        if ph is None:
            p.modscr = p.dint("modscr", [2, 48, 128])
        elif ph == 1:
            p.modscr = p.dout("modscr_o", [2, 48, 128])
            p.modT_o = p.dout("modT_o", [128, 96])
        else:
            p.modscr = p.din("modscr_i", [2, 48, 128])
            p.modT_i = p.din("modT_i", [128, 96])
        p.b_modscr = Buf("modscr")
        p.b_modT_o = Buf("modT_o")
        kinds = {None: ("int", "int"), 1: (None, None), 2: ("out", None), 3: ("in", "out"), 4: (None, "in"), 5: ("out", "in"), 6: ("in", None)}[ph]
        mk = {"int": p.dint, "in": p.din, "out": p.dout, None: (lambda *a: None)}
        p.xa = mk[kinds[0]]("xa", [T, D])
        p.xb = mk[kinds[1]]("xb", [T, D])
        p.relay = ph in (1, 2, 4, 5)
        if ph in (1, 4):
            p.kst = p.dout("kst_o", [NG, 128, 8 * GT], BF16)
            p.vst = p.dout("vst_o", [NG, 128, G * 2048], BF16)
        elif ph in (2, 5):
            p.kst = p.din("kst_i", [NG, 128, 8 * GT], BF16)
            p.vst = p.din("vst_i", [NG, 128, G * 2048], BF16)
        p.b_kvst = Buf("kvst")
        sizes = dict(EX_SIZES)
        loc_ph = {1: 1, 2: 2, 3: 3, 4: 4, 5: 5}
        gath_ph = {1: (2,), 2: (3,), 3: (4, 5), 4: (5,), 5: (6,)}
        p.loc = {}
        p.gath = {}
        for k, (r, cdim) in sizes.items():
            if p.mode == "host":
                if ph is None or loc_ph[k] == ph:
                    p.loc[k] = p.dout(f"loc{k}", [r, cdim])
                if ph is None or ph in gath_ph[k]:
                    p.gath[k] = p.din(f"gath{k}", [NCORES * r, cdim])
            else:
                p.loc[k] = p.dint(f"loc{k}", [r, cdim])
                p.gath[k] = p.dint(f"gath{k}", [NCORES * r, cdim])
        p.b_loc = {k: Buf(f"loc{k}") for k in sizes}
        p.b_gath = {k: Buf(f"gath{k}") for k in sizes}
        p.b_xa = [Buf(f"xa{g}") for g in range(NG)]
        p.b_xb = [Buf(f"xb{g}") for g in range(NG)]
        p.b_out = [Buf(f"out{g}") for g in range(NG)]

        p.ident_f, p.b_ident_f = p.sb("ident_f", [128, 128])
        p.ident_b, p.b_ident_b = p.sb("ident_b", [128, 128], BF16)
        p.ones_f, p.b_ones_f = p.sb("ones_f", [128, 128])
        p.ones_b, p.b_ones_b = p.sb("ones_b", [128, 8], BF16)
        p.ret_dt, p.b_ret_dt = p.sb("ret_dt", [128, 512])
        p.ret_xi, p.b_ret_xi = p.sb("ret_xi", [128, 4])
        p.ret_zs, p.b_ret_zs = p.sb("ret_zs", [128, 4])
        p.neg, p.b_neg = p.sb("negm", [128, 128])
        p.ut, p.b_ut = p.sb("utm", [128, 128])
        p.inv_freq, p.b_inv_freq = p.sb("inv_freq_s", [128, 1])
        p.sel, p.b_sel = p.sb("sel_s", [128, NCORES])
        p.selmat, p.b_selmat = p.sb("selmat_s", [NCORES * HW, HW])
        p.adabT, p.b_adabT = p.sb("adabT", [128, 2, 48])
        p.cT_f, p.b_cT_f = p.sb("cT_f", [128, KC])
        p.nf, p.b_nf = p.sb("nf_s", [128, 1])
        p.ret_coef, p.b_ret_coef = p.sb("ret_coef_s", [128, NCORES * H])
        p.valid, p.b_valid = p.sb("valid_s", [128, NCORES * H])
        p.msel, p.b_msel = p.sb("msel_s", [NCORES, NCORES * H])
        p.consts = [p.b_ident_f, p.b_ident_b, p.b_ones_f, p.b_ones_b]
        p.modT, p.b_modT = p.sb("modT", [128, 2, 48])
        p.ntgT, p.b_ntgT = p.sb("ntgT", [128, 2, KC])
        p.nfgT, p.b_nfgT = p.sb("nfgT", [128, 2, KC])
        p.gsc, p.b_gsc = p.sb("gsc", [128, 4, KC])
        p.ret_gnT, p.b_ret_gnT = p.sb("ret_gnT", [128, 16])
        p.ml_gnT, p.b_ml_gnT = p.sb("ml_gnT", [128, 16])
        p.ml_cwT, p.b_ml_cwT = p.sb("ml_cwT", [128, 64])
        p.ml_cbT, p.b_ml_cbT = p.sb("ml_cbT", [128, 16])
        p.ffn_cwT, p.b_ffn_cwT = p.sb("ffn_cwT", [128, 2, 132])
        p.ffn_cbT, p.b_ffn_cbT = p.sb("ffn_cbT", [128, 2, 44])
        p.bg_bc, p.b_bg_bc = p.sb("bg_bc", [128, 8])
        p.gt_bc, p.b_gt_bc = p.sb("gt_bc", [128, D])
        p.cT_b, p.b_cT_b = p.sb("cT_b", [128, KC], BF16)
        p.stage, p.b_stage = p.sb("stage", [128, 128])
        p.NR = 3
        p.wt = []
        p.b_wt = []
        for i in range(p.NR):
            t, b = p.sb(f"wt{i}", [128, 4096], BF16)
            p.wt.append(t)
            p.b_wt.append(b)
        p.ring_pos = 0
        p.rot_bank = 0
        p.rot_acc = 0
        p.rot_ub = 0
        p.b_actT = [Buf(f"actT{i}") for i in range(22)]
        p.x_g, _ = p.sb("x_g", [128, G, D])
        p.b_xg = [Buf(f"xg{c}") for c in range(G)]
        p.sg_all = p.x_g[:].bitcast(BF16)
        p.xn, p.b_xn = p.sb("xn", [128, D])
        p.xn2, p.b_xn2 = p.sb("xn2", [128, D])
        p.junk, p.b_junk = p.sb("junk", [128, D], BF16)
        p.uhalo, p.b_uhalo = p.sb("uhalo", [128, 44, HW])
        p.ss, p.b_ss = p.sb("ss", [128, 8])
        p.rstd, p.b_rstd = p.sb("rstd", [128, 8])
        p.hT, p.b_hT = p.sb("hT", [128, KC, GT], BF16)
        p.xh, p.b_xh = p.sb("xh", [32, D])
        p.hTh, p.b_hTh = p.sb("hTh", [128, KC, 32], BF16)
        p.big_a, p.b_big_a = p.sb("big_a", [128, 22 * GT], BF16)
        p.qT, p.b_qT = p.sb("qT", [128, 8, GT], BF16)
        p.kT, p.b_kT = p.sb("kT", [128, 8, GT], BF16)
        p.v_all, _ = p.sb("v_all", [128, G, 2048], BF16)
        p.b_v = [Buf(f"v{c}") for c in range(G)]
        p.R, _ = p.sb("R", [128, 8, 512])
        p.b_R = [Buf(f"R{i}") for i in range(8)]
        p.Rb, _ = p.sb("Rb", [128, 8, 512], BF16)
        p.b_Rb = [Buf(f"Rb{i}") for i in range(8)]
        p.nst, p.b_nst = p.sb("nst", [128, 8])
        p.nstb, p.b_nstb = p.sb("nstb", [128, 8], BF16)
        p.fsum, p.b_fsum = p.sb("fsum", [128, 4])
        p.wk = []
        p.b_wk = []
        for i in range(5):
            t, b = p.sb(f"wk{i}", [128, 512])
            p.wk.append(t)
            p.b_wk.append(b)
        p.cos, p.b_cos = p.sb("cos", [128, GT])
        p.sin, p.b_sin = p.sb("sin", [128, GT])
        p.yg, p.b_yg = p.sb("yg", [128, 2048], BF16)
        p.sTm, p.b_sTm = p.sb("sTm", [128, 512], BF16)
        p.kz, p.b_kz = p.sb("kz", [128, 1024], BF16)
        p.sm, p.b_sm = p.sb("sm", [128, 96])
        p.mcoef, p.b_mcoef = p.sb("mcoef", [128, NCORES * H])
        p.gat, p.b_gat = p.sb("gat", [128, G, 8])
        p.gmx, p.b_gmx = p.sb("gmx", [128, G, 8])
        p.gm, p.b_gm = p.sb("gm", [128, 7, 4 * G])
        p.ubuf = []
        p.b_ubuf = []
        for i in range(2):
            t, b = p.sb(f"ubuf{i}", [128, HW + GT])
            p.ubuf.append(t)
            p.b_ubuf.append(b)
        p.P = [p.ps(f"P{i}", [128, 512]) for i in range(6)]
        p.b_P = [Buf(f"P{i}") for i in range(6)]
        p.Pb = [p.ps(f"Pb{i}", [128, 1024], BF16) for i in range(2)]
        p.b_Pb = [Buf(f"Pb{i}") for i in range(2)]

    def dma_group(self, q, key, pairs, reads=(), writes=()):
        S = self.S
        k = S._dma_sem(key)
        S._wait(q, S._deps(reads, writes))
        for out, in_ in pairs:
            ins = S.eng[q].dma_start(out=out, in_=in_)
            S.cnt[k] += 16
            ins.then_inc(S.sems[k], 16)
            S.ninst += 1
        S._record((k, S.cnt[k]), reads, writes)

    def dump(self, name, ap, bufs):
        if not self.debug:
            return
        shape = list(ap.shape)
        d = self.nc.dram_tensor("dbg_" + name, shape, ap.dtype, kind="ExternalOutput").ap()
        b = Buf("dbg_" + name)
        self.dbg_bufs.append(b)
        self.S.dma("sp", d, ap, reads=bufs, writes=[b], key="dbg_" + name)

    def load_T(self, src_rows, n, dst_ap, dst_buf):
        p, S = self, self.S
        S.dma("sp", p.stage[0:n, :], src_rows, writes=[p.b_stage], key="stage")
        S.op("pe", lambda e: e.transpose(p.P[5][:, 0:n], p.stage[0:n, :], p.ident_f[0:n, 0:n]),
             reads=[p.b_stage, p.b_ident_f], writes=[p.b_P[5]])
        S.op("dve", lambda e: e.tensor_copy(out=dst_ap, in_=p.P[5][:, 0:n]), reads=[p.b_P[5]], writes=[dst_buf])

    def ring(self):
        i = self.ring_pos % self.NR
        self.ring_pos += 1
        return self.wt[i], self.b_wt[i], f"w{i}"

    def load_std_piece(self, W2d, c0, w=512):
        t, b, key = self.ring()
        view = t[:, 0:8 * w].rearrange("p (k n) -> p k n", k=8)
        src = W2d.rearrange("(k p) n -> p k n", p=128)
        pairs = [(view[:, 0:4, :], src[:, 0:4, c0:c0 + w]), (view[:, 4:8, :], src[:, 4:8, c0:c0 + w])]
        self.dma_group("pool", key, pairs, writes=[b])
        return view, b

    def load_rows_piece(self, W2d, k0, nk, c0, w):
        t, b, key = self.ring()
        view = t[:, 0:nk * w].rearrange("p (k n) -> p k n", k=nk)
        src = W2d.rearrange("(k p) n -> p k n", p=128)
        pairs = []
        step = 4
        for a in range(0, nk, step):
            e = min(nk, a + step)
            pairs.append((view[:, a:e, :], src[:, k0 + a:k0 + e, c0:c0 + w]))
        self.dma_group("pool", key, pairs, writes=[b])
        return view, b

    def _setup(self):
        p, S = self, self.S
        loads = [(p.ident_f, p.b_ident_f, p.t_ident), (p.ret_dt, p.b_ret_dt, p.t_ret_dt),
                 (p.ret_xi, p.b_ret_xi, p.t_ret_xi), (p.ret_zs, p.b_ret_zs, p.t_ret_zs),
                 (p.neg, p.b_neg, p.t_neg), (p.ut, p.b_ut, p.t_ut), (p.inv_freq, p.b_inv_freq, p.t_inv_freq),
                 (p.sel, p.b_sel, p.t_sel), (p.nf, p.b_nf, p.t_nf), (p.ret_coef, p.b_ret_coef, p.t_ret_coef),
                 (p.valid, p.b_valid, p.t_valid), (p.msel, p.b_msel, p.t_msel)]
        self.dma_group("sp", "setup", [(t[:], src) for t, b, src in loads], writes=[b for t, b, s in loads])
        S.dma("sp", p.bg_bc[:], p.ml_bg[0:1, :].partition_broadcast(128).rearrange("p o n -> p (o n)"),
              writes=[p.b_bg_bc], key="setup2")
        S.op("dve", lambda e: e.memset(p.ones_f[:], 1.0), writes=[p.b_ones_f])
        S.op("dve", lambda e: e.memset(p.ones_b[:], 1.0), writes=[p.b_ones_b])
        S.op("dve", lambda e: e.memset(p.xh[:], 0.0), writes=[p.b_xh])
        S.op("dve", lambda e: e.tensor_copy(out=p.ident_b[:], in_=p.ident_f[:]), reads=[p.b_ident_f], writes=[p.b_ident_b])
        vec = [(p.ntgT, p.b_ntgT, p.ntg), (p.nfgT, p.b_nfgT, p.nfg), (p.ffn_cwT, p.b_ffn_cwT, p.ffn_cw), (p.ffn_cbT, p.b_ffn_cbT, p.ffn_cb),
               (p.ret_gnT, p.b_ret_gnT, p.ret_gn), (p.ml_gnT, p.b_ml_gnT, p.ml_gn), (p.ml_cwT, p.b_ml_cwT, p.ml_cw), (p.ml_cbT, p.b_ml_cbT, p.ml_cb),
               (p.adabT, p.b_adabT, p.ada_b), (p.cT_f, p.b_cT_f, p.c_in), (p.selmat, p.b_selmat, p.t_selmat)]
        self.dma_group("sp", "setup3", [(t[:], s) for t, b, s in vec], writes=[b for t, b, s in vec])
        if p.mod_layers:
            S.op("act", lambda e: e.activation(out=p.cT_b[:], in_=p.cT_f[:], func=AF.Silu), reads=[p.b_cT_f], writes=[p.b_cT_b])
            for l in p.mod_layers:
                for pc in range(12):
                    wv, wb = self.load_std_piece(p.ada_w[l], pc * 512)
                    for ct in range(4):
                        col = l * 48 + pc * 4 + ct
                        for kc in range(KC):
                            S.op("pe", lambda e: e.matmul(p.P[4][:, col:col + 1], lhsT=wv[:, kc, ct * 128:(ct + 1) * 128],
                                                          rhs=p.cT_b[:, kc:kc + 1], start=(kc == 0), stop=(kc == KC - 1)),
                                 reads=[wb, p.b_cT_b], writes=[p.b_P[4]])
            for l in p.mod_layers:
                S.op("dve", lambda e: e.tensor_tensor(out=p.modT[:, l, :], in0=p.adabT[:, l, :], in1=p.P[4][:, l * 48:(l + 1) * 48], op=ALU.add),
                     reads=[p.b_P[4], p.b_adabT], writes=[p.b_modT])
                S.op("pe", lambda e: e.transpose(p.P[5][0:48, 0:128], p.modT[:, l, :], p.ident_f[:, :]),
                     reads=[p.b_modT, p.b_ident_f], writes=[p.b_P[5]])
                S.op("dve", lambda e: e.tensor_copy(out=p.stage[0:48, :], in_=p.P[5][0:48, 0:128]), reads=[p.b_P[5]], writes=[p.b_stage])
                S.dma("sp", p.modscr[l], p.stage[0:48, :], reads=[p.b_stage], writes=[p.b_modscr], key="modscr")
            if p.phase == 1:
                S.dma("sp", p.modT_o[:, :], p.modT[:].rearrange("p l n -> p (l n)"), reads=[p.b_modT], writes=[p.b_modT_o], key="modT_o")
        else:
            S.dma("sp", p.modT[:].rearrange("p l n -> p (l n)"), p.modT_i[:, :], writes=[p.b_modT], key="modT_i")
        for l in p.layers:
            S.op("dve", lambda e: e.scalar_tensor_tensor(out=p.gsc[:, 2 * l, :], in0=p.modT[:, l, 8:16], scalar=1.0,
                                                         in1=p.ntgT[:, l, :], op0=ALU.add, op1=ALU.mult),
                 reads=[p.b_modT, p.b_ntgT], writes=[p.b_gsc])
            S.op("dve", lambda e: e.scalar_tensor_tensor(out=p.gsc[:, 2 * l + 1, :], in0=p.modT[:, l, 32:40], scalar=1.0,
                                                         in1=p.nfgT[:, l, :], op0=ALU.add, op1=ALU.mult),
                 reads=[p.b_modT, p.b_nfgT], writes=[p.b_gsc])

    def load_gate(self, l, which):
        p, S = self, self.S
        r0 = 16 if which == 0 else 40
        src = p.modscr[l, r0:r0 + 8, :].rearrange("(o a) b -> o (a b)", o=1).partition_broadcast(128).rearrange("p o n -> p (o n)")
        S.dma("sp", p.gt_bc[:], src, reads=[p.b_modscr], writes=[p.b_gt_bc], key="gt")

    def norm_rows(self, xt, bx, npart, gidx, sh, dst_fn, dst_buf, col):
        p, S = self, self.S
        S.op("act", lambda e: e.activation(out=p.xn[0:npart, :], in_=xt, func=AF.Square, accum_out=p.ss[0:npart, col:col + 1]),
             reads=[bx], writes=[p.b_xn, p.b_ss])
        S.op("dve", lambda e: e.tensor_scalar(out=p.ss[0:npart, col:col + 1], in0=p.ss[0:npart, col:col + 1], scalar1=1.0 / D, scalar2=EPS,
                                              op0=ALU.mult, op1=ALU.add), reads=[p.b_ss], writes=[p.b_ss])
        S.op("act", lambda e: e.activation(out=p.ss[0:npart, col:col + 1], in_=p.ss[0:npart, col:col + 1], func=AF.Sqrt),
             reads=[p.b_ss], writes=[p.b_ss])
        S.op("dve", lambda e: e.reciprocal(out=p.rstd[0:npart, col:col + 1], in_=p.ss[0:npart, col:col + 1]),
             reads=[p.b_ss], writes=[p.b_rstd])
        S.op("act", lambda e: e.activation(out=p.xn[0:npart, :], in_=xt, func=AF.Copy, scale=p.rstd[0:npart, col:col + 1]),
             reads=[bx, p.b_rstd], writes=[p.b_xn])
        for half in range(2):
            bank = p.P[half]
            for k4 in range(4):
                kc = half * 4 + k4
                S.op("pe", lambda e: e.transpose(bank[:, k4 * 128:k4 * 128 + npart], p.xn[0:npart, kc * 128:(kc + 1) * 128],
                                                 p.ident_f[0:npart, 0:npart]),
                     reads=[p.b_xn, p.b_ident_f], writes=[p.b_P[half]])
            for k4 in range(4):
                kc = half * 4 + k4
                src = bank[:, k4 * 128:k4 * 128 + npart]
                if kc % 2 == 0:
                    S.op("act", lambda e: e.activation(out=dst_fn(kc), in_=src, func=AF.Identity,
                                                       scale=p.gsc[:, gidx, kc:kc + 1], bias=sh[:, kc:kc + 1]),
                         reads=[p.b_P[half], p.b_gsc, p.b_modT], writes=[dst_buf])
                else:
                    S.op("dve", lambda e: e.tensor_scalar(out=dst_fn(kc), in0=src, scalar1=p.gsc[:, gidx, kc:kc + 1],
                                                          scalar2=sh[:, kc:kc + 1], op0=ALU.mult, op1=ALU.add),
                         reads=[p.b_P[half], p.b_gsc, p.b_modT], writes=[dst_buf])

    def norm_group(self, src, src_bufs, g, gidx, sh):
        p, S = self, self.S
        for c in range(G):
            r0 = g * GT + c * CH
            S.dma("sp", p.x_g[:, c, :], src[r0:r0 + CH, :], reads=src_bufs, writes=[p.b_xg[c]], key=f"xg{c}")
        for c in range(G):
            S.op("act", lambda e: e.activation(out=p.junk[:], in_=p.x_g[:, c, :], func=AF.Square, accum_out=p.ss[:, c:c + 1]),
                 reads=[p.b_xg[c]], writes=[p.b_junk, p.b_ss])
        S.op("dve", lambda e: e.tensor_scalar(out=p.ss[:, 0:G], in0=p.ss[:, 0:G], scalar1=1.0 / D, scalar2=EPS, op0=ALU.mult, op1=ALU.add),
             reads=[p.b_ss], writes=[p.b_ss])
        S.op("act", lambda e: e.activation(out=p.ss[:, 0:G], in_=p.ss[:, 0:G], func=AF.Sqrt), reads=[p.b_ss], writes=[p.b_ss])
        S.op("dve", lambda e: e.reciprocal(out=p.rstd[:, 0:G], in_=p.ss[:, 0:G]), reads=[p.b_ss], writes=[p.b_rstd])
        for c in range(G):
            xn, b_xn = (p.xn, p.b_xn) if c % 2 == 0 else (p.xn2, p.b_xn2)
            S.op("act", lambda e: e.activation(out=xn[:], in_=p.x_g[:, c, :], func=AF.Copy, scale=p.rstd[:, c:c + 1]),
                 reads=[p.b_xg[c], p.b_rstd], writes=[b_xn])
            for half in range(2):
                bi = 2 * (c % 2) + half
                bank = p.P[bi]
                for k4 in range(4):
                    kc = half * 4 + k4
                    S.op("pe", lambda e: e.transpose(bank[:, k4 * 128:(k4 + 1) * 128], xn[:, kc * 128:(kc + 1) * 128], p.ident_f[:, :]),
                         reads=[b_xn, p.b_ident_f], writes=[p.b_P[bi]])
                for k4 in range(4):
                    kc = half * 4 + k4
                    srcp = bank[:, k4 * 128:(k4 + 1) * 128]
                    dst = p.hT[:, kc, c * CH:(c + 1) * CH]
                    if half == 0:
                        S.op("act", lambda e: e.activation(out=dst, in_=srcp, func=AF.Identity,
                                                           scale=p.gsc[:, gidx, kc:kc + 1], bias=sh[:, kc:kc + 1]),
                             reads=[p.b_P[bi], p.b_gsc, p.b_modT], writes=[p.b_hT])
                    else:
                        S.op("dve", lambda e: e.tensor_scalar(out=dst, in0=srcp, scalar1=p.gsc[:, gidx, kc:kc + 1],
                                                              scalar2=sh[:, kc:kc + 1], op0=ALU.mult, op1=ALU.add),
                             reads=[p.b_P[bi], p.b_gsc, p.b_modT], writes=[p.b_hT])

    def load_halo(self, src, src_bufs, g, ex):
        p, S = self, self.S
        if g > 0:
            r0 = g * GT - HW
            S.dma("sp", p.xh[0:HW, :], src[r0:r0 + HW, :], reads=src_bufs, writes=[p.b_xh], key="xh")
        else:
            gt = p.gath[ex]
            nr = NCORES * HW
            S.dma("sp", p.xn[0:nr, :], gt[:, :], reads=[p.b_gath[ex]], writes=[p.b_xn], key="xnh")
            for half in range(2):
                S.op("pe", lambda e: e.matmul(p.P[half][0:HW, :], lhsT=p.selmat[0:nr, 0:HW], rhs=p.xn[0:nr, half * 512:(half + 1) * 512],
                                              start=True, stop=True), reads=[p.b_selmat, p.b_xn], writes=[p.b_P[half]])
                S.op("dve", lambda e: e.tensor_copy(out=p.xh[0:HW, half * 512:(half + 1) * 512], in_=p.P[half][0:HW, :]),
                     reads=[p.b_P[half]], writes=[p.b_xh])

    def norm_halo(self, gidx, sh):
        p = self
        self.norm_rows(p.xh[0:32, :], p.b_xh, 32, gidx, sh, lambda kc: p.hTh[:, kc, :], p.b_hTh, 4)

    def rope_tables(self, g):
        p, S = self, self.S
        posi = p.wk[4][:].bitcast(I32)
        src = p.pos[0:1, g * GT:(g + 1) * GT].partition_broadcast(128).rearrange("p o n -> p (o n)")
        S.dma("sp", posi, src, writes=[p.b_wk[4]], key="posi")
        ang, b_ang = p.wk[0], p.b_wk[0]
        S.op("dve", lambda e: e.tensor_copy(out=p.wk[1][:], in_=posi), reads=[p.b_wk[4]], writes=[p.b_wk[1]])
        S.op("dve", lambda e: e.tensor_scalar(out=ang[:], in0=p.wk[1][:], scalar1=p.inv_freq[:, 0:1], scalar2=None, op0=ALU.mult),
             reads=[p.b_wk[1], p.b_inv_freq], writes=[b_ang])
        for dst, b_dst, shift in ((p.sin, p.b_sin, 0.0), (p.cos, p.b_cos, 0.5 * math.pi)):
            xs, b_xs = p.wk[1], p.b_wk[1]
            kf, b_kf = p.wk[2], p.b_wk[2]
            ki = p.wk[3][:].bitcast(I32)
            b_ki = p.b_wk[3]
            S.op("dve", lambda e: e.tensor_scalar(out=xs[:], in0=ang[:], scalar1=shift, scalar2=None, op0=ALU.add),
                 reads=[b_ang], writes=[b_xs])
            S.op("dve", lambda e: e.tensor_scalar(out=kf[:], in0=xs[:], scalar1=1.0 / TWO_PI, scalar2=None, op0=ALU.mult),
                 reads=[b_xs], writes=[b_kf])
            S.op("dve", lambda e: e.tensor_copy(out=ki, in_=kf[:]), reads=[b_kf], writes=[b_ki])
            S.op("dve", lambda e: e.tensor_copy(out=kf[:], in_=ki), reads=[b_ki], writes=[b_kf])
            S.op("dve", lambda e: e.scalar_tensor_tensor(out=xs[:], in0=kf[:], scalar=-C1, in1=xs[:], op0=ALU.mult, op1=ALU.add),
                 reads=[b_kf, b_xs], writes=[b_xs])
            S.op("dve", lambda e: e.scalar_tensor_tensor(out=xs[:], in0=kf[:], scalar=-C2, in1=xs[:], op0=ALU.mult, op1=ALU.add),
                 reads=[b_kf, b_xs], writes=[b_xs])
            S.op("dve", lambda e: e.tensor_scalar(out=kf[:], in0=xs[:], scalar1=-math.pi, scalar2=TWO_PI, op0=ALU.is_lt, op1=ALU.mult),
                 reads=[b_xs], writes=[b_kf])
            S.op("dve", lambda e: e.tensor_tensor(out=xs[:], in0=xs[:], in1=kf[:], op=ALU.add), reads=[b_xs, b_kf], writes=[b_xs])
            S.op("dve", lambda e: e.tensor_scalar(out=kf[:], in0=xs[:], scalar1=math.pi, scalar2=-TWO_PI, op0=ALU.is_gt, op1=ALU.mult),
                 reads=[b_xs], writes=[b_kf])
            S.op("dve", lambda e: e.tensor_tensor(out=xs[:], in0=xs[:], in1=kf[:], op=ALU.add), reads=[b_xs, b_kf], writes=[b_xs])
            S.op("dve", lambda e: e.tensor_scalar(out=xs[:], in0=xs[:], scalar1=-PI_SAFE, scalar2=PI_SAFE, op0=ALU.max, op1=ALU.min),
                 reads=[b_xs], writes=[b_xs])
            S.op("act", lambda e: e.activation(out=dst[:], in_=xs[:], func=AF.Sin), reads=[b_xs], writes=[b_dst])

    def rope_pair(self, hh, dst, b_dst):
        p, S = self, self.S
        i1, i2 = 2 * (hh % 2), 2 * (hh % 2) + 1
        b1, b2 = p.P[i1], p.P[i2]
        A, Bm, C_, Dm = p.wk[0], p.wk[1], p.wk[2], p.wk[3]
        S.op("dve", lambda e: e.tensor_tensor(out=A[:], in0=b1[:], in1=p.cos[:], op=ALU.mult), reads=[p.b_P[i1], p.b_cos], writes=[p.b_wk[0]])
        S.op("dve", lambda e: e.tensor_tensor(out=Bm[:], in0=b2[:], in1=p.sin[:], op=ALU.mult), reads=[p.b_P[i2], p.b_sin], writes=[p.b_wk[1]])
        S.op("dve", lambda e: e.tensor_tensor(out=C_[:], in0=b1[:], in1=p.sin[:], op=ALU.mult), reads=[p.b_P[i1], p.b_sin], writes=[p.b_wk[2]])
        S.op("dve", lambda e: e.tensor_tensor(out=Dm[:], in0=b2[:], in1=p.cos[:], op=ALU.mult), reads=[p.b_P[i2], p.b_cos], writes=[p.b_wk[3]])
        S.op("dve", lambda e: e.tensor_tensor(out=dst[:, 2 * hh, :], in0=A[:], in1=Bm[:], op=ALU.subtract),
             reads=[p.b_wk[0], p.b_wk[1]], writes=[b_dst])
        S.op("dve", lambda e: e.tensor_tensor(out=dst[:, 2 * hh + 1, :], in0=C_[:], in1=Dm[:], op=ALU.add),
             reads=[p.b_wk[2], p.b_wk[3]], writes=[b_dst])

    def proj_A(self, wv, wb, ct, bank_i, hT=None, b_hT=None, n=GT):
        p, S = self, self.S
        hT = p.hT if hT is None else hT
        b_hT = p.b_hT if b_hT is None else b_hT
        for kc in range(KC):
            S.op("pe", lambda e: e.matmul(p.P[bank_i][:, 0:n], lhsT=wv[:, kc, ct * 128:(ct + 1) * 128], rhs=hT[:, kc, 0:n],
                                          start=(kc == 0), stop=(kc == KC - 1)),
                 reads=[wb, b_hT], writes=[p.b_P[bank_i]])

    def proj_B(self, wv, wb, c, bank_i, w=512):
        p, S = self, self.S
        for kc in range(KC):
            S.op("pe", lambda e: e.matmul(p.P[bank_i][:, 0:w], lhsT=p.hT[:, kc, c * CH:(c + 1) * CH], rhs=wv[:, kc, 0:w],
                                          start=(kc == 0), stop=(kc == KC - 1)),
                 reads=[wb, p.b_hT], writes=[p.b_P[bank_i]])

    def exchange(self, k):
        p, S = self, self.S
        if p.mode == "host":
            return
        S.wait_bufs("pool", [p.b_loc[k], p.b_gath[k]])
        ins = p.nc.gpsimd.collective_compute("AllGather", ALU.bypass, replica_groups=[list(range(NCORES))],
                                             ins=[p.loc[k][:, :]], outs=[p.gath[k][:, :]])
        key = S._dma_sem(f"cc{k}")
        S.cnt[key] += 16
        ins.then_inc(S.sems[key], 16)
        S._record((key, S.cnt[key]), [p.b_loc[k]], [p.b_gath[k]])

    def make_kz(self, c, scale_ap_fn):
        p, S = self, self.S
        cs = slice(c * CH, (c + 1) * CH)
        for i in range(8):
            S.op("pe", lambda e: e.transpose(p.Pb[0][:, i * 128:(i + 1) * 128], p.kT[:, i, cs], p.ident_b[:, :]),
                 reads=[p.b_kT, p.b_ident_b], writes=[p.b_Pb[0]])
        for hh in range(H):
            sc, sbufs = scale_ap_fn(hh)
            S.op("act", lambda e: e.activation(out=p.kz[:, hh * 256:(hh + 1) * 256], in_=p.Pb[0][:, hh * 256:(hh + 1) * 256],
                                               func=AF.Copy, scale=sc),
                 reads=[p.b_Pb[0]] + sbufs, writes=[p.b_kz])

    def state_update(self, c, hh, decay, dbufs=()):
        p, S = self, self.S
        dbufs = list(dbufs)
        for half in range(2):
            i = 2 * hh + half
            S.op("pe", lambda e: e.matmul(p.P[2 + half][:, :], lhsT=p.kz[:, hh * 256 + half * 128: hh * 256 + (half + 1) * 128],
                                          rhs=p.v_all[:, c, hh * 512:(hh + 1) * 512], start=True, stop=True),
                 reads=[p.b_kz, p.b_v[c]], writes=[p.b_P[2 + half]])
            S.op("dve", lambda e: e.scalar_tensor_tensor(out=p.R[:, i, :], in0=p.R[:, i, :], scalar=decay, in1=p.P[2 + half][:, :],
                                                         op0=ALU.mult, op1=ALU.add),
                 reads=[p.b_R[i], p.b_P[2 + half]] + dbufs, writes=[p.b_R[i]])

    def refresh_Rb(self, hh):
        p, S = self, self.S
        for half in range(2):
            i = 2 * hh + half
            S.op("act", lambda e: e.activation(out=p.Rb[:, i, :], in_=p.R[:, i, :], func=AF.Copy),
                 reads=[p.b_R[i]], writes=[p.b_Rb[i]])

    def groupnorm_heads(self, ywk, c, gate):
        p, S = self, self.S
        for hh in range(H):
            S.op("dve", lambda e: e.bn_stats(out=p.sm[:, 6 * hh:6 * hh + 6], in_=p.wk[ywk[hh]][:]), reads=[p.b_wk[ywk[hh]]], writes=[p.b_sm])
        for hh in range(H):
            S.op("dve", lambda e: e.bn_aggr(out=p.sm[:, 24 + 2 * hh:26 + 2 * hh], in_=p.sm[:, 6 * hh:6 * hh + 6]), reads=[p.b_sm], writes=[p.b_sm])
        var = p.sm[:, 24:32].rearrange("p (h t) -> p h t", t=2)[:, :, 1]
        S.op("dve", lambda e: e.tensor_scalar(out=p.sm[:, 32:36], in0=var, scalar1=EPS, scalar2=None, op0=ALU.add), reads=[p.b_sm], writes=[p.b_sm])
        S.op("act", lambda e: e.activation(out=p.sm[:, 32:36], in_=p.sm[:, 32:36], func=AF.Ln), reads=[p.b_sm], writes=[p.b_sm])
        S.op("act", lambda e: e.activation(out=p.sm[:, 32:36], in_=p.sm[:, 32:36], func=AF.Exp, scale=-0.5), reads=[p.b_sm], writes=[p.b_sm])
        if gate:
            for hh in range(H):
                w = p.wk[ywk[hh]]
                S.op("dve", lambda e: e.tensor_scalar(out=w[:], in0=w[:], scalar1=p.sm[:, 24 + 2 * hh:25 + 2 * hh], scalar2=p.sm[:, 32 + hh:33 + hh],
                                                      op0=ALU.subtract, op1=ALU.mult), reads=[p.b_wk[ywk[hh]], p.b_sm], writes=[p.b_wk[ywk[hh]]])
            for hh in range(H):
                w = p.wk[ywk[hh]]
                sgv = p.sg_all[:, c, hh * 512:(hh + 1) * 512]
                S.op("dve", lambda e: e.tensor_tensor(out=p.yg[:, hh * 512:(hh + 1) * 512], in0=w[:], in1=sgv, op=ALU.mult),
                     reads=[p.b_wk[ywk[hh]], p.b_xg[c]], writes=[p.b_yg])
        else:
            for hh in range(H):
                w = p.wk[ywk[hh]]
                S.op("dve", lambda e: e.tensor_scalar(out=p.yg[:, hh * 512:(hh + 1) * 512], in0=w[:], scalar1=p.sm[:, 24 + 2 * hh:25 + 2 * hh],
                                                      scalar2=p.sm[:, 32 + hh:33 + hh], op0=ALU.subtract, op1=ALU.mult),
                     reads=[p.b_wk[ywk[hh]], p.b_sm], writes=[p.b_yg])

    def make_ynT(self, c, gnT, b_gnT):
        p, S = self, self.S
        ynT = p.big_a[:, 0:16 * GT].rearrange("p (k n) -> p k n", k=16)
        for kc in range(16):
            bi = kc // 8
            S.op("pe", lambda e: e.transpose(p.Pb[bi][:, (kc % 8) * 128:(kc % 8 + 1) * 128], p.yg[:, kc * 128:(kc + 1) * 128], p.ident_b[:, :]),
                 reads=[p.b_yg, p.b_ident_b], writes=[p.b_Pb[bi]])
        for kc in range(16):
            bi = kc // 8
            src = p.Pb[bi][:, (kc % 8) * 128:(kc % 8 + 1) * 128]
            dst = ynT[:, kc, c * CH:(c + 1) * CH]
            if bi == 0:
                S.op("act", lambda e: e.activation(out=dst, in_=src, func=AF.Copy, scale=gnT[:, kc:kc + 1]),
                     reads=[p.b_Pb[bi], b_gnT], writes=[p.b_big_a])
            else:
                S.op("dve", lambda e: e.tensor_scalar(out=dst, in0=src, scalar1=gnT[:, kc:kc + 1], scalar2=None, op0=ALU.mult),
                     reads=[p.b_Pb[bi], b_gnT], writes=[p.b_big_a])

    def out_proj(self, g, W_out, src, src_bufs, dst, dst_bufs, halo_ex):
        p, S = self, self.S
        ynT = p.big_a[:, 0:16 * GT].rearrange("p (k n) -> p k n", k=16)
        for c in range(G):
            r0 = g * GT + c * CH
            S.dma("sp", p.x_g[:, c, :], src[r0:r0 + CH, :], reads=src_bufs, writes=[p.b_xg[c]], key=f"xg{c}")
        for cp in range(4):
            wv, wb = self.load_rows_piece(W_out, 0, 16, cp * 256, 256)
            for c in range(G):
                bi = (cp * G + c) % 6
                for kc in range(16):
                    S.op("pe", lambda e: e.matmul(p.P[bi][:, 0:256], lhsT=ynT[:, kc, c * CH:(c + 1) * CH], rhs=wv[:, kc, :],
                                                  start=(kc == 0), stop=(kc == 15)),
                         reads=[wb, p.b_big_a], writes=[p.b_P[bi]])
                S.op("dve", lambda e: e.tensor_tensor(out=p.wk[c][:, 0:256], in0=p.P[bi][:, 0:256], in1=p.gt_bc[:, cp * 256:(cp + 1) * 256], op=ALU.mult),
                     reads=[p.b_P[bi], p.b_gt_bc], writes=[p.b_wk[c]])
            for c in range(G):
                S.op("dve", lambda e: e.tensor_tensor(out=p.x_g[:, c, cp * 256:(cp + 1) * 256], in0=p.x_g[:, c, cp * 256:(cp + 1) * 256],
                                                      in1=p.wk[c][:, 0:256], op=ALU.add),
                     reads=[p.b_wk[c], p.b_xg[c]], writes=[p.b_xg[c]])
        self.store_group(g, dst, dst_bufs, halo_ex)

    def store_group(self, g, dst, dst_bufs, halo_ex):
        p, S = self, self.S
        pairs = [(dst[g * GT + c * CH: g * GT + (c + 1) * CH, :], p.x_g[:, c, :]) for c in range(G)]
        self.dma_group("sp", f"st_{dst_bufs[0].name[:2]}", pairs, reads=p.b_xg, writes=[dst_bufs[g]])
        if g == NG - 1 and halo_ex is not None:
            S.dma("sp", p.loc[halo_ex][:, :], p.x_g[128 - HW:128, G - 1, :], reads=[p.b_xg[G - 1]], writes=[p.b_loc[halo_ex]], key=f"loc{halo_ex}")

    def kv_store(self, g):
        p = self
        self.dma_group("sp", "kvst", [(p.kst[g], p.kT[:].rearrange("p a n -> p (a n)")), (p.vst[g], p.v_all[:].rearrange("p a n -> p (a n)"))],
                       reads=[p.b_kT] + p.b_v, writes=[p.b_kvst])

    def kv_load(self, g):
        p = self
        self.dma_group("sp", "kvld", [(p.kT[:].rearrange("p a n -> p (a n)"), p.kst[g]), (p.v_all[:].rearrange("p a n -> p (a n)"), p.vst[g])],
                       writes=[p.b_kT] + p.b_v)

    def ret_group(self, g, full):
        p, S = self, self.S
        sh = p.modT[:, 0, 0:8]
        self.norm_group(p.x, [], g, 0, sh)
        self.rope_tables(g)
        W = p.ret_w_in
        reuse = full and p.relay
        plist = ([0, 1] if full else []) + ([] if reuse else [2, 3])
        if reuse:
            self.kv_load(g)
        for pc in plist:
            wv, wb = self.load_std_piece(W, pc * 512)
            dst, b_dst = (p.qT, p.b_qT) if pc < 2 else (p.kT, p.b_kT)
            for ct in range(4):
                self.proj_A(wv, wb, ct, ct)
            for j in range(2):
                self.rope_pair(2 * (pc % 2) + j, dst, b_dst)
        for hh in ([] if reuse else range(H)):
            wv, wb = self.load_std_piece(W, 2048 + hh * 512)
            for c in range(G):
                self.proj_B(wv, wb, c, c)
                dstv = p.v_all[:, c, hh * 512:(hh + 1) * 512]
                if c % 2 == 0:
                    S.op("act", lambda e: e.activation(out=dstv, in_=p.P[c][:, :], func=AF.Copy), reads=[p.b_P[c]], writes=[p.b_v[c]])
                else:
                    S.op("dve", lambda e: e.tensor_copy(out=dstv, in_=p.P[c][:, :]), reads=[p.b_P[c]], writes=[p.b_v[c]])
        if (not full) and p.relay:
            self.kv_store(g)
        if full:
            for hh in range(H):
                wv, wb = self.load_std_piece(W, 4096 + hh * 512)
                for c in range(G):
                    self.proj_B(wv, wb, c, c)
                    dsts = p.sg_all[:, c, hh * 512:(hh + 1) * 512]
                    S.op("act", lambda e: e.activation(out=dsts, in_=p.P[c][:, :], func=AF.Silu), reads=[p.b_P[c]], writes=[p.b_xg[c]])
        if full and g == 0:
            self.dump("hT", p.hT[:], [p.b_hT])
            self.dump("cos", p.cos[:], [p.b_cos])
            self.dump("sin", p.sin[:], [p.b_sin])
            self.dump("qT", p.qT[:], [p.b_qT])
            self.dump("kT", p.kT[:], [p.b_kT])
            self.dump("v_all", p.v_all[:], p.b_v)
            self.dump("sg_all", p.sg_all, p.b_xg)
        for c in range(G):
            self.ret_chunk(c, full)
            if full and g == 0 and c == 1:
                self.dump("yg", p.yg[:], [p.b_yg])
                self.dump("sm", p.sm[:], [p.b_sm])
                self.dump("wk0", p.wk[0][:], [p.b_wk[0]])
                self.dump("wk3", p.wk[3][:], [p.b_wk[3]])
                self.dump("sTm", p.sTm[:], [p.b_sTm])
                self.dump("kz", p.kz[:], [p.b_kz])
                self.dump("R", p.R[:], p.b_R)
        if full and g == 0:
            self.dump("ynT", p.big_a[:, 0:16 * GT], [p.b_big_a])
        if full:
            self.out_proj(g, p.ret_w_out, p.x, [], p.xa, p.b_xa, 2)
        if full and g == 0:
            self.dump("xg", p.x_g[:], p.b_xg)

    def ret_chunk(self, c, full):
        p, S = self, self.S
        cs = slice(c * CH, (c + 1) * CH)
        self.make_kz(c, lambda hh: (p.ret_zs[:, hh:hh + 1], [p.b_ret_zs]))
        gam = [float(np.exp(128.0 * np.log(1.0 - 2.0 ** (-5.0 - h)))) for h in range(H)]
        if not full:
            for hh in range(H):
                self.state_update(c, hh, gam[hh])
            return
        for hh in range(H):
            for half in range(2):
                S.op("pe", lambda e: e.matmul(p.P[4][:, hh * 128:(hh + 1) * 128], lhsT=p.kT[:, 2 * hh + half, cs], rhs=p.qT[:, 2 * hh + half, cs],
                                              start=(half == 0), stop=(half == 1)),
                     reads=[p.b_kT, p.b_qT], writes=[p.b_P[4]])
        S.op("dve", lambda e: e.tensor_tensor(out=p.sTm[:], in0=p.P[4][:, :], in1=p.ret_dt[:], op=ALU.mult),
             reads=[p.b_P[4], p.b_ret_dt], writes=[p.b_sTm])
        for hh in range(H):
            S.op("pe", lambda e: e.matmul(p.P[0][:, :], lhsT=p.sTm[:, hh * 128:(hh + 1) * 128], rhs=p.v_all[:, c, hh * 512:(hh + 1) * 512],
                                          start=True, stop=True), reads=[p.b_sTm, p.b_v[c]], writes=[p.b_P[0]])
            for half in range(2):
                i = 2 * hh + half
                S.op("pe", lambda e: e.matmul(p.P[1][:, :], lhsT=p.qT[:, i, cs], rhs=p.Rb[:, i, :], start=(half == 0), stop=(half == 1)),
                     reads=[p.b_qT, p.b_Rb[i]], writes=[p.b_P[1]])
            S.op("act", lambda e: e.activation(out=p.wk[4][:], in_=p.P[1][:, :], func=AF.Copy, scale=p.ret_xi[:, hh:hh + 1]),
                 reads=[p.b_P[1], p.b_ret_xi], writes=[p.b_wk[4]])
            S.op("dve", lambda e: e.tensor_tensor(out=p.wk[hh][:], in0=p.wk[4][:], in1=p.P[0][:, :], op=ALU.add),
                 reads=[p.b_wk[4], p.b_P[0]], writes=[p.b_wk[hh]])
            self.state_update(c, hh, gam[hh])
            self.refresh_Rb(hh)
        self.groupnorm_heads([0, 1, 2, 3], c, gate=True)
        self.make_ynT(c, p.ret_gnT, p.b_ret_gnT)

    def ret_layer(self):
        p, S = self, self.S
        for i in range(8):
            S.op("dve", lambda e: e.memset(p.R[:, i, :], 0.0), writes=[p.b_R[i]])
        if p.stop_after == "dbg_ret":
            for hh in range(H):
                self.refresh_Rb(hh)
            self.load_gate(0, 0)
            self.dump("modT", p.modT[:], [p.b_modT])
            self.dump("gsc", p.gsc[:], [p.b_gsc])
            self.dump("gt_bc", p.gt_bc[:], [p.b_gt_bc])
            self.ret_group(0, full=True)
            return
        if p.phase in (None, 1):
            self.ret_part_a()
        if p.phase is None:
            self.exchange(1)
        if p.phase in (None, 2):
            self.ret_part_b()

    def ret_part_a(self):
        p, S = self, self.S
        for g in range(NG):
            self.ret_group(g, full=False)
        self.dma_group("sp", "loc1", [(p.loc[1][i * 128:(i + 1) * 128, :], p.R[:, i, :]) for i in range(8)],
                       reads=p.b_R, writes=[p.b_loc[1]])

    def ret_part_b(self):
        p, S = self, self.S
        self.combine_state(1, p.ret_coef, p.b_ret_coef, nrows=8)
        for hh in range(H):
            self.refresh_Rb(hh)
        self.load_gate(0, 0)
        for g in range(NG):
            self.ret_group(g, full=True)

    def combine_state(self, ex, coef, b_coef, nrows):
        p, S = self, self.S
        gt = p.gath[ex]
        per = nrows * 128 if ex == 1 else 9 * 128
        for i in range(8):
            hh = i // 2
            for cp in range(NCORES):
                tmp, bt = p.wk[cp % 2], p.b_wk[cp % 2]
                r0 = cp * per + i * 128
                S.dma("sp", tmp[:], gt[r0:r0 + 128, :], reads=[p.b_gath[ex]], writes=[bt], key=f"cmb{cp % 2}")
                if cp == 0:
                    S.op("dve", lambda e: e.tensor_scalar(out=p.R[:, i, :], in0=tmp[:], scalar1=coef[:, cp * H + hh: cp * H + hh + 1], scalar2=None,
                                                          op0=ALU.mult), reads=[bt, b_coef], writes=[p.b_R[i]])
                else:
                    S.op("dve", lambda e: e.scalar_tensor_tensor(out=p.R[:, i, :], in0=tmp[:], scalar=coef[:, cp * H + hh: cp * H + hh + 1],
                                                                 in1=p.R[:, i, :], op0=ALU.mult, op1=ALU.add),
                         reads=[bt, b_coef, p.b_R[i]], writes=[p.b_R[i]])

    def ffn_layer(self, l, src, src_bufs, dst, dst_bufs, ex_in, ex_out, final):
        p, S = self, self.S
        S.barrier()
        self.load_gate(l, 1)
        sh = p.modT[:, l, 24:32]
        gidx = 2 * l + 1
        actT = p.big_a[:, :].rearrange("p (k n) -> p k n", k=22)
        cw = p.ffn_cwT
        cb = p.ffn_cbT
        Wup = p.ffn_w_up[l]
        Wdn = p.ffn_w_down[l]
        for g in range(NG):
            if g == 0:
                self.load_halo(src, src_bufs, g, ex_in)
                self.norm_halo(gidx, sh)
            self.norm_group(src, src_bufs, g, gidx, sh)
            srcw = Wup.rearrange("(k p) n -> p k n", p=128)

            def load_up(pc):
                t, b, key = self.ring()
                wv = t[:, 0:4096].rearrange("p (k n) -> p k n", k=8)
                pairs = []
                for kh in range(2):
                    pairs.append((wv[:, 4 * kh:4 * kh + 4, 0:256], srcw[:, 4 * kh:4 * kh + 4, pc * 256:(pc + 1) * 256]))
                    pairs.append((wv[:, 4 * kh:4 * kh + 4, 256:512], srcw[:, 4 * kh:4 * kh + 4, DFF + pc * 256: DFF + (pc + 1) * 256]))
                self.dma_group("pool", key, pairs, writes=[b])
                return wv, b

            loaders = [(lambda pc=pc: load_up(pc)) for pc in range(11)]
            loaders += [(lambda cp=cp, kh=kh: self.load_rows_piece(Wdn, 11 * kh, 11, cp * 256, 256)) for cp in range(4) for kh in range(2)]
            loaded = {}

            def get_piece(i, ahead=2):
                for j in range(i, min(i + ahead + 1, len(loaders))):
                    if j not in loaded:
                        loaded[j] = loaders[j]()
                return loaded.pop(i)

            banks = [0, 1, 2, 3, 5]
            for pc in range(11):
                wv, b = get_piece(pc)
                for j in range(2):
                    tl = []
                    for ct in (j, 2 + j):
                        tile_i = 2 * pc + j
                        chan = tile_i if ct < 2 else 22 + tile_i
                        bi = banks[self.rot_bank % len(banks)]
                        self.rot_bank += 1
                        ai = self.rot_acc % 5
                        self.rot_acc += 1
                        ui = self.rot_ub % 2
                        self.rot_ub += 1
                        tl.append((ct, chan, bi, p.wk[ai], p.b_wk[ai], p.ubuf[ui], p.b_ubuf[ui]))
                    for ct, chan, bi, acc, b_acc, ub, b_ub in tl:
                        self.proj_A(wv, b, ct, bi)
                        if g == 0:
                            for kc in range(KC):
                                S.op("pe", lambda e: e.matmul(p.P[4][:, 0:HW], lhsT=wv[:, kc, ct * 128:(ct + 1) * 128], rhs=p.hTh[:, kc, 0:HW],
                                                              start=(kc == 0), stop=(kc == KC - 1)), reads=[b, p.b_hTh], writes=[p.b_P[4]])
                            S.op("dve", lambda e: e.tensor_scalar(out=ub[:, 0:HW], in0=p.P[4][:, 0:HW], scalar1=p.nf[:, 0:1], scalar2=None, op0=ALU.mult),
                                 reads=[p.b_P[4], p.b_nf], writes=[b_ub])
                        else:
                            S.op("act", lambda e: e.activation(out=ub[:, 0:HW], in_=p.uhalo[:, chan, :], func=AF.Copy), reads=[p.b_uhalo], writes=[b_ub])
                    for ct, chan, bi, acc, b_acc, ub, b_ub in tl:
                        S.op("act", lambda e: e.activation(out=ub[:, HW:HW + GT], in_=p.P[bi][:, :], func=AF.Copy), reads=[p.b_P[bi]], writes=[b_ub])
                        S.op("act", lambda e: e.activation(out=acc[:], in_=p.P[bi][:, :], func=AF.Identity, scale=cw[:, l, 88 + chan:89 + chan],
                                                           bias=cb[:, l, chan:chan + 1]), reads=[p.b_P[bi], p.b_ffn_cwT, p.b_ffn_cbT], writes=[b_acc])
                    if g < NG - 1:
                        for ct, chan, bi, acc, b_acc, ub, b_ub in tl:
                            S.op("act", lambda e: e.activation(out=p.uhalo[:, chan, :], in_=ub[:, GT:GT + HW], func=AF.Copy), reads=[b_ub], writes=[p.b_uhalo])
                    for tap, off in ((1, HW - 1), (0, HW - 2)):
                        for ct, chan, bi, acc, b_acc, ub, b_ub in tl:
                            S.op("dve", lambda e: e.scalar_tensor_tensor(out=acc[:], in0=ub[:, off:off + GT], scalar=cw[:, l, tap * 44 + chan:tap * 44 + chan + 1],
                                                                         in1=acc[:], op0=ALU.mult, op1=ALU.add), reads=[b_ub, b_acc, p.b_ffn_cwT], writes=[b_acc])
                    (_, _, _, aa, b_aa, _, _), (_, _, _, ab, b_ab, _, _) = tl
                    S.op("act", lambda e: e.activation(out=aa[:], in_=aa[:], func=AF.Silu), reads=[b_aa], writes=[b_aa])
                    S.op("dve", lambda e: e.tensor_tensor(out=actT[:, 2 * pc + j, :], in0=aa[:], in1=ab[:], op=ALU.mult),
                         reads=[b_aa, b_ab], writes=[p.b_actT[2 * pc + j]])
            for cp in range(4):
                for kh in range(2):
                    wv, wb = get_piece(11 + cp * 2 + kh)
                    for c in range(G):
                        for k in range(11):
                            S.op("pe", lambda e: e.matmul(p.P[c][:, 0:256], lhsT=actT[:, 11 * kh + k, c * CH:(c + 1) * CH], rhs=wv[:, k, :],
                                                          start=(kh == 0 and k == 0), stop=(kh == 1 and k == 10)),
                                 reads=[wb, p.b_actT[11 * kh + k]], writes=[p.b_P[c]])
                for c in range(G):
                    S.op("dve", lambda e: e.tensor_tensor(out=p.wk[c][:, 0:256], in0=p.P[c][:, 0:256], in1=p.gt_bc[:, cp * 256:(cp + 1) * 256], op=ALU.mult),
                         reads=[p.b_P[c], p.b_gt_bc], writes=[p.b_wk[c]])
                for c in range(G):
                    S.op("dve", lambda e: e.tensor_tensor(out=p.x_g[:, c, cp * 256:(cp + 1) * 256], in0=p.x_g[:, c, cp * 256:(cp + 1) * 256],
                                                          in1=p.wk[c][:, 0:256], op=ALU.add),
                         reads=[p.b_wk[c], p.b_xg[c]], writes=[p.b_xg[c]])
            if final:
                self.final_norm_group()
            self.store_group(g, dst, dst_bufs, ex_out)
        S.barrier()

    def final_norm_group(self):
        p, S = self, self.S
        for c in range(G):
            S.op("act", lambda e: e.activation(out=p.xn[:], in_=p.x_g[:, c, :], func=AF.Square, accum_out=p.ss[:, c:c + 1]),
                 reads=[p.b_xg[c]], writes=[p.b_xn, p.b_ss])
        S.op("dve", lambda e: e.tensor_scalar(out=p.ss[:, 0:G], in0=p.ss[:, 0:G], scalar1=1.0 / D, scalar2=EPS, op0=ALU.mult, op1=ALU.add),
             reads=[p.b_ss], writes=[p.b_ss])
        S.op("act", lambda e: e.activation(out=p.ss[:, 0:G], in_=p.ss[:, 0:G], func=AF.Sqrt), reads=[p.b_ss], writes=[p.b_ss])
        S.op("dve", lambda e: e.reciprocal(out=p.rstd[:, 0:G], in_=p.ss[:, 0:G]), reads=[p.b_ss], writes=[p.b_rstd])
        for c in range(G):
            S.op("act", lambda e: e.activation(out=p.x_g[:, c, :], in_=p.x_g[:, c, :], func=AF.Copy, scale=p.rstd[:, c:c + 1]),
                 reads=[p.b_xg[c], p.b_rstd], writes=[p.b_xg[c]])
            for half, (t, b) in enumerate(((p.cos, p.b_cos), (p.sin, p.b_sin))):
                S.op("dve", lambda e: e.tensor_tensor(out=p.x_g[:, c, half * 512:(half + 1) * 512], in0=p.x_g[:, c, half * 512:(half + 1) * 512],
                                                      in1=t[:], op=ALU.mult), reads=[p.b_xg[c], b], writes=[p.b_xg[c]])

    LN16 = math.log(16.0)

    def ml_group(self, g, full):
        p, S = self, self.S
        sh = p.modT[:, 1, 0:8]
        W = p.ml_w_in
        if g == 0:
            self.load_halo(p.xb, p.b_xb, g, 3)
            self.norm_halo(2, sh)
        self.norm_group(p.xb, p.b_xb, g, 2, sh)
        reuse = full and p.relay
        plist = ([0, 1] if full else []) + ([] if reuse else [2, 3])
        if reuse:
            self.kv_load(g)
        for pc in plist:
            wv, wb = self.load_std_piece(W, pc * 512)
            dst, b_dst = (p.qT, p.b_qT) if pc < 2 else (p.kT, p.b_kT)
            for pr in range(2):
                tl = []
                for ct in (2 * pr, 2 * pr + 1):
                    tl.append((ct, pc * 4 + ct, p.wk[ct], p.b_wk[ct], p.ubuf[ct % 2], p.b_ubuf[ct % 2]))
                for ct, cti, acc, b_acc, ub, b_ub in tl:
                    self.proj_A(wv, wb, ct, ct)
                    if g == 0:
                        for kc in range(KC):
                            S.op("pe", lambda e: e.matmul(p.P[4][:, 0:HW], lhsT=wv[:, kc, ct * 128:(ct + 1) * 128], rhs=p.hTh[:, kc, 0:HW],
                                                          start=(kc == 0), stop=(kc == KC - 1)), reads=[wb, p.b_hTh], writes=[p.b_P[4]])
                        S.op("dve", lambda e: e.tensor_scalar(out=ub[:, 0:HW], in0=p.P[4][:, 0:HW], scalar1=p.nf[:, 0:1], scalar2=None, op0=ALU.mult),
                             reads=[p.b_P[4], p.b_nf], writes=[b_ub])
                    else:
                        S.op("act", lambda e: e.activation(out=ub[:, 0:HW], in_=p.uhalo[:, cti, :], func=AF.Copy), reads=[p.b_uhalo], writes=[b_ub])
                for ct, cti, acc, b_acc, ub, b_ub in tl:
                    S.op("act", lambda e: e.activation(out=ub[:, HW:HW + GT], in_=p.P[ct][:, :], func=AF.Copy), reads=[p.b_P[ct]], writes=[b_ub])
                    S.op("act", lambda e: e.activation(out=acc[:], in_=p.P[ct][:, :], func=AF.Identity, scale=p.ml_cwT[:, 48 + cti:49 + cti],
                                                       bias=p.ml_cbT[:, cti:cti + 1]), reads=[p.b_P[ct], p.b_ml_cwT, p.b_ml_cbT], writes=[b_acc])
                if g < NG - 1:
                    for ct, cti, acc, b_acc, ub, b_ub in tl:
                        S.op("act", lambda e: e.activation(out=p.uhalo[:, cti, :], in_=ub[:, GT:GT + HW], func=AF.Copy), reads=[b_ub], writes=[p.b_uhalo])
                for j in (2, 1, 0):
                    off = HW - (3 - j)
                    for ct, cti, acc, b_acc, ub, b_ub in tl:
                        S.op("dve", lambda e: e.scalar_tensor_tensor(out=acc[:], in0=ub[:, off:off + GT], scalar=p.ml_cwT[:, j * 16 + cti: j * 16 + cti + 1],
                                                                     in1=acc[:], op0=ALU.mult, op1=ALU.add), reads=[b_ub, b_acc, p.b_ml_cwT], writes=[b_acc])
                for ct, cti, acc, b_acc, ub, b_ub in tl:
                    S.op("act", lambda e: e.activation(out=dst[:, cti % 8, :], in_=acc[:], func=AF.Silu), reads=[b_acc], writes=[b_dst])
        for hh in ([] if reuse else range(H)):
            wv, wb = self.load_std_piece(W, 2048 + hh * 512)
            for c in range(G):
                self.proj_B(wv, wb, c, c)
                dstv = p.v_all[:, c, hh * 512:(hh + 1) * 512]
                if c % 2 == 0:
                    S.op("act", lambda e: e.activation(out=dstv, in_=p.P[c][:, :], func=AF.Copy), reads=[p.b_P[c]], writes=[p.b_v[c]])
                else:
                    S.op("dve", lambda e: e.tensor_copy(out=dstv, in_=p.P[c][:, :]), reads=[p.b_P[c]], writes=[p.b_v[c]])
        if (not full) and p.relay:
            self.kv_store(g)
        wv, wb = self.load_std_piece(W, 6144, w=8)
        for c in range(G):
            self.proj_B(wv, wb, c, c, w=8)
            S.op("dve", lambda e: e.tensor_tensor(out=p.gat[:, c, :], in0=p.P[c][:, 0:8], in1=p.bg_bc[:], op=ALU.add),
                 reads=[p.b_P[c], p.b_bg_bc], writes=[p.b_gat])
        if full:
            for hh in range(H):
                wv, wb = self.load_std_piece(W, 4096 + hh * 512)
                for c in range(G):
                    self.proj_B(wv, wb, c, c)
                    dsts = p.sg_all[:, c, hh * 512:(hh + 1) * 512]
                    S.op("act", lambda e: e.activation(out=dsts, in_=p.P[c][:, :], func=AF.Sigmoid), reads=[p.b_P[c]], writes=[p.b_xg[c]])
        self.ml_gates_group(full)
        for c in range(G):
            self.ml_chunk(c, full)
        if full:
            self.out_proj(g, p.ml_w_out, p.xb, p.b_xb, p.xa, p.b_xa, 5)

    def ml_gates_group(self, full):
        p, S = self, self.S
        gm, bg = p.gm, p.b_gm
        v3 = lambda k: gm[:, k, :].rearrange("p (c h) -> p c h", h=4)
        z = p.gat[:, :, 4:8]
        li = p.gat[:, :, 0:4]
        S.op("act", lambda e: e.activation(out=v3(0), in_=z, func=AF.Exp, scale=-1.0), reads=[p.b_gat], writes=[bg])
        S.op("dve", lambda e: e.tensor_scalar(out=gm[:, 0, :], in0=gm[:, 0, :], scalar1=1.0, scalar2=None, op0=ALU.add), reads=[bg], writes=[bg])
        S.op("act", lambda e: e.activation(out=gm[:, 0, :], in_=gm[:, 0, :], func=AF.Ln), reads=[bg], writes=[bg])
        S.op("dve", lambda e: e.tensor_scalar(out=gm[:, 0, :], in0=gm[:, 0, :], scalar1=-1.0, scalar2=None, op0=ALU.mult), reads=[bg], writes=[bg])
        n = 4 * G
        S.op("pe", lambda e: e.matmul(p.P[5][:, 0:n], lhsT=p.ut[:, :], rhs=gm[:, 0, :], start=True, stop=True), reads=[p.b_ut, bg], writes=[p.b_P[5]])
        S.op("pe", lambda e: e.matmul(p.P[5][:, n:2 * n], lhsT=p.ones_f[:, :], rhs=gm[:, 0, :], start=True, stop=True), reads=[p.b_ones_f, bg], writes=[p.b_P[5]])
        S.op("dve", lambda e: e.tensor_copy(out=gm[:, 1, :], in_=p.P[5][:, 0:n]), reads=[p.b_P[5]], writes=[bg])
        S.op("dve", lambda e: e.tensor_copy(out=gm[:, 2, :], in_=p.P[5][:, n:2 * n]), reads=[p.b_P[5]], writes=[bg])
        S.op("dve", lambda e: e.tensor_tensor(out=gm[:, 3, :], in0=gm[:, 2, :], in1=gm[:, 1, :], op=ALU.subtract), reads=[bg], writes=[bg])
        S.op("dve", lambda e: e.scalar_tensor_tensor(out=v3(3), in0=v3(3), scalar=-self.LN16, in1=li, op0=ALU.add, op1=ALU.add),
             reads=[bg, p.b_gat], writes=[bg])
        S.op("act", lambda e: e.activation(out=gm[:, 3, :], in_=gm[:, 3, :], func=AF.Exp), reads=[bg], writes=[bg])
        S.op("act", lambda e: e.activation(out=gm[:, 4, :], in_=gm[:, 2, :], func=AF.Exp), reads=[bg], writes=[bg])
        gx = p.gmx[:].rearrange("p c (h t) -> p c h t", t=2)
        for t in range(2):
            S.op("dve", lambda e: e.tensor_copy(out=gx[:, :, :, t], in_=v3(4)), reads=[bg], writes=[p.b_gmx])
        if not full:
            for c in range(G):
                S.op("dve", lambda e: e.tensor_tensor(out=p.fsum[:], in0=p.fsum[:], in1=gm[:, 2, 4 * c:4 * c + 4], op=ALU.add),
                     reads=[bg, p.b_fsum], writes=[p.b_fsum])
        else:
            S.op("dve", lambda e: e.tensor_tensor(out=v3(5), in0=li, in1=v3(1), op=ALU.subtract), reads=[bg, p.b_gat], writes=[bg])
            S.op("dve", lambda e: e.tensor_scalar(out=gm[:, 5, :], in0=gm[:, 5, :], scalar1=-self.LN16, scalar2=None, op0=ALU.add), reads=[bg], writes=[bg])
            S.op("act", lambda e: e.activation(out=gm[:, 6, :], in_=gm[:, 1, :], func=AF.Exp), reads=[bg], writes=[bg])

    def ml_chunk(self, c, full):
        p, S = self, self.S
        cs = slice(c * CH, (c + 1) * CH)
        sm, bs = p.sm, p.b_sm
        gm, bg = p.gm, p.b_gm
        col = 4 * c
        g_b = lambda hh: gm[:, 1, col + hh:col + hh + 1]
        g_ws = lambda hh: gm[:, 3, col + hh:col + hh + 1]
        g_sp = lambda hh: gm[:, 4, col + hh:col + hh + 1]
        g_bj = lambda hh: gm[:, 5, col + hh:col + hh + 1]
        g_wi = lambda hh: gm[:, 6, col + hh:col + hh + 1]
        self.make_kz(c, lambda hh: (g_ws(hh), [bg]))
        if full:
            for hh in range(H):
                S.op("dve", lambda e: e.tensor_scalar(out=p.xn[:, hh * 128:(hh + 1) * 128], in0=p.ident_f[:, :], scalar1=g_b(hh), scalar2=None, op0=ALU.mult),
                     reads=[bg, p.b_ident_f], writes=[p.b_xn])
                S.op("pe", lambda e: e.matmul(p.P[5][:, hh * 128:(hh + 1) * 128], lhsT=p.ones_f[:, :], rhs=p.xn[:, hh * 128:(hh + 1) * 128], start=True, stop=False),
                     reads=[p.b_ones_f, p.b_xn], writes=[p.b_P[5]])
                S.op("pe", lambda e: e.matmul(p.P[5][:, hh * 128:(hh + 1) * 128], lhsT=p.ident_f[:, :], rhs=p.neg[:, :], start=False, stop=True),
                     reads=[p.b_ident_f, p.b_neg], writes=[p.b_P[5]])
            for hh in range(H):
                S.op("act", lambda e: e.activation(out=p.xn2[:, hh * 128:(hh + 1) * 128], in_=p.P[5][:, hh * 128:(hh + 1) * 128], func=AF.Exp,
                                                   bias=g_bj(hh)), reads=[p.b_P[5], bg], writes=[p.b_xn2])
            for hh in range(H):
                for half in range(2):
                    S.op("pe", lambda e: e.matmul(p.P[4][:, hh * 128:(hh + 1) * 128], lhsT=p.kT[:, 2 * hh + half, cs], rhs=p.qT[:, 2 * hh + half, cs],
                                                  start=(half == 0), stop=(half == 1)), reads=[p.b_kT, p.b_qT], writes=[p.b_P[4]])
            S.op("dve", lambda e: e.tensor_tensor(out=p.sTm[:], in0=p.P[4][:, :], in1=p.xn2[:, 0:512], op=ALU.mult),
                 reads=[p.b_P[4], p.b_xn2], writes=[p.b_sTm])
        if full:
            for hh in range(H):
                S.op("pe", lambda e: e.matmul(p.P[5][:, 2 * hh:2 * hh + 1], lhsT=p.sTm[:, hh * 128:(hh + 1) * 128], rhs=p.ones_b[:, 0:1], start=True, stop=True),
                     reads=[p.b_sTm, p.b_ones_b], writes=[p.b_P[5]])
                for half in range(2):
                    i = 2 * hh + half
                    S.op("pe", lambda e: e.matmul(p.P[5][:, 2 * hh + 1:2 * hh + 2], lhsT=p.qT[:, i, cs], rhs=p.nstb[:, i:i + 1], start=(half == 0), stop=(half == 1)),
                         reads=[p.b_qT, p.b_nstb], writes=[p.b_P[5]])
            S.op("dve", lambda e: e.tensor_copy(out=sm[:, 64:72], in_=p.P[5][:, 0:8]), reads=[p.b_P[5]], writes=[bs])
            dv = sm[:, 64:72].rearrange("p (h t) -> p h t", t=2)
            S.op("dve", lambda e: e.tensor_tensor(out=sm[:, 72:76], in0=dv[:, :, 1], in1=gm[:, 6, col:col + 4], op=ALU.mult), reads=[bs, bg], writes=[bs])
            S.op("dve", lambda e: e.tensor_tensor(out=sm[:, 72:76], in0=sm[:, 72:76], in1=dv[:, :, 0], op=ALU.add), reads=[bs], writes=[bs])
            S.op("dve", lambda e: e.scalar_tensor_tensor(out=sm[:, 76:80], in0=sm[:, 72:76], scalar=-1.0, in1=sm[:, 72:76], op0=ALU.mult, op1=ALU.max),
                 reads=[bs], writes=[bs])
            S.op("dve", lambda e: e.tensor_scalar(out=sm[:, 76:80], in0=sm[:, 76:80], scalar1=1.0, scalar2=None, op0=ALU.max), reads=[bs], writes=[bs])
            S.op("dve", lambda e: e.reciprocal(out=sm[:, 80:84], in_=sm[:, 76:80]), reads=[bs], writes=[bs])
        for hh in range(H):
            if full:
                S.op("pe", lambda e: e.matmul(p.P[0][:, :], lhsT=p.sTm[:, hh * 128:(hh + 1) * 128], rhs=p.v_all[:, c, hh * 512:(hh + 1) * 512],
                                              start=True, stop=True), reads=[p.b_sTm, p.b_v[c]], writes=[p.b_P[0]])
                for half in range(2):
                    i = 2 * hh + half
                    S.op("pe", lambda e: e.matmul(p.P[1][:, :], lhsT=p.qT[:, i, cs], rhs=p.Rb[:, i, :], start=(half == 0), stop=(half == 1)),
                         reads=[p.b_qT, p.b_Rb[i]], writes=[p.b_P[1]])
                S.op("act", lambda e: e.activation(out=p.wk[4][:], in_=p.P[1][:, :], func=AF.Copy, scale=g_wi(hh)),
                     reads=[p.b_P[1], bg], writes=[p.b_wk[4]])
                S.op("dve", lambda e: e.tensor_tensor(out=p.wk[hh][:], in0=p.wk[4][:], in1=p.P[0][:, :], op=ALU.add),
                     reads=[p.b_wk[4], p.b_P[0]], writes=[p.b_wk[hh]])
                so = p.sg_all[:, c, hh * 512:(hh + 1) * 512]
                S.op("dve", lambda e: e.scalar_tensor_tensor(out=p.wk[hh][:], in0=p.wk[hh][:], scalar=sm[:, 80 + hh:81 + hh], in1=so, op0=ALU.mult, op1=ALU.mult),
                     reads=[p.b_wk[hh], bs, p.b_xg[c]], writes=[p.b_wk[hh]])
            self.state_update(c, hh, g_sp(hh), [bg])
            if full:
                self.refresh_Rb(hh)
        for i in range(8):
            hh, half = i // 2, i % 2
            S.op("pe", lambda e: e.matmul(p.P[5][:, 16 + i:17 + i], lhsT=p.kz[:, hh * 256 + half * 128: hh * 256 + (half + 1) * 128],
                                          rhs=p.ones_b[:, 0:1], start=True, stop=True), reads=[p.b_kz, p.b_ones_b], writes=[p.b_P[5]])
        S.op("dve", lambda e: e.tensor_tensor(out=p.nst[:], in0=p.nst[:], in1=p.gmx[:, c, :], op=ALU.mult), reads=[p.b_nst, p.b_gmx], writes=[p.b_nst])
        S.op("dve", lambda e: e.tensor_tensor(out=p.nst[:], in0=p.nst[:], in1=p.P[5][:, 16:24], op=ALU.add), reads=[p.b_nst, p.b_P[5]], writes=[p.b_nst])
        if full:
            S.op("act", lambda e: e.activation(out=p.nstb[:], in_=p.nst[:], func=AF.Copy), reads=[p.b_nst], writes=[p.b_nstb])
        if full:
            self.groupnorm_heads([0, 1, 2, 3], c, gate=False)
            self.make_ynT(c, p.ml_gnT, p.b_ml_gnT)

    def ml_layer(self):
        p, S = self, self.S
        if p.phase in (None, 4):
            self.ml_part_a()
        if p.phase is None:
            self.exchange(4)
        if p.phase in (None, 5):
            self.ml_part_b()

    def ml_part_a(self):
        p, S = self, self.S
        for i in range(8):
            S.op("dve", lambda e: e.memset(p.R[:, i, :], 0.0), writes=[p.b_R[i]])
        S.op("dve", lambda e: e.memset(p.nst[:], 0.0), writes=[p.b_nst])
        S.op("dve", lambda e: e.memset(p.fsum[:], 0.0), writes=[p.b_fsum])
        for g in range(NG):
            self.ml_group(g, full=False)
        S.op("dve", lambda e: e.memset(p.wk[4][:], 0.0), writes=[p.b_wk[4]])
        S.op("dve", lambda e: e.tensor_copy(out=p.wk[4][:, 0:8], in_=p.nst[:]), reads=[p.b_nst], writes=[p.b_wk[4]])
        S.op("dve", lambda e: e.tensor_copy(out=p.wk[4][:, 8:12], in_=p.fsum[:]), reads=[p.b_fsum], writes=[p.b_wk[4]])
        pairs = [(p.loc[4][i * 128:(i + 1) * 128, :], p.R[:, i, :]) for i in range(8)] + [(p.loc[4][1024:1152, :], p.wk[4][:])]
        self.dma_group("sp", "loc4", pairs, reads=p.b_R + [p.b_wk[4]], writes=[p.b_loc[4]])

    def ml_part_b(self):
        p, S = self, self.S
        gt = p.gath[4]
        pairs = [(p.stage[cp:cp + 1, 0:4], gt[cp * 1152 + 1024: cp * 1152 + 1025, 8:12]) for cp in range(NCORES)]
        self.dma_group("sp", "stage", pairs, reads=[p.b_gath[4]], writes=[p.b_stage])
        for cp in range(NCORES):
            S.op("dve", lambda e: e.tensor_tensor(out=p.stage[0:8, 32 + cp * 4:36 + cp * 4], in0=p.msel[0:8, cp * 4:cp * 4 + 4], in1=p.stage[0:8, 0:4], op=ALU.mult),
                 reads=[p.b_stage, p.b_msel], writes=[p.b_stage])
        S.op("pe", lambda e: e.matmul(p.P[5][:, 0:32], lhsT=p.ones_f[0:8, :], rhs=p.stage[0:8, 32:64], start=True, stop=True),
             reads=[p.b_ones_f, p.b_stage], writes=[p.b_P[5]])
        S.op("act", lambda e: e.activation(out=p.mcoef[:], in_=p.P[5][:, 0:32], func=AF.Exp), reads=[p.b_P[5]], writes=[p.b_mcoef])
        S.op("dve", lambda e: e.tensor_tensor(out=p.mcoef[:], in0=p.mcoef[:], in1=p.valid[:], op=ALU.mult), reads=[p.b_mcoef, p.b_valid], writes=[p.b_mcoef])
        self.combine_state(4, p.mcoef, p.b_mcoef, nrows=9)
        S.op("dve", lambda e: e.memset(p.nst[:], 0.0), writes=[p.b_nst])
        for cp in range(NCORES):
            tmp, bt = p.wk[cp % 2], p.b_wk[cp % 2]
            r0 = cp * 1152 + 1024
            S.dma("sp", tmp[:, 0:8], gt[r0:r0 + 128, 0:8], reads=[p.b_gath[4]], writes=[bt], key=f"cmb{cp % 2}")
            for hh in range(H):
                S.op("dve", lambda e: e.scalar_tensor_tensor(out=p.nst[:, 2 * hh:2 * hh + 2], in0=tmp[:, 2 * hh:2 * hh + 2],
                                                             scalar=p.mcoef[:, cp * H + hh:cp * H + hh + 1], in1=p.nst[:, 2 * hh:2 * hh + 2],
                                                             op0=ALU.mult, op1=ALU.add), reads=[bt, p.b_mcoef, p.b_nst], writes=[p.b_nst])
        for hh in range(H):
            self.refresh_Rb(hh)
        S.op("act", lambda e: e.activation(out=p.nstb[:], in_=p.nst[:], func=AF.Copy), reads=[p.b_nst], writes=[p.b_nstb])
        self.load_gate(1, 0)
        for g in range(NG):
            self.ml_group(g, full=True)

    def _body(self):
        p, S = self, self.S
        ph = p.phase
        if ph in (None, 1, 2):
            self.ret_layer()
        if p.stop_after == "dbg_ret":
            return
        if p.stop_after == "ret":
            return self.copy_out(p.xa, p.b_xa)
        if ph is None:
            self.exchange(2)
        if ph in (None, 3):
            self.ffn_layer(0, p.xa, p.b_xa, p.xb, p.b_xb, 2, 3, final=False)
        if p.stop_after == "ffn0":
            return self.copy_out(p.xb, p.b_xb)
        if ph is None:
            self.exchange(3)
        if ph in (None, 4, 5):
            self.ml_layer()
        if p.stop_after == "ml":
            return self.copy_out(p.xa, p.b_xa)
        if ph is None:
            self.exchange(5)
        if ph in (None, 6):
            S.dma("sp", p.cos[:], p.final_g[0:1, 0:512].partition_broadcast(128).rearrange("p o n -> p (o n)"), writes=[p.b_cos], key="fg0")
            S.dma("sp", p.sin[:], p.final_g[0:1, 512:1024].partition_broadcast(128).rearrange("p o n -> p (o n)"), writes=[p.b_sin], key="fg1")
            self.ffn_layer(1, p.xa, p.b_xa, p.out, p.b_out, 5, None, final=True)

    def copy_out(self, src, src_bufs):
        p, S = self, self.S
        for g in range(NG):
            for c in range(G):
                r0 = g * GT + c * CH
                S.dma("sp", p.x_g[:, c, :], src[r0:r0 + CH, :], reads=src_bufs, writes=[p.b_xg[c]], key=f"xg{c}")
            pairs = [(p.out[g * GT + c * CH: g * GT + (c + 1) * CH, :], p.x_g[:, c, :]) for c in range(G)]
            self.dma_group("sp", "st_ou", pairs, reads=p.b_xg, writes=[p.b_out[g]])

    def _finish(self):
        p, S = self, self.S
        bufs = list(p.b_out) + [p.b_loc[k] for k in p.b_loc] + p.dbg_bufs + list(p.b_xa) + list(p.b_xb) + [p.b_modscr, p.b_modT_o, p.b_kvst]
        S.wait_bufs("sp", bufs)
        S.barrier()


_PROG_CACHE = {}
MODE = "host6"
STOP_AFTER = None


def _get_prog(mode, stop_after, phase=None):
    key = (mode, stop_after, phase)
    if key not in _PROG_CACHE:
        pr = Prog("host" if mode.startswith("host") else mode, stop_after, phase)
        pr.build()
        _PROG_CACHE[key] = pr
    return _PROG_CACHE[key]


def _in_maps(inputs):
    f = lambda a: np.ascontiguousarray(np.asarray(a), dtype=np.float32)
    tabs, lg = _const_tables()
    x = f(inputs["x"]).reshape(SEQ, D)
    pos = np.ascontiguousarray(np.asarray(inputs["positions"]).astype(np.int32)).reshape(SEQ)
    shared = {
        "cT": np.ascontiguousarray(f(inputs["c"]).reshape(KC, 128).T),
        "ada_w": f(inputs["ada_w"]),
        "ada_bT": np.ascontiguousarray(f(inputs["ada_b"]).reshape(2, 48, 128).transpose(2, 0, 1)),
        "ntgT": np.ascontiguousarray(f(inputs["norm_tok_g"]).reshape(2, KC, 128).transpose(2, 0, 1)),
        "nfgT": np.ascontiguousarray(f(inputs["norm_ffn_g"]).reshape(2, KC, 128).transpose(2, 0, 1)),
        "ret_w_in": f(inputs["ret_w_in"]).reshape(D, 6144),
        "ret_gnT": np.ascontiguousarray(f(inputs["ret_gn_g"]).reshape(16, 128).T),
        "ret_w_out": f(inputs["ret_w_out"]).reshape(2048, D),
        "ml_w_in": f(inputs["ml_w_in"]).reshape(D, 6152),
        "ml_b_gate": f(inputs["ml_b_gate"]).reshape(1, 8),
        "ml_cwT": np.ascontiguousarray(f(inputs["ml_conv_w"]).reshape(64, 128).T),
        "ml_cbT": np.ascontiguousarray(f(inputs["ml_conv_b"]).reshape(16, 128).T),
        "ml_gnT": np.ascontiguousarray(f(inputs["ml_gn_g"]).reshape(16, 128).T),
        "ml_w_out": f(inputs["ml_w_out"]).reshape(2048, D),
        "ffn_w_up": f(inputs["ffn_w_up"]),
        "ffn_cwT": np.ascontiguousarray(f(inputs["ffn_conv_w"]).reshape(2, 132, 128).transpose(2, 0, 1)),
        "ffn_cbT": np.ascontiguousarray(f(inputs["ffn_conv_b"]).reshape(2, 44, 128).transpose(2, 0, 1)),
        "ffn_w_down": f(inputs["ffn_w_down"]),
        "final_g": f(inputs["final_g"]).reshape(1, D),
    }
    shared.update(tabs)
    maps = []
    for c in range(NCORES):
        m = dict(shared)
        m["x"] = x[c * T:(c + 1) * T]
        m["pos"] = pos[c * T:(c + 1) * T].reshape(1, T)
        m.update(_core_tables(c, lg))
        maps.append(m)
    return maps


def _launch(pr, maps, extra):
    ms = []
    for c, m in enumerate(maps):
        mm = {k: v for k, v in m.items() if k in pr.in_names}
        for k, v in extra.items():
            if k in pr.in_names:
                mm[k] = v[c] if isinstance(v, list) else v
        ms.append(mm)
    return run_bass_kernel_spmd(pr.nc, ms, core_ids=list(range(NCORES))).results


def kernel(**inputs):
    maps = _in_maps(inputs)
    if MODE == "host6":
        extra = {}
        res = None
        for ph in range(1, 7):
            pr = _get_prog("host", None, ph)
            res = _launch(pr, maps, extra)
            for k in (1, 2, 3, 4, 5):
                if f"loc{k}" in res[0]:
                    extra[f"gath{k}"] = np.concatenate([res[c][f"loc{k}"] for c in range(NCORES)], axis=0)
            for nm in ("xa", "xb"):
                if nm in res[0]:
                    extra[nm] = [res[c][nm] for c in range(NCORES)]
            if "kst_o" in res[0]:
                extra["kst_i"] = [res[c]["kst_o"] for c in range(NCORES)]
                extra["vst_i"] = [res[c]["vst_o"] for c in range(NCORES)]
            if "modT_o" in res[0]:
                extra["modT_i"] = [res[c]["modT_o"] for c in range(NCORES)]
                extra["modscr_i"] = [res[c]["modscr_o"] for c in range(NCORES)]
    elif MODE == "host":
        pr = _get_prog("host", STOP_AFTER)
        extra = {f"gath{k}": np.zeros((NCORES * r, cdim), np.float32) for k, (r, cdim) in EX_SIZES.items()}
        order = {None: [1, 2, 3, 4, 5], "ret": [1], "ffn0": [1, 2], "ml": [1, 2, 3, 4]}[STOP_AFTER]
        res = None
        for step in range(len(order) + 1):
            res = _launch(pr, maps, extra)
            if step < len(order):
                k = order[step]
                extra[f"gath{k}"] = np.concatenate([res[c][f"loc{k}"] for c in range(NCORES)], axis=0)
    else:
        pr = _get_prog("cc", None)
        res = _launch(pr, maps, {})
    out = np.concatenate([res[c]["out"] for c in range(NCORES)], axis=0)
    return out.reshape(1, SEQ, D).astype(np.float32)
```
